# Optimizing a Trainium2 kernel written in Bass

```python
import math
import jax, jax.numpy as jnp
from jax import lax
import numpy as np

D_MODEL = 2048
BATCH = 4
SEQ = 4096
DEPTH = 1

PLE_DIM = 256
ROPE_THETA = 500000.0
ROPE_FRACTION = 4
Q_BLOCK = 128
NORM_EPS = 1e-6
NEG_INF = -1e30

DIFF_HEADS = 8
DIFF_SUB_DIM = 64
DIFF_V_DIM = 2 * DIFF_SUB_DIM
NSA_HEADS = 16
NSA_KV_GROUPS = 2
NSA_GROUP_SIZE = NSA_HEADS // NSA_KV_GROUPS
NSA_HEAD_DIM = 64
CMP_BLOCK = 32
CMP_STRIDE = 16
CMP_HIDDEN = 256
SLC_BLOCK = 64
SLC_TOP_N = 16
SLC_FORCED_BONUS = 1e4
WINDOW = 512
D_FF = 4 * D_MODEL

DIFF_W = DIFF_HEADS * DIFF_V_DIM
NSA_W = NSA_HEADS * NSA_HEAD_DIM
NSA_KV_W = NSA_KV_GROUPS * NSA_HEAD_DIM
IN_SIZES = (DIFF_W, DIFF_W, DIFF_W,
            NSA_W,
            NSA_KV_W, NSA_KV_W,
            NSA_KV_W, NSA_KV_W,
            NSA_KV_W, NSA_KV_W,
            3 * NSA_HEADS,
            D_MODEL, D_MODEL)
IN_W = 3 * DIFF_W + NSA_W + 6 * NSA_KV_W + 3 * NSA_HEADS + 2 * D_MODEL

kernel_name = 'hybrid_diffattn_nsa_sqrelu_ple'


def rms_norm(x, g=None):
    xf = x.astype(jnp.float32)
    y = xf * lax.rsqrt(jnp.mean(xf * xf, axis=-1, keepdims=True) + NORM_EPS)
    if g is not None:
        y = y * g.astype(jnp.float32)
    return y.astype(x.dtype)


def rope_tables(positions, rot_dim):
    inv_freq = jnp.power(ROPE_THETA, -jnp.arange(0, rot_dim, 2, dtype=jnp.float32) / rot_dim)
    ang = positions.astype(jnp.float32)[..., None] * inv_freq
    return jnp.cos(ang), jnp.sin(ang)


def apply_partial_rope(x, cos, sin):
    half = cos.shape[-1]
    r = 2 * half
    x1 = x[..., :half].astype(jnp.float32)
    x2 = x[..., half:r].astype(jnp.float32)
    c = cos[:, :, None, :]
    s = sin[:, :, None, :]
    rot = jnp.concatenate([x1 * c - x2 * s, x2 * c + x1 * s], axis=-1).astype(x.dtype)
    return jnp.concatenate([rot, x[..., r:]], axis=-1)


def masked_softmax(s, mask, axis=-1):
    p = jax.nn.softmax(jnp.where(mask, s, NEG_INF), axis=axis)
    return p * mask


def to_blocks(a):
    b, s = a.shape[:2]
    return jnp.moveaxis(a.reshape((b, s // Q_BLOCK, Q_BLOCK) + a.shape[2:]), 1, 0)


def from_blocks(a):
    a = jnp.moveaxis(a, 0, 1)
    return a.reshape((a.shape[0], a.shape[1] * a.shape[2]) + a.shape[3:])


def split_cols(z, sizes):
    outs, off = [], 0
    for size in sizes:
        outs.append(z[..., off:off + size])
        off += size
    return outs


def diff_attention(q, k, v, lam, lambda_init, subln_g):
    b, s = q.shape[:2]
    scale = DIFF_SUB_DIM ** -0.5
    key_pos = jnp.arange(s)
    vf = v.astype(jnp.float32)

    def block(args):
        qb, bi = args
        sc = jnp.einsum('bqhcd,bkhcd->bhcqk', qb, k,
                        preferred_element_type=jnp.float32) * scale
        q_pos = bi * Q_BLOCK + jnp.arange(Q_BLOCK)
        mask = key_pos[None, :] <= q_pos[:, None]
        prob = masked_softmax(sc, mask)
        attn = prob[:, :, 0] - lam * prob[:, :, 1]
        o = jnp.einsum('bhqk,bkhd->bqhd', attn, vf)
        o = rms_norm(o, subln_g) * (1.0 - lambda_init)
        return o.astype(v.dtype)

    out = lax.map(block, (to_blocks(q), jnp.arange(s // Q_BLOCK)))
    return from_blocks(out).reshape(b, s, DIFF_W)


def compress_tokens(kv, pos_emb, w1, w2):
    b, s, g, d = kv.shape
    n_cmp = (s - CMP_BLOCK) // CMP_STRIDE + 1
    idx = np.arange(n_cmp)[:, None] * CMP_STRIDE + np.arange(CMP_BLOCK)[None, :]
    blocks = kv[:, idx] + pos_emb[:, None, :]
    blocks = blocks.transpose(0, 1, 3, 2, 4).reshape(b, n_cmp, g, CMP_BLOCK * d)
    return jax.nn.gelu(blocks @ w1) @ w2


def nsa_attention(q_rot, q_plain, kc, vc, ks, vs, kw, vw, gates):
    b, s, g, r, d = q_rot.shape
    n_cmp = kc.shape[1]
    n_sel = s // SLC_BLOCK
    top_n = min(SLC_TOP_N, n_sel)
    scale = NSA_HEAD_DIM ** -0.5
    cmp_end = jnp.arange(n_cmp) * CMP_STRIDE + CMP_BLOCK - 1
    cs = np.arange(n_cmp) * CMP_STRIDE
    ss = np.arange(n_sel) * SLC_BLOCK
    overlap = jnp.asarray(((cs[:, None] < ss[None, :] + SLC_BLOCK) &
                           (cs[:, None] + CMP_BLOCK > ss[None, :])).astype(np.float32))
    vc_f = vc.astype(jnp.float32)
    ks_blocks = ks.reshape(b, n_sel, SLC_BLOCK, g, d).transpose(0, 3, 1, 2, 4)
    vs_blocks = vs.astype(jnp.float32).reshape(b, n_sel, SLC_BLOCK, g, d).transpose(0, 3, 1, 2, 4)
    kw_pad = jnp.pad(kw, ((0, 0), (WINDOW, 0), (0, 0), (0, 0)))
    vw_pad = jnp.pad(vw.astype(jnp.float32), ((0, 0), (WINDOW, 0), (0, 0), (0, 0)))
    b_ix = jnp.arange(b)[:, None, None, None]
    g_ix = jnp.arange(g)[None, :, None, None]
    sel_j = jnp.arange(n_sel)

    def block(args):
        qr, qp, gb, bi = args
        q_pos = bi * Q_BLOCK + jnp.arange(Q_BLOCK)
        s_c = jnp.einsum('bqgrd,bngd->bgrqn', qp, kc,
                         preferred_element_type=jnp.float32) * scale
        p_c = masked_softmax(s_c, cmp_end[None, :] <= q_pos[:, None])
        o_c = jnp.einsum('bgrqn,bngd->bqgrd', p_c, vc_f)
        imp = jnp.einsum('bgrqn,nj->bgqj', p_c, overlap)
        q_blk = q_pos // SLC_BLOCK
        valid = sel_j[None, :] <= q_blk[:, None]
        forced = ((sel_j[None, :] == 0) | (sel_j[None, :] == q_blk[:, None]) |
                  (sel_j[None, :] == q_blk[:, None] - 1))
        score = jnp.where(valid, imp + SLC_FORCED_BONUS * forced, -1.0)
        _, sel = lax.top_k(score, top_n)
        k_sel = ks_blocks[b_ix, g_ix, sel]
        v_sel = vs_blocks[b_ix, g_ix, sel]
        s_s = jnp.einsum('bqgrd,bgqtkd->bgrqtk', qr, k_sel,
                         preferred_element_type=jnp.float32) * scale
        sel_pos = sel[..., None] * SLC_BLOCK + jnp.arange(SLC_BLOCK)
        m_s = (sel_pos <= q_pos[None, None, :, None, None])[:, :, None]
        p_s = masked_softmax(s_s, m_s, axis=(-2, -1))
        o_s = jnp.einsum('bgrqtk,bgqtkd->bqgrd', p_s, v_sel)
        start = bi * Q_BLOCK
        k_win = lax.dynamic_slice_in_dim(kw_pad, start, WINDOW + Q_BLOCK, axis=1)
        v_win = lax.dynamic_slice_in_dim(vw_pad, start, WINDOW + Q_BLOCK, axis=1)
        win_pos = start - WINDOW + jnp.arange(WINDOW + Q_BLOCK)
        dist = q_pos[:, None] - win_pos[None, :]
        m_w = (dist >= 0) & (dist < WINDOW) & (win_pos[None, :] >= 0)
        s_w = jnp.einsum('bqgrd,bkgd->bgrqk', qr, k_win,
                         preferred_element_type=jnp.float32) * scale
        p_w = masked_softmax(s_w, m_w)
        o_w = jnp.einsum('bgrqk,bkgd->bqgrd', p_w, v_win)
        gf = gb.astype(jnp.float32)
        o = gf[..., 0:1] * o_c + gf[..., 1:2] * o_s + gf[..., 2:3] * o_w
        return o.astype(q_rot.dtype)

    out = lax.map(block, (to_blocks(q_rot), to_blocks(q_plain), to_blocks(gates),
                          jnp.arange(s // Q_BLOCK)))
    return from_blocks(out).reshape(b, s, NSA_W)


def hybrid_layer(x, p_i, cos, sin, lambda_init, norm_mix, w_in, diff_q_norm, diff_k_norm,
                 diff_lambda, diff_subln, nsa_q_norm, nsa_k_norm, cmp_pos, cmp_w1, cmp_w2,
                 w_proj_diff, w_proj_nsa, w_out, norm_mlp, w_mlp_up, w_mlp_down,
                 w_ple_proj, norm_ple, w_ple_gate):
    b, s, _ = x.shape
    h = rms_norm(x, norm_mix)
    z = h @ w_in
    (dq, dk, dv, nq, kc_r, vc_r, ks_r, vs_r, kw_r, vw_r,
     ng, g_a, g_b) = split_cols(z, IN_SIZES)

    dq = rms_norm(dq.reshape(b, s, 2 * DIFF_HEADS, DIFF_SUB_DIM), diff_q_norm)
    dk = rms_norm(dk.reshape(b, s, 2 * DIFF_HEADS, DIFF_SUB_DIM), diff_k_norm)
    dq = apply_partial_rope(dq, cos, sin).reshape(b, s, DIFF_HEADS, 2, DIFF_SUB_DIM)
    dk = apply_partial_rope(dk, cos, sin).reshape(b, s, DIFF_HEADS, 2, DIFF_SUB_DIM)
    dv = dv.reshape(b, s, DIFF_HEADS, DIFF_V_DIM)
    lp = diff_lambda.astype(jnp.float32)
    lam = (jnp.exp(jnp.sum(lp[0] * lp[1])) - jnp.exp(jnp.sum(lp[2] * lp[3]))
           + lambda_init)
    y_a = diff_attention(dq, dk, dv, lam, lambda_init, diff_subln)

    nq = rms_norm(nq.reshape(b, s, NSA_HEADS, NSA_HEAD_DIM), nsa_q_norm)
    q_rot = apply_partial_rope(nq, cos, sin)
    grp = (b, s, NSA_KV_GROUPS, NSA_GROUP_SIZE, NSA_HEAD_DIM)
    kv_shape = (b, s, NSA_KV_GROUPS, NSA_HEAD_DIM)
    kc = rms_norm(compress_tokens(kc_r.reshape(kv_shape), cmp_pos[0], cmp_w1[0], cmp_w2[0]),
                  nsa_k_norm)
    vc = compress_tokens(vc_r.reshape(kv_shape), cmp_pos[1], cmp_w1[1], cmp_w2[1])
    ks = apply_partial_rope(rms_norm(ks_r.reshape(kv_shape), nsa_k_norm), cos, sin)
    kw = apply_partial_rope(rms_norm(kw_r.reshape(kv_shape), nsa_k_norm), cos, sin)
    gates = jax.nn.sigmoid(ng.reshape(b, s, NSA_KV_GROUPS, NSA_GROUP_SIZE, 3))
    y_b = nsa_attention(q_rot.reshape(grp), nq.reshape(grp), kc, vc, ks,
                        vs_r.reshape(kv_shape), kw, vw_r.reshape(kv_shape), gates)

    merged = (jax.nn.sigmoid(g_a) * (y_a @ w_proj_diff) +
              jax.nn.sigmoid(g_b) * (y_b @ w_proj_nsa))
    x = x + merged @ w_out

    h2 = rms_norm(x, norm_mlp)
    x = x + jnp.square(jax.nn.relu(h2 @ w_mlp_up)) @ w_mlp_down

    e = rms_norm(p_i @ w_ple_proj, norm_ple)
    x = x + jax.nn.sigmoid(rms_norm(x) @ w_ple_gate) * e
    return x


def setup_inputs(seed: int = 0) -> dict:
    key = jax.random.key(seed)
    ks = jax.random.split(key, 24)

    def nrm(k, shape, scale):
        return jax.random.normal(k, shape, dtype=jnp.float32) * scale

    def gain(k, n):
        return 1.0 + 0.1 * jax.random.normal(k, (DEPTH, n), dtype=jnp.float32)

    return {
        'x': nrm(ks[0], (BATCH, SEQ, D_MODEL), 1.0),
        'p': nrm(ks[1], (DEPTH, BATCH, SEQ, PLE_DIM), 1.0),
        'positions': jnp.broadcast_to(jnp.arange(SEQ, dtype=jnp.int32), (BATCH, SEQ)),
        'norm_mix': gain(ks[2], D_MODEL),
        'w_in': nrm(ks[3], (DEPTH, D_MODEL, IN_W), D_MODEL ** -0.5),
        'diff_q_norm': gain(ks[4], DIFF_SUB_DIM),
        'diff_k_norm': gain(ks[5], DIFF_SUB_DIM),
        'diff_lambda': nrm(ks[6], (DEPTH, 4, DIFF_SUB_DIM), 0.1),
        'diff_subln': gain(ks[7], DIFF_V_DIM),
        'nsa_q_norm': gain(ks[8], NSA_HEAD_DIM),
        'nsa_k_norm': gain(ks[9], NSA_HEAD_DIM),
        'cmp_pos': nrm(ks[10], (DEPTH, 2, CMP_BLOCK, NSA_HEAD_DIM), 0.1),
        'cmp_w1': nrm(ks[11], (DEPTH, 2, CMP_BLOCK * NSA_HEAD_DIM, CMP_HIDDEN),
                      (CMP_BLOCK * NSA_HEAD_DIM) ** -0.5),
        'cmp_w2': nrm(ks[12], (DEPTH, 2, CMP_HIDDEN, NSA_HEAD_DIM), CMP_HIDDEN ** -0.5),
        'w_proj_diff': nrm(ks[13], (DEPTH, DIFF_W, D_MODEL), DIFF_W ** -0.5),
        'w_proj_nsa': nrm(ks[14], (DEPTH, NSA_W, D_MODEL), NSA_W ** -0.5),
        'w_out': nrm(ks[15], (DEPTH, D_MODEL, D_MODEL), D_MODEL ** -0.5),
        'norm_mlp': gain(ks[16], D_MODEL),
        'w_mlp_up': nrm(ks[17], (DEPTH, D_MODEL, D_FF), D_MODEL ** -0.5),
        'w_mlp_down': nrm(ks[18], (DEPTH, D_FF, D_MODEL), D_FF ** -0.5),
        'w_ple_proj': nrm(ks[19], (DEPTH, PLE_DIM, D_MODEL), PLE_DIM ** -0.5),
        'norm_ple': gain(ks[20], D_MODEL),
        'w_ple_gate': nrm(ks[21], (DEPTH, D_MODEL, D_MODEL), D_MODEL ** -0.5),
    }


def reference(x, p, positions, norm_mix, w_in, diff_q_norm, diff_k_norm, diff_lambda,
              diff_subln, nsa_q_norm, nsa_k_norm, cmp_pos, cmp_w1, cmp_w2, w_proj_diff,
              w_proj_nsa, w_out, norm_mlp, w_mlp_up, w_mlp_down, w_ple_proj, norm_ple,
              w_ple_gate):
    cos, sin = rope_tables(positions, NSA_HEAD_DIM // ROPE_FRACTION)
    for i in range(DEPTH):
        lambda_init = 0.8 - 0.6 * math.exp(-0.3 * i)
        x = hybrid_layer(x, p[i], cos, sin, lambda_init, norm_mix[i], w_in[i],
                         diff_q_norm[i], diff_k_norm[i], diff_lambda[i], diff_subln[i],
                         nsa_q_norm[i], nsa_k_norm[i], cmp_pos[i], cmp_w1[i], cmp_w2[i],
                         w_proj_diff[i], w_proj_nsa[i], w_out[i], norm_mlp[i],
                         w_mlp_up[i], w_mlp_down[i], w_ple_proj[i], norm_ple[i],
                         w_ple_gate[i])
    return x
```

```python
import contextlib
import math
import numpy as np
import ml_dtypes
import concourse.bass as bass
import concourse.mybir as mybir
from concourse.bass_utils import run_bass_kernel_spmd

F32, BF16, I32 = mybir.dt.float32, mybir.dt.bfloat16, mybir.dt.int32
AF = mybir.ActivationFunctionType
ALU = mybir.AluOpType
AX = mybir.AxisListType

D = 2048
SEQ = 4096
NOWN = 2048
IN_W = 9008
DFF = 8192
EPS = 1e-6
TWO_PI = 2.0 * math.pi


class Sched:
    def __init__(self, nc, es):
        self.nc = nc
        self.eng = {"pe": nc.tensor, "act": nc.scalar, "dve": nc.vector, "pool": nc.gpsimd, "sp": nc.sync}
        self.sem = {e: es.enter_context(nc.semaphore("s_" + e)) for e in ("pe", "act", "dve", "pool")}
        self.cnt = {e: 0 for e in self.sem}
        self.NDS = 12
        self.dsem = {q: [es.enter_context(nc.semaphore(f"d_{q}{i}")) for i in range(self.NDS)] for q in ("sp", "pool", "act")}
        self.dcnt = {q: [0] * self.NDS for q in self.dsem}
        self.dnext = {q: 0 for q in self.dsem}
        self.waited = {e: {} for e in self.eng}
        self.res = {}
        self.semname = {}
        self.n_wait = 0
        self.n_inst = 0

    def _wait(self, e, tok):
        if tok is None:
            return
        sem, val, owner = tok
        if e == "pe" and owner == "pe":
            return
        w = self.waited[e]
        if w.get(id(sem), 0) >= val:
            return
        self.eng[e].wait_ge(sem, val)
        self.n_wait += 1
        w[id(sem)] = val

    def _deps(self, e, reads, writes):
        for k in reads:
            r = self.res.get(k)
            if r:
                self._wait(e, r[0])
        for k in writes:
            r = self.res.get(k)
            if r:
                self._wait(e, r[0])
                for t in r[1].values():
                    self._wait(e, t)

    def _commit(self, tok, reads, writes):
        for k in reads:
            r = self.res.setdefault(k, [None, {}])
            r[1][id(tok[0])] = tok
        for k in writes:
            self.res[k] = [tok, {}]

    def op(self, e, reads, writes, meth, *args, **kw):
        ps_r = [k for k in reads if isinstance(k, tuple) and k[0] == "ps"]
        if ps_r:
            reads = [k for k in reads if k not in ps_r]
            writes = list(writes) + ps_r
        self._deps(e, reads, writes)
        self.cnt[e] += 1
        tok = (self.sem[e], self.cnt[e], e)
        getattr(self.eng[e], meth)(*args, **kw).then_inc(self.sem[e], 1)
        self.n_inst += 1
        self._commit(tok, reads, writes)
        return tok

    def dma(self, q, reads, writes, out, in_, **kw):
        i = self.dnext[q]
        self.dnext[q] = (i + 1) % self.NDS
        sem = self.dsem[q][i]
        if self.dcnt[q][i] > 0:
            self._wait(q, (sem, 16 * self.dcnt[q][i], "dma"))
        self._deps(q, reads, writes)
        self.dcnt[q][i] += 1
        tok = (sem, 16 * self.dcnt[q][i], "dma")
        self.eng[q].dma_start(out=out, in_=in_, **kw).then_inc(sem, 16)
        self.n_inst += 1
        self._commit(tok, reads, writes)
        return tok

    def barrier(self):
        toks = [(self.sem[e], self.cnt[e], "bar") for e in self.sem if self.cnt[e] > 0]
        for q in self.dsem:
            for i in range(self.NDS):
                if self.dcnt[q][i] > 0:
                    toks.append((self.dsem[q][i], 16 * self.dcnt[q][i], "dma"))
        for e in self.eng:
            for t in toks:
                self._wait(e, t)
        self.res = {}


PD_DEBUG = {"topk": True, "sel": True, "win": True, "skip_pc": False}


def own_block(j, hf):
    return 2 * j + ((j & 1) ^ hf)


def build_nc(dbg=False, stop_after=None):
    nc = bass.Bass("TRN2", target_bir_lowering=False)

    def din(name, shape, dt=F32):
        return nc.dram_tensor(name, list(shape), dt, kind="ExternalInput").ap()

    def dscr(name, shape, dt=BF16):
        if dbg:
            return nc.dram_tensor(name, list(shape), dt, kind="ExternalOutput").ap()
        return nc.dram_tensor(name, list(shape), dt).ap()

    x_all = din("x_all", [SEQ, D])
    x_own = din("x_own", [NOWN, D])
    p_own = din("p_own", [NOWN, 256])
    posT_all = din("posT_all", [128, 32], I32)
    posT_own = din("posT_own", [128, 16], I32)
    norm_mix = din("norm_mix", [1, D])
    w_in = din("w_in", [D, IN_W])
    diff_q_norm = din("diff_q_norm", [1, 64])
    diff_k_norm = din("diff_k_norm", [1, 64])
    diff_lambda = din("diff_lambda", [1, 256])
    diff_subln = din("diff_subln", [1, 128])
    nsa_q_norm = din("nsa_q_norm", [1, 64])
    nsa_k_norm = din("nsa_k_norm", [1, 64])
    cmp_posT = din("cmp_posT", [64, 2, 32])
    cmp_w1 = din("cmp_w1", [2 * 2048, 256])
    cmp_w2 = din("cmp_w2", [2 * 256, 64])
    w_proj_diff = din("w_proj_diff", [1024, D])
    w_proj_nsa = din("w_proj_nsa", [1024, D])
    w_out = din("w_out", [D, D])
    norm_mlp = din("norm_mlp", [1, D])
    w_mlp_up = din("w_mlp_up", [D, DFF])
    w_mlp_down = din("w_mlp_down", [DFF, D])
    w_ple_proj = din("w_ple_proj", [256, D])
    norm_ple = din("norm_ple", [1, D])
    w_ple_gate = din("w_ple_gate", [D, D])
    c_invf = din("c_invf", [128, 8])
    c_dmask = din("c_dmask", [128, 8, 512], BF16)
    c_m128 = din("c_m128", [128, 2, 8, 128], BF16)
    c_mc = din("c_mc", [128, 16, 2, 128], BF16)
    c_valid = din("c_valid", [128, 16, 64])
    c_addc = din("c_addc", [128, 16, 64])
    c_ovl = din("c_ovl", [128, 2, 64], BF16)
    c_eexp = din("c_eexp", [64, 32, 128], BF16)

    out_d = nc.dram_tensor("out", [NOWN, D], F32, kind="ExternalOutput").ap()

    W_IN = dscr("W_IN", [D, IN_W]) if not dbg else nc.dram_tensor("W_IN", [D, IN_W], BF16).ap()
    mk = lambda n, s: nc.dram_tensor(n, list(s), BF16).ap()
    CW1 = mk("CW1", [2 * 2048, 256]); CW2 = mk("CW2", [2 * 256, 64])
    WPD = mk("WPD", [1024, D]); WPN = mk("WPN", [1024, D]); WOUT = mk("WOUT", [D, D])
    WUP = mk("WUP", [D, DFF]); WDN = mk("WDN", [DFF, D]); WPP = mk("WPP", [256, D]); WPG = mk("WPG", [D, D])
    QDT = dscr("QDT", [8, 128, NOWN])
    KDT = dscr("KDT", [8, 128, SEQ])
    VD = dscr("VD", [SEQ, 1024])
    NQRT = dscr("NQRT", [8, 128, NOWN])
    NQPT = dscr("NQPT", [8, 128, NOWN])
    KCRT = dscr("KCRT", [128, SEQ]); VCRT = dscr("VCRT", [128, SEQ])
    KST = dscr("KST", [2, 128, SEQ]); KWT = dscr("KWT", [2, 128, SEQ])
    VS = dscr("VS", [SEQ, 128]); VW = dscr("VW", [SEQ, 128])
    SGT = dscr("SGT", [2, D, NOWN])
    YAT = dscr("YAT", [8, 128, NOWN])
    YBT = dscr("YBT", [8, 128, NOWN])

    es = contextlib.ExitStack()
    with es:
        S = Sched(nc, es)

        def sbt(stack, name, shape, dt):
            return stack.enter_context(nc.sbuf_tensor(name, list(shape), dt))

        PS = [es.enter_context(nc.psum_tensor(f"ps{i}", [128, 1024], F32)) for i in range(4)]

        def bank(i):
            return PS[i // 2][:, (i % 2) * 512:(i % 2) * 512 + 512], ("ps", i)

        def bank_bf(i):
            return PS[i // 2][:, (i % 2) * 512:(i % 2) * 512 + 512].bitcast(BF16), ("ps", i)

        ident = sbt(es, "ident", [128, 128], BF16)
        idf = sbt(es, "idf", [128, 128], F32)
        mhalf = sbt(es, "mhalf", [128, 64], F32)
        gates = sbt(es, "gates", [128, 16, 48], F32)
        CSA = sbt(es, "CSA", [128, 32, 16], F32)
        CSO = sbt(es, "CSO", [128, 16, 16], F32)
        S.op("pool", [], ["idf"], "iota", idf[:], pattern=[[1, 128]], base=0, channel_multiplier=-1,
             allow_small_or_imprecise_dtypes=True)
        S.op("dve", ["idf"], ["ident"], "tensor_single_scalar", out=ident[:], in_=idf[:], scalar=0.0, op=ALU.is_equal)
        S.op("pool", [], ["mhalf"], "memset", mhalf[:], -0.5)

        def rstd_from_ss(ss_ap, out_ap, n, keys_r, keys_w, inv_n):
            S.op("pool", keys_r, keys_w, "tensor_scalar", out=out_ap, in0=ss_ap, scalar1=inv_n, scalar2=EPS,
                 op0=ALU.mult, op1=ALU.add)
            P = out_ap.shape[0]
            S.op("pool", keys_w + ["mhalf"], keys_w, "tensor_tensor", out=out_ap, in0=out_ap, in1=mhalf[0:P, 0:n], op=ALU.pow)

        def conv(dst, src, R, key, step=256):
            for r0 in range(0, R, step):
                r1 = min(R, r0 + step)
                S.dma("pool", [], [(key, r0)], out=dst[r0:r1, :], in_=src[r0:r1, :], max_dma_last_dim=8192)

        for c0, n in ((0, 512), (512, 512), (3072, 512), (3584, 512), (4864, 48)) + tuple((4912 + i * 512, 512) for i in range(8)) + \
                ((1024, 512), (1536, 512), (2048, 512), (2560, 512), (4096, 512), (4608, 256)):
            S.dma("pool", [], [("W_IN", c0)], out=W_IN[:, c0:c0 + n], in_=w_in[:, c0:c0 + n], max_dma_last_dim=8192)
        pending_conv = []

        def conv_later(dst, src, R, key, step=256):
            for r0 in range(0, R, step):
                r1 = min(R, r0 + step)
                pending_conv.append((dst, src, r0, r1, key))

        def conv_pop(n=1):
            for _ in range(n):
                if pending_conv:
                    dst, src, r0, r1, key = pending_conv.pop(0)
                    S.dma("pool", [], [(key, r0)], out=dst[r0:r1, :], in_=src[r0:r1, :], max_dma_last_dim=8192)

        conv_later(CW1, cmp_w1, 4096, "CW1", 1024); conv_later(CW2, cmp_w2, 512, "CW2", 512)
        conv_later(WPD, w_proj_diff, 1024, "WPD"); conv_later(WPN, w_proj_nsa, 1024, "WPN"); conv_later(WOUT, w_out, D, "WOUT")
        conv_later(WUP, w_mlp_up, D, "WUP"); conv_later(WDN, w_mlp_down, DFF, "WDN", 1024)
        conv_later(WPP, w_ple_proj, 256, "WPP"); conv_later(WPG, w_ple_gate, D, "WPG")

        with contextlib.ExitStack() as ps_:
            invf = sbt(ps_, "invf", [128, 8], F32)
            S.dma("sp", [], ["invf"], out=invf[:], in_=c_invf[:, :])
            for nm, src, nb, CS in (("a", posT_all, 32, CSA), ("o", posT_own, 16, CSO)):
                pi = sbt(ps_, "pi" + nm, [128, nb], I32)
                pf = sbt(ps_, "pf" + nm, [128, nb], F32)
                ang = sbt(ps_, "ang" + nm, [128, nb, 8], F32)
                kf = sbt(ps_, "kf" + nm, [128, nb, 8], F32)
                ki = sbt(ps_, "ki" + nm, [128, nb, 8], I32)
                r0 = sbt(ps_, "r0" + nm, [128, nb, 8], F32)
                r1 = sbt(ps_, "r1" + nm, [128, nb, 8], F32)
                mm = sbt(ps_, "mm" + nm, [128, nb, 8], F32)
                S.dma("sp", [], ["pi" + nm], out=pi[:], in_=src[:, :])
                S.op("dve", ["pi" + nm], ["pf" + nm], "tensor_copy", out=pf[:], in_=pi[:])
                S.op("dve", ["pf" + nm, "invf"], ["ang" + nm], "tensor_tensor", out=ang[:],
                     in0=pf[:].unsqueeze(2).broadcast_to([128, nb, 8]),
                     in1=invf[:].unsqueeze(1).broadcast_to([128, nb, 8]), op=ALU.mult)
                S.op("dve", ["ang" + nm], ["kf" + nm], "tensor_scalar", out=kf[:], in0=ang[:], scalar1=1.0 / TWO_PI,
                     scalar2=None, op0=ALU.mult)
                S.op("dve", ["kf" + nm], ["ki" + nm], "tensor_copy", out=ki[:], in_=kf[:])
                S.op("dve", ["ki" + nm], ["kf" + nm], "tensor_copy", out=kf[:], in_=ki[:])
                S.op("dve", ["kf" + nm, "ang" + nm], ["r0" + nm], "scalar_tensor_tensor", out=r0[:], in0=kf[:],
                     scalar=-TWO_PI, in1=ang[:], op0=ALU.mult, op1=ALU.add)
                for which, shift in ((1, 0.0), (0, math.pi / 2)):
                    S.op("dve", ["r0" + nm], ["r1" + nm], "tensor_scalar", out=r1[:], in0=r0[:], scalar1=shift,
                         scalar2=None, op0=ALU.add)
                    for thr, op_, add in ((math.pi, ALU.is_gt, -TWO_PI), (-math.pi, ALU.is_lt, TWO_PI)):
                        S.op("dve", ["r1" + nm], ["mm" + nm], "tensor_scalar", out=mm[:], in0=r1[:], scalar1=thr,
                             scalar2=add, op0=op_, op1=ALU.mult)
                        S.op("dve", ["r1" + nm, "mm" + nm], ["r1" + nm], "tensor_tensor", out=r1[:], in0=r1[:],
                             in1=mm[:], op=ALU.add)
                    S.op("dve", ["r1" + nm], ["r1" + nm], "tensor_scalar", out=r1[:], in0=r1[:], scalar1=math.pi,
                         scalar2=-math.pi, op0=ALU.min, op1=ALU.max)
                    S.op("act", ["r1" + nm], ["CS" + nm], "activation", out=CS[:, :, which * 8:which * 8 + 8],
                         in_=r1[:], func=AF.Sin)
            S.barrier()

        with contextlib.ExitStack() as pa:
            gmix = sbt(pa, "gmix", [128, D], F32)
            S.dma("sp", [], ["gmix"], out=gmix[:], in_=norm_mix.broadcast_to([128, D]))
            gq = {}
            for nm, src in (("dq", diff_q_norm), ("dk", diff_k_norm), ("nq", nsa_q_norm), ("nk", nsa_k_norm)):
                gq[nm] = sbt(pa, "g_" + nm, [128, 64], F32)
                S.dma("sp", [], ["g_" + nm], out=gq[nm][:], in_=src.broadcast_to([128, 64]))
            xt = [sbt(pa, f"xt{i}", [128, D], F32) for i in range(2)]
            hb = [sbt(pa, f"hb{i}", [128, D], BF16) for i in range(2)]
            hT = sbt(pa, "hT", [128, 16, 2048], BF16)
            wc = [sbt(pa, f"wc{i}", [128, 16, 512], BF16) for i in range(2)]
            ss = sbt(pa, "ss", [128, 16], F32)
            rs = sbt(pa, "rs", [128, 16], F32)
            NZ = 3
            sqt = [sbt(pa, f"sqt{i}", [128, 512], BF16) for i in range(NZ)]
            ssh = [sbt(pa, f"ssh{i}", [128, 8], F32) for i in range(NZ)]
            rsh = [sbt(pa, f"rsh{i}", [128, 8], F32) for i in range(NZ)]
            zn = [sbt(pa, f"zn{i}", [128, 512], F32) for i in range(NZ)]
            zr = [sbt(pa, f"zr{i}", [128, 512], BF16) for i in range(NZ)]
            zp = [sbt(pa, f"zp{i}", [128, 512], BF16) for i in range(NZ)]
            rt = [sbt(pa, f"rt{i}", [128, 4, 8, 8], F32) for i in range(NZ)]
            rot = [sbt(pa, f"rot{i}", [128, 8, 16], F32) for i in range(NZ)]
            stg = [sbt(pa, f"stg{i}", [128, 4, 2048], BF16) for i in range(3)]
            vst = [sbt(pa, f"vst{i}", [128, 512], BF16) for i in range(2)]
            cnt = {"w": 0, "z": 0, "v": 0, "ps": 0, "tp": 0}

            def build_hT(xsrc, blk0, ssname):
                for i in range(16):
                    xb = xt[i % 2]; xk = f"xt{i % 2}"
                    S.dma("sp", [], [xk], out=xb[:], in_=xsrc[(blk0 + i) * 128:(blk0 + i + 1) * 128, :])
                    hk = f"hb{i % 2}"
                    S.op("act", [xk], [hk, ("ss", i)], "activation", out=hb[i % 2][:], in_=xb[:], func=AF.Square,
                         accum_out=ss[:, i:i + 1])
                    rstd_from_ss(ss[:, i:i + 1], rs[:, i:i + 1], 1, [("ss", i)], [("rs", i)], 1.0 / D)
                    S.op("dve", [xk, ("rs", i), "gmix"], [hk], "scalar_tensor_tensor", out=hb[i % 2][:], in0=xb[:],
                         scalar=rs[:, i:i + 1], in1=gmix[:], op0=ALU.mult, op1=ALU.mult)
                    for half in range(2):
                        bi = 6 + half
                        pb, pk = bank_bf(bi)
                        for k in range(8):
                            kk = half * 8 + k
                            S.op("pe", [hk, "ident"], [pk], "transpose", pb[:, k * 128:(k + 1) * 128],
                                 hb[i % 2][:, kk * 128:(kk + 1) * 128], ident[:])
                        dst = hT[:, half * 8:half * 8 + 8, i * 128:(i + 1) * 128]
                        srcv = pb.rearrange("p (k t) -> p k t", k=8)
                        if half == 0:
                            S.op("act", [pk], [("hT", i, 0)], "activation", out=dst, in_=srcv, func=AF.Copy)
                        else:
                            S.op("dve", [pk], [("hT", i, 1)], "tensor_copy", out=dst, in_=srcv)

            def load_w(c0, n):
                i = cnt["w"] % 2; cnt["w"] += 1
                S.dma("sp", [("W_IN", c0)], [f"wc{i}"], out=wc[i][:, :, 0:n],
                      in_=W_IN[:, c0:c0 + n].rearrange("(k p) c -> p k c", p=128))
                return wc[i], f"wc{i}"

            def mm_tm(w, wk, blk, n, coff=0):
                bi = cnt["ps"] % 4; cnt["ps"] += 1
                pb, pk = bank(bi)
                if cnt.get("conv"):
                    conv_pop(1)
                for k in range(16):
                    S.op("pe", [wk, ("hT", blk, 0), ("hT", blk, 1)], [pk], "matmul", pb[:, 0:n], lhsT=hT[:, k, blk * 128:(blk + 1) * 128],
                         rhs=w[:, k, coff:coff + n], start=(k == 0), stop=(k == 15))
                return pb, pk

            def normrope(pb, pk, c0, nh, gname, cs, csk, blk, want_plain=False):
                i = cnt["z"] % NZ; cnt["z"] += 1
                n = nh * 64
                pv = pb[:, c0:c0 + n]
                pv3 = pv.rearrange("p (h d) -> p h d", d=64)
                S.op("act", [pk], [f"sqt{i}"], "activation", out=sqt[i][:, 0:n], in_=pv, func=AF.Square)
                S.op("dve", [f"sqt{i}"], [f"ssh{i}"], "tensor_reduce", out=ssh[i][:, 0:nh],
                     in_=sqt[i][:, 0:n].rearrange("p (h d) -> p h d", d=64), axis=AX.X, op=ALU.add)
                rstd_from_ss(ssh[i][:, 0:nh], rsh[i][:, 0:nh], nh, [f"ssh{i}"], [f"rsh{i}"], 1.0 / 64)
                z3 = zn[i][:, 0:n].rearrange("p (h d) -> p h d", d=64)
                S.op("dve", [pk, "g_" + gname], [f"zn{i}"], "tensor_tensor", out=z3, in0=pv3,
                     in1=gq[gname][:].unsqueeze(1).broadcast_to([128, nh, 64]), op=ALU.mult)
                x1 = z3[:, :, 0:8]; x2 = z3[:, :, 8:16]
                cc = cs[:, blk, 0:8].unsqueeze(1).broadcast_to([128, nh, 8])
                sn = cs[:, blk, 8:16].unsqueeze(1).broadcast_to([128, nh, 8])
                r = rt[i]; rk = f"rt{i}"
                ro = rot[i][:, 0:nh, :]
                S.op("pool", [f"zn{i}", csk], [(rk, 0)], "tensor_tensor", out=r[:, 0, 0:nh, :], in0=x1, in1=cc, op=ALU.mult)
                S.op("pool", [f"zn{i}", csk], [(rk, 1)], "tensor_tensor", out=r[:, 1, 0:nh, :], in0=x2, in1=sn, op=ALU.mult)
                S.op("pool", [f"zn{i}", csk], [(rk, 2)], "tensor_tensor", out=r[:, 2, 0:nh, :], in0=x2, in1=cc, op=ALU.mult)
                S.op("pool", [f"zn{i}", csk], [(rk, 3)], "tensor_tensor", out=r[:, 3, 0:nh, :], in0=x1, in1=sn, op=ALU.mult)
                S.op("pool", [(rk, 0), (rk, 1)], [f"rot{i}"], "tensor_tensor", out=ro[:, :, 0:8],
                     in0=r[:, 0, 0:nh, :], in1=r[:, 1, 0:nh, :], op=ALU.subtract)
                S.op("pool", [(rk, 2), (rk, 3), f"rot{i}"], [f"rot{i}"], "tensor_tensor", out=ro[:, :, 8:16],
                     in0=r[:, 2, 0:nh, :], in1=r[:, 3, 0:nh, :], op=ALU.add)
                rb = rsh[i][:, 0:nh].unsqueeze(2)
                zr3 = zr[i][:, 0:n].rearrange("p (h d) -> p h d", d=64)
                S.op("dve", [f"zn{i}", f"rsh{i}"], [f"zr{i}"], "tensor_tensor", out=zr3[:, :, 16:64], in0=z3[:, :, 16:64],
                     in1=rb.broadcast_to([128, nh, 48]), op=ALU.mult)
                S.op("dve", [f"rot{i}", f"rsh{i}", f"zr{i}"], [f"zr{i}"], "tensor_tensor", out=zr3[:, :, 0:16], in0=ro,
                     in1=rb.broadcast_to([128, nh, 16]), op=ALU.mult)
                if want_plain:
                    zp3 = zp[i][:, 0:n].rearrange("p (h d) -> p h d", d=64)
                    S.op("dve", [f"zn{i}", f"rsh{i}"], [f"zp{i}"], "tensor_tensor", out=zp3, in0=z3,
                         in1=rb.broadcast_to([128, nh, 64]), op=ALU.mult)
                return i

            def transp_to(src_ap, src_key, ncol128, dst_ap, dst_key):
                bi = 4 + cnt["tp"] % 2; cnt["tp"] += 1
                pb, pk = bank_bf(bi)
                for c in range(ncol128):
                    S.op("pe", [src_key, "ident"], [pk], "transpose", pb[:, c * 128:(c + 1) * 128],
                         src_ap[:, c * 128:(c + 1) * 128], ident[:])
                S.op("act", [pk], [dst_key], "activation", out=dst_ap,
                     in_=pb[:, 0:ncol128 * 128].rearrange("p (c t) -> p c t", c=ncol128), func=AF.Copy)

            sidx = {"i": 0}

            def new_stage():
                i = sidx["i"] % 3; sidx["i"] += 1
                return stg[i], f"stg{i}"

            build_hT(x_own, 0, "o")
            for (c0, gname, dstR, dstP) in ((0, "dq", QDT, None), (512, "dq", QDT, None),
                                            (3072, "nq", NQRT, NQPT), (3584, "nq", NQRT, NQPT)):
                w, wk = load_w(c0, 512)
                sR, sRk = new_stage()
                if dstP is not None:
                    sP, sPk = new_stage()
                for blk in range(16):
                    pb, pk = mm_tm(w, wk, blk, 512)
                    i = normrope(pb, pk, 0, 8, gname, CSO, "CSo", blk, want_plain=dstP is not None)
                    transp_to(zr[i], f"zr{i}", 4, sR[:, :, blk * 128:(blk + 1) * 128], (sRk, blk))
                    if dstP is not None:
                        transp_to(zp[i], f"zp{i}", 4, sP[:, :, blk * 128:(blk + 1) * 128], (sPk, blk))
                h0 = (c0 % 1024) // 128
                S.dma("pool", [(sRk, b) for b in range(16)], [("QN", c0, 0)], out=dstR[h0:h0 + 4].rearrange("h p t -> p h t"),
                      in_=sR[:])
                if dstP is not None:
                    S.dma("pool", [(sPk, b) for b in range(16)], [("QN", c0, 1)],
                          out=dstP[h0:h0 + 4].rearrange("h p t -> p h t"), in_=sP[:])
            w, wk = load_w(4864, 48)
            for blk in range(16):
                pb, pk = mm_tm(w, wk, blk, 48)
                S.op("act", [pk], [("gates", blk)], "activation", out=gates[:, blk, :], in_=pb[:, 0:48], func=AF.Sigmoid)
            for gi in range(2):
                for cc_ in range(4):
                    c0 = 4912 + gi * 2048 + cc_ * 512
                    w, wk = load_w(c0, 512)
                    for f in range(4):
                        for tg in range(4):
                            bi = cnt["ps"] % 4; cnt["ps"] += 1
                            pb, pk = bank(bi)
                            for k in range(16):
                                S.op("pe", [wk] + [("hT", tg * 4 + b, hh) for b in range(4) for hh in range(2)], [pk], "matmul", pb[:, :],
                                     lhsT=w[:, k, f * 128:(f + 1) * 128], rhs=hT[:, k, tg * 512:(tg + 1) * 512],
                                     start=(k == 0), stop=(k == 15))
                            vi = cnt["v"] % 2; cnt["v"] += 1
                            S.op("act", [pk], [f"vst{vi}"], "activation", out=vst[vi][:], in_=pb[:, :], func=AF.Sigmoid)
                            fr = cc_ * 512 + f * 128
                            S.dma("pool", [f"vst{vi}"], [("SGT", gi, fr, tg)], out=SGT[gi, fr:fr + 128, tg * 512:(tg + 1) * 512],
                                  in_=vst[vi][:])
            for hp in range(2):
                t0 = hp * 2048
                cnt["conv"] = 1
                build_hT(x_all, hp * 16, "a")
                for c0 in (1024, 1536):
                    w, wk = load_w(c0, 512)
                    sR, sRk = new_stage()
                    for blk in range(16):
                        pb, pk = mm_tm(w, wk, blk, 512)
                        i = normrope(pb, pk, 0, 8, "dk", CSA, "CSa", hp * 16 + blk)
                        transp_to(zr[i], f"zr{i}", 4, sR[:, :, blk * 128:(blk + 1) * 128], (sRk, blk))
                    h0 = (c0 - 1024) // 128
                    S.dma("pool", [(sRk, b) for b in range(16)], [("KDT", h0, hp)],
                          out=KDT[h0:h0 + 4, :, t0:t0 + 2048].rearrange("h p t -> p h t"), in_=sR[:])
                for c0 in (2048, 2560):
                    w, wk = load_w(c0, 512)
                    for blk in range(16):
                        pb, pk = mm_tm(w, wk, blk, 512)
                        vi = cnt["v"] % 2; cnt["v"] += 1
                        S.op("act", [pk], [f"vst{vi}"], "activation", out=vst[vi][:], in_=pb[:, :], func=AF.Copy)
                        S.dma("pool", [f"vst{vi}"], [("VD", c0, hp, blk)],
                              out=VD[t0 + blk * 128:t0 + (blk + 1) * 128, c0 - 2048:c0 - 2048 + 512], in_=vst[vi][:])
                w, wk = load_w(4096, 512)
                w2_, wk2 = load_w(4608, 256)
                sE, sEk = new_stage()
                sF, sFk = new_stage()
                for blk in range(16):
                    gb = hp * 16 + blk
                    pb, pk = mm_tm(w, wk, blk, 512)
                    vi = cnt["v"] % 2; cnt["v"] += 1
                    S.op("act", [pk], [f"vst{vi}"], "activation", out=vst[vi][:, 0:256], in_=pb[:, 0:256], func=AF.Copy)
                    S.op("act", [pk], [f"vst{vi}"], "activation", out=vst[vi][:, 256:384], in_=pb[:, 384:512], func=AF.Copy)
                    S.dma("pool", [f"vst{vi}"], [("VS", gb)], out=VS[gb * 128:(gb + 1) * 128, :], in_=vst[vi][:, 256:384])
                    transp_to(vst[vi], f"vst{vi}", 2, sE[:, 0:2, blk * 128:(blk + 1) * 128], (sEk, blk))
                    i = normrope(pb, pk, 256, 2, "nk", CSA, "CSa", gb)
                    zi = cnt["z"] % NZ; cnt["z"] += 1
                    dup = zp[zi]; dk_ = f"zp{zi}"
                    S.op("dve", [f"zr{i}"], [dk_], "tensor_copy", out=dup[:, 0:256].rearrange("p (g r d) -> p g r d", g=2, r=2),
                         in_=zr[i][:, 0:128].rearrange("p (g d) -> p g d", g=2).unsqueeze(2).broadcast_to([128, 2, 2, 64]))
                    transp_to(dup, dk_, 2, sE[:, 2:4, blk * 128:(blk + 1) * 128], (sEk, blk))
                    pb2, pk2 = mm_tm(w2_, wk2, blk, 256)
                    vi = cnt["v"] % 2; cnt["v"] += 1
                    S.op("act", [pk2], [f"vst{vi}"], "activation", out=vst[vi][:, 0:128], in_=pb2[:, 128:256], func=AF.Copy)
                    S.dma("pool", [f"vst{vi}"], [("VW", gb)], out=VW[gb * 128:(gb + 1) * 128, :], in_=vst[vi][:, 0:128])
                    i = normrope(pb2, pk2, 0, 2, "nk", CSA, "CSa", gb)
                    zi = cnt["z"] % NZ; cnt["z"] += 1
                    dup = zp[zi]; dk_ = f"zp{zi}"
                    S.op("dve", [f"zr{i}"], [dk_], "tensor_copy", out=dup[:, 0:256].rearrange("p (g r d) -> p g r d", g=2, r=2),
                         in_=zr[i][:, 0:128].rearrange("p (g d) -> p g d", g=2).unsqueeze(2).broadcast_to([128, 2, 2, 64]))
                    transp_to(dup, dk_, 2, sF[:, 0:2, blk * 128:(blk + 1) * 128], (sFk, blk))
                rE = [(sEk, b) for b in range(16)]
                S.dma("pool", rE, [("KCRT", hp)], out=KCRT[:, t0:t0 + 2048], in_=sE[:, 0, :])
                S.dma("pool", rE, [("VCRT", hp)], out=VCRT[:, t0:t0 + 2048], in_=sE[:, 1, :])
                S.dma("pool", rE, [("KST", hp)], out=KST[:, :, t0:t0 + 2048].rearrange("g p t -> p g t"), in_=sE[:, 2:4, :])
                S.dma("pool", [(sFk, b) for b in range(16)], [("KWT", hp)],
                      out=KWT[:, :, t0:t0 + 2048].rearrange("g p t -> p g t"), in_=sF[:, 0:2, :])
            conv_pop(len(pending_conv))
            S.barrier()

        if stop_after == "PA":
            _finish(nc, S, out_d, es)
            return nc

        build_rest(nc, S, es, locals(), dbg=dbg, stop_after=stop_after)
    return nc


def _finish(nc, S, out_d, es):
    z = es.enter_context(nc.sbuf_tensor("zfin", [128, D], F32))
    S.op("dve", [], ["zfin"], "memset", z[:], 0.0)
    for i in range(16):
        S.dma("sp", ["zfin"], [("out", i)], out=out_d[i * 128:(i + 1) * 128, :], in_=z[:])
    S.barrier()


def build_rest(nc, S, es, L, dbg=False, stop_after=None):
    g_ = lambda n: L[n]
    sbt, bank, bank_bf, ident, gates, rstd_from_ss = (g_("sbt"), g_("bank"), g_("bank_bf"), g_("ident"), g_("gates"),
                                                      g_("rstd_from_ss"))
    out_d = g_("out_d")

    def finish():
        _finish(nc, S, out_d, es)

    KCT = sbt(es, "KCT", [128, 2, 2, 256], BF16)
    VCO = sbt(es, "VCO", [128, 2, 2, 129], BF16)
    S.op("dve", [], ["KCT"], "memset", KCT[:], 0.0)
    S.op("dve", [], ["VCO"], "memset", VCO[:], 0.0)

    CW1, CW2, KCRT, VCRT = g_("CW1"), g_("CW2"), g_("KCRT"), g_("VCRT")
    with contextlib.ExitStack() as pb_:
        kvT = [sbt(pb_, f"kvT{i}", [128, SEQ], BF16) for i in range(2)]
        S.dma("sp", [("KCRT", 0), ("KCRT", 1)], ["kvT0"], out=kvT[0][:], in_=KCRT[:, :])
        S.dma("sp", [("VCRT", 0), ("VCRT", 1)], ["kvT1"], out=kvT[1][:], in_=VCRT[:, :])
        W1 = [[sbt(pb_, f"W1_{kv}{g}", [128, 32, 256], BF16) for g in range(2)] for kv in range(2)]
        W2 = [sbt(pb_, f"W2_{kv}", [128, 2, 64], BF16) for kv in range(2)]
        for kv in range(2):
            for half in range(2):
                S.op("dve" if half == 0 else "pool", [], [(f"W1_{kv}", half, "z")], "memset", W1[kv][half][(1 - half) * 64:(2 - half) * 64, :, :], 0.0)
                for l4 in range(4):
                    S.dma("sp", [("CW1", r) for r in range(0, 4096, 1024)], [(f"W1_{kv}", half)], out=W1[kv][half][half * 64:(half + 1) * 64, l4 * 8:(l4 + 1) * 8, :],
                          in_=CW1[kv * 2048 + l4 * 512:kv * 2048 + (l4 + 1) * 512, :].rearrange("(l d) h -> d l h", d=64))
            S.dma("sp", [("CW2", 0)], [f"W2_{kv}"], out=W2[kv][:], in_=CW2[kv * 256:(kv + 1) * 256, :].rearrange("(c p) d -> p c d", p=128))
        posf = sbt(pb_, "posf", [128, 2, 32], F32)
        posb = sbt(pb_, "posb", [128, 2, 32], BF16)
        for half in range(2):
            S.dma("sp", [], ["posf"], out=posf[half * 64:(half + 1) * 64], in_=g_("cmp_posT")[:, :, :])
        S.op("dve", ["posf"], ["posb"], "tensor_copy", out=posb[:], in_=posf[:])
        gk = sbt(pb_, "gk", [128, 64], F32)
        S.dma("sp", [], ["gk"], out=gk[:], in_=g_("nsa_k_norm").broadcast_to([128, 64]))
        biasT = sbt(pb_, "biasT", [128, 4], F32)
        GH = sbt(pb_, "GH", [128, 8, 256], BF16)
        S.op("dve", [], [("GH", a, b, c) for a in range(2) for b in range(2) for c in range(2)], "memset", GH[:], 0.0)
        u = [sbt(pb_, f"u{i}", [128, 256], F32) for i in range(2)]
        u2 = [sbt(pb_, f"u2{i}", [128, 256], F32) for i in range(2)]
        sg = [sbt(pb_, f"sg{i}", [128, 256], F32) for i in range(2)]
        ssk = sbt(pb_, "ssk", [128, 4], F32)
        rsk = sbt(pb_, "rsk", [128, 4], F32)
        kcn2 = [sbt(pb_, f"kcn2{i}", [128, 256], BF16) for i in range(2)]
        for i in range(2):
            S.op("dve", [], [f"kcn2{i}"], "memset", kcn2[i][:], 0.0)
        junkb = sbt(pb_, "junkb", [128, 64], F32)
        S.op("dve", ["VCO"], ["VCO"], "memset", VCO[:, :, :, 64:65], 1.0)
        for g in range(2):
            S.dma("sp", ["VCO"], ["VCO"], out=VCO[:, g, :, 65:129], in_=g_("c_ovl")[:, :, :])
        pbb, pbk = bank(4)
        for kv in range(2):
            for hc in range(2):
                col = kv * 2 + hc
                for l in range(32):
                    S.op("pe", [(f"W1_{kv}", 0), (f"W1_{kv}", 0, "z"), "posb"], [pbk], "matmul", pbb[:, col:col + 1],
                         lhsT=W1[kv][0][:, l, hc * 128:(hc + 1) * 128], rhs=posb[:, kv, l:l + 1],
                         start=(l == 0), stop=(l == 31), skip_group_check=True)
        S.op("act", [pbk], ["biasT"], "activation", out=biasT[:], in_=pbb[:, 0:4], func=AF.Copy)
        n = 0
        for kv in range(2):
            for g in range(2):
                for hc in range(2):
                    pb, pk = bank(n % 4)
                    for l in range(32):
                        S.op("pe", [(f"W1_{kv}", g), (f"W1_{kv}", g, "z"), f"kvT{kv}"], [pk], "matmul", pb[:, 0:255],
                             lhsT=W1[kv][g][:, l, hc * 128:(hc + 1) * 128],
                             rhs=kvT[kv][:, l:l + 4065:16], start=(l == 0), stop=(l == 31))
                    i = n % 2
                    col = kv * 2 + hc
                    S.op("act", [pk, "biasT"], [f"u{i}"], "activation", out=u[i][:, 0:255], in_=pb[:, 0:255],
                         func=AF.Identity, bias=biasT[:, col:col + 1], scale=1.0)
                    S.op("pool", [f"u{i}"], [f"u2{i}"], "tensor_tensor", out=u2[i][:, 0:255], in0=u[i][:, 0:255],
                         in1=u[i][:, 0:255], op=ALU.mult)
                    S.op("pool", [f"u2{i}"], [f"u2{i}"], "tensor_scalar", out=u2[i][:, 0:255], in0=u2[i][:, 0:255],
                         scalar1=0.044715, scalar2=1.0, op0=ALU.mult, op1=ALU.add)
                    S.op("pool", [f"u2{i}", f"u{i}"], [f"u2{i}"], "tensor_tensor", out=u2[i][:, 0:255], in0=u2[i][:, 0:255],
                         in1=u[i][:, 0:255], op=ALU.mult)
                    S.op("act", [f"u2{i}"], [f"sg{i}"], "activation", out=sg[i][:, 0:255], in_=u2[i][:, 0:255],
                         func=AF.Sigmoid, scale=1.5957691216057308)
                    S.op("dve", [f"u{i}", f"sg{i}"], [("GH", kv, g, hc)], "tensor_tensor", out=GH[:, kv * 4 + g * 2 + hc, 0:255],
                         in0=u[i][:, 0:255], in1=sg[i][:, 0:255], op=ALU.mult)
                    n += 1
        n = 0
        for kv in range(2):
            for g in range(2):
                for c in range(2):
                    nn = 128 if c == 0 else 127
                    pb, pk = bank(n % 4)
                    for hc in range(2):
                        S.op("pe", [("GH", kv, g, hc), f"W2_{kv}"], [pk], "matmul", pb[0:nn, 0:64],
                             lhsT=GH[:, kv * 4 + g * 2 + hc, c * 128:c * 128 + nn], rhs=W2[kv][:, hc, :],
                             start=(hc == 0), stop=(hc == 1))
                    if kv == 0:
                        i = n % 2
                        col = g * 2 + c
                        S.op("act", [pk], ["junkb", ("ssk", col)], "activation", out=junkb[0:nn, :], in_=pb[0:nn, 0:64],
                             func=AF.Square, accum_out=ssk[0:nn, col:col + 1])
                        rstd_from_ss(ssk[0:nn, col:col + 1], rsk[0:nn, col:col + 1], 1, [("ssk", col)], [("rsk", col)], 1.0 / 64)
                        S.op("dve", [pk, ("rsk", col), "gk"], [f"kcn2{i}"], "scalar_tensor_tensor", out=kcn2[i][0:nn, 0:64],
                             in0=pb[0:nn, 0:64], scalar=rsk[0:nn, col:col + 1], in1=gk[0:nn, :], op0=ALU.mult, op1=ALU.mult)
                        S.op("dve", [f"kcn2{i}"], [f"kcn2{i}"], "tensor_copy", out=kcn2[i][0:nn, 192:256], in_=kcn2[i][0:nn, 0:64])
                        tb, tk = bank_bf(6 + i)
                        for par in range(2):
                            S.op("pe", [f"kcn2{i}", "ident"], [tk], "transpose", tb[:, par * 128:par * 128 + nn], kcn2[i][0:nn, par * 128:(par + 1) * 128],
                                 ident[0:nn, 0:nn])
                        S.op("act", [tk, "KCT"], ["KCT"], "activation", out=KCT[:, :, g, c * 128:c * 128 + nn],
                             in_=tb[:, 0:256].rearrange("p (a n) -> p a n", a=2)[:, :, 0:nn], func=AF.Copy)
                    else:
                        S.op("act", [pk, "VCO"], ["VCO"], "activation", out=VCO[0:nn, g, c, 0:64], in_=pb[0:nn, 0:64], func=AF.Copy)
                    n += 1
        if dbg:
            DKC = nc.dram_tensor("DKC", [128, 2, 2, 256], BF16, kind="ExternalOutput").ap()
            DVC = nc.dram_tensor("DVC", [128, 2, 2, 129], BF16, kind="ExternalOutput").ap()
            S.dma("sp", ["KCT"], ["DKC"], out=DKC[:, :, :, :], in_=KCT[:])
            S.dma("sp", ["VCO"], ["DVC"], out=DVC[:, :, :, :], in_=VCO[:])
        S.barrier()
    if stop_after == "PB":
        return finish()

    QDT, KDT, VD, YAT = g_("QDT"), g_("KDT"), g_("VD"), g_("YAT")
    with contextlib.ExitStack() as pc:
      if not PD_DEBUG["skip_pc"]:
        dm = sbt(pc, "dm", [128, 8, 512], BF16)
        S.dma("sp", [], ["dm"], out=dm[:], in_=g_("c_dmask")[:, :, :])
        lt = sbt(pc, "lt", [128, 256], F32)
        S.dma("sp", [], ["lt"], out=lt[:], in_=g_("diff_lambda").broadcast_to([128, 256]))
        prod = sbt(pc, "prod", [128, 128], F32)
        sums = sbt(pc, "sums", [128, 2], F32)
        ex = sbt(pc, "ex", [128, 2], F32)
        nlam = sbt(pc, "nlam", [128, 1], F32)
        S.op("dve", ["lt"], ["prod"], "tensor_tensor", out=prod[:].rearrange("p (a d) -> p a d", a=2),
             in0=lt[:].rearrange("p (a b d) -> p a b d", a=2, b=2)[:, :, 0, :],
             in1=lt[:].rearrange("p (a b d) -> p a b d", a=2, b=2)[:, :, 1, :], op=ALU.mult)
        S.op("dve", ["prod"], ["sums"], "tensor_reduce", out=sums[:], in_=prod[:].rearrange("p (a d) -> p a d", a=2),
             axis=AX.X, op=ALU.add)
        S.op("act", ["sums"], ["ex"], "activation", out=ex[:], in_=sums[:], func=AF.Exp)
        S.op("dve", ["ex"], ["nlam"], "tensor_tensor", out=nlam[:], in0=ex[:, 1:2], in1=ex[:, 0:1], op=ALU.subtract)
        S.op("dve", ["nlam"], ["nlam"], "tensor_scalar", out=nlam[:], in0=nlam[:], scalar1=-0.2, scalar2=None, op0=ALU.add)
        gsub = sbt(pc, "gsub", [128, 128], F32)
        S.dma("sp", [], ["gsub"], out=gsub[:], in_=g_("diff_subln").broadcast_to([128, 128]))
        S.op("dve", ["gsub"], ["gsub"], "tensor_scalar", out=gsub[:], in0=gsub[:], scalar1=0.8, scalar2=None, op0=ALU.mult)
        KT = [sbt(pc, f"KT{i}", [128, SEQ], BF16) for i in range(2)]
        VT = [sbt(pc, f"VT{i}", [128, 32, 129], BF16) for i in range(2)]
        QT = [[sbt(pc, f"QT{i}{c}", [128, NOWN], BF16) for c in range(2)] for i in range(2)]
        for i in range(2):
            for c in range(2):
                S.op("dve", [], [("QTz", i, c)], "memset", QT[i][c][(1 - c) * 64:(2 - c) * 64, :], 0.0)
        YAs = [sbt(pc, f"YAs{i}", [128, NOWN], BF16) for i in range(2)]
        E = [sbt(pc, f"E{i}", [128, 512], BF16) for i in range(4)]
        rd = [sbt(pc, f"rd{i}", [128, 4], F32) for i in range(2)]
        t0_ = [sbt(pc, f"t0_{i}", [128, 128], F32) for i in range(2)]
        o_ = [sbt(pc, f"o_{i}", [128, 128], F32) for i in range(2)]
        yb_ = [sbt(pc, f"yb_{i}", [128, 128], BF16) for i in range(2)]
        junkc = sbt(pc, "junkc", [128, 128], BF16)
        for i in range(2):
            S.op("dve", [], [("VT1", i)], "memset", VT[i][:, :, 128:129], 1.0)
        nf = 0
        LA = 2

        def pc_loads(h):
            i = h % 2
            S.dma("sp", [("KDT", (h // 4) * 4, 0), ("KDT", (h // 4) * 4, 1)], [("KT", i)], out=KT[i][:], in_=KDT[h, :, :])
            for q4 in range(8):
                S.dma("sp", [("VD", c0, hp, b) for c0 in (2048, 2560) for hp in range(2) for b in range(16)], [("VT", i, q4)],
                      out=VT[i][:, q4 * 4:(q4 + 1) * 4, 0:128],
                      in_=VD[q4 * 512:(q4 + 1) * 512, h * 128:(h + 1) * 128].rearrange("(kb p) d -> p kb d", p=128))
            for c in range(2):
                S.dma("sp", [("QN", 0, 0), ("QN", 512, 0)], [("QT", i, c)], out=QT[i][c][c * 64:(c + 1) * 64, :], in_=QDT[h, c * 64:(c + 1) * 64, :])

        def pc_front(u, n):
            h, G, kb, c = u
            i = h % 2
            r = kb - 8 * G
            pb, pk = bank(n % 4)
            e = E[n % 4]; ek = f"E{n % 4}"
            S.op("pe", [("KT", i), ("QT", i, c), ("QTz", i, c)], [pk], "matmul", pb[:, :], lhsT=KT[i][:, kb * 128:(kb + 1) * 128],
                 rhs=QT[i][c][:, G * 512:(G + 1) * 512], start=True, stop=True)
            S.op("act", [pk], [ek], "activation", out=e[:], in_=pb[:, :], func=AF.Exp, scale=0.125)
            if r >= 0:
                S.op("dve", [ek, "dm"], [ek], "tensor_tensor", out=e[:], in0=e[:], in1=dm[:, r, :], op=ALU.mult)

        def pc_back(u, n):
            h, G, kb, c = u
            i = h % 2
            r = kb - 8 * G
            e = E[n % 4]; ek = f"E{n % 4}"
            for t in range(4):
                if r > 2 * t + 1:
                    continue
                a = c * 4 + t
                ab, ak = bank(4 + a // 3)
                off = (a % 3) * 129
                S.op("pe", [ek, ("VT", i, kb // 4), ("VT1", i)], [ak], "matmul", ab[:, off:off + 129],
                     lhsT=e[:, t * 128:(t + 1) * 128], rhs=VT[i][:, kb, :], start=(kb == 0 and a % 3 == 0),
                     stop=(kb == 8 * G + 2 * t + 1), skip_group_check=True)

        def pc_final(h, G):
            nonlocal nf
            i = h % 2
            for t in range(4):
                f = nf % 2; nf += 1
                a0, a1 = t, 4 + t
                b0, k0 = bank(4 + a0 // 3); b1, k1 = bank(4 + a1 // 3)
                O0 = b0[:, (a0 % 3) * 129:(a0 % 3) * 129 + 129]
                O1 = b1[:, (a1 % 3) * 129:(a1 % 3) * 129 + 129]
                rk = f"rd{f}"
                S.op("dve", [k0], [(rk, 0)], "reciprocal", out=rd[f][:, 0:1], in_=O0[:, 128:129])
                S.op("dve", [k1], [(rk, 1)], "reciprocal", out=rd[f][:, 1:2], in_=O1[:, 128:129])
                S.op("dve", [(rk, 1), "nlam"], [(rk, 2)], "tensor_tensor", out=rd[f][:, 2:3], in0=rd[f][:, 1:2], in1=nlam[:], op=ALU.mult)
                S.op("dve", [k0, (rk, 0)], [f"t0_{f}"], "tensor_scalar", out=t0_[f][:], in0=O0[:, 0:128], scalar1=rd[f][:, 0:1],
                     scalar2=None, op0=ALU.mult)
                S.op("dve", [k1, (rk, 2), f"t0_{f}"], [f"o_{f}"], "scalar_tensor_tensor", out=o_[f][:], in0=O1[:, 0:128],
                     scalar=rd[f][:, 2:3], in1=t0_[f][:], op0=ALU.mult, op1=ALU.add)
                S.op("act", [f"o_{f}"], ["junkc", (rk, 3)], "activation", out=junkc[:], in_=o_[f][:], func=AF.Square,
                     accum_out=rd[f][:, 3:4])
                rstd_from_ss(rd[f][:, 3:4], rd[f][:, 3:4], 1, [(rk, 3)], [(rk, 3)], 1.0 / 128)
                S.op("dve", [f"o_{f}", (rk, 3), "gsub"], [f"yb_{f}"], "scalar_tensor_tensor", out=yb_[f][:], in0=o_[f][:],
                     scalar=rd[f][:, 3:4], in1=gsub[:], op0=ALU.mult, op1=ALU.mult)
                tb, tk = bank_bf(7)
                S.op("pe", [f"yb_{f}", "ident"], [tk], "transpose", tb[:, 0:128], yb_[f][:], ident[:])
                q0 = (G * 4 + t) * 128
                S.op("act", [tk], [("YAs", i, G * 4 + t)], "activation", out=YAs[i][:, q0:q0 + 128], in_=tb[:, 0:128], func=AF.Copy)
            if G == 3:
                S.dma("pool", [("YAs", i, b) for b in range(16)], [("YAT", h)], out=YAT[h, :, :], in_=YAs[i][:])

        units = [(h, G, kb, c) for h in range(8) for G in range(4) for kb in range(8 * G + 8) for c in range(2)]
        pc_loads(0)
        for n in range(len(units) + LA):
            if n < len(units):
                u = units[n]
                if u[1] == 0 and u[2] == 2 and u[3] == 0 and u[0] + 1 < 8:
                    pc_loads(u[0] + 1)
                pc_front(u, n)
            if n >= LA:
                ub = units[n - LA]
                pc_back(ub, n - LA)
                if ub[2] == 8 * ub[1] + 7 and ub[3] == 1:
                    pc_final(ub[0], ub[1])
        S.barrier()
    if stop_after == "PC":
        return finish()

    NQRT, NQPT, KST, KWT, VS, VW, YBT = g_("NQRT"), g_("NQPT"), g_("KST"), g_("KWT"), g_("VS"), g_("VW"), g_("YBT")
    with contextlib.ExitStack() as pd:
        m128 = sbt(pd, "m128", [128, 2, 8, 128], BF16)
        mc = sbt(pd, "mc", [128, 16, 2, 128], BF16)
        valid = sbt(pd, "valid", [128, 16, 64], F32)
        addc = sbt(pd, "addc", [128, 16, 64], F32)
        eexp = sbt(pd, "eexp", [128, 32, 128], BF16)
        S.op("dve", [], ["eexpz"], "memset", eexp[64:128, :, :], 0.0)
        S.dma("sp", [], ["m128"], out=m128[:], in_=g_("c_m128")[:, :, :, :])
        S.dma("sp", [], ["mc"], out=mc[:], in_=g_("c_mc")[:, :, :, :])
        S.dma("sp", [], ["valid"], out=valid[:], in_=g_("c_valid")[:, :, :])
        S.dma("sp", [], ["addc"], out=addc[:], in_=g_("c_addc")[:, :, :])
        S.dma("sp", [], ["eexp"], out=eexp[0:64, :, :], in_=g_("c_eexp")[:, :, :])
        KS = [sbt(pd, f"KS{par}", [128, SEQ], BF16) for par in range(2)]
        KW = [sbt(pd, f"KW{par}", [128, SEQ], BF16) for par in range(2)]
        for par in range(2):
            S.op("dve", [], [("KSz", par)], "memset", KS[par][(1 - par) * 64:(2 - par) * 64, :], 0.0)
            S.op("pool", [], [("KWz", par)], "memset", KW[par][(1 - par) * 64:(2 - par) * 64, :], 0.0)
        VS1 = sbt(pd, "VS1", [128, 32, 65], BF16)
        VW1 = sbt(pd, "VW1", [128, 32, 65], BF16)
        QR = sbt(pd, "QR", [128, 4, NOWN], BF16)
        QP = sbt(pd, "QP", [128, 4, NOWN], BF16)
        ecmp = sbt(pd, "ecmp", [128, 2, 2, 512], BF16)
        es_ = [sbt(pd, f"es{i}", [128, 512], BF16) for i in range(4)]
        M4 = [sbt(pd, f"M4{i}", [128, 512], BF16) for i in range(2)]
        Y = [sbt(pd, f"Y{i}", [128, 8, 64], F32) for i in range(2)]
        Yb = sbt(pd, "Yb", [128, 512], BF16)
        IMP = sbt(pd, "IMP", [128, 64], F32)
        sc = sbt(pd, "sc", [128, 64], F32)
        sc2 = sbt(pd, "sc2", [128, 64], F32)
        m8a = sbt(pd, "m8a", [128, 8], F32)
        m8b = sbt(pd, "m8b", [128, 8], F32)
        selb = sbt(pd, "selb", [128, 128], BF16)
        selT = [sbt(pd, f"selT{i}", [128, 128], BF16) for i in range(2)]
        S.op("dve", [], ["selbz"], "memset", selb[:, 64:128], 0.0)
        den8 = sbt(pd, "den8", [128, 8], F32)
        rd8 = sbt(pd, "rd8", [128, 8], F32)
        coef = sbt(pd, "coef", [128, 8], F32)
        ystg = sbt(pd, "ystg", [128, 4, NOWN], BF16)
        S.op("dve", [], ["VS1o"], "memset", VS1[:, :, 64:65], 1.0)
        S.op("dve", [], ["VW1o"], "memset", VW1[:, :, 64:65], 1.0)
        nsc = 0
        for g in range(2):
            for par in range(2):
                S.dma("sp", [("KST", 0), ("KST", 1)], [("KS", par)], out=KS[par][par * 64:(par + 1) * 64, :], in_=KST[g, par * 64:(par + 1) * 64, :])
                S.dma("sp", [("KWT", 0), ("KWT", 1)], [("KW", par)], out=KW[par][par * 64:(par + 1) * 64, :], in_=KWT[g, par * 64:(par + 1) * 64, :])
            for q4 in range(4):
                S.dma("sp", [("VS", b) for b in range(32)], [("VS1", q4)], out=VS1[:, q4 * 8:(q4 + 1) * 8, 0:64],
                      in_=VS[q4 * 1024:(q4 + 1) * 1024, g * 64:(g + 1) * 64].rearrange("(kb p) d -> p kb d", p=128))
                S.dma("sp", [("VW", b) for b in range(32)], [("VW1", q4)], out=VW1[:, q4 * 8:(q4 + 1) * 8, 0:64],
                      in_=VW[q4 * 1024:(q4 + 1) * 1024, g * 64:(g + 1) * 64].rearrange("(kb p) d -> p kb d", p=128))
            S.dma("sp", [("QN", 3072, 0), ("QN", 3584, 0)], ["QR"], out=QR[:], in_=NQRT[g * 4:(g + 1) * 4].rearrange("h p t -> p h t"))
            S.dma("sp", [("QN", 3072, 1), ("QN", 3584, 1)], ["QP"], out=QP[:], in_=NQPT[g * 4:(g + 1) * 4].rearrange("h p t -> p h t"))
            def gv_of(j):
                return gates[:, j, g * 24:(g + 1) * 24].rearrange("p (r b) -> p r b", b=3)

            def fin_branch(j, nper, width, branch, first):
                Yj = Y[j % 2]; yk = f"Y{j % 2}"
                nb_ = (8 + nper - 1) // nper
                for b in range(nb_):
                    ab, ak = bank(4 + b)
                    nh = min(nper, 8 - b * nper)
                    S.op("dve", [ak], ["den8"], "tensor_scalar", out=den8[:, b * nper:b * nper + nh].unsqueeze(2),
                         in0=ab[:, 0:nh * width].rearrange("p (r c) -> p r c", c=width)[:, :, 64:65], scalar1=1e-30,
                         scalar2=None, op0=ALU.max)
                S.op("dve", ["den8"], ["rd8"], "reciprocal", out=rd8[:], in_=den8[:])
                S.op("dve", ["rd8", ("gates", j)], ["coef"], "tensor_tensor", out=coef[:], in0=rd8[:], in1=gv_of(j)[:, :, branch], op=ALU.mult)
                for r in range(8):
                    ab, ak = bank(4 + r // nper)
                    off = (r % nper) * width
                    if first:
                        S.op("dve", [ak, "coef"], [(yk, r)], "tensor_scalar", out=Yj[:, r, :], in0=ab[:, off:off + 64],
                             scalar1=coef[:, r:r + 1], scalar2=None, op0=ALU.mult)
                    else:
                        S.op("dve", [ak, "coef", (yk, r)], [(yk, r)], "scalar_tensor_tensor", out=Yj[:, r, :], in0=ab[:, off:off + 64],
                             scalar=coef[:, r:r + 1], in1=Yj[:, r, :], op0=ALU.mult, op1=ALU.add)

            def chain(j):
                nonlocal nsc
                q0 = j * 128
                for c in range(2):
                    nn = 128 if c == 0 else 127
                    for par in range(2):
                        pb, pk = bank(nsc % 4); nsc += 1
                        S.op("pe", ["KCT", "QP"], [pk], "matmul", pb[0:nn, :].rearrange("p (a q) -> p a q", a=4),
                             lhsT=KCT[:, par, g, c * 128:c * 128 + nn], rhs=QP[:, :, q0:q0 + 128],
                             start=True, stop=True)
                        S.op("act", [pk], [("ecmp", c, par)], "activation", out=ecmp[0:nn, c, par, :], in_=pb[0:nn, :], func=AF.Exp, scale=0.125)
                        ev = ecmp[0:nn, c, par, :].rearrange("p (a q) -> p a q", a=4)
                        S.op("dve", [("ecmp", c, par), "mc"], [("ecmp", c, par)], "tensor_tensor", out=ev, in0=ev,
                             in1=mc[0:nn, j, c, :].unsqueeze(1).broadcast_to([nn, 4, 128]), op=ALU.mult)
                for r in range(8):
                    ii, par = r // 2, r % 2
                    ab, ak = bank(4 + r // 3)
                    off = (r % 3) * 129
                    for c in range(2):
                        nn = 128 if c == 0 else 127
                        S.op("pe", [("ecmp", c, par), "VCO"], [ak], "matmul", ab[:, off:off + 129],
                             lhsT=ecmp[0:nn, c, par, ii * 128:(ii + 1) * 128], rhs=VCO[0:nn, g, c, :],
                             start=(c == 0 and r % 3 == 0), stop=(c == 1), skip_group_check=True)
                fin_branch(j, 3, 129, 0, True)
                for r in range(8):
                    ab, ak = bank(4 + r // 3)
                    off = (r % 3) * 129
                    if r == 0:
                        S.op("dve", [ak, "rd8"], ["IMP"], "tensor_scalar", out=IMP[:], in0=ab[:, off + 65:off + 129], scalar1=rd8[:, 0:1],
                             scalar2=None, op0=ALU.mult)
                    else:
                        S.op("dve", [ak, "rd8", "IMP"], ["IMP"], "scalar_tensor_tensor", out=IMP[:], in0=ab[:, off + 65:off + 129],
                             scalar=rd8[:, r:r + 1], in1=IMP[:], op0=ALU.mult, op1=ALU.add)
                S.op("dve", ["IMP", "valid"], ["sc"], "tensor_tensor", out=sc[:], in0=IMP[:], in1=valid[:, j, :], op=ALU.mult)
                S.op("dve", ["sc", "addc"], ["sc"], "tensor_tensor", out=sc[:], in0=sc[:], in1=addc[:, j, :], op=ALU.add)
                S.op("dve", ["sc"], ["m8a"], "max", out=m8a[:], in_=sc[:])
                S.op("dve", ["sc", "m8a"], ["sc2"], "match_replace", out=sc2[:], in_to_replace=m8a[:], in_values=sc[:], imm_value=-3.0)
                S.op("dve", ["sc2"], ["m8b"], "max", out=m8b[:], in_=sc2[:])
                S.op("dve", ["sc", "m8b"], ["selb"], "tensor_scalar", out=selb[:, 0:64], in0=sc[:], scalar1=m8b[:, 7:8], scalar2=None, op0=ALU.is_ge)
                tb, tk = bank_bf(7)
                S.op("pe", ["selb", "selbz", "ident"], [tk], "transpose", tb[:, 0:128], selb[:], ident[:])
                S.op("act", [tk], [f"selT{j % 2}"], "activation", out=selT[j % 2][:], in_=tb[:, 0:128], func=AF.Copy)

            def maskgen(j, kb4):
                jp = j & 1
                nkb = 2 * j + 2
                nb = min(4, nkb - kb4)
                mi = (kb4 // 4) % 2
                pm, pmk = bank(6)
                for q_ in range(nb):
                    S.op("pe", ["eexp", "eexpz", f"selT{j % 2}"], [pmk], "matmul", pm[:, q_ * 128:(q_ + 1) * 128], lhsT=eexp[:, kb4 + q_, :],
                         rhs=selT[j % 2][:, :], start=True, stop=True, skip_group_check=True)
                ncaus = sum(1 for q_ in range(nb) if kb4 + q_ >= 2 * j)
                nplain = nb - ncaus
                if nplain > 0:
                    S.op("act", [pmk], [(f"M4{mi}", q2) for q2 in range(nplain)], "activation", out=M4[mi][:, 0:nplain * 128],
                         in_=pm[:, 0:nplain * 128], func=AF.Copy)
                for q_ in range(nplain, nb):
                    kb = kb4 + q_
                    S.op("dve", [pmk, "m128"], [(f"M4{mi}", q_)], "tensor_tensor", out=M4[mi][:, q_ * 128:(q_ + 1) * 128],
                         in0=pm[:, q_ * 128:(q_ + 1) * 128], in1=m128[:, jp, 6 + (kb - 2 * j), :], op=ALU.mult)

            def attn_units(j, kbs, Kt, Kk, V1, Vk, maskfn, pre=None):
                nonlocal nsc
                q0 = j * 128
                units = [(kb, par) for kb in kbs for par in range(2)]
                base = nsc
                nsc += len(units)
                LA_ = 2
                for n in range(len(units) + LA_):
                    if n < len(units):
                        kb, par = units[n]
                        if pre is not None and par == 0:
                            pre(kb)
                        mk_ = maskfn(kb)
                        pb, pk = bank((base + n) % 4)
                        e = es_[(base + n) % 4]; ek = f"es{(base + n) % 4}"
                        S.op("pe", [(Kk, par), (Kk + "z", par), "QR"], [pk], "matmul", pb[:, :].rearrange("p (a q) -> p a q", a=4),
                             lhsT=Kt[par][:, kb * 128:(kb + 1) * 128], rhs=QR[:, :, q0:q0 + 128], start=True, stop=True)
                        S.op("act", [pk], [ek], "activation", out=e[:], in_=pb[:, :], func=AF.Exp, scale=0.125)
                        if mk_ is not None:
                            map_, mkey = mk_
                            ev = e[:].rearrange("p (a q) -> p a q", a=4)
                            S.op("dve", [ek, mkey], [ek], "tensor_tensor", out=ev, in0=ev,
                                 in1=map_.unsqueeze(1).broadcast_to([128, 4, 128]), op=ALU.mult)
                    if n >= LA_:
                        kb, par = units[n - LA_]
                        e = es_[(base + n - LA_) % 4]; ek = f"es{(base + n - LA_) % 4}"
                        for ii in range(4):
                            r = 2 * ii + par
                            ab, ak = bank(4 + r // 4)
                            off = (r % 4) * 65
                            S.op("pe", [ek, (Vk, kb // 8), Vk + "o"], [ak], "matmul", ab[:, off:off + 65], lhsT=e[:, ii * 128:(ii + 1) * 128],
                                 rhs=V1[:, kb, :], start=(kb == kbs[0] and r % 4 == 0), stop=(kb == kbs[-1]), skip_group_check=True)

            chain(0)
            for j in range(16):
                jp = j & 1
                q0 = j * 128
                if j + 1 < 16:
                    chain(j + 1)
                nkb = 2 * j + 2
                if PD_DEBUG["sel"]:
                    maskgen(j, 0)

                    def pre(kb, j=j, nkb=nkb):
                        if kb % 4 == 0 and kb + 4 < nkb:
                            maskgen(j, kb + 4)

                    def smask(kb):
                        return (M4[(kb // 4) % 2][:, (kb % 4) * 128:(kb % 4 + 1) * 128], (f"M4{(kb // 4) % 2}", kb % 4))

                    attn_units(j, list(range(nkb)), KS, "KS", VS1, "VS1", smask, pre)
                    fin_branch(j, 4, 65, 1, False)
                if PD_DEBUG["win"]:
                    kbs = [kb for kb in range(2 * j - 4, 2 * j + 2) if kb >= 0]

                    def wmask(kb, j=j, jp=jp):
                        idx = kb - (2 * j - 4)
                        if idx in (2, 3):
                            return None
                        return (m128[:, jp, idx, :], "m128")

                    attn_units(j, kbs, KW, "KW", VW1, "VW1", wmask)
                    fin_branch(j, 4, 65, 2, False)
                yk = f"Y{j % 2}"
                S.op("act", [(yk, r) for r in range(8)], ["Yb"], "activation", out=Yb[:], in_=Y[j % 2][:].rearrange("p r d -> p (r d)"), func=AF.Copy)
                tb, tk = bank_bf(7)
                for ii in range(4):
                    S.op("pe", ["Yb", "ident"], [tk], "transpose", tb[:, ii * 128:(ii + 1) * 128], Yb[:, ii * 128:(ii + 1) * 128], ident[:])
                S.op("act", [tk], [("ystg", j)], "activation", out=ystg[:, :, q0:q0 + 128],
                     in_=tb[:, 0:512].rearrange("p (c t) -> p c t", c=4), func=AF.Copy)
            S.dma("pool", [("ystg", j) for j in range(16)], [("YBT", g)], out=YBT[g * 4:(g + 1) * 4].rearrange("h p t -> p h t"), in_=ystg[:])
        S.barrier()
    if stop_after == "PD":
        return finish()

    build_rowlocal(nc, S, es, L)


def build_rowlocal(nc, S, es, L):
    g_ = lambda n: L[n]
    sbt, bank, bank_bf, ident, rstd_from_ss = g_("sbt"), g_("bank"), g_("bank_bf"), g_("ident"), g_("rstd_from_ss")
    x_own, p_own, out_d = g_("x_own"), g_("p_own"), g_("out_d")
    YAT, YBT, SGT = g_("YAT"), g_("YBT"), g_("SGT")
    WPD, WPN, WOUT, WUP, WDN, WPP, WPG = g_("WPD"), g_("WPN"), g_("WOUT"), g_("WUP"), g_("WDN"), g_("WPP"), g_("WPG")
    with contextlib.ExitStack() as pe_:
        gvec = sbt(pe_, "gvec", [128, D], F32)
        aT = sbt(pe_, "aT", [128, 64, 512], BF16)
        wbig = [sbt(pe_, f"wbig{i}", [128, 16, 512], BF16) for i in range(2)]
        x1 = sbt(pe_, "x1", [128, 4, D], F32)
        hT2 = sbt(pe_, "hT2", [128, 16, 512], BF16)
        wpp = sbt(pe_, "wpp", [128, 2, D], BF16)
        hb2 = sbt(pe_, "hb2", [128, D], BF16)
        sgt = [sbt(pe_, f"sgt{i}", [128, 2, 512], BF16) for i in range(2)]
        t1 = sbt(pe_, "t1", [128, 512], F32)
        t2 = sbt(pe_, "t2", [128, 512], F32)
        xin = [sbt(pe_, f"xin{i}", [128, 512], F32) for i in range(2)]
        rl = [sbt(pe_, f"rl{i}", [128, 512], F32) for i in range(2)]
        gt = [sbt(pe_, f"gt{i}", [128, 512], F32) for i in range(2)]
        ev = [sbt(pe_, f"ev{i}", [128, 512], F32) for i in range(2)]
        ss1 = sbt(pe_, "ss1", [128, 8], F32)
        rs1 = sbt(pe_, "rs1", [128, 8], F32)
        ssE = sbt(pe_, "ssE", [128, 16], F32)
        ssE4 = sbt(pe_, "ssE4", [128, 4], F32)
        rsE = sbt(pe_, "rsE", [128, 4], F32)
        pt = sbt(pe_, "pt", [128, 256], F32)
        ptb = sbt(pe_, "ptb", [128, 256], BF16)
        pT = sbt(pe_, "pT", [128, 2, 512], BF16)
        S.dma("sp", [("WPP", 0)], ["wpp"], out=wpp[:], in_=WPP[:, :].rearrange("(k p) c -> p k c", p=128))
        cn = {"w": 0, "ps": 0, "sg": 0, "x": 0, "r": 0, "g": 0, "e": 0}

        def nextw():
            i = cn["w"] % 2; cn["w"] += 1
            return wbig[i], f"wbig{i}"

        def nextbank():
            b = cn["ps"] % 4; cn["ps"] += 1
            return bank(b)

        def to_hT2(blk):
            for half in range(2):
                pb, pk = bank_bf(6 + half)
                for k in range(8):
                    kk = half * 8 + k
                    S.op("pe", ["hb2", "ident"], [pk], "transpose", pb[:, k * 128:(k + 1) * 128], hb2[:, kk * 128:(kk + 1) * 128], ident[:])
                dst = hT2[:, half * 8:half * 8 + 8, blk * 128:(blk + 1) * 128]
                srcv = pb.rearrange("p (k t) -> p k t", k=8)
                if half == 0:
                    S.op("act", [pk], [("hT2", blk, 0)], "activation", out=dst, in_=srcv, func=AF.Copy)
                else:
                    S.op("dve", [pk], [("hT2", blk, 1)], "tensor_copy", out=dst, in_=srcv)

        hT2keys = [("hT2", b, h) for b in range(4) for h in range(2)]
        for tt in range(4):
            tok0 = tt * 512
            S.dma("sp", [("YAT", h) for h in range(8)], [("aT", k) for k in range(8)], out=aT[:, 0:8, :],
                  in_=YAT[:, :, tok0:tok0 + 512].rearrange("h p t -> p h t"))
            S.dma("sp", [("YBT", 0), ("YBT", 1)], [("aT", k) for k in range(8, 16)], out=aT[:, 8:16, :],
                  in_=YBT[:, :, tok0:tok0 + 512].rearrange("h p t -> p h t"))
            for cc in range(4):
                w, wk = nextw()
                S.dma("sp", [("WPD", r) for r in range(0, 1024, 256)], [(wk, 0)], out=w[:, 0:8, :],
                      in_=WPD[:, cc * 512:(cc + 1) * 512].rearrange("(k p) c -> p k c", p=128))
                S.dma("sp", [("WPN", r) for r in range(0, 1024, 256)], [(wk, 1)], out=w[:, 8:16, :],
                      in_=WPN[:, cc * 512:(cc + 1) * 512].rearrange("(k p) c -> p k c", p=128))
                for f in range(4):
                    fidx = cc * 4 + f
                    si = cn["sg"] % 2; cn["sg"] += 1
                    for gi in range(2):
                        S.dma("sp", [("SGT", gi, fidx * 128, tt)], [(f"sgt{si}", gi)], out=sgt[si][:, gi, :],
                              in_=SGT[gi, fidx * 128:(fidx + 1) * 128, tok0:tok0 + 512])
                    pA, pAk = nextbank()
                    pB, pBk = nextbank()
                    for k in range(8):
                        S.op("pe", [(wk, 0), ("aT", k)], [pAk], "matmul", pA[:, :], lhsT=w[:, k, f * 128:(f + 1) * 128], rhs=aT[:, k, :],
                             start=(k == 0), stop=(k == 7))
                    for k in range(8):
                        S.op("pe", [(wk, 1), ("aT", 8 + k)], [pBk], "matmul", pB[:, :], lhsT=w[:, 8 + k, f * 128:(f + 1) * 128],
                             rhs=aT[:, 8 + k, :], start=(k == 0), stop=(k == 7))
                    S.op("dve", [pAk, (f"sgt{si}", 0)], ["t1"], "tensor_tensor", out=t1[:], in0=pA[:, :], in1=sgt[si][:, 0, :], op=ALU.mult)
                    S.op("dve", [pBk, (f"sgt{si}", 1)], ["t2"], "tensor_tensor", out=t2[:], in0=pB[:, :], in1=sgt[si][:, 1, :], op=ALU.mult)
                    S.op("pool", ["t1", "t2"], [("aT", 16 + fidx)], "tensor_tensor", out=aT[:, 16 + fidx, :], in0=t1[:], in1=t2[:], op=ALU.add)
            for cc in range(4):
                w, wk = nextw()
                S.dma("sp", [("WOUT", r) for r in range(0, D, 256)], [(wk, 0), (wk, 1)], out=w[:],
                      in_=WOUT[:, cc * 512:(cc + 1) * 512].rearrange("(k p) c -> p k c", p=128))
                for blk in range(4):
                    pb, pk = nextbank()
                    for k in range(16):
                        S.op("pe", [(wk, 0), (wk, 1), ("aT", 16 + k)], [pk], "matmul", pb[:, :], lhsT=aT[:, 16 + k, blk * 128:(blk + 1) * 128],
                             rhs=w[:, k, :], start=(k == 0), stop=(k == 15))
                    xi = cn["x"] % 2; cn["x"] += 1
                    S.dma("sp", [], [f"xin{xi}"], out=xin[xi][:], in_=x_own[tok0 + blk * 128:tok0 + (blk + 1) * 128, cc * 512:(cc + 1) * 512])
                    S.op("dve", [pk, f"xin{xi}"], [("x1", blk, cc)], "tensor_tensor", out=x1[:, blk, cc * 512:(cc + 1) * 512], in0=pb[:, :],
                         in1=xin[xi][:], op=ALU.add)
            S.dma("sp", [], ["gvec"], out=gvec[:], in_=g_("norm_mlp").broadcast_to([128, D]))
            for blk in range(4):
                xk = [("x1", blk, c) for c in range(4)]
                S.op("act", xk, ["hb2", ("ss1", blk)], "activation", out=hb2[:], in_=x1[:, blk, :], func=AF.Square, accum_out=ss1[:, blk:blk + 1])
                rstd_from_ss(ss1[:, blk:blk + 1], rs1[:, blk:blk + 1], 1, [("ss1", blk)], [("rs1", blk)], 1.0 / D)
                S.op("dve", xk + [("rs1", blk), "gvec"], ["hb2"], "scalar_tensor_tensor", out=hb2[:], in0=x1[:, blk, :],
                     scalar=rs1[:, blk:blk + 1], in1=gvec[:], op0=ALU.mult, op1=ALU.mult)
                to_hT2(blk)
            for uc in range(16):
                w, wk = nextw()
                S.dma("sp", [("WUP", r) for r in range(0, D, 256)], [(wk, 0), (wk, 1)], out=w[:],
                      in_=WUP[:, uc * 512:(uc + 1) * 512].rearrange("(k p) c -> p k c", p=128))
                for f in range(4):
                    pb, pk = nextbank()
                    for k in range(16):
                        S.op("pe", [(wk, 0), (wk, 1)] + hT2keys, [pk], "matmul", pb[:, :], lhsT=w[:, k, f * 128:(f + 1) * 128], rhs=hT2[:, k, :],
                             start=(k == 0), stop=(k == 15))
                    ri = cn["r"] % 2; cn["r"] += 1
                    S.op("act", [pk], [f"rl{ri}"], "activation", out=rl[ri][:], in_=pb[:, :], func=AF.Relu)
                    S.op("pool", [f"rl{ri}"], [("aT", uc * 4 + f)], "tensor_tensor", out=aT[:, uc * 4 + f, :], in0=rl[ri][:], in1=rl[ri][:], op=ALU.mult)
            for fc in range(4):
                base = 0 if fc % 2 == 0 else 4
                for kg in range(4):
                    w, wk = nextw()
                    S.dma("sp", [("WDN", r) for r in range(0, DFF, 1024)], [(wk, 0), (wk, 1)], out=w[:],
                          in_=WDN[kg * 2048:(kg + 1) * 2048, fc * 512:(fc + 1) * 512].rearrange("(k p) c -> p k c", p=128))
                    for k in range(16):
                        ffc = kg * 16 + k
                        for blk in range(4):
                            pb, pk = bank(base + blk)
                            S.op("pe", [(wk, 0), (wk, 1), ("aT", ffc)], [pk], "matmul", pb[:, :], lhsT=aT[:, ffc, blk * 128:(blk + 1) * 128],
                                 rhs=w[:, k, :], start=(ffc == 0), stop=(ffc == 63))
                for blk in range(4):
                    pb, pk = bank(base + blk)
                    S.op("dve", [pk, ("x1", blk, fc)], [("x1", blk, fc)], "tensor_tensor", out=x1[:, blk, fc * 512:(fc + 1) * 512], in0=pb[:, :],
                         in1=x1[:, blk, fc * 512:(fc + 1) * 512], op=ALU.add)
            S.dma("sp", [], ["gvec"], out=gvec[:], in_=g_("norm_ple").broadcast_to([128, D]))
            for blk in range(4):
                xk = [("x1", blk, c) for c in range(4)]
                S.op("act", xk, ["hb2", ("ss1", 4 + blk)], "activation", out=hb2[:], in_=x1[:, blk, :], func=AF.Square,
                     accum_out=ss1[:, 4 + blk:5 + blk])
                rstd_from_ss(ss1[:, 4 + blk:5 + blk], rs1[:, 4 + blk:5 + blk], 1, [("ss1", 4 + blk)], [("rs1", 4 + blk)], 1.0 / D)
                S.op("dve", xk + [("rs1", 4 + blk)], ["hb2"], "tensor_scalar", out=hb2[:], in0=x1[:, blk, :], scalar1=rs1[:, 4 + blk:5 + blk],
                     scalar2=None, op0=ALU.mult)
                to_hT2(blk)
                S.dma("sp", [], ["pt"], out=pt[:], in_=p_own[tok0 + blk * 128:tok0 + (blk + 1) * 128, :])
                S.op("dve", ["pt"], ["ptb"], "tensor_copy", out=ptb[:], in_=pt[:])
                pb, pk = bank_bf(6)
                for k in range(2):
                    S.op("pe", ["ptb", "ident"], [pk], "transpose", pb[:, k * 128:(k + 1) * 128], ptb[:, k * 128:(k + 1) * 128], ident[:])
                S.op("act", [pk], [("pT", blk)], "activation", out=pT[:, :, blk * 128:(blk + 1) * 128],
                     in_=pb[:, 0:256].rearrange("p (k t) -> p k t", k=2), func=AF.Copy)
            for blk in range(4):
                for cc in range(4):
                    pb, pk = bank(4 + cn["e"] % 2); cn["e"] += 1
                    for k in range(2):
                        S.op("pe", [("pT", blk), "wpp"], [pk], "matmul", pb[:, :], lhsT=pT[:, k, blk * 128:(blk + 1) * 128],
                             rhs=wpp[:, k, cc * 512:(cc + 1) * 512], start=(k == 0), stop=(k == 1))
                    S.op("act", [pk], ["hb2", ("ssE", blk * 4 + cc)], "activation", out=hb2[:, 0:512], in_=pb[:, :], func=AF.Square,
                         accum_out=ssE[:, blk * 4 + cc:blk * 4 + cc + 1])
            S.op("dve", [("ssE", i) for i in range(16)], ["ssE4"], "tensor_reduce", out=ssE4[:], in_=ssE[:].rearrange("p (b c) -> p b c", c=4),
                 axis=AX.X, op=ALU.add)
            rstd_from_ss(ssE4[:], rsE[:], 4, ["ssE4"], ["rsE"], 1.0 / D)
            for cc in range(4):
                w, wk = nextw()
                S.dma("sp", [("WPG", r) for r in range(0, D, 256)], [(wk, 0), (wk, 1)], out=w[:],
                      in_=WPG[:, cc * 512:(cc + 1) * 512].rearrange("(k p) c -> p k c", p=128))
                for blk in range(4):
                    pg, pgk = nextbank()
                    for k in range(16):
                        S.op("pe", [(wk, 0), (wk, 1), ("hT2", blk, 0), ("hT2", blk, 1)], [pgk], "matmul", pg[:, :],
                             lhsT=hT2[:, k, blk * 128:(blk + 1) * 128], rhs=w[:, k, :], start=(k == 0), stop=(k == 15))
                    pe2, pe2k = bank(4 + cn["e"] % 2); cn["e"] += 1
                    for k in range(2):
                        S.op("pe", [("pT", blk), "wpp"], [pe2k], "matmul", pe2[:, :], lhsT=pT[:, k, blk * 128:(blk + 1) * 128],
                             rhs=wpp[:, k, cc * 512:(cc + 1) * 512], start=(k == 0), stop=(k == 1))
                    gi = cn["g"] % 2; cn["g"] += 1
                    S.op("act", [pgk], [f"gt{gi}"], "activation", out=gt[gi][:], in_=pg[:, :], func=AF.Sigmoid)
                    S.op("dve", [pe2k, "rsE", "gvec"], [f"ev{gi}"], "scalar_tensor_tensor", out=ev[gi][:], in0=pe2[:, :], scalar=rsE[:, blk:blk + 1],
                         in1=gvec[:, cc * 512:(cc + 1) * 512], op0=ALU.mult, op1=ALU.mult)
                    S.op("pool", [f"ev{gi}", f"gt{gi}"], [f"ev{gi}"], "tensor_tensor", out=ev[gi][:], in0=ev[gi][:], in1=gt[gi][:], op=ALU.mult)
                    S.op("pool", [f"ev{gi}", ("x1", blk, cc)], [("x1", blk, cc)], "tensor_tensor", out=x1[:, blk, cc * 512:(cc + 1) * 512],
                         in0=x1[:, blk, cc * 512:(cc + 1) * 512], in1=ev[gi][:], op=ALU.add)
            for blk in range(4):
                S.dma("sp", [("x1", blk, c) for c in range(4)], [("out", tt, blk)], out=out_d[tok0 + blk * 128:tok0 + (blk + 1) * 128, :],
                      in_=x1[:, blk, :])
        S.barrier()


def _consts(hf):
    bf = ml_dtypes.bfloat16
    c = {}
    invf = np.power(np.float32(500000.0), -np.arange(0, 16, 2, dtype=np.float32) / np.float32(16)).astype(np.float32)
    c["c_invf"] = np.ascontiguousarray(np.broadcast_to(invf[None, :], (128, 8))).astype(np.float32)
    k = np.arange(128)[:, None, None]
    r = np.arange(8)[None, :, None]
    qq = np.arange(512)[None, None, :]
    t = qq // 128
    qpos = (2 * t + ((t & 1) ^ hf)) * 128 + (qq % 128)
    c["c_dmask"] = ((r * 128 + k) <= qpos).astype(bf)
    m128 = np.zeros((128, 2, 8, 128), np.float32)
    kk = np.arange(128)[:, None]
    mq = np.arange(128)[None, :]
    for jp in range(2):
        p = jp ^ hf
        q = p * 128 + mq
        for idx in range(6):
            key = (idx - 4) * 128 + kk
            dist = q - key
            m128[:, jp, idx, :] = ((dist >= 0) & (dist < 512))
        for idx in range(6, 8):
            key = (idx - 6) * 128 + kk
            m128[:, jp, idx, :] = (key <= q)
    c["c_m128"] = m128.astype(bf)
    mc = np.zeros((128, 16, 2, 128), np.float32)
    valid = np.zeros((128, 16, 64), np.float32)
    addc = np.zeros((128, 16, 64), np.float32)
    sel = np.arange(64)[None, :]
    for j in range(16):
        qp = own_block(j, hf) * 128 + np.arange(128)
        for ch in range(2):
            ng = ch * 128 + np.arange(128)
            mc[:, j, ch, :] = ((ng[:, None] <= 254) & (16 * ng[:, None] + 31 <= qp[None, :]))
        qb = (qp // 64)[:, None]
        v = sel <= qb
        f = (sel == 0) | (sel == qb) | (sel == qb - 1)
        valid[:, j, :] = v
        addc[:, j, :] = np.where(v, 1e4 * f, -1.0)
    c["c_mc"] = mc.astype(bf)
    c["c_valid"] = valid
    c["c_addc"] = addc.astype(np.float32)
    ovl = np.zeros((128, 2, 64), np.float32)
    for ch in range(2):
        ng = ch * 128 + np.arange(128)
        cs = 16 * ng[:, None]
        ssb = 64 * np.arange(64)[None, :]
        ovl[:, ch, :] = ((cs < ssb + 64) & (cs + 32 > ssb) & (ng[:, None] <= 254))
    c["c_ovl"] = ovl.astype(bf)
    jj = np.arange(64)[:, None, None]
    kb = np.arange(32)[None, :, None]
    k2 = np.arange(128)[None, None, :]
    c["c_eexp"] = (jj == 2 * kb + k2 // 64).astype(bf)
    return c


def make_in_maps(inputs):
    f = lambda a: np.ascontiguousarray(np.asarray(a))
    x = f(inputs["x"]); p = f(inputs["p"])[0]; pos = f(inputs["positions"]).astype(np.int32)
    shared = {
        "norm_mix": f(inputs["norm_mix"]).reshape(1, D),
        "w_in": f(inputs["w_in"])[0],
        "diff_q_norm": f(inputs["diff_q_norm"]).reshape(1, 64),
        "diff_k_norm": f(inputs["diff_k_norm"]).reshape(1, 64),
        "diff_lambda": f(inputs["diff_lambda"]).reshape(1, 256),
        "diff_subln": f(inputs["diff_subln"]).reshape(1, 128),
        "nsa_q_norm": f(inputs["nsa_q_norm"]).reshape(1, 64),
        "nsa_k_norm": f(inputs["nsa_k_norm"]).reshape(1, 64),
        "cmp_posT": f(np.transpose(f(inputs["cmp_pos"])[0], (2, 0, 1))),
        "cmp_w1": f(inputs["cmp_w1"])[0].reshape(4096, 256),
        "cmp_w2": f(inputs["cmp_w2"])[0].reshape(512, 64),
        "w_proj_diff": f(inputs["w_proj_diff"])[0],
        "w_proj_nsa": f(inputs["w_proj_nsa"])[0],
        "w_out": f(inputs["w_out"])[0],
        "norm_mlp": f(inputs["norm_mlp"]).reshape(1, D),
        "w_mlp_up": f(inputs["w_mlp_up"])[0],
        "w_mlp_down": f(inputs["w_mlp_down"])[0],
        "w_ple_proj": f(inputs["w_ple_proj"])[0],
        "norm_ple": f(inputs["norm_ple"]).reshape(1, D),
        "w_ple_gate": f(inputs["w_ple_gate"])[0],
    }
    cst = [_consts(0), _consts(1)]
    maps = []
    for c in range(8):
        b, hf = c // 2, c % 2
        blks = [own_block(j, hf) for j in range(16)]
        rows = np.concatenate([np.arange(bk * 128, (bk + 1) * 128) for bk in blks])
        m = dict(shared)
        m.update(cst[hf])
        m["x_all"] = x[b]
        m["x_own"] = f(x[b][rows])
        m["p_own"] = f(p[b][rows])
        m["posT_all"] = f(pos[b].reshape(32, 128).T)
        m["posT_own"] = f(pos[b][rows].reshape(16, 128).T)
        maps.append(m)
    return maps


def assemble(outs):
    res = np.zeros((4, SEQ, D), np.float32)
    for c in range(8):
        b, hf = c // 2, c % 2
        o = np.asarray(outs[c])
        for j in range(16):
            bk = own_block(j, hf)
            res[b, bk * 128:(bk + 1) * 128] = o[j * 128:(j + 1) * 128]
    return res


def kernel(**inputs):
    nc = build_nc()
    maps = make_in_maps(inputs)
    r = run_bass_kernel_spmd(nc, maps, core_ids=list(range(8)))
    return assemble([r.results[c]["out"] for c in range(8)])
```

```python
import contextlib
import math
import numpy as np
import ml_dtypes
import concourse.bass as bass
import concourse.mybir as mybir
from concourse.bass_utils import run_bass_kernel_spmd

F32, BF16, I32 = mybir.dt.float32, mybir.dt.bfloat16, mybir.dt.int32
AF = mybir.ActivationFunctionType
ALU = mybir.AluOpType
AX = mybir.AxisListType

D = 2048
SEQ = 4096
NOWN = 2048
IN_W = 9008
DFF = 8192
EPS = 1e-6
TWO_PI = 2.0 * math.pi


class Sched:
    def __init__(self, nc, es):
        self.nc = nc
        self.eng = {"pe": nc.tensor, "act": nc.scalar, "dve": nc.vector, "pool": nc.gpsimd, "sp": nc.sync}
        self.sem = {e: es.enter_context(nc.semaphore("s_" + e)) for e in ("pe", "act", "dve", "pool")}
        self.cnt = {e: 0 for e in self.sem}
        self.NDS = 12
        self.dsem = {q: [es.enter_context(nc.semaphore(f"d_{q}{i}")) for i in range(self.NDS)] for q in ("sp", "pool", "act")}
        self.dcnt = {q: [0] * self.NDS for q in self.dsem}
        self.dnext = {q: 0 for q in self.dsem}
        self.waited = {e: {} for e in self.eng}
        self.res = {}
        self.semname = {}
        self.n_wait = 0
        self.n_inst = 0

    def _wait(self, e, tok):
        if tok is None:
            return
        sem, val, owner = tok
        if e == "pe" and owner == "pe":
            return
        w = self.waited[e]
        if w.get(id(sem), 0) >= val:
            return
        self.eng[e].wait_ge(sem, val)
        self.n_wait += 1
        w[id(sem)] = val

    def _deps(self, e, reads, writes):
        for k in reads:
            r = self.res.get(k)
            if r:
                self._wait(e, r[0])
        for k in writes:
            r = self.res.get(k)
            if r:
                self._wait(e, r[0])
                for t in r[1].values():
                    self._wait(e, t)

    def _commit(self, tok, reads, writes):
        for k in reads:
            r = self.res.setdefault(k, [None, {}])
            r[1][id(tok[0])] = tok
        for k in writes:
            self.res[k] = [tok, {}]

    def op(self, e, reads, writes, meth, *args, **kw):
        ps_r = [k for k in reads if isinstance(k, tuple) and k[0] == "ps"]
        if ps_r:
            reads = [k for k in reads if k not in ps_r]
            writes = list(writes) + ps_r
        self._deps(e, reads, writes)
        self.cnt[e] += 1
        tok = (self.sem[e], self.cnt[e], e)
        getattr(self.eng[e], meth)(*args, **kw).then_inc(self.sem[e], 1)
        self.n_inst += 1
        self._commit(tok, reads, writes)
        return tok

    def dma(self, q, reads, writes, out, in_, **kw):
        i = self.dnext[q]
        self.dnext[q] = (i + 1) % self.NDS
        sem = self.dsem[q][i]
        if self.dcnt[q][i] > 0:
            self._wait(q, (sem, 16 * self.dcnt[q][i], "dma"))
        self._deps(q, reads, writes)
        self.dcnt[q][i] += 1
        tok = (sem, 16 * self.dcnt[q][i], "dma")
        self.eng[q].dma_start(out=out, in_=in_, **kw).then_inc(sem, 16)
        self.n_inst += 1
        self._commit(tok, reads, writes)
        return tok

    def barrier(self):
        toks = [(self.sem[e], self.cnt[e], "bar") for e in self.sem if self.cnt[e] > 0]
        for q in self.dsem:
            for i in range(self.NDS):
                if self.dcnt[q][i] > 0:
                    toks.append((self.dsem[q][i], 16 * self.dcnt[q][i], "dma"))
        for e in self.eng:
            for t in toks:
                self._wait(e, t)
        self.res = {}


PD_DEBUG = {"topk": True, "sel": True, "win": True, "skip_pc": False}


def own_block(j, hf):
    return 2 * j + ((j & 1) ^ hf)


def build_nc(dbg=False, stop_after=None):
    nc = bass.Bass("TRN2", target_bir_lowering=False)

    def din(name, shape, dt=F32):
        return nc.dram_tensor(name, list(shape), dt, kind="ExternalInput").ap()

    def dscr(name, shape, dt=BF16):
        if dbg:
            return nc.dram_tensor(name, list(shape), dt, kind="ExternalOutput").ap()
        return nc.dram_tensor(name, list(shape), dt).ap()

    x_all = din("x_all", [SEQ, D])
    x_own = din("x_own", [NOWN, D])
    p_own = din("p_own", [NOWN, 256])
    posT_all = din("posT_all", [128, 32], I32)
    posT_own = din("posT_own", [128, 16], I32)
    norm_mix = din("norm_mix", [1, D])
    w_in = din("w_in", [D, IN_W])
    diff_q_norm = din("diff_q_norm", [1, 64])
    diff_k_norm = din("diff_k_norm", [1, 64])
    diff_lambda = din("diff_lambda", [1, 256])
    diff_subln = din("diff_subln", [1, 128])
    nsa_q_norm = din("nsa_q_norm", [1, 64])
    nsa_k_norm = din("nsa_k_norm", [1, 64])
    cmp_posT = din("cmp_posT", [64, 2, 32])
    cmp_w1 = din("cmp_w1", [2 * 2048, 256])
    cmp_w2 = din("cmp_w2", [2 * 256, 64])
    w_proj_diff = din("w_proj_diff", [1024, D])
    w_proj_nsa = din("w_proj_nsa", [1024, D])
    w_out = din("w_out", [D, D])
    norm_mlp = din("norm_mlp", [1, D])
    w_mlp_up = din("w_mlp_up", [D, DFF])
    w_mlp_down = din("w_mlp_down", [DFF, D])
    w_ple_proj = din("w_ple_proj", [256, D])
    norm_ple = din("norm_ple", [1, D])
    w_ple_gate = din("w_ple_gate", [D, D])
    c_invf = din("c_invf", [128, 8])
    c_dmask = din("c_dmask", [128, 8, 512], BF16)
    c_m128 = din("c_m128", [128, 2, 8, 128], BF16)
    c_mc = din("c_mc", [128, 16, 2, 128], BF16)
    c_valid = din("c_valid", [128, 16, 64])
    c_addc = din("c_addc", [128, 16, 64])
    c_ovl = din("c_ovl", [128, 2, 64], BF16)
    c_eexp = din("c_eexp", [64, 32, 128], BF16)

    out_d = nc.dram_tensor("out", [NOWN, D], F32, kind="ExternalOutput").ap()

    W_IN = dscr("W_IN", [D, IN_W]) if not dbg else nc.dram_tensor("W_IN", [D, IN_W], BF16).ap()
    mk = lambda n, s: nc.dram_tensor(n, list(s), BF16).ap()
    CW1 = mk("CW1", [2 * 2048, 256]); CW2 = mk("CW2", [2 * 256, 64])
    WPD = mk("WPD", [1024, D]); WPN = mk("WPN", [1024, D]); WOUT = mk("WOUT", [D, D])
    WUP = mk("WUP", [D, DFF]); WDN = mk("WDN", [DFF, D]); WPP = mk("WPP", [256, D]); WPG = mk("WPG", [D, D])
    QDT = dscr("QDT", [8, 128, NOWN])
    KDT = dscr("KDT", [8, 128, SEQ])
    VD = dscr("VD", [SEQ, 1024])
    NQRT = dscr("NQRT", [8, 128, NOWN])
    NQPT = dscr("NQPT", [8, 128, NOWN])
    KCRT = dscr("KCRT", [128, SEQ]); VCRT = dscr("VCRT", [128, SEQ])
    KST = dscr("KST", [2, 128, SEQ]); KWT = dscr("KWT", [2, 128, SEQ])
    VS = dscr("VS", [SEQ, 128]); VW = dscr("VW", [SEQ, 128])
    SGT = dscr("SGT", [2, D, NOWN])
    YAT = dscr("YAT", [8, 128, NOWN])
    YBT = dscr("YBT", [8, 128, NOWN])

    es = contextlib.ExitStack()
    with es:
        S = Sched(nc, es)

        def sbt(stack, name, shape, dt):
            return stack.enter_context(nc.sbuf_tensor(name, list(shape), dt))

        PS = [es.enter_context(nc.psum_tensor(f"ps{i}", [128, 1024], F32)) for i in range(4)]

        def bank(i):
            return PS[i // 2][:, (i % 2) * 512:(i % 2) * 512 + 512], ("ps", i)

        def bank_bf(i):
            return PS[i // 2][:, (i % 2) * 512:(i % 2) * 512 + 512].bitcast(BF16), ("ps", i)

        ident = sbt(es, "ident", [128, 128], BF16)
        idf = sbt(es, "idf", [128, 128], F32)
        mhalf = sbt(es, "mhalf", [128, 64], F32)
        gates = sbt(es, "gates", [128, 16, 48], F32)
        CSA = sbt(es, "CSA", [128, 32, 16], F32)
        CSO = sbt(es, "CSO", [128, 16, 16], F32)
        S.op("pool", [], ["idf"], "iota", idf[:], pattern=[[1, 128]], base=0, channel_multiplier=-1,
             allow_small_or_imprecise_dtypes=True)
        S.op("dve", ["idf"], ["ident"], "tensor_single_scalar", out=ident[:], in_=idf[:], scalar=0.0, op=ALU.is_equal)
        S.op("pool", [], ["mhalf"], "memset", mhalf[:], -0.5)

        def rstd_from_ss(ss_ap, out_ap, n, keys_r, keys_w, inv_n):
            S.op("pool", keys_r, keys_w, "tensor_scalar", out=out_ap, in0=ss_ap, scalar1=inv_n, scalar2=EPS,
                 op0=ALU.mult, op1=ALU.add)
            P = out_ap.shape[0]
            S.op("pool", keys_w + ["mhalf"], keys_w, "tensor_tensor", out=out_ap, in0=out_ap, in1=mhalf[0:P, 0:n], op=ALU.pow)

        def conv(dst, src, R, key, step=256):
            for r0 in range(0, R, step):
                r1 = min(R, r0 + step)
                S.dma("pool", [], [(key, r0)], out=dst[r0:r1, :], in_=src[r0:r1, :], max_dma_last_dim=8192)

        for c0, n in ((0, 512), (512, 512), (3072, 512), (3584, 512), (4864, 48)) + tuple((4912 + i * 512, 512) for i in range(8)) + \
                ((1024, 512), (1536, 512), (2048, 512), (2560, 512), (4096, 512), (4608, 256)):
            S.dma("pool", [], [("W_IN", c0)], out=W_IN[:, c0:c0 + n], in_=w_in[:, c0:c0 + n], max_dma_last_dim=8192)
        pending_conv = []

        def conv_later(dst, src, R, key, step=256):
            for r0 in range(0, R, step):
                r1 = min(R, r0 + step)
                pending_conv.append((dst, src, r0, r1, key))

        def conv_pop(n=1):
            for _ in range(n):
                if pending_conv:
                    dst, src, r0, r1, key = pending_conv.pop(0)
                    S.dma("pool", [], [(key, r0)], out=dst[r0:r1, :], in_=src[r0:r1, :], max_dma_last_dim=8192)

        conv_later(CW1, cmp_w1, 4096, "CW1", 1024); conv_later(CW2, cmp_w2, 512, "CW2", 512)
        conv_later(WPD, w_proj_diff, 1024, "WPD"); conv_later(WPN, w_proj_nsa, 1024, "WPN"); conv_later(WOUT, w_out, D, "WOUT")
        conv_later(WUP, w_mlp_up, D, "WUP"); conv_later(WDN, w_mlp_down, DFF, "WDN", 1024)
        conv_later(WPP, w_ple_proj, 256, "WPP"); conv_later(WPG, w_ple_gate, D, "WPG")

        with contextlib.ExitStack() as ps_:
            invf = sbt(ps_, "invf", [128, 8], F32)
            S.dma("sp", [], ["invf"], out=invf[:], in_=c_invf[:, :])
            for nm, src, nb, CS in (("a", posT_all, 32, CSA), ("o", posT_own, 16, CSO)):
                pi = sbt(ps_, "pi" + nm, [128, nb], I32)
                pf = sbt(ps_, "pf" + nm, [128, nb], F32)
                ang = sbt(ps_, "ang" + nm, [128, nb, 8], F32)
                kf = sbt(ps_, "kf" + nm, [128, nb, 8], F32)
                ki = sbt(ps_, "ki" + nm, [128, nb, 8], I32)
                r0 = sbt(ps_, "r0" + nm, [128, nb, 8], F32)
                r1 = sbt(ps_, "r1" + nm, [128, nb, 8], F32)
                mm = sbt(ps_, "mm" + nm, [128, nb, 8], F32)
                S.dma("sp", [], ["pi" + nm], out=pi[:], in_=src[:, :])
                S.op("dve", ["pi" + nm], ["pf" + nm], "tensor_copy", out=pf[:], in_=pi[:])
                S.op("dve", ["pf" + nm, "invf"], ["ang" + nm], "tensor_tensor", out=ang[:],
                     in0=pf[:].unsqueeze(2).broadcast_to([128, nb, 8]),
                     in1=invf[:].unsqueeze(1).broadcast_to([128, nb, 8]), op=ALU.mult)
                S.op("dve", ["ang" + nm], ["kf" + nm], "tensor_scalar", out=kf[:], in0=ang[:], scalar1=1.0 / TWO_PI,
                     scalar2=None, op0=ALU.mult)
                S.op("dve", ["kf" + nm], ["ki" + nm], "tensor_copy", out=ki[:], in_=kf[:])
                S.op("dve", ["ki" + nm], ["kf" + nm], "tensor_copy", out=kf[:], in_=ki[:])
                S.op("dve", ["kf" + nm, "ang" + nm], ["r0" + nm], "scalar_tensor_tensor", out=r0[:], in0=kf[:],
                     scalar=-TWO_PI, in1=ang[:], op0=ALU.mult, op1=ALU.add)
                for which, shift in ((1, 0.0), (0, math.pi / 2)):
                    S.op("dve", ["r0" + nm], ["r1" + nm], "tensor_scalar", out=r1[:], in0=r0[:], scalar1=shift,
                         scalar2=None, op0=ALU.add)
                    for thr, op_, add in ((math.pi, ALU.is_gt, -TWO_PI), (-math.pi, ALU.is_lt, TWO_PI)):
                        S.op("dve", ["r1" + nm], ["mm" + nm], "tensor_scalar", out=mm[:], in0=r1[:], scalar1=thr,
                             scalar2=add, op0=op_, op1=ALU.mult)
                        S.op("dve", ["r1" + nm, "mm" + nm], ["r1" + nm], "tensor_tensor", out=r1[:], in0=r1[:],
                             in1=mm[:], op=ALU.add)
                    S.op("dve", ["r1" + nm], ["r1" + nm], "tensor_scalar", out=r1[:], in0=r1[:], scalar1=math.pi,
                         scalar2=-math.pi, op0=ALU.min, op1=ALU.max)
                    S.op("act", ["r1" + nm], ["CS" + nm], "activation", out=CS[:, :, which * 8:which * 8 + 8],
                         in_=r1[:], func=AF.Sin)
            S.barrier()

        with contextlib.ExitStack() as pa:
            gmix = sbt(pa, "gmix", [128, D], F32)
            S.dma("sp", [], ["gmix"], out=gmix[:], in_=norm_mix.broadcast_to([128, D]))
            gq = {}
            for nm, src in (("dq", diff_q_norm), ("dk", diff_k_norm), ("nq", nsa_q_norm), ("nk", nsa_k_norm)):
                gq[nm] = sbt(pa, "g_" + nm, [128, 64], F32)
                S.dma("sp", [], ["g_" + nm], out=gq[nm][:], in_=src.broadcast_to([128, 64]))
            xt = [sbt(pa, f"xt{i}", [128, D], F32) for i in range(2)]
            hb = [sbt(pa, f"hb{i}", [128, D], BF16) for i in range(2)]
            hT = sbt(pa, "hT", [128, 16, 2048], BF16)
            wc = [sbt(pa, f"wc{i}", [128, 16, 512], BF16) for i in range(2)]
            ss = sbt(pa, "ss", [128, 16], F32)
            rs = sbt(pa, "rs", [128, 16], F32)
            NZ = 3
            sqt = [sbt(pa, f"sqt{i}", [128, 512], BF16) for i in range(NZ)]
            ssh = [sbt(pa, f"ssh{i}", [128, 8], F32) for i in range(NZ)]
            rsh = [sbt(pa, f"rsh{i}", [128, 8], F32) for i in range(NZ)]
            zn = [sbt(pa, f"zn{i}", [128, 512], F32) for i in range(NZ)]
            zr = [sbt(pa, f"zr{i}", [128, 512], BF16) for i in range(NZ)]
            zp = [sbt(pa, f"zp{i}", [128, 512], BF16) for i in range(NZ)]
            rt = [sbt(pa, f"rt{i}", [128, 4, 8, 8], F32) for i in range(NZ)]
            rot = [sbt(pa, f"rot{i}", [128, 8, 16], F32) for i in range(NZ)]
            stg = [sbt(pa, f"stg{i}", [128, 4, 2048], BF16) for i in range(2)]
            stgH = sbt(pa, "stgH", [128, 2, 2048], BF16)
            NV = 6
            vst = [sbt(pa, f"vst{i}", [128, 512], BF16) for i in range(NV)]
            dupb = [sbt(pa, f"dupb{i}", [128, 256], BF16) for i in range(NV)]
            dfr = []
            LAB = 2

            def defer(fn):
                dfr.append(fn)

            def run_deferred(keep):
                while len(dfr) > keep:
                    dfr.pop(0)()
            cnt = {"w": 0, "z": 0, "v": 0, "ps": 0, "tp": 0, "d": 0}

            def build_hT(xsrc, blk0, ssname):
                for i in range(16):
                    xb = xt[i % 2]; xk = f"xt{i % 2}"
                    S.dma("sp", [], [xk], out=xb[:], in_=xsrc[(blk0 + i) * 128:(blk0 + i + 1) * 128, :])
                    hk = f"hb{i % 2}"
                    S.op("act", [xk], [hk, ("ss", i)], "activation", out=hb[i % 2][:], in_=xb[:], func=AF.Square,
                         accum_out=ss[:, i:i + 1])
                    rstd_from_ss(ss[:, i:i + 1], rs[:, i:i + 1], 1, [("ss", i)], [("rs", i)], 1.0 / D)
                    S.op("dve", [xk, ("rs", i), "gmix"], [hk], "scalar_tensor_tensor", out=hb[i % 2][:], in0=xb[:],
                         scalar=rs[:, i:i + 1], in1=gmix[:], op0=ALU.mult, op1=ALU.mult)
                    for half in range(2):
                        bi = 6 + half
                        pb, pk = bank_bf(bi)
                        for k in range(8):
                            kk = half * 8 + k
                            S.op("pe", [hk, "ident"], [pk], "transpose", pb[:, k * 128:(k + 1) * 128],
                                 hb[i % 2][:, kk * 128:(kk + 1) * 128], ident[:])
                        dst = hT[:, half * 8:half * 8 + 8, i * 128:(i + 1) * 128]
                        srcv = pb.rearrange("p (k t) -> p k t", k=8)
                        if half == 0:
                            S.op("act", [pk], [("hT", i, 0)], "activation", out=dst, in_=srcv, func=AF.Copy)
                        else:
                            S.op("dve", [pk], [("hT", i, 1)], "tensor_copy", out=dst, in_=srcv)

            def load_w(c0, n):
                i = cnt["w"] % 2; cnt["w"] += 1
                S.dma("sp", [("W_IN", c0)], [f"wc{i}"], out=wc[i][:, :, 0:n],
                      in_=W_IN[:, c0:c0 + n].rearrange("(k p) c -> p k c", p=128))
                return wc[i], f"wc{i}"

            def mm_tm(w, wk, blk, n, coff=0):
                bi = cnt["ps"] % 4; cnt["ps"] += 1
                pb, pk = bank(bi)
                if cnt.get("conv"):
                    conv_pop(1)
                for k in range(16):
                    S.op("pe", [wk, ("hT", blk, 0), ("hT", blk, 1)], [pk], "matmul", pb[:, 0:n], lhsT=hT[:, k, blk * 128:(blk + 1) * 128],
                         rhs=w[:, k, coff:coff + n], start=(k == 0), stop=(k == 15))
                return pb, pk

            def normrope(pb, pk, c0, nh, gname, cs, csk, blk, want_plain=False):
                i = cnt["z"] % NZ; cnt["z"] += 1
                n = nh * 64
                pv = pb[:, c0:c0 + n]
                pv3 = pv.rearrange("p (h d) -> p h d", d=64)
                S.op("act", [pk], [f"sqt{i}"], "activation", out=sqt[i][:, 0:n], in_=pv, func=AF.Square)
                S.op("dve", [f"sqt{i}"], [f"ssh{i}"], "tensor_reduce", out=ssh[i][:, 0:nh],
                     in_=sqt[i][:, 0:n].rearrange("p (h d) -> p h d", d=64), axis=AX.X, op=ALU.add)
                rstd_from_ss(ssh[i][:, 0:nh], rsh[i][:, 0:nh], nh, [f"ssh{i}"], [f"rsh{i}"], 1.0 / 64)
                z3 = zn[i][:, 0:n].rearrange("p (h d) -> p h d", d=64)
                S.op("dve", [pk, "g_" + gname], [f"zn{i}"], "tensor_tensor", out=z3, in0=pv3,
                     in1=gq[gname][:].unsqueeze(1).broadcast_to([128, nh, 64]), op=ALU.mult)
                x1 = z3[:, :, 0:8]; x2 = z3[:, :, 8:16]
                cc = cs[:, blk, 0:8].unsqueeze(1).broadcast_to([128, nh, 8])
                sn = cs[:, blk, 8:16].unsqueeze(1).broadcast_to([128, nh, 8])
                r = rt[i]; rk = f"rt{i}"
                ro = rot[i][:, 0:nh, :]
                S.op("pool", [f"zn{i}", csk], [(rk, 0)], "tensor_tensor", out=r[:, 0, 0:nh, :], in0=x1, in1=cc, op=ALU.mult)
                S.op("pool", [f"zn{i}", csk], [(rk, 1)], "tensor_tensor", out=r[:, 1, 0:nh, :], in0=x2, in1=sn, op=ALU.mult)
                S.op("pool", [f"zn{i}", csk], [(rk, 2)], "tensor_tensor", out=r[:, 2, 0:nh, :], in0=x2, in1=cc, op=ALU.mult)
                S.op("pool", [f"zn{i}", csk], [(rk, 3)], "tensor_tensor", out=r[:, 3, 0:nh, :], in0=x1, in1=sn, op=ALU.mult)
                S.op("pool", [(rk, 0), (rk, 1)], [f"rot{i}"], "tensor_tensor", out=ro[:, :, 0:8],
                     in0=r[:, 0, 0:nh, :], in1=r[:, 1, 0:nh, :], op=ALU.subtract)
                S.op("pool", [(rk, 2), (rk, 3), f"rot{i}"], [f"rot{i}"], "tensor_tensor", out=ro[:, :, 8:16],
                     in0=r[:, 2, 0:nh, :], in1=r[:, 3, 0:nh, :], op=ALU.add)
                rb = rsh[i][:, 0:nh].unsqueeze(2)
                zr3 = zr[i][:, 0:n].rearrange("p (h d) -> p h d", d=64)
                S.op("dve", [f"zn{i}", f"rsh{i}"], [f"zr{i}"], "tensor_tensor", out=zr3[:, :, 16:64], in0=z3[:, :, 16:64],
                     in1=rb.broadcast_to([128, nh, 48]), op=ALU.mult)
                S.op("dve", [f"rot{i}", f"rsh{i}", f"zr{i}"], [f"zr{i}"], "tensor_tensor", out=zr3[:, :, 0:16], in0=ro,
                     in1=rb.broadcast_to([128, nh, 16]), op=ALU.mult)
                if want_plain:
                    zp3 = zp[i][:, 0:n].rearrange("p (h d) -> p h d", d=64)
                    S.op("dve", [f"zn{i}", f"rsh{i}"], [f"zp{i}"], "tensor_tensor", out=zp3, in0=z3,
                         in1=rb.broadcast_to([128, nh, 64]), op=ALU.mult)
                return i

            def transp_to(src_ap, src_key, ncol128, dst_ap, dst_key):
                bi = 4 + cnt["tp"] % 2; cnt["tp"] += 1
                pb, pk = bank_bf(bi)
                for c in range(ncol128):
                    S.op("pe", [src_key, "ident"], [pk], "transpose", pb[:, c * 128:(c + 1) * 128],
                         src_ap[:, c * 128:(c + 1) * 128], ident[:])
                S.op("act", [pk], [dst_key], "activation", out=dst_ap,
                     in_=pb[:, 0:ncol128 * 128].rearrange("p (c t) -> p c t", c=ncol128), func=AF.Copy)

            sidx = {"i": 0}

            def new_stage():
                i = sidx["i"] % 2; sidx["i"] += 1
                return stg[i], f"stg{i}"

            build_hT(x_own, 0, "o")
            for (c0, gname, dstR, dstP) in ((0, "dq", QDT, None), (512, "dq", QDT, None),
                                            (3072, "nq", NQRT, NQPT), (3584, "nq", NQRT, NQPT)):
                w, wk = load_w(c0, 512)
                sR, sRk = new_stage()
                if dstP is not None:
                    sP, sPk = new_stage()
                for blk in range(16):
                    pb, pk = mm_tm(w, wk, blk, 512)
                    i = normrope(pb, pk, 0, 8, gname, CSO, "CSo", blk, want_plain=dstP is not None)
                    defer(lambda i=i, blk=blk, sR=sR, sRk=sRk: transp_to(zr[i], f"zr{i}", 4, sR[:, :, blk * 128:(blk + 1) * 128], (sRk, blk)))
                    if dstP is not None:
                        defer(lambda i=i, blk=blk, sP=sP, sPk=sPk: transp_to(zp[i], f"zp{i}", 4, sP[:, :, blk * 128:(blk + 1) * 128], (sPk, blk)))
                    run_deferred(LAB * (2 if dstP is not None else 1))
                run_deferred(0)
                h0 = (c0 % 1024) // 128
                S.dma("pool", [(sRk, b) for b in range(16)], [("QN", c0, 0)], out=dstR[h0:h0 + 4].rearrange("h p t -> p h t"),
                      in_=sR[:])
                if dstP is not None:
                    S.dma("pool", [(sPk, b) for b in range(16)], [("QN", c0, 1)],
                          out=dstP[h0:h0 + 4].rearrange("h p t -> p h t"), in_=sP[:])
            w, wk = load_w(4864, 48)
            for blk in range(16):
                pb, pk = mm_tm(w, wk, blk, 48)
                S.op("act", [pk], [("gates", blk)], "activation", out=gates[:, blk, :], in_=pb[:, 0:48], func=AF.Sigmoid)
            for gi in range(2):
                for cc_ in range(4):
                    c0 = 4912 + gi * 2048 + cc_ * 512
                    w, wk = load_w(c0, 512)
                    for f in range(4):
                        for tg in range(4):
                            bi = cnt["ps"] % 4; cnt["ps"] += 1
                            pb, pk = bank(bi)
                            for k in range(16):
                                S.op("pe", [wk] + [("hT", tg * 4 + b, hh) for b in range(4) for hh in range(2)], [pk], "matmul", pb[:, :],
                                     lhsT=w[:, k, f * 128:(f + 1) * 128], rhs=hT[:, k, tg * 512:(tg + 1) * 512],
                                     start=(k == 0), stop=(k == 15))
                            vi = cnt["v"] % NV; cnt["v"] += 1
                            S.op("act", [pk], [f"vst{vi}"], "activation", out=vst[vi][:], in_=pb[:, :], func=AF.Sigmoid)
                            fr = cc_ * 512 + f * 128
                            S.dma("pool", [f"vst{vi}"], [("SGT", gi, fr, tg)], out=SGT[gi, fr:fr + 128, tg * 512:(tg + 1) * 512],
                                  in_=vst[vi][:])
            for hp in range(2):
                t0 = hp * 2048
                cnt["conv"] = 1
                build_hT(x_all, hp * 16, "a")
                for c0 in (1024, 1536):
                    w, wk = load_w(c0, 512)
                    sR, sRk = new_stage()
                    for blk in range(16):
                        pb, pk = mm_tm(w, wk, blk, 512)
                        i = normrope(pb, pk, 0, 8, "dk", CSA, "CSa", hp * 16 + blk)
                        defer(lambda i=i, blk=blk, sR=sR, sRk=sRk: transp_to(zr[i], f"zr{i}", 4, sR[:, :, blk * 128:(blk + 1) * 128], (sRk, blk)))
                        run_deferred(LAB)
                    run_deferred(0)
                    h0 = (c0 - 1024) // 128
                    S.dma("pool", [(sRk, b) for b in range(16)], [("KDT", h0, hp)],
                          out=KDT[h0:h0 + 4, :, t0:t0 + 2048].rearrange("h p t -> p h t"), in_=sR[:])
                for c0 in (2048, 2560):
                    w, wk = load_w(c0, 512)
                    for blk in range(16):
                        pb, pk = mm_tm(w, wk, blk, 512)
                        vi = cnt["v"] % NV; cnt["v"] += 1
                        S.op("act", [pk], [f"vst{vi}"], "activation", out=vst[vi][:], in_=pb[:, :], func=AF.Copy)
                        S.dma("pool", [f"vst{vi}"], [("VD", c0, hp, blk)],
                              out=VD[t0 + blk * 128:t0 + (blk + 1) * 128, c0 - 2048:c0 - 2048 + 512], in_=vst[vi][:])
                w, wk = load_w(4096, 512)
                w2_, wk2 = load_w(4608, 256)
                sE, sEk = new_stage()
                sF, sFk = stgH, "stgH"
                for blk in range(16):
                    gb = hp * 16 + blk
                    pb, pk = mm_tm(w, wk, blk, 512)
                    vi = cnt["v"] % NV; cnt["v"] += 1
                    S.op("act", [pk], [f"vst{vi}"], "activation", out=vst[vi][:, 0:256], in_=pb[:, 0:256], func=AF.Copy)
                    S.op("act", [pk], [f"vst{vi}"], "activation", out=vst[vi][:, 256:384], in_=pb[:, 384:512], func=AF.Copy)
                    S.dma("pool", [f"vst{vi}"], [("VS", gb)], out=VS[gb * 128:(gb + 1) * 128, :], in_=vst[vi][:, 256:384])
                    defer(lambda vi=vi, blk=blk: transp_to(vst[vi], f"vst{vi}", 2, sE[:, 0:2, blk * 128:(blk + 1) * 128], (sEk, blk)))
                    i = normrope(pb, pk, 256, 2, "nk", CSA, "CSa", gb)
                    zi = cnt["d"] % NV; cnt["d"] += 1
                    dup = dupb[zi]; dk_ = f"dupb{zi}"
                    S.op("dve", [f"zr{i}"], [dk_], "tensor_copy", out=dup[:, 0:256].rearrange("p (g r d) -> p g r d", g=2, r=2),
                         in_=zr[i][:, 0:128].rearrange("p (g d) -> p g d", g=2).unsqueeze(2).broadcast_to([128, 2, 2, 64]))
                    defer(lambda dup=dup, dk_=dk_, blk=blk: transp_to(dup, dk_, 2, sE[:, 2:4, blk * 128:(blk + 1) * 128], (sEk, blk)))
                    pb2, pk2 = mm_tm(w2_, wk2, blk, 256)
                    vi = cnt["v"] % NV; cnt["v"] += 1
                    S.op("act", [pk2], [f"vst{vi}"], "activation", out=vst[vi][:, 0:128], in_=pb2[:, 128:256], func=AF.Copy)
                    S.dma("pool", [f"vst{vi}"], [("VW", gb)], out=VW[gb * 128:(gb + 1) * 128, :], in_=vst[vi][:, 0:128])
                    i = normrope(pb2, pk2, 0, 2, "nk", CSA, "CSa", gb)
                    zi = cnt["d"] % NV; cnt["d"] += 1
                    dup = dupb[zi]; dk_ = f"dupb{zi}"
                    S.op("dve", [f"zr{i}"], [dk_], "tensor_copy", out=dup[:, 0:256].rearrange("p (g r d) -> p g r d", g=2, r=2),
                         in_=zr[i][:, 0:128].rearrange("p (g d) -> p g d", g=2).unsqueeze(2).broadcast_to([128, 2, 2, 64]))
                    defer(lambda dup=dup, dk_=dk_, blk=blk: transp_to(dup, dk_, 2, sF[:, 0:2, blk * 128:(blk + 1) * 128], (sFk, blk)))
                    run_deferred(3)
                run_deferred(0)
                rE = [(sEk, b) for b in range(16)]
                S.dma("pool", rE, [("KCRT", hp)], out=KCRT[:, t0:t0 + 2048], in_=sE[:, 0, :])
                S.dma("pool", rE, [("VCRT", hp)], out=VCRT[:, t0:t0 + 2048], in_=sE[:, 1, :])
                S.dma("pool", rE, [("KST", hp)], out=KST[:, :, t0:t0 + 2048].rearrange("g p t -> p g t"), in_=sE[:, 2:4, :])
                S.dma("pool", [(sFk, b) for b in range(16)], [("KWT", hp)],
                      out=KWT[:, :, t0:t0 + 2048].rearrange("g p t -> p g t"), in_=sF[:, 0:2, :])
            conv_pop(len(pending_conv))
            S.barrier()

        if stop_after == "PA":
            _finish(nc, S, out_d, es)
            return nc

        build_rest(nc, S, es, locals(), dbg=dbg, stop_after=stop_after)
    return nc


def _finish(nc, S, out_d, es):
    z = es.enter_context(nc.sbuf_tensor("zfin", [128, D], F32))
    S.op("dve", [], ["zfin"], "memset", z[:], 0.0)
    for i in range(16):
        S.dma("sp", ["zfin"], [("out", i)], out=out_d[i * 128:(i + 1) * 128, :], in_=z[:])
    S.barrier()


def build_rest(nc, S, es, L, dbg=False, stop_after=None):
    g_ = lambda n: L[n]
    sbt, bank, bank_bf, ident, gates, rstd_from_ss = (g_("sbt"), g_("bank"), g_("bank_bf"), g_("ident"), g_("gates"),
                                                      g_("rstd_from_ss"))
    out_d = g_("out_d")

    def finish():
        _finish(nc, S, out_d, es)

    KCT = sbt(es, "KCT", [128, 2, 2, 256], BF16)
    VCO = sbt(es, "VCO", [128, 2, 2, 129], BF16)
    S.op("dve", [], ["KCT"], "memset", KCT[:], 0.0)
    S.op("dve", [], ["VCO"], "memset", VCO[:], 0.0)

    CW1, CW2, KCRT, VCRT = g_("CW1"), g_("CW2"), g_("KCRT"), g_("VCRT")
    with contextlib.ExitStack() as pb_:
        kvT = [sbt(pb_, f"kvT{i}", [128, SEQ], BF16) for i in range(2)]
        S.dma("sp", [("KCRT", 0), ("KCRT", 1)], ["kvT0"], out=kvT[0][:], in_=KCRT[:, :])
        S.dma("sp", [("VCRT", 0), ("VCRT", 1)], ["kvT1"], out=kvT[1][:], in_=VCRT[:, :])
        W1 = [[sbt(pb_, f"W1_{kv}{g}", [128, 32, 256], BF16) for g in range(2)] for kv in range(2)]
        W2 = [sbt(pb_, f"W2_{kv}", [128, 2, 64], BF16) for kv in range(2)]
        for kv in range(2):
            for half in range(2):
                S.op("dve" if half == 0 else "pool", [], [(f"W1_{kv}", half, "z")], "memset", W1[kv][half][(1 - half) * 64:(2 - half) * 64, :, :], 0.0)
                for l4 in range(4):
                    S.dma("sp", [("CW1", r) for r in range(0, 4096, 1024)], [(f"W1_{kv}", half)], out=W1[kv][half][half * 64:(half + 1) * 64, l4 * 8:(l4 + 1) * 8, :],
                          in_=CW1[kv * 2048 + l4 * 512:kv * 2048 + (l4 + 1) * 512, :].rearrange("(l d) h -> d l h", d=64))
            S.dma("sp", [("CW2", 0)], [f"W2_{kv}"], out=W2[kv][:], in_=CW2[kv * 256:(kv + 1) * 256, :].rearrange("(c p) d -> p c d", p=128))
        posf = sbt(pb_, "posf", [128, 2, 32], F32)
        posb = sbt(pb_, "posb", [128, 2, 32], BF16)
        for half in range(2):
            S.dma("sp", [], ["posf"], out=posf[half * 64:(half + 1) * 64], in_=g_("cmp_posT")[:, :, :])
        S.op("dve", ["posf"], ["posb"], "tensor_copy", out=posb[:], in_=posf[:])
        gk = sbt(pb_, "gk", [128, 64], F32)
        S.dma("sp", [], ["gk"], out=gk[:], in_=g_("nsa_k_norm").broadcast_to([128, 64]))
        biasT = sbt(pb_, "biasT", [128, 4], F32)
        GH = sbt(pb_, "GH", [128, 8, 256], BF16)
        S.op("dve", [], [("GH", a, b, c) for a in range(2) for b in range(2) for c in range(2)], "memset", GH[:], 0.0)
        u = [sbt(pb_, f"u{i}", [128, 256], F32) for i in range(2)]
        u2 = [sbt(pb_, f"u2{i}", [128, 256], F32) for i in range(2)]
        sg = [sbt(pb_, f"sg{i}", [128, 256], F32) for i in range(2)]
        ssk = sbt(pb_, "ssk", [128, 4], F32)
        rsk = sbt(pb_, "rsk", [128, 4], F32)
        kcn2 = [sbt(pb_, f"kcn2{i}", [128, 256], BF16) for i in range(2)]
        for i in range(2):
            S.op("dve", [], [f"kcn2{i}"], "memset", kcn2[i][:], 0.0)
        junkb = sbt(pb_, "junkb", [128, 64], F32)
        S.op("dve", ["VCO"], ["VCO"], "memset", VCO[:, :, :, 64:65], 1.0)
        for g in range(2):
            S.dma("sp", ["VCO"], ["VCO"], out=VCO[:, g, :, 65:129], in_=g_("c_ovl")[:, :, :])
        pbb, pbk = bank(4)
        for kv in range(2):
            for hc in range(2):
                col = kv * 2 + hc
                for l in range(32):
                    S.op("pe", [(f"W1_{kv}", 0), (f"W1_{kv}", 0, "z"), "posb"], [pbk], "matmul", pbb[:, col:col + 1],
                         lhsT=W1[kv][0][:, l, hc * 128:(hc + 1) * 128], rhs=posb[:, kv, l:l + 1],
                         start=(l == 0), stop=(l == 31), skip_group_check=True)
        S.op("act", [pbk], ["biasT"], "activation", out=biasT[:], in_=pbb[:, 0:4], func=AF.Copy)
        n = 0
        for kv in range(2):
            for g in range(2):
                for hc in range(2):
                    pb, pk = bank(n % 4)
                    for l in range(32):
                        S.op("pe", [(f"W1_{kv}", g), (f"W1_{kv}", g, "z"), f"kvT{kv}"], [pk], "matmul", pb[:, 0:255],
                             lhsT=W1[kv][g][:, l, hc * 128:(hc + 1) * 128],
                             rhs=kvT[kv][:, l:l + 4065:16], start=(l == 0), stop=(l == 31))
                    i = n % 2
                    col = kv * 2 + hc
                    S.op("act", [pk, "biasT"], [f"u{i}"], "activation", out=u[i][:, 0:255], in_=pb[:, 0:255],
                         func=AF.Identity, bias=biasT[:, col:col + 1], scale=1.0)
                    S.op("pool", [f"u{i}"], [f"u2{i}"], "tensor_tensor", out=u2[i][:, 0:255], in0=u[i][:, 0:255],
                         in1=u[i][:, 0:255], op=ALU.mult)
                    S.op("pool", [f"u2{i}"], [f"u2{i}"], "tensor_scalar", out=u2[i][:, 0:255], in0=u2[i][:, 0:255],
                         scalar1=0.044715, scalar2=1.0, op0=ALU.mult, op1=ALU.add)
                    S.op("pool", [f"u2{i}", f"u{i}"], [f"u2{i}"], "tensor_tensor", out=u2[i][:, 0:255], in0=u2[i][:, 0:255],
                         in1=u[i][:, 0:255], op=ALU.mult)
                    S.op("act", [f"u2{i}"], [f"sg{i}"], "activation", out=sg[i][:, 0:255], in_=u2[i][:, 0:255],
                         func=AF.Sigmoid, scale=1.5957691216057308)
                    S.op("dve", [f"u{i}", f"sg{i}"], [("GH", kv, g, hc)], "tensor_tensor", out=GH[:, kv * 4 + g * 2 + hc, 0:255],
                         in0=u[i][:, 0:255], in1=sg[i][:, 0:255], op=ALU.mult)
                    n += 1
        n = 0
        for kv in range(2):
            for g in range(2):
                for c in range(2):
                    nn = 128 if c == 0 else 127
                    pb, pk = bank(n % 4)
                    for hc in range(2):
                        S.op("pe", [("GH", kv, g, hc), f"W2_{kv}"], [pk], "matmul", pb[0:nn, 0:64],
                             lhsT=GH[:, kv * 4 + g * 2 + hc, c * 128:c * 128 + nn], rhs=W2[kv][:, hc, :],
                             start=(hc == 0), stop=(hc == 1))
                    if kv == 0:
                        i = n % 2
                        col = g * 2 + c
                        S.op("act", [pk], ["junkb", ("ssk", col)], "activation", out=junkb[0:nn, :], in_=pb[0:nn, 0:64],
                             func=AF.Square, accum_out=ssk[0:nn, col:col + 1])
                        rstd_from_ss(ssk[0:nn, col:col + 1], rsk[0:nn, col:col + 1], 1, [("ssk", col)], [("rsk", col)], 1.0 / 64)
                        S.op("dve", [pk, ("rsk", col), "gk"], [f"kcn2{i}"], "scalar_tensor_tensor", out=kcn2[i][0:nn, 0:64],
                             in0=pb[0:nn, 0:64], scalar=rsk[0:nn, col:col + 1], in1=gk[0:nn, :], op0=ALU.mult, op1=ALU.mult)
                        S.op("dve", [f"kcn2{i}"], [f"kcn2{i}"], "tensor_copy", out=kcn2[i][0:nn, 192:256], in_=kcn2[i][0:nn, 0:64])
                        tb, tk = bank_bf(6 + i)
                        for par in range(2):
                            S.op("pe", [f"kcn2{i}", "ident"], [tk], "transpose", tb[:, par * 128:par * 128 + nn], kcn2[i][0:nn, par * 128:(par + 1) * 128],
                                 ident[0:nn, 0:nn])
                        S.op("act", [tk, "KCT"], ["KCT"], "activation", out=KCT[:, :, g, c * 128:c * 128 + nn],
                             in_=tb[:, 0:256].rearrange("p (a n) -> p a n", a=2)[:, :, 0:nn], func=AF.Copy)
                    else:
                        S.op("act", [pk, "VCO"], ["VCO"], "activation", out=VCO[0:nn, g, c, 0:64], in_=pb[0:nn, 0:64], func=AF.Copy)
                    n += 1
        if dbg:
            DKC = nc.dram_tensor("DKC", [128, 2, 2, 256], BF16, kind="ExternalOutput").ap()
            DVC = nc.dram_tensor("DVC", [128, 2, 2, 129], BF16, kind="ExternalOutput").ap()
            S.dma("sp", ["KCT"], ["DKC"], out=DKC[:, :, :, :], in_=KCT[:])
            S.dma("sp", ["VCO"], ["DVC"], out=DVC[:, :, :, :], in_=VCO[:])
        S.barrier()
    if stop_after == "PB":
        return finish()

    QDT, KDT, VD, YAT = g_("QDT"), g_("KDT"), g_("VD"), g_("YAT")
    with contextlib.ExitStack() as pc:
      if not PD_DEBUG["skip_pc"]:
        dm = sbt(pc, "dm", [128, 8, 512], BF16)
        S.dma("sp", [], ["dm"], out=dm[:], in_=g_("c_dmask")[:, :, :])
        lt = sbt(pc, "lt", [128, 256], F32)
        S.dma("sp", [], ["lt"], out=lt[:], in_=g_("diff_lambda").broadcast_to([128, 256]))
        prod = sbt(pc, "prod", [128, 128], F32)
        sums = sbt(pc, "sums", [128, 2], F32)
        ex = sbt(pc, "ex", [128, 2], F32)
        nlam = sbt(pc, "nlam", [128, 1], F32)
        S.op("dve", ["lt"], ["prod"], "tensor_tensor", out=prod[:].rearrange("p (a d) -> p a d", a=2),
             in0=lt[:].rearrange("p (a b d) -> p a b d", a=2, b=2)[:, :, 0, :],
             in1=lt[:].rearrange("p (a b d) -> p a b d", a=2, b=2)[:, :, 1, :], op=ALU.mult)
        S.op("dve", ["prod"], ["sums"], "tensor_reduce", out=sums[:], in_=prod[:].rearrange("p (a d) -> p a d", a=2),
             axis=AX.X, op=ALU.add)
        S.op("act", ["sums"], ["ex"], "activation", out=ex[:], in_=sums[:], func=AF.Exp)
        S.op("dve", ["ex"], ["nlam"], "tensor_tensor", out=nlam[:], in0=ex[:, 1:2], in1=ex[:, 0:1], op=ALU.subtract)
        S.op("dve", ["nlam"], ["nlam"], "tensor_scalar", out=nlam[:], in0=nlam[:], scalar1=-0.2, scalar2=None, op0=ALU.add)
        gsub = sbt(pc, "gsub", [128, 128], F32)
        S.dma("sp", [], ["gsub"], out=gsub[:], in_=g_("diff_subln").broadcast_to([128, 128]))
        S.op("dve", ["gsub"], ["gsub"], "tensor_scalar", out=gsub[:], in0=gsub[:], scalar1=0.8, scalar2=None, op0=ALU.mult)
        KT = [sbt(pc, f"KT{i}", [128, SEQ], BF16) for i in range(2)]
        VT = [sbt(pc, f"VT{i}", [128, 32, 129], BF16) for i in range(2)]
        QT = [[sbt(pc, f"QT{i}{c}", [128, NOWN], BF16) for c in range(2)] for i in range(2)]
        for i in range(2):
            for c in range(2):
                S.op("dve", [], [("QTz", i, c)], "memset", QT[i][c][(1 - c) * 64:(2 - c) * 64, :], 0.0)
        YAs = [sbt(pc, f"YAs{i}", [128, NOWN], BF16) for i in range(2)]
        E = [sbt(pc, f"E{i}", [128, 512], BF16) for i in range(4)]
        rd = [sbt(pc, f"rd{i}", [128, 4], F32) for i in range(2)]
        t0_ = [sbt(pc, f"t0_{i}", [128, 128], F32) for i in range(2)]
        o_ = [sbt(pc, f"o_{i}", [128, 128], F32) for i in range(2)]
        yb_ = [sbt(pc, f"yb_{i}", [128, 128], BF16) for i in range(2)]
        junkc = sbt(pc, "junkc", [128, 128], BF16)
        for i in range(2):
            S.op("dve", [], [("VT1", i)], "memset", VT[i][:, :, 128:129], 1.0)
        nf = 0
        LA = 3

        def pc_loads(h):
            i = h % 2
            S.dma("sp", [("KDT", (h // 4) * 4, 0), ("KDT", (h // 4) * 4, 1)], [("KT", i)], out=KT[i][:], in_=KDT[h, :, :])
            for q4 in range(8):
                S.dma("sp", [("VD", c0, hp, b) for c0 in (2048, 2560) for hp in range(2) for b in range(16)], [("VT", i, q4)],
                      out=VT[i][:, q4 * 4:(q4 + 1) * 4, 0:128],
                      in_=VD[q4 * 512:(q4 + 1) * 512, h * 128:(h + 1) * 128].rearrange("(kb p) d -> p kb d", p=128))
            for c in range(2):
                S.dma("sp", [("QN", 0, 0), ("QN", 512, 0)], [("QT", i, c)], out=QT[i][c][c * 64:(c + 1) * 64, :], in_=QDT[h, c * 64:(c + 1) * 64, :])

        def pc_front(u, n):
            h, G, kb, c = u
            i = h % 2
            r = kb - 8 * G
            pb, pk = bank(n % 4)
            e = E[n % 4]; ek = f"E{n % 4}"
            S.op("pe", [("KT", i), ("QT", i, c), ("QTz", i, c)], [pk], "matmul", pb[:, :], lhsT=KT[i][:, kb * 128:(kb + 1) * 128],
                 rhs=QT[i][c][:, G * 512:(G + 1) * 512], start=True, stop=True)
            S.op("act", [pk], [ek], "activation", out=e[:], in_=pb[:, :], func=AF.Exp, scale=0.125)
            if r >= 0:
                S.op("dve", [ek, "dm"], [ek], "tensor_tensor", out=e[:], in0=e[:], in1=dm[:, r, :], op=ALU.mult)

        def pc_back(u, n):
            h, G, kb, c = u
            i = h % 2
            r = kb - 8 * G
            e = E[n % 4]; ek = f"E{n % 4}"
            for t in range(4):
                if r > 2 * t + 1:
                    continue
                a = c * 4 + t
                ab, ak = bank(4 + a // 3)
                off = (a % 3) * 129
                S.op("pe", [ek, ("VT", i, kb // 4), ("VT1", i)], [ak], "matmul", ab[:, off:off + 129],
                     lhsT=e[:, t * 128:(t + 1) * 128], rhs=VT[i][:, kb, :], start=(kb == 0 and a % 3 == 0),
                     stop=(kb == 8 * G + 2 * t + 1), skip_group_check=True)

        def pc_final(h, G):
            nonlocal nf
            i = h % 2
            for t in range(4):
                f = nf % 2; nf += 1
                a0, a1 = t, 4 + t
                b0, k0 = bank(4 + a0 // 3); b1, k1 = bank(4 + a1 // 3)
                O0 = b0[:, (a0 % 3) * 129:(a0 % 3) * 129 + 129]
                O1 = b1[:, (a1 % 3) * 129:(a1 % 3) * 129 + 129]
                rk = f"rd{f}"
                S.op("dve", [k0], [(rk, 0)], "reciprocal", out=rd[f][:, 0:1], in_=O0[:, 128:129])
                S.op("dve", [k1], [(rk, 1)], "reciprocal", out=rd[f][:, 1:2], in_=O1[:, 128:129])
                S.op("dve", [(rk, 1), "nlam"], [(rk, 2)], "tensor_tensor", out=rd[f][:, 2:3], in0=rd[f][:, 1:2], in1=nlam[:], op=ALU.mult)
                S.op("dve", [k0, (rk, 0)], [f"t0_{f}"], "tensor_scalar", out=t0_[f][:], in0=O0[:, 0:128], scalar1=rd[f][:, 0:1],
                     scalar2=None, op0=ALU.mult)
                S.op("dve", [k1, (rk, 2), f"t0_{f}"], [f"o_{f}"], "scalar_tensor_tensor", out=o_[f][:], in0=O1[:, 0:128],
                     scalar=rd[f][:, 2:3], in1=t0_[f][:], op0=ALU.mult, op1=ALU.add)
                S.op("act", [f"o_{f}"], ["junkc", (rk, 3)], "activation", out=junkc[:], in_=o_[f][:], func=AF.Square,
                     accum_out=rd[f][:, 3:4])
                rstd_from_ss(rd[f][:, 3:4], rd[f][:, 3:4], 1, [(rk, 3)], [(rk, 3)], 1.0 / 128)
                S.op("dve", [f"o_{f}", (rk, 3), "gsub"], [f"yb_{f}"], "scalar_tensor_tensor", out=yb_[f][:], in0=o_[f][:],
                     scalar=rd[f][:, 3:4], in1=gsub[:], op0=ALU.mult, op1=ALU.mult)
                tb, tk = bank_bf(7)
                S.op("pe", [f"yb_{f}", "ident"], [tk], "transpose", tb[:, 0:128], yb_[f][:], ident[:])
                q0 = (G * 4 + t) * 128
                S.op("act", [tk], [("YAs", i, G * 4 + t)], "activation", out=YAs[i][:, q0:q0 + 128], in_=tb[:, 0:128], func=AF.Copy)
            if G == 3:
                S.dma("pool", [("YAs", i, b) for b in range(16)], [("YAT", h)], out=YAT[h, :, :], in_=YAs[i][:])

        units = [(h, G, kb, c) for h in range(8) for G in range(4) for kb in range(8 * G + 8) for c in range(2)]
        pc_loads(0)
        for n in range(len(units) + LA):
            if n < len(units):
                u = units[n]
                if u[1] == 0 and u[2] == 2 and u[3] == 0 and u[0] + 1 < 8:
                    pc_loads(u[0] + 1)
                pc_front(u, n)
            if n >= LA:
                ub = units[n - LA]
                pc_back(ub, n - LA)
                if ub[2] == 8 * ub[1] + 7 and ub[3] == 1:
                    pc_final(ub[0], ub[1])
        S.barrier()
    if stop_after == "PC":
        return finish()

    NQRT, NQPT, KST, KWT, VS, VW, YBT = g_("NQRT"), g_("NQPT"), g_("KST"), g_("KWT"), g_("VS"), g_("VW"), g_("YBT")
    with contextlib.ExitStack() as pd:
        m128 = sbt(pd, "m128", [128, 2, 8, 128], BF16)
        mc = sbt(pd, "mc", [128, 16, 2, 128], BF16)
        valid = sbt(pd, "valid", [128, 16, 64], F32)
        addc = sbt(pd, "addc", [128, 16, 64], F32)
        eexp = sbt(pd, "eexp", [128, 32, 128], BF16)
        S.op("dve", [], ["eexpz"], "memset", eexp[64:128, :, :], 0.0)
        S.dma("sp", [], ["m128"], out=m128[:], in_=g_("c_m128")[:, :, :, :])
        S.dma("sp", [], ["mc"], out=mc[:], in_=g_("c_mc")[:, :, :, :])
        S.dma("sp", [], ["valid"], out=valid[:], in_=g_("c_valid")[:, :, :])
        S.dma("sp", [], ["addc"], out=addc[:], in_=g_("c_addc")[:, :, :])
        S.dma("sp", [], ["eexp"], out=eexp[0:64, :, :], in_=g_("c_eexp")[:, :, :])
        KS = [sbt(pd, f"KS{par}", [128, SEQ], BF16) for par in range(2)]
        KW = [sbt(pd, f"KW{par}", [128, SEQ], BF16) for par in range(2)]
        for par in range(2):
            S.op("dve", [], [("KSz", par)], "memset", KS[par][(1 - par) * 64:(2 - par) * 64, :], 0.0)
            S.op("pool", [], [("KWz", par)], "memset", KW[par][(1 - par) * 64:(2 - par) * 64, :], 0.0)
        VS1 = sbt(pd, "VS1", [128, 32, 65], BF16)
        VW1 = sbt(pd, "VW1", [128, 32, 65], BF16)
        QR = sbt(pd, "QR", [128, 4, NOWN], BF16)
        QP = sbt(pd, "QP", [128, 4, NOWN], BF16)
        ecmp = sbt(pd, "ecmp", [128, 2, 2, 512], BF16)
        es_ = [sbt(pd, f"es{i}", [128, 512], BF16) for i in range(4)]
        M4 = [sbt(pd, f"M4{i}", [128, 512], BF16) for i in range(2)]
        Y = [sbt(pd, f"Y{i}", [128, 8, 64], F32) for i in range(2)]
        Yb = sbt(pd, "Yb", [128, 512], BF16)
        IMP = sbt(pd, "IMP", [128, 64], F32)
        sc = sbt(pd, "sc", [128, 64], F32)
        sc2 = sbt(pd, "sc2", [128, 64], F32)
        m8a = sbt(pd, "m8a", [128, 8], F32)
        m8b = sbt(pd, "m8b", [128, 8], F32)
        selb = sbt(pd, "selb", [128, 128], BF16)
        selT = [sbt(pd, f"selT{i}", [128, 128], BF16) for i in range(2)]
        S.op("dve", [], ["selbz"], "memset", selb[:, 64:128], 0.0)
        den8 = sbt(pd, "den8", [128, 8], F32)
        rd8 = sbt(pd, "rd8", [128, 8], F32)
        coef = sbt(pd, "coef", [128, 8], F32)
        ystg = sbt(pd, "ystg", [128, 4, NOWN], BF16)
        S.op("dve", [], ["VS1o"], "memset", VS1[:, :, 64:65], 1.0)
        S.op("dve", [], ["VW1o"], "memset", VW1[:, :, 64:65], 1.0)
        nsc = 0
        for g in range(2):
            for par in range(2):
                S.dma("sp", [("KST", 0), ("KST", 1)], [("KS", par)], out=KS[par][par * 64:(par + 1) * 64, :], in_=KST[g, par * 64:(par + 1) * 64, :])
                S.dma("sp", [("KWT", 0), ("KWT", 1)], [("KW", par)], out=KW[par][par * 64:(par + 1) * 64, :], in_=KWT[g, par * 64:(par + 1) * 64, :])
            for q4 in range(4):
                S.dma("sp", [("VS", b) for b in range(32)], [("VS1", q4)], out=VS1[:, q4 * 8:(q4 + 1) * 8, 0:64],
                      in_=VS[q4 * 1024:(q4 + 1) * 1024, g * 64:(g + 1) * 64].rearrange("(kb p) d -> p kb d", p=128))
                S.dma("sp", [("VW", b) for b in range(32)], [("VW1", q4)], out=VW1[:, q4 * 8:(q4 + 1) * 8, 0:64],
                      in_=VW[q4 * 1024:(q4 + 1) * 1024, g * 64:(g + 1) * 64].rearrange("(kb p) d -> p kb d", p=128))
            S.dma("sp", [("QN", 3072, 0), ("QN", 3584, 0)], ["QR"], out=QR[:], in_=NQRT[g * 4:(g + 1) * 4].rearrange("h p t -> p h t"))
            S.dma("sp", [("QN", 3072, 1), ("QN", 3584, 1)], ["QP"], out=QP[:], in_=NQPT[g * 4:(g + 1) * 4].rearrange("h p t -> p h t"))
            def gv_of(j):
                return gates[:, j, g * 24:(g + 1) * 24].rearrange("p (r b) -> p r b", b=3)

            def fin_branch(j, nper, width, branch, first):
                Yj = Y[j % 2]; yk = f"Y{j % 2}"
                nb_ = (8 + nper - 1) // nper
                for b in range(nb_):
                    ab, ak = bank(4 + b)
                    nh = min(nper, 8 - b * nper)
                    S.op("dve", [ak], ["den8"], "tensor_scalar", out=den8[:, b * nper:b * nper + nh].unsqueeze(2),
                         in0=ab[:, 0:nh * width].rearrange("p (r c) -> p r c", c=width)[:, :, 64:65], scalar1=1e-30,
                         scalar2=None, op0=ALU.max)
                S.op("dve", ["den8"], ["rd8"], "reciprocal", out=rd8[:], in_=den8[:])
                S.op("dve", ["rd8", ("gates", j)], ["coef"], "tensor_tensor", out=coef[:], in0=rd8[:], in1=gv_of(j)[:, :, branch], op=ALU.mult)
                for r in range(8):
                    ab, ak = bank(4 + r // nper)
                    off = (r % nper) * width
                    if first:
                        S.op("dve", [ak, "coef"], [(yk, r)], "tensor_scalar", out=Yj[:, r, :], in0=ab[:, off:off + 64],
                             scalar1=coef[:, r:r + 1], scalar2=None, op0=ALU.mult)
                    else:
                        S.op("dve", [ak, "coef", (yk, r)], [(yk, r)], "scalar_tensor_tensor", out=Yj[:, r, :], in0=ab[:, off:off + 64],
                             scalar=coef[:, r:r + 1], in1=Yj[:, r, :], op0=ALU.mult, op1=ALU.add)

            def chain(j):
                nonlocal nsc
                q0 = j * 128
                for c in range(2):
                    nn = 128 if c == 0 else 127
                    for par in range(2):
                        pb, pk = bank(nsc % 4); nsc += 1
                        S.op("pe", ["KCT", "QP"], [pk], "matmul", pb[0:nn, :].rearrange("p (a q) -> p a q", a=4),
                             lhsT=KCT[:, par, g, c * 128:c * 128 + nn], rhs=QP[:, :, q0:q0 + 128],
                             start=True, stop=True)
                        S.op("act", [pk], [("ecmp", c, par)], "activation", out=ecmp[0:nn, c, par, :], in_=pb[0:nn, :], func=AF.Exp, scale=0.125)
                        ev = ecmp[0:nn, c, par, :].rearrange("p (a q) -> p a q", a=4)
                        S.op("dve", [("ecmp", c, par), "mc"], [("ecmp", c, par)], "tensor_tensor", out=ev, in0=ev,
                             in1=mc[0:nn, j, c, :].unsqueeze(1).broadcast_to([nn, 4, 128]), op=ALU.mult)
                for r in range(8):
                    ii, par = r // 2, r % 2
                    ab, ak = bank(4 + r // 3)
                    off = (r % 3) * 129
                    for c in range(2):
                        nn = 128 if c == 0 else 127
                        S.op("pe", [("ecmp", c, par), "VCO"], [ak], "matmul", ab[:, off:off + 129],
                             lhsT=ecmp[0:nn, c, par, ii * 128:(ii + 1) * 128], rhs=VCO[0:nn, g, c, :],
                             start=(c == 0 and r % 3 == 0), stop=(c == 1), skip_group_check=True)
                fin_branch(j, 3, 129, 0, True)
                for r in range(8):
                    ab, ak = bank(4 + r // 3)
                    off = (r % 3) * 129
                    if r == 0:
                        S.op("dve", [ak, "rd8"], ["IMP"], "tensor_scalar", out=IMP[:], in0=ab[:, off + 65:off + 129], scalar1=rd8[:, 0:1],
                             scalar2=None, op0=ALU.mult)
                    else:
                        S.op("dve", [ak, "rd8", "IMP"], ["IMP"], "scalar_tensor_tensor", out=IMP[:], in0=ab[:, off + 65:off + 129],
                             scalar=rd8[:, r:r + 1], in1=IMP[:], op0=ALU.mult, op1=ALU.add)
                S.op("dve", ["IMP", "valid"], ["sc"], "tensor_tensor", out=sc[:], in0=IMP[:], in1=valid[:, j, :], op=ALU.mult)
                S.op("dve", ["sc", "addc"], ["sc"], "tensor_tensor", out=sc[:], in0=sc[:], in1=addc[:, j, :], op=ALU.add)
                S.op("dve", ["sc"], ["m8a"], "max", out=m8a[:], in_=sc[:])
                S.op("dve", ["sc", "m8a"], ["sc2"], "match_replace", out=sc2[:], in_to_replace=m8a[:], in_values=sc[:], imm_value=-3.0)
                S.op("dve", ["sc2"], ["m8b"], "max", out=m8b[:], in_=sc2[:])
                S.op("dve", ["sc", "m8b"], ["selb"], "tensor_scalar", out=selb[:, 0:64], in0=sc[:], scalar1=m8b[:, 7:8], scalar2=None, op0=ALU.is_ge)
                tb, tk = bank_bf(7)
                S.op("pe", ["selb", "selbz", "ident"], [tk], "transpose", tb[:, 0:128], selb[:], ident[:])
                S.op("act", [tk], [f"selT{j % 2}"], "activation", out=selT[j % 2][:], in_=tb[:, 0:128], func=AF.Copy)

            def maskgen(j, kb4):
                jp = j & 1
                nkb = 2 * j + 2
                nb = min(4, nkb - kb4)
                mi = (kb4 // 4) % 2
                pm, pmk = bank(6)
                for q_ in range(nb):
                    S.op("pe", ["eexp", "eexpz", f"selT{j % 2}"], [pmk], "matmul", pm[:, q_ * 128:(q_ + 1) * 128], lhsT=eexp[:, kb4 + q_, :],
                         rhs=selT[j % 2][:, :], start=True, stop=True, skip_group_check=True)
                ncaus = sum(1 for q_ in range(nb) if kb4 + q_ >= 2 * j)
                nplain = nb - ncaus
                if nplain > 0:
                    S.op("act", [pmk], [(f"M4{mi}", q2) for q2 in range(nplain)], "activation", out=M4[mi][:, 0:nplain * 128],
                         in_=pm[:, 0:nplain * 128], func=AF.Copy)
                for q_ in range(nplain, nb):
                    kb = kb4 + q_
                    S.op("dve", [pmk, "m128"], [(f"M4{mi}", q_)], "tensor_tensor", out=M4[mi][:, q_ * 128:(q_ + 1) * 128],
                         in0=pm[:, q_ * 128:(q_ + 1) * 128], in1=m128[:, jp, 6 + (kb - 2 * j), :], op=ALU.mult)

            def attn_units(j, kbs, Kt, Kk, V1, Vk, maskfn, pre=None):
                nonlocal nsc
                q0 = j * 128
                units = [(kb, par) for kb in kbs for par in range(2)]
                base = nsc
                nsc += len(units)
                LA_ = 3
                for n in range(len(units) + LA_):
                    if n < len(units):
                        kb, par = units[n]
                        if pre is not None and par == 0:
                            pre(kb)
                        mk_ = maskfn(kb)
                        pb, pk = bank((base + n) % 4)
                        e = es_[(base + n) % 4]; ek = f"es{(base + n) % 4}"
                        S.op("pe", [(Kk, par), (Kk + "z", par), "QR"], [pk], "matmul", pb[:, :].rearrange("p (a q) -> p a q", a=4),
                             lhsT=Kt[par][:, kb * 128:(kb + 1) * 128], rhs=QR[:, :, q0:q0 + 128], start=True, stop=True)
                        S.op("act", [pk], [ek], "activation", out=e[:], in_=pb[:, :], func=AF.Exp, scale=0.125)
                        if mk_ is not None:
                            map_, mkey = mk_
                            ev = e[:].rearrange("p (a q) -> p a q", a=4)
                            S.op("dve", [ek, mkey], [ek], "tensor_tensor", out=ev, in0=ev,
                                 in1=map_.unsqueeze(1).broadcast_to([128, 4, 128]), op=ALU.mult)
                    if n >= LA_:
                        kb, par = units[n - LA_]
                        e = es_[(base + n - LA_) % 4]; ek = f"es{(base + n - LA_) % 4}"
                        for ii in range(4):
                            r = 2 * ii + par
                            ab, ak = bank(4 + r // 4)
                            off = (r % 4) * 65
                            S.op("pe", [ek, (Vk, kb // 8), Vk + "o"], [ak], "matmul", ab[:, off:off + 65], lhsT=e[:, ii * 128:(ii + 1) * 128],
                                 rhs=V1[:, kb, :], start=(kb == kbs[0] and r % 4 == 0), stop=(kb == kbs[-1]), skip_group_check=True)

            chain(0)
            for j in range(16):
                jp = j & 1
                q0 = j * 128
                if j + 1 < 16:
                    chain(j + 1)
                nkb = 2 * j + 2
                if PD_DEBUG["sel"]:
                    maskgen(j, 0)

                    def pre(kb, j=j, nkb=nkb):
                        if kb % 4 == 0 and kb + 4 < nkb:
                            maskgen(j, kb + 4)

                    def smask(kb):
                        return (M4[(kb // 4) % 2][:, (kb % 4) * 128:(kb % 4 + 1) * 128], (f"M4{(kb // 4) % 2}", kb % 4))

                    attn_units(j, list(range(nkb)), KS, "KS", VS1, "VS1", smask, pre)
                    fin_branch(j, 4, 65, 1, False)
                if PD_DEBUG["win"]:
                    kbs = [kb for kb in range(2 * j - 4, 2 * j + 2) if kb >= 0]

                    def wmask(kb, j=j, jp=jp):
                        idx = kb - (2 * j - 4)
                        if idx in (2, 3):
                            return None
                        return (m128[:, jp, idx, :], "m128")

                    attn_units(j, kbs, KW, "KW", VW1, "VW1", wmask)
                    fin_branch(j, 4, 65, 2, False)
                yk = f"Y{j % 2}"
                S.op("act", [(yk, r) for r in range(8)], ["Yb"], "activation", out=Yb[:], in_=Y[j % 2][:].rearrange("p r d -> p (r d)"), func=AF.Copy)
                tb, tk = bank_bf(7)
                for ii in range(4):
                    S.op("pe", ["Yb", "ident"], [tk], "transpose", tb[:, ii * 128:(ii + 1) * 128], Yb[:, ii * 128:(ii + 1) * 128], ident[:])
                S.op("act", [tk], [("ystg", j)], "activation", out=ystg[:, :, q0:q0 + 128],
                     in_=tb[:, 0:512].rearrange("p (c t) -> p c t", c=4), func=AF.Copy)
            S.dma("pool", [("ystg", j) for j in range(16)], [("YBT", g)], out=YBT[g * 4:(g + 1) * 4].rearrange("h p t -> p h t"), in_=ystg[:])
        S.barrier()
    if stop_after == "PD":
        return finish()

    build_rowlocal(nc, S, es, L)


def build_rowlocal(nc, S, es, L):
    g_ = lambda n: L[n]
    sbt, bank, bank_bf, ident, rstd_from_ss = g_("sbt"), g_("bank"), g_("bank_bf"), g_("ident"), g_("rstd_from_ss")
    x_own, p_own, out_d = g_("x_own"), g_("p_own"), g_("out_d")
    YAT, YBT, SGT = g_("YAT"), g_("YBT"), g_("SGT")
    WPD, WPN, WOUT, WUP, WDN, WPP, WPG = g_("WPD"), g_("WPN"), g_("WOUT"), g_("WUP"), g_("WDN"), g_("WPP"), g_("WPG")
    with contextlib.ExitStack() as pe_:
        gvec = sbt(pe_, "gvec", [128, D], F32)
        aT = sbt(pe_, "aT", [128, 64, 512], BF16)
        wbig = [sbt(pe_, f"wbig{i}", [128, 16, 512], BF16) for i in range(2)]
        x1 = sbt(pe_, "x1", [128, 4, D], F32)
        hT2 = sbt(pe_, "hT2", [128, 16, 512], BF16)
        wpp = sbt(pe_, "wpp", [128, 2, D], BF16)
        hb2 = sbt(pe_, "hb2", [128, D], BF16)
        sgt = [sbt(pe_, f"sgt{i}", [128, 2, 512], BF16) for i in range(2)]
        t1 = sbt(pe_, "t1", [128, 512], F32)
        t2 = sbt(pe_, "t2", [128, 512], F32)
        xin = [sbt(pe_, f"xin{i}", [128, 512], F32) for i in range(2)]
        rl = [sbt(pe_, f"rl{i}", [128, 512], F32) for i in range(2)]
        gt = [sbt(pe_, f"gt{i}", [128, 512], F32) for i in range(2)]
        ev = [sbt(pe_, f"ev{i}", [128, 512], F32) for i in range(2)]
        ss1 = sbt(pe_, "ss1", [128, 8], F32)
        rs1 = sbt(pe_, "rs1", [128, 8], F32)
        ssE = sbt(pe_, "ssE", [128, 16], F32)
        ssE4 = sbt(pe_, "ssE4", [128, 4], F32)
        rsE = sbt(pe_, "rsE", [128, 4], F32)
        pt = sbt(pe_, "pt", [128, 256], F32)
        ptb = sbt(pe_, "ptb", [128, 256], BF16)
        pT = sbt(pe_, "pT", [128, 2, 512], BF16)
        S.dma("sp", [("WPP", 0)], ["wpp"], out=wpp[:], in_=WPP[:, :].rearrange("(k p) c -> p k c", p=128))
        cn = {"w": 0, "ps": 0, "sg": 0, "x": 0, "r": 0, "g": 0, "e": 0}

        def nextw():
            i = cn["w"] % 2; cn["w"] += 1
            return wbig[i], f"wbig{i}"

        def nextbank():
            b = cn["ps"] % 4; cn["ps"] += 1
            return bank(b)

        def to_hT2(blk):
            for half in range(2):
                pb, pk = bank_bf(6 + half)
                for k in range(8):
                    kk = half * 8 + k
                    S.op("pe", ["hb2", "ident"], [pk], "transpose", pb[:, k * 128:(k + 1) * 128], hb2[:, kk * 128:(kk + 1) * 128], ident[:])
                dst = hT2[:, half * 8:half * 8 + 8, blk * 128:(blk + 1) * 128]
                srcv = pb.rearrange("p (k t) -> p k t", k=8)
                if half == 0:
                    S.op("act", [pk], [("hT2", blk, 0)], "activation", out=dst, in_=srcv, func=AF.Copy)
                else:
                    S.op("dve", [pk], [("hT2", blk, 1)], "tensor_copy", out=dst, in_=srcv)

        hT2keys = [("hT2", b, h) for b in range(4) for h in range(2)]
        for tt in range(4):
            tok0 = tt * 512
            S.dma("sp", [("YAT", h) for h in range(8)], [("aT", k) for k in range(8)], out=aT[:, 0:8, :],
                  in_=YAT[:, :, tok0:tok0 + 512].rearrange("h p t -> p h t"))
            S.dma("sp", [("YBT", 0), ("YBT", 1)], [("aT", k) for k in range(8, 16)], out=aT[:, 8:16, :],
                  in_=YBT[:, :, tok0:tok0 + 512].rearrange("h p t -> p h t"))
            for cc in range(4):
                w, wk = nextw()
                S.dma("sp", [("WPD", r) for r in range(0, 1024, 256)], [(wk, 0)], out=w[:, 0:8, :],
                      in_=WPD[:, cc * 512:(cc + 1) * 512].rearrange("(k p) c -> p k c", p=128))
                S.dma("sp", [("WPN", r) for r in range(0, 1024, 256)], [(wk, 1)], out=w[:, 8:16, :],
                      in_=WPN[:, cc * 512:(cc + 1) * 512].rearrange("(k p) c -> p k c", p=128))
                for f in range(4):
                    fidx = cc * 4 + f
                    si = cn["sg"] % 2; cn["sg"] += 1
                    for gi in range(2):
                        S.dma("sp", [("SGT", gi, fidx * 128, tt)], [(f"sgt{si}", gi)], out=sgt[si][:, gi, :],
                              in_=SGT[gi, fidx * 128:(fidx + 1) * 128, tok0:tok0 + 512])
                    pA, pAk = nextbank()
                    pB, pBk = nextbank()
                    for k in range(8):
                        S.op("pe", [(wk, 0), ("aT", k)], [pAk], "matmul", pA[:, :], lhsT=w[:, k, f * 128:(f + 1) * 128], rhs=aT[:, k, :],
                             start=(k == 0), stop=(k == 7))
                    for k in range(8):
                        S.op("pe", [(wk, 1), ("aT", 8 + k)], [pBk], "matmul", pB[:, :], lhsT=w[:, 8 + k, f * 128:(f + 1) * 128],
                             rhs=aT[:, 8 + k, :], start=(k == 0), stop=(k == 7))
                    S.op("dve", [pAk, (f"sgt{si}", 0)], ["t1"], "tensor_tensor", out=t1[:], in0=pA[:, :], in1=sgt[si][:, 0, :], op=ALU.mult)
                    S.op("dve", [pBk, (f"sgt{si}", 1)], ["t2"], "tensor_tensor", out=t2[:], in0=pB[:, :], in1=sgt[si][:, 1, :], op=ALU.mult)
                    S.op("pool", ["t1", "t2"], [("aT", 16 + fidx)], "tensor_tensor", out=aT[:, 16 + fidx, :], in0=t1[:], in1=t2[:], op=ALU.add)
            for cc in range(4):
                w, wk = nextw()
                S.dma("sp", [("WOUT", r) for r in range(0, D, 256)], [(wk, 0), (wk, 1)], out=w[:],
                      in_=WOUT[:, cc * 512:(cc + 1) * 512].rearrange("(k p) c -> p k c", p=128))
                for blk in range(4):
                    pb, pk = nextbank()
                    for k in range(16):
                        S.op("pe", [(wk, 0), (wk, 1), ("aT", 16 + k)], [pk], "matmul", pb[:, :], lhsT=aT[:, 16 + k, blk * 128:(blk + 1) * 128],
                             rhs=w[:, k, :], start=(k == 0), stop=(k == 15))
                    xi = cn["x"] % 2; cn["x"] += 1
                    S.dma("sp", [], [f"xin{xi}"], out=xin[xi][:], in_=x_own[tok0 + blk * 128:tok0 + (blk + 1) * 128, cc * 512:(cc + 1) * 512])
                    S.op("dve", [pk, f"xin{xi}"], [("x1", blk, cc)], "tensor_tensor", out=x1[:, blk, cc * 512:(cc + 1) * 512], in0=pb[:, :],
                         in1=xin[xi][:], op=ALU.add)
            S.dma("sp", [], ["gvec"], out=gvec[:], in_=g_("norm_mlp").broadcast_to([128, D]))
            for blk in range(4):
                xk = [("x1", blk, c) for c in range(4)]
                S.op("act", xk, ["hb2", ("ss1", blk)], "activation", out=hb2[:], in_=x1[:, blk, :], func=AF.Square, accum_out=ss1[:, blk:blk + 1])
                rstd_from_ss(ss1[:, blk:blk + 1], rs1[:, blk:blk + 1], 1, [("ss1", blk)], [("rs1", blk)], 1.0 / D)
                S.op("dve", xk + [("rs1", blk), "gvec"], ["hb2"], "scalar_tensor_tensor", out=hb2[:], in0=x1[:, blk, :],
                     scalar=rs1[:, blk:blk + 1], in1=gvec[:], op0=ALU.mult, op1=ALU.mult)
                to_hT2(blk)
            for uc in range(16):
                w, wk = nextw()
                S.dma("sp", [("WUP", r) for r in range(0, D, 256)], [(wk, 0), (wk, 1)], out=w[:],
                      in_=WUP[:, uc * 512:(uc + 1) * 512].rearrange("(k p) c -> p k c", p=128))
                for f in range(4):
                    pb, pk = nextbank()
                    for k in range(16):
                        S.op("pe", [(wk, 0), (wk, 1)] + hT2keys, [pk], "matmul", pb[:, :], lhsT=w[:, k, f * 128:(f + 1) * 128], rhs=hT2[:, k, :],
                             start=(k == 0), stop=(k == 15))
                    ri = cn["r"] % 2; cn["r"] += 1
                    S.op("act", [pk], [f"rl{ri}"], "activation", out=rl[ri][:], in_=pb[:, :], func=AF.Relu)
                    S.op("pool", [f"rl{ri}"], [("aT", uc * 4 + f)], "tensor_tensor", out=aT[:, uc * 4 + f, :], in0=rl[ri][:], in1=rl[ri][:], op=ALU.mult)
            for fc in range(4):
                base = 0 if fc % 2 == 0 else 4
                for kg in range(4):
                    w, wk = nextw()
                    S.dma("sp", [("WDN", r) for r in range(0, DFF, 1024)], [(wk, 0), (wk, 1)], out=w[:],
                          in_=WDN[kg * 2048:(kg + 1) * 2048, fc * 512:(fc + 1) * 512].rearrange("(k p) c -> p k c", p=128))
                    for k in range(16):
                        ffc = kg * 16 + k
                        for blk in range(4):
                            pb, pk = bank(base + blk)
                            S.op("pe", [(wk, 0), (wk, 1), ("aT", ffc)], [pk], "matmul", pb[:, :], lhsT=aT[:, ffc, blk * 128:(blk + 1) * 128],
                                 rhs=w[:, k, :], start=(ffc == 0), stop=(ffc == 63))
                for blk in range(4):
                    pb, pk = bank(base + blk)
                    S.op("dve", [pk, ("x1", blk, fc)], [("x1", blk, fc)], "tensor_tensor", out=x1[:, blk, fc * 512:(fc + 1) * 512], in0=pb[:, :],
                         in1=x1[:, blk, fc * 512:(fc + 1) * 512], op=ALU.add)
            S.dma("sp", [], ["gvec"], out=gvec[:], in_=g_("norm_ple").broadcast_to([128, D]))
            for blk in range(4):
                xk = [("x1", blk, c) for c in range(4)]
                S.op("act", xk, ["hb2", ("ss1", 4 + blk)], "activation", out=hb2[:], in_=x1[:, blk, :], func=AF.Square,
                     accum_out=ss1[:, 4 + blk:5 + blk])
                rstd_from_ss(ss1[:, 4 + blk:5 + blk], rs1[:, 4 + blk:5 + blk], 1, [("ss1", 4 + blk)], [("rs1", 4 + blk)], 1.0 / D)
                S.op("dve", xk + [("rs1", 4 + blk)], ["hb2"], "tensor_scalar", out=hb2[:], in0=x1[:, blk, :], scalar1=rs1[:, 4 + blk:5 + blk],
                     scalar2=None, op0=ALU.mult)
                to_hT2(blk)
                S.dma("sp", [], ["pt"], out=pt[:], in_=p_own[tok0 + blk * 128:tok0 + (blk + 1) * 128, :])
                S.op("dve", ["pt"], ["ptb"], "tensor_copy", out=ptb[:], in_=pt[:])
                pb, pk = bank_bf(6)
                for k in range(2):
                    S.op("pe", ["ptb", "ident"], [pk], "transpose", pb[:, k * 128:(k + 1) * 128], ptb[:, k * 128:(k + 1) * 128], ident[:])
                S.op("act", [pk], [("pT", blk)], "activation", out=pT[:, :, blk * 128:(blk + 1) * 128],
                     in_=pb[:, 0:256].rearrange("p (k t) -> p k t", k=2), func=AF.Copy)
            for blk in range(4):
                for cc in range(4):
                    pb, pk = bank(4 + cn["e"] % 2); cn["e"] += 1
                    for k in range(2):
                        S.op("pe", [("pT", blk), "wpp"], [pk], "matmul", pb[:, :], lhsT=pT[:, k, blk * 128:(blk + 1) * 128],
                             rhs=wpp[:, k, cc * 512:(cc + 1) * 512], start=(k == 0), stop=(k == 1))
                    S.op("act", [pk], ["hb2", ("ssE", blk * 4 + cc)], "activation", out=hb2[:, 0:512], in_=pb[:, :], func=AF.Square,
                         accum_out=ssE[:, blk * 4 + cc:blk * 4 + cc + 1])
            S.op("dve", [("ssE", i) for i in range(16)], ["ssE4"], "tensor_reduce", out=ssE4[:], in_=ssE[:].rearrange("p (b c) -> p b c", c=4),
                 axis=AX.X, op=ALU.add)
            rstd_from_ss(ssE4[:], rsE[:], 4, ["ssE4"], ["rsE"], 1.0 / D)
            for cc in range(4):
                w, wk = nextw()
                S.dma("sp", [("WPG", r) for r in range(0, D, 256)], [(wk, 0), (wk, 1)], out=w[:],
                      in_=WPG[:, cc * 512:(cc + 1) * 512].rearrange("(k p) c -> p k c", p=128))
                for blk in range(4):
                    pg, pgk = nextbank()
                    for k in range(16):
                        S.op("pe", [(wk, 0), (wk, 1), ("hT2", blk, 0), ("hT2", blk, 1)], [pgk], "matmul", pg[:, :],
                             lhsT=hT2[:, k, blk * 128:(blk + 1) * 128], rhs=w[:, k, :], start=(k == 0), stop=(k == 15))
                    pe2, pe2k = bank(4 + cn["e"] % 2); cn["e"] += 1
                    for k in range(2):
                        S.op("pe", [("pT", blk), "wpp"], [pe2k], "matmul", pe2[:, :], lhsT=pT[:, k, blk * 128:(blk + 1) * 128],
                             rhs=wpp[:, k, cc * 512:(cc + 1) * 512], start=(k == 0), stop=(k == 1))
                    gi = cn["g"] % 2; cn["g"] += 1
                    S.op("act", [pgk], [f"gt{gi}"], "activation", out=gt[gi][:], in_=pg[:, :], func=AF.Sigmoid)
                    S.op("dve", [pe2k, "rsE", "gvec"], [f"ev{gi}"], "scalar_tensor_tensor", out=ev[gi][:], in0=pe2[:, :], scalar=rsE[:, blk:blk + 1],
                         in1=gvec[:, cc * 512:(cc + 1) * 512], op0=ALU.mult, op1=ALU.mult)
                    S.op("pool", [f"ev{gi}", f"gt{gi}"], [f"ev{gi}"], "tensor_tensor", out=ev[gi][:], in0=ev[gi][:], in1=gt[gi][:], op=ALU.mult)
                    S.op("pool", [f"ev{gi}", ("x1", blk, cc)], [("x1", blk, cc)], "tensor_tensor", out=x1[:, blk, cc * 512:(cc + 1) * 512],
                         in0=x1[:, blk, cc * 512:(cc + 1) * 512], in1=ev[gi][:], op=ALU.add)
            for blk in range(4):
                S.dma("sp", [("x1", blk, c) for c in range(4)], [("out", tt, blk)], out=out_d[tok0 + blk * 128:tok0 + (blk + 1) * 128, :],
                      in_=x1[:, blk, :])
        S.barrier()


def _consts(hf):
    bf = ml_dtypes.bfloat16
    c = {}
    invf = np.power(np.float32(500000.0), -np.arange(0, 16, 2, dtype=np.float32) / np.float32(16)).astype(np.float32)
    c["c_invf"] = np.ascontiguousarray(np.broadcast_to(invf[None, :], (128, 8))).astype(np.float32)
    k = np.arange(128)[:, None, None]
    r = np.arange(8)[None, :, None]
    qq = np.arange(512)[None, None, :]
    t = qq // 128
    qpos = (2 * t + ((t & 1) ^ hf)) * 128 + (qq % 128)
    c["c_dmask"] = ((r * 128 + k) <= qpos).astype(bf)
    m128 = np.zeros((128, 2, 8, 128), np.float32)
    kk = np.arange(128)[:, None]
    mq = np.arange(128)[None, :]
    for jp in range(2):
        p = jp ^ hf
        q = p * 128 + mq
        for idx in range(6):
            key = (idx - 4) * 128 + kk
            dist = q - key
            m128[:, jp, idx, :] = ((dist >= 0) & (dist < 512))
        for idx in range(6, 8):
            key = (idx - 6) * 128 + kk
            m128[:, jp, idx, :] = (key <= q)
    c["c_m128"] = m128.astype(bf)
    mc = np.zeros((128, 16, 2, 128), np.float32)
    valid = np.zeros((128, 16, 64), np.float32)
    addc = np.zeros((128, 16, 64), np.float32)
    sel = np.arange(64)[None, :]
    for j in range(16):
        qp = own_block(j, hf) * 128 + np.arange(128)
        for ch in range(2):
            ng = ch * 128 + np.arange(128)
            mc[:, j, ch, :] = ((ng[:, None] <= 254) & (16 * ng[:, None] + 31 <= qp[None, :]))
        qb = (qp // 64)[:, None]
        v = sel <= qb
        f = (sel == 0) | (sel == qb) | (sel == qb - 1)
        valid[:, j, :] = v
        addc[:, j, :] = np.where(v, 1e4 * f, -1.0)
    c["c_mc"] = mc.astype(bf)
    c["c_valid"] = valid
    c["c_addc"] = addc.astype(np.float32)
    ovl = np.zeros((128, 2, 64), np.float32)
    for ch in range(2):
        ng = ch * 128 + np.arange(128)
        cs = 16 * ng[:, None]
        ssb = 64 * np.arange(64)[None, :]
        ovl[:, ch, :] = ((cs < ssb + 64) & (cs + 32 > ssb) & (ng[:, None] <= 254))
    c["c_ovl"] = ovl.astype(bf)
    jj = np.arange(64)[:, None, None]
    kb = np.arange(32)[None, :, None]
    k2 = np.arange(128)[None, None, :]
    c["c_eexp"] = (jj == 2 * kb + k2 // 64).astype(bf)
    return c


def make_in_maps(inputs):
    f = lambda a: np.ascontiguousarray(np.asarray(a))
    x = f(inputs["x"]); p = f(inputs["p"])[0]; pos = f(inputs["positions"]).astype(np.int32)
    shared = {
        "norm_mix": f(inputs["norm_mix"]).reshape(1, D),
        "w_in": f(inputs["w_in"])[0],
        "diff_q_norm": f(inputs["diff_q_norm"]).reshape(1, 64),
        "diff_k_norm": f(inputs["diff_k_norm"]).reshape(1, 64),
        "diff_lambda": f(inputs["diff_lambda"]).reshape(1, 256),
        "diff_subln": f(inputs["diff_subln"]).reshape(1, 128),
        "nsa_q_norm": f(inputs["nsa_q_norm"]).reshape(1, 64),
        "nsa_k_norm": f(inputs["nsa_k_norm"]).reshape(1, 64),
        "cmp_posT": f(np.transpose(f(inputs["cmp_pos"])[0], (2, 0, 1))),
        "cmp_w1": f(inputs["cmp_w1"])[0].reshape(4096, 256),
        "cmp_w2": f(inputs["cmp_w2"])[0].reshape(512, 64),
        "w_proj_diff": f(inputs["w_proj_diff"])[0],
        "w_proj_nsa": f(inputs["w_proj_nsa"])[0],
        "w_out": f(inputs["w_out"])[0],
        "norm_mlp": f(inputs["norm_mlp"]).reshape(1, D),
        "w_mlp_up": f(inputs["w_mlp_up"])[0],
        "w_mlp_down": f(inputs["w_mlp_down"])[0],
        "w_ple_proj": f(inputs["w_ple_proj"])[0],
        "norm_ple": f(inputs["norm_ple"]).reshape(1, D),
        "w_ple_gate": f(inputs["w_ple_gate"])[0],
    }
    cst = [_consts(0), _consts(1)]
    maps = []
    for c in range(8):
        b, hf = c // 2, c % 2
        blks = [own_block(j, hf) for j in range(16)]
        rows = np.concatenate([np.arange(bk * 128, (bk + 1) * 128) for bk in blks])
        m = dict(shared)
        m.update(cst[hf])
        m["x_all"] = x[b]
        m["x_own"] = f(x[b][rows])
        m["p_own"] = f(p[b][rows])
        m["posT_all"] = f(pos[b].reshape(32, 128).T)
        m["posT_own"] = f(pos[b][rows].reshape(16, 128).T)
        maps.append(m)
    return maps


def assemble(outs):
    res = np.zeros((4, SEQ, D), np.float32)
    for c in range(8):
        b, hf = c // 2, c % 2
        o = np.asarray(outs[c])
        for j in range(16):
            bk = own_block(j, hf)
            res[b, bk * 128:(bk + 1) * 128] = o[j * 128:(j + 1) * 128]
    return res


def kernel(**inputs):
    nc = build_nc()
    maps = make_in_maps(inputs)
    r = run_bass_kernel_spmd(nc, maps, core_ids=list(range(8)))
    return assemble([r.results[c]["out"] for c in range(8)])
```

```python
import contextlib
import math
import numpy as np
import ml_dtypes
import concourse.bass as bass
import concourse.mybir as mybir
from concourse.bass_utils import run_bass_kernel_spmd

F32, BF16, I32 = mybir.dt.float32, mybir.dt.bfloat16, mybir.dt.int32
AF = mybir.ActivationFunctionType
ALU = mybir.AluOpType
AX = mybir.AxisListType

D = 2048
SEQ = 4096
NOWN = 2048
IN_W = 9008
DFF = 8192
EPS = 1e-6
TWO_PI = 2.0 * math.pi


class Sched:
    def __init__(self, nc, es):
        self.nc = nc
        self.eng = {"pe": nc.tensor, "act": nc.scalar, "dve": nc.vector, "pool": nc.gpsimd, "sp": nc.sync}
        self.sem = {e: es.enter_context(nc.semaphore("s_" + e)) for e in ("pe", "act", "dve", "pool")}
        self.cnt = {e: 0 for e in self.sem}
        self.NDS = 12
        self.dsem = {q: [es.enter_context(nc.semaphore(f"d_{q}{i}")) for i in range(self.NDS)] for q in ("sp", "pool", "act")}
        self.dcnt = {q: [0] * self.NDS for q in self.dsem}
        self.dnext = {q: 0 for q in self.dsem}
        self.waited = {e: {} for e in self.eng}
        self.res = {}
        self.semname = {}
        self.n_wait = 0
        self.n_inst = 0

    def _wait(self, e, tok):
        if tok is None:
            return
        sem, val, owner = tok
        if e == "pe" and owner == "pe":
            return
        w = self.waited[e]
        if w.get(id(sem), 0) >= val:
            return
        self.eng[e].wait_ge(sem, val)
        self.n_wait += 1
        w[id(sem)] = val

    def _deps(self, e, reads, writes):
        for k in reads:
            r = self.res.get(k)
            if r:
                self._wait(e, r[0])
        for k in writes:
            r = self.res.get(k)
            if r:
                self._wait(e, r[0])
                for t in r[1].values():
                    self._wait(e, t)

    def _commit(self, tok, reads, writes):
        for k in reads:
            r = self.res.setdefault(k, [None, {}])
            r[1][id(tok[0])] = tok
        for k in writes:
            self.res[k] = [tok, {}]

    def op(self, e, reads, writes, meth, *args, **kw):
        ps_r = [k for k in reads if isinstance(k, tuple) and k[0] == "ps"]
        if ps_r:
            reads = [k for k in reads if k not in ps_r]
            writes = list(writes) + ps_r
        self._deps(e, reads, writes)
        self.cnt[e] += 1
        tok = (self.sem[e], self.cnt[e], e)
        getattr(self.eng[e], meth)(*args, **kw).then_inc(self.sem[e], 1)
        self.n_inst += 1
        self._commit(tok, reads, writes)
        return tok

    def dma(self, q, reads, writes, out, in_, **kw):
        i = self.dnext[q]
        self.dnext[q] = (i + 1) % self.NDS
        sem = self.dsem[q][i]
        if self.dcnt[q][i] > 0:
            self._wait(q, (sem, 16 * self.dcnt[q][i], "dma"))
        self._deps(q, reads, writes)
        self.dcnt[q][i] += 1
        tok = (sem, 16 * self.dcnt[q][i], "dma")
        self.eng[q].dma_start(out=out, in_=in_, **kw).then_inc(sem, 16)
        self.n_inst += 1
        self._commit(tok, reads, writes)
        return tok

    def barrier(self):
        toks = [(self.sem[e], self.cnt[e], "bar") for e in self.sem if self.cnt[e] > 0]
        for q in self.dsem:
            for i in range(self.NDS):
                if self.dcnt[q][i] > 0:
                    toks.append((self.dsem[q][i], 16 * self.dcnt[q][i], "dma"))
        for e in self.eng:
            for t in toks:
                self._wait(e, t)
        self.res = {}


PD_DEBUG = {"topk": True, "sel": True, "win": True, "skip_pc": False}


def own_block(j, hf):
    return 2 * j + ((j & 1) ^ hf)


def build_nc(dbg=False, stop_after=None):
    nc = bass.Bass("TRN2", target_bir_lowering=False)

    def din(name, shape, dt=F32):
        return nc.dram_tensor(name, list(shape), dt, kind="ExternalInput").ap()

    def dscr(name, shape, dt=BF16):
        if dbg:
            return nc.dram_tensor(name, list(shape), dt, kind="ExternalOutput").ap()
        return nc.dram_tensor(name, list(shape), dt).ap()

    x_all = din("x_all", [SEQ, D])
    x_own = din("x_own", [NOWN, D])
    p_own = din("p_own", [NOWN, 256])
    posT_all = din("posT_all", [128, 32], I32)
    posT_own = din("posT_own", [128, 16], I32)
    norm_mix = din("norm_mix", [1, D])
    w_in = din("w_in", [D, IN_W])
    diff_q_norm = din("diff_q_norm", [1, 64])
    diff_k_norm = din("diff_k_norm", [1, 64])
    diff_lambda = din("diff_lambda", [1, 256])
    diff_subln = din("diff_subln", [1, 128])
    nsa_q_norm = din("nsa_q_norm", [1, 64])
    nsa_k_norm = din("nsa_k_norm", [1, 64])
    cmp_posT = din("cmp_posT", [64, 2, 32])
    cmp_w1 = din("cmp_w1", [2 * 2048, 256])
    cmp_w2 = din("cmp_w2", [2 * 256, 64])
    w_proj_diff = din("w_proj_diff", [1024, D])
    w_proj_nsa = din("w_proj_nsa", [1024, D])
    w_out = din("w_out", [D, D])
    norm_mlp = din("norm_mlp", [1, D])
    w_mlp_up = din("w_mlp_up", [D, DFF])
    w_mlp_down = din("w_mlp_down", [DFF, D])
    w_ple_proj = din("w_ple_proj", [256, D])
    norm_ple = din("norm_ple", [1, D])
    w_ple_gate = din("w_ple_gate", [D, D])
    c_invf = din("c_invf", [128, 8])
    c_dmask = din("c_dmask", [128, 8, 512], BF16)
    c_m128 = din("c_m128", [128, 2, 8, 128], BF16)
    c_mc = din("c_mc", [128, 16, 2, 128], BF16)
    c_valid = din("c_valid", [128, 16, 64])
    c_addc = din("c_addc", [128, 16, 64])
    c_ovl = din("c_ovl", [128, 2, 64], BF16)
    c_eexp = din("c_eexp", [64, 32, 128], BF16)

    out_d = nc.dram_tensor("out", [NOWN, D], F32, kind="ExternalOutput").ap()

    W_IN = dscr("W_IN", [D, IN_W]) if not dbg else nc.dram_tensor("W_IN", [D, IN_W], BF16).ap()
    mk = lambda n, s: nc.dram_tensor(n, list(s), BF16).ap()
    CW1 = mk("CW1", [2 * 2048, 256]); CW2 = mk("CW2", [2 * 256, 64])
    WPD = mk("WPD", [1024, D]); WPN = mk("WPN", [1024, D]); WOUT = mk("WOUT", [D, D])
    WUP = mk("WUP", [D, DFF]); WDN = mk("WDN", [DFF, D]); WPP = mk("WPP", [256, D]); WPG = mk("WPG", [D, D])
    QDT = dscr("QDT", [8, 128, NOWN])
    KDT = dscr("KDT", [8, 128, SEQ])
    VD = dscr("VD", [SEQ, 1024])
    NQRT = dscr("NQRT", [8, 128, NOWN])
    NQPT = dscr("NQPT", [8, 128, NOWN])
    KCRT = dscr("KCRT", [128, SEQ]); VCRT = dscr("VCRT", [128, SEQ])
    KST = dscr("KST", [2, 128, SEQ]); KWT = dscr("KWT", [2, 128, SEQ])
    VS = dscr("VS", [SEQ, 128]); VW = dscr("VW", [SEQ, 128])
    SGT = dscr("SGT", [2, D, NOWN])
    YAT = dscr("YAT", [8, 128, NOWN])
    YBT = dscr("YBT", [8, 128, NOWN])

    es = contextlib.ExitStack()
    with es:
        S = Sched(nc, es)

        def sbt(stack, name, shape, dt):
            return stack.enter_context(nc.sbuf_tensor(name, list(shape), dt))

        PS = [es.enter_context(nc.psum_tensor(f"ps{i}", [128, 1024], F32)) for i in range(4)]

        def bank(i):
            return PS[i // 2][:, (i % 2) * 512:(i % 2) * 512 + 512], ("ps", i)

        def bank_bf(i):
            return PS[i // 2][:, (i % 2) * 512:(i % 2) * 512 + 512].bitcast(BF16), ("ps", i)

        ident = sbt(es, "ident", [128, 128], BF16)
        idf = sbt(es, "idf", [128, 128], F32)
        mhalf = sbt(es, "mhalf", [128, 64], F32)
        gates = sbt(es, "gates", [128, 16, 48], F32)
        CSA = sbt(es, "CSA", [128, 32, 16], F32)
        CSO = sbt(es, "CSO", [128, 16, 16], F32)
        S.op("pool", [], ["idf"], "iota", idf[:], pattern=[[1, 128]], base=0, channel_multiplier=-1,
             allow_small_or_imprecise_dtypes=True)
        S.op("dve", ["idf"], ["ident"], "tensor_single_scalar", out=ident[:], in_=idf[:], scalar=0.0, op=ALU.is_equal)
        S.op("pool", [], ["mhalf"], "memset", mhalf[:], -0.5)

        def rstd_from_ss(ss_ap, out_ap, n, keys_r, keys_w, inv_n):
            S.op("pool", keys_r, keys_w, "tensor_scalar", out=out_ap, in0=ss_ap, scalar1=inv_n, scalar2=EPS,
                 op0=ALU.mult, op1=ALU.add)
            P = out_ap.shape[0]
            S.op("pool", keys_w + ["mhalf"], keys_w, "tensor_tensor", out=out_ap, in0=out_ap, in1=mhalf[0:P, 0:n], op=ALU.pow)

        def conv(dst, src, R, key, step=256):
            for r0 in range(0, R, step):
                r1 = min(R, r0 + step)
                S.dma("pool", [], [(key, r0)], out=dst[r0:r1, :], in_=src[r0:r1, :], max_dma_last_dim=8192)

        for c0, n in ((0, 512), (512, 512), (3072, 512), (3584, 512), (4864, 48)) + tuple((4912 + i * 512, 512) for i in range(8)) + \
                ((1024, 512), (1536, 512), (2048, 512), (2560, 512), (4096, 512), (4608, 256)):
            S.dma("pool", [], [("W_IN", c0)], out=W_IN[:, c0:c0 + n], in_=w_in[:, c0:c0 + n], max_dma_last_dim=8192)
        pending_conv = []

        def conv_later(dst, src, R, key, step=256):
            for r0 in range(0, R, step):
                r1 = min(R, r0 + step)
                pending_conv.append((dst, src, r0, r1, key))

        def conv_pop(n=1):
            for _ in range(n):
                if pending_conv:
                    dst, src, r0, r1, key = pending_conv.pop(0)
                    S.dma("pool", [], [(key, r0)], out=dst[r0:r1, :], in_=src[r0:r1, :], max_dma_last_dim=8192)

        conv_later(CW1, cmp_w1, 4096, "CW1", 1024); conv_later(CW2, cmp_w2, 512, "CW2", 512)
        conv_later(WPD, w_proj_diff, 1024, "WPD"); conv_later(WPN, w_proj_nsa, 1024, "WPN"); conv_later(WOUT, w_out, D, "WOUT")
        conv_later(WUP, w_mlp_up, D, "WUP"); conv_later(WDN, w_mlp_down, DFF, "WDN", 1024)
        conv_later(WPP, w_ple_proj, 256, "WPP"); conv_later(WPG, w_ple_gate, D, "WPG")

        with contextlib.ExitStack() as ps_:
            invf = sbt(ps_, "invf", [128, 8], F32)
            S.dma("sp", [], ["invf"], out=invf[:], in_=c_invf[:, :])
            for nm, src, nb, CS in (("a", posT_all, 32, CSA), ("o", posT_own, 16, CSO)):
                pi = sbt(ps_, "pi" + nm, [128, nb], I32)
                pf = sbt(ps_, "pf" + nm, [128, nb], F32)
                ang = sbt(ps_, "ang" + nm, [128, nb, 8], F32)
                kf = sbt(ps_, "kf" + nm, [128, nb, 8], F32)
                ki = sbt(ps_, "ki" + nm, [128, nb, 8], I32)
                r0 = sbt(ps_, "r0" + nm, [128, nb, 8], F32)
                r1 = sbt(ps_, "r1" + nm, [128, nb, 8], F32)
                mm = sbt(ps_, "mm" + nm, [128, nb, 8], F32)
                S.dma("sp", [], ["pi" + nm], out=pi[:], in_=src[:, :])
                S.op("dve", ["pi" + nm], ["pf" + nm], "tensor_copy", out=pf[:], in_=pi[:])
                S.op("dve", ["pf" + nm, "invf"], ["ang" + nm], "tensor_tensor", out=ang[:],
                     in0=pf[:].unsqueeze(2).broadcast_to([128, nb, 8]),
                     in1=invf[:].unsqueeze(1).broadcast_to([128, nb, 8]), op=ALU.mult)
                S.op("dve", ["ang" + nm], ["kf" + nm], "tensor_scalar", out=kf[:], in0=ang[:], scalar1=1.0 / TWO_PI,
                     scalar2=None, op0=ALU.mult)
                S.op("dve", ["kf" + nm], ["ki" + nm], "tensor_copy", out=ki[:], in_=kf[:])
                S.op("dve", ["ki" + nm], ["kf" + nm], "tensor_copy", out=kf[:], in_=ki[:])
                S.op("dve", ["kf" + nm, "ang" + nm], ["r0" + nm], "scalar_tensor_tensor", out=r0[:], in0=kf[:],
                     scalar=-TWO_PI, in1=ang[:], op0=ALU.mult, op1=ALU.add)
                for which, shift in ((1, 0.0), (0, math.pi / 2)):
                    S.op("dve", ["r0" + nm], ["r1" + nm], "tensor_scalar", out=r1[:], in0=r0[:], scalar1=shift,
                         scalar2=None, op0=ALU.add)
                    for thr, op_, add in ((math.pi, ALU.is_gt, -TWO_PI), (-math.pi, ALU.is_lt, TWO_PI)):
                        S.op("dve", ["r1" + nm], ["mm" + nm], "tensor_scalar", out=mm[:], in0=r1[:], scalar1=thr,
                             scalar2=add, op0=op_, op1=ALU.mult)
                        S.op("dve", ["r1" + nm, "mm" + nm], ["r1" + nm], "tensor_tensor", out=r1[:], in0=r1[:],
                             in1=mm[:], op=ALU.add)
                    S.op("dve", ["r1" + nm], ["r1" + nm], "tensor_scalar", out=r1[:], in0=r1[:], scalar1=math.pi,
                         scalar2=-math.pi, op0=ALU.min, op1=ALU.max)
                    S.op("act", ["r1" + nm], ["CS" + nm], "activation", out=CS[:, :, which * 8:which * 8 + 8],
                         in_=r1[:], func=AF.Sin)
            S.barrier()

        with contextlib.ExitStack() as pa:
            gmix = sbt(pa, "gmix", [128, D], F32)
            S.dma("sp", [], ["gmix"], out=gmix[:], in_=norm_mix.broadcast_to([128, D]))
            gq = {}
            for nm, src in (("dq", diff_q_norm), ("dk", diff_k_norm), ("nq", nsa_q_norm), ("nk", nsa_k_norm)):
                gq[nm] = sbt(pa, "g_" + nm, [128, 64], F32)
                S.dma("sp", [], ["g_" + nm], out=gq[nm][:], in_=src.broadcast_to([128, 64]))
            xt = [sbt(pa, f"xt{i}", [128, D], F32) for i in range(2)]
            hb = [sbt(pa, f"hb{i}", [128, D], BF16) for i in range(2)]
            hT = sbt(pa, "hT", [128, 16, 2048], BF16)
            wc = [sbt(pa, f"wc{i}", [128, 16, 512], BF16) for i in range(2)]
            ss = sbt(pa, "ss", [128, 16], F32)
            rs = sbt(pa, "rs", [128, 16], F32)
            NZ = 3
            sqt = [sbt(pa, f"sqt{i}", [128, 512], BF16) for i in range(NZ)]
            ssh = [sbt(pa, f"ssh{i}", [128, 8], F32) for i in range(NZ)]
            rsh = [sbt(pa, f"rsh{i}", [128, 8], F32) for i in range(NZ)]
            zn = [sbt(pa, f"zn{i}", [128, 512], F32) for i in range(NZ)]
            zr = [sbt(pa, f"zr{i}", [128, 512], BF16) for i in range(NZ)]
            zp = [sbt(pa, f"zp{i}", [128, 512], BF16) for i in range(NZ)]
            rt = [sbt(pa, f"rt{i}", [128, 4, 8, 8], F32) for i in range(NZ)]
            rot = [sbt(pa, f"rot{i}", [128, 8, 16], F32) for i in range(NZ)]
            stg = [sbt(pa, f"stg{i}", [128, 4, 2048], BF16) for i in range(2)]
            stgH = sbt(pa, "stgH", [128, 2, 2048], BF16)
            NV = 6
            vst = [sbt(pa, f"vst{i}", [128, 512], BF16) for i in range(NV)]
            dupb = [sbt(pa, f"dupb{i}", [128, 256], BF16) for i in range(NV)]
            dfr = []
            LAB = 2

            def defer(fn):
                dfr.append(fn)

            def run_deferred(keep):
                while len(dfr) > keep:
                    dfr.pop(0)()
            cnt = {"w": 0, "z": 0, "v": 0, "ps": 0, "tp": 0, "d": 0}

            def build_hT(xsrc, blk0, ssname):
                for i in range(16):
                    xb = xt[i % 2]; xk = f"xt{i % 2}"
                    S.dma("sp", [], [xk], out=xb[:], in_=xsrc[(blk0 + i) * 128:(blk0 + i + 1) * 128, :])
                    hk = f"hb{i % 2}"
                    S.op("act", [xk], [hk, ("ss", i)], "activation", out=hb[i % 2][:], in_=xb[:], func=AF.Square,
                         accum_out=ss[:, i:i + 1])
                    rstd_from_ss(ss[:, i:i + 1], rs[:, i:i + 1], 1, [("ss", i)], [("rs", i)], 1.0 / D)
                    S.op("dve", [xk, ("rs", i), "gmix"], [hk], "scalar_tensor_tensor", out=hb[i % 2][:], in0=xb[:],
                         scalar=rs[:, i:i + 1], in1=gmix[:], op0=ALU.mult, op1=ALU.mult)
                    for half in range(2):
                        bi = 6 + half
                        pb, pk = bank_bf(bi)
                        for k in range(8):
                            kk = half * 8 + k
                            S.op("pe", [hk, "ident"], [pk], "transpose", pb[:, k * 128:(k + 1) * 128],
                                 hb[i % 2][:, kk * 128:(kk + 1) * 128], ident[:])
                        dst = hT[:, half * 8:half * 8 + 8, i * 128:(i + 1) * 128]
                        srcv = pb.rearrange("p (k t) -> p k t", k=8)
                        if half == 0:
                            S.op("act", [pk], [("hT", i, 0)], "activation", out=dst, in_=srcv, func=AF.Copy)
                        else:
                            S.op("dve", [pk], [("hT", i, 1)], "tensor_copy", out=dst, in_=srcv)

            def load_w(c0, n):
                i = cnt["w"] % 2; cnt["w"] += 1
                S.dma("sp", [("W_IN", c0)], [f"wc{i}"], out=wc[i][:, :, 0:n],
                      in_=W_IN[:, c0:c0 + n].rearrange("(k p) c -> p k c", p=128))
                return wc[i], f"wc{i}"

            def mm_tm(w, wk, blk, n, coff=0):
                bi = cnt["ps"] % 4; cnt["ps"] += 1
                pb, pk = bank(bi)
                if cnt.get("conv") and cnt["conv"] <= 5:
                    conv_pop(1); cnt["conv"] += 1
                for k in range(16):
                    S.op("pe", [wk, ("hT", blk, 0), ("hT", blk, 1)], [pk], "matmul", pb[:, 0:n], lhsT=hT[:, k, blk * 128:(blk + 1) * 128],
                         rhs=w[:, k, coff:coff + n], start=(k == 0), stop=(k == 15))
                return pb, pk

            def normrope(pb, pk, c0, nh, gname, cs, csk, blk, want_plain=False):
                i = cnt["z"] % NZ; cnt["z"] += 1
                n = nh * 64
                pv = pb[:, c0:c0 + n]
                pv3 = pv.rearrange("p (h d) -> p h d", d=64)
                S.op("act", [pk], [f"sqt{i}"], "activation", out=sqt[i][:, 0:n], in_=pv, func=AF.Square)
                S.op("dve", [f"sqt{i}"], [f"ssh{i}"], "tensor_reduce", out=ssh[i][:, 0:nh],
                     in_=sqt[i][:, 0:n].rearrange("p (h d) -> p h d", d=64), axis=AX.X, op=ALU.add)
                rstd_from_ss(ssh[i][:, 0:nh], rsh[i][:, 0:nh], nh, [f"ssh{i}"], [f"rsh{i}"], 1.0 / 64)
                z3 = zn[i][:, 0:n].rearrange("p (h d) -> p h d", d=64)
                S.op("dve", [pk, "g_" + gname], [f"zn{i}"], "tensor_tensor", out=z3, in0=pv3,
                     in1=gq[gname][:].unsqueeze(1).broadcast_to([128, nh, 64]), op=ALU.mult)
                x1 = z3[:, :, 0:8]; x2 = z3[:, :, 8:16]
                cc = cs[:, blk, 0:8].unsqueeze(1).broadcast_to([128, nh, 8])
                sn = cs[:, blk, 8:16].unsqueeze(1).broadcast_to([128, nh, 8])
                r = rt[i]; rk = f"rt{i}"
                ro = rot[i][:, 0:nh, :]
                S.op("pool", [f"zn{i}", csk], [(rk, 0)], "tensor_tensor", out=r[:, 0, 0:nh, :], in0=x1, in1=cc, op=ALU.mult)
                S.op("pool", [f"zn{i}", csk], [(rk, 1)], "tensor_tensor", out=r[:, 1, 0:nh, :], in0=x2, in1=sn, op=ALU.mult)
                S.op("pool", [f"zn{i}", csk], [(rk, 2)], "tensor_tensor", out=r[:, 2, 0:nh, :], in0=x2, in1=cc, op=ALU.mult)
                S.op("pool", [f"zn{i}", csk], [(rk, 3)], "tensor_tensor", out=r[:, 3, 0:nh, :], in0=x1, in1=sn, op=ALU.mult)
                S.op("pool", [(rk, 0), (rk, 1)], [f"rot{i}"], "tensor_tensor", out=ro[:, :, 0:8],
                     in0=r[:, 0, 0:nh, :], in1=r[:, 1, 0:nh, :], op=ALU.subtract)
                S.op("pool", [(rk, 2), (rk, 3), f"rot{i}"], [f"rot{i}"], "tensor_tensor", out=ro[:, :, 8:16],
                     in0=r[:, 2, 0:nh, :], in1=r[:, 3, 0:nh, :], op=ALU.add)
                rb = rsh[i][:, 0:nh].unsqueeze(2)
                zr3 = zr[i][:, 0:n].rearrange("p (h d) -> p h d", d=64)
                S.op("dve", [f"zn{i}", f"rsh{i}"], [f"zr{i}"], "tensor_tensor", out=zr3[:, :, 16:64], in0=z3[:, :, 16:64],
                     in1=rb.broadcast_to([128, nh, 48]), op=ALU.mult)
                S.op("dve", [f"rot{i}", f"rsh{i}", f"zr{i}"], [f"zr{i}"], "tensor_tensor", out=zr3[:, :, 0:16], in0=ro,
                     in1=rb.broadcast_to([128, nh, 16]), op=ALU.mult)
                if want_plain:
                    zp3 = zp[i][:, 0:n].rearrange("p (h d) -> p h d", d=64)
                    S.op("dve", [f"zn{i}", f"rsh{i}"], [f"zp{i}"], "tensor_tensor", out=zp3, in0=z3,
                         in1=rb.broadcast_to([128, nh, 64]), op=ALU.mult)
                return i

            def transp_to(src_ap, src_key, ncol128, dst_ap, dst_key):
                bi = 4 + cnt["tp"] % 2; cnt["tp"] += 1
                pb, pk = bank_bf(bi)
                for c in range(ncol128):
                    S.op("pe", [src_key, "ident"], [pk], "transpose", pb[:, c * 128:(c + 1) * 128],
                         src_ap[:, c * 128:(c + 1) * 128], ident[:])
                S.op("act", [pk], [dst_key], "activation", out=dst_ap,
                     in_=pb[:, 0:ncol128 * 128].rearrange("p (c t) -> p c t", c=ncol128), func=AF.Copy)

            sidx = {"i": 0}

            def new_stage():
                i = sidx["i"] % 2; sidx["i"] += 1
                return stg[i], f"stg{i}"

            build_hT(x_own, 0, "o")
            for (c0, gname, dstR, dstP) in ((0, "dq", QDT, None), (512, "dq", QDT, None),
                                            (3072, "nq", NQRT, NQPT), (3584, "nq", NQRT, NQPT)):
                w, wk = load_w(c0, 512)
                sR, sRk = new_stage()
                if dstP is not None:
                    sP, sPk = new_stage()
                for blk in range(16):
                    pb, pk = mm_tm(w, wk, blk, 512)
                    i = normrope(pb, pk, 0, 8, gname, CSO, "CSo", blk, want_plain=dstP is not None)
                    defer(lambda i=i, blk=blk, sR=sR, sRk=sRk: transp_to(zr[i], f"zr{i}", 4, sR[:, :, blk * 128:(blk + 1) * 128], (sRk, blk)))
                    if dstP is not None:
                        defer(lambda i=i, blk=blk, sP=sP, sPk=sPk: transp_to(zp[i], f"zp{i}", 4, sP[:, :, blk * 128:(blk + 1) * 128], (sPk, blk)))
                    run_deferred(LAB * (2 if dstP is not None else 1))
                run_deferred(0)
                h0 = (c0 % 1024) // 128
                S.dma("pool", [(sRk, b) for b in range(16)], [("QN", c0, 0)], out=dstR[h0:h0 + 4].rearrange("h p t -> p h t"),
                      in_=sR[:])
                if dstP is not None:
                    S.dma("pool", [(sPk, b) for b in range(16)], [("QN", c0, 1)],
                          out=dstP[h0:h0 + 4].rearrange("h p t -> p h t"), in_=sP[:])
            w, wk = load_w(4864, 48)
            for blk in range(16):
                pb, pk = mm_tm(w, wk, blk, 48)
                S.op("act", [pk], [("gates", blk)], "activation", out=gates[:, blk, :], in_=pb[:, 0:48], func=AF.Sigmoid)
            for gi in range(2):
                for cc_ in range(4):
                    c0 = 4912 + gi * 2048 + cc_ * 512
                    w, wk = load_w(c0, 512)
                    for f in range(4):
                        for tg in range(4):
                            bi = cnt["ps"] % 4; cnt["ps"] += 1
                            pb, pk = bank(bi)
                            for k in range(16):
                                S.op("pe", [wk] + [("hT", tg * 4 + b, hh) for b in range(4) for hh in range(2)], [pk], "matmul", pb[:, :],
                                     lhsT=w[:, k, f * 128:(f + 1) * 128], rhs=hT[:, k, tg * 512:(tg + 1) * 512],
                                     start=(k == 0), stop=(k == 15))
                            vi = cnt["v"] % NV; cnt["v"] += 1
                            S.op("act", [pk], [f"vst{vi}"], "activation", out=vst[vi][:], in_=pb[:, :], func=AF.Sigmoid)
                            fr = cc_ * 512 + f * 128
                            S.dma("pool", [f"vst{vi}"], [("SGT", gi, fr, tg)], out=SGT[gi, fr:fr + 128, tg * 512:(tg + 1) * 512],
                                  in_=vst[vi][:])
            for hp in range(2):
                t0 = hp * 2048
                cnt["conv"] = 1
                build_hT(x_all, hp * 16, "a")
                for c0 in (1024, 1536):
                    w, wk = load_w(c0, 512)
                    sR, sRk = new_stage()
                    for blk in range(16):
                        pb, pk = mm_tm(w, wk, blk, 512)
                        i = normrope(pb, pk, 0, 8, "dk", CSA, "CSa", hp * 16 + blk)
                        defer(lambda i=i, blk=blk, sR=sR, sRk=sRk: transp_to(zr[i], f"zr{i}", 4, sR[:, :, blk * 128:(blk + 1) * 128], (sRk, blk)))
                        run_deferred(LAB)
                    run_deferred(0)
                    h0 = (c0 - 1024) // 128
                    S.dma("pool", [(sRk, b) for b in range(16)], [("KDT", h0, hp)],
                          out=KDT[h0:h0 + 4, :, t0:t0 + 2048].rearrange("h p t -> p h t"), in_=sR[:])
                for c0 in (2048, 2560):
                    w, wk = load_w(c0, 512)
                    for blk in range(16):
                        pb, pk = mm_tm(w, wk, blk, 512)
                        vi = cnt["v"] % NV; cnt["v"] += 1
                        S.op("act", [pk], [f"vst{vi}"], "activation", out=vst[vi][:], in_=pb[:, :], func=AF.Copy)
                        S.dma("pool", [f"vst{vi}"], [("VD", c0, hp, blk)],
                              out=VD[t0 + blk * 128:t0 + (blk + 1) * 128, c0 - 2048:c0 - 2048 + 512], in_=vst[vi][:])
                w, wk = load_w(4096, 512)
                w2_, wk2 = load_w(4608, 256)
                sE, sEk = new_stage()
                sF, sFk = stgH, "stgH"
                for blk in range(16):
                    gb = hp * 16 + blk
                    pb, pk = mm_tm(w, wk, blk, 512)
                    vi = cnt["v"] % NV; cnt["v"] += 1
                    S.op("act", [pk], [f"vst{vi}"], "activation", out=vst[vi][:, 0:256], in_=pb[:, 0:256], func=AF.Copy)
                    S.op("act", [pk], [f"vst{vi}"], "activation", out=vst[vi][:, 256:384], in_=pb[:, 384:512], func=AF.Copy)
                    S.dma("pool", [f"vst{vi}"], [("VS", gb)], out=VS[gb * 128:(gb + 1) * 128, :], in_=vst[vi][:, 256:384])
                    defer(lambda vi=vi, blk=blk: transp_to(vst[vi], f"vst{vi}", 2, sE[:, 0:2, blk * 128:(blk + 1) * 128], (sEk, blk)))
                    i = normrope(pb, pk, 256, 2, "nk", CSA, "CSa", gb)
                    zi = cnt["d"] % NV; cnt["d"] += 1
                    dup = dupb[zi]; dk_ = f"dupb{zi}"
                    S.op("dve", [f"zr{i}"], [dk_], "tensor_copy", out=dup[:, 0:256].rearrange("p (g r d) -> p g r d", g=2, r=2),
                         in_=zr[i][:, 0:128].rearrange("p (g d) -> p g d", g=2).unsqueeze(2).broadcast_to([128, 2, 2, 64]))
                    defer(lambda dup=dup, dk_=dk_, blk=blk: transp_to(dup, dk_, 2, sE[:, 2:4, blk * 128:(blk + 1) * 128], (sEk, blk)))
                    pb2, pk2 = mm_tm(w2_, wk2, blk, 256)
                    vi = cnt["v"] % NV; cnt["v"] += 1
                    S.op("act", [pk2], [f"vst{vi}"], "activation", out=vst[vi][:, 0:128], in_=pb2[:, 128:256], func=AF.Copy)
                    S.dma("pool", [f"vst{vi}"], [("VW", gb)], out=VW[gb * 128:(gb + 1) * 128, :], in_=vst[vi][:, 0:128])
                    i = normrope(pb2, pk2, 0, 2, "nk", CSA, "CSa", gb)
                    zi = cnt["d"] % NV; cnt["d"] += 1
                    dup = dupb[zi]; dk_ = f"dupb{zi}"
                    S.op("dve", [f"zr{i}"], [dk_], "tensor_copy", out=dup[:, 0:256].rearrange("p (g r d) -> p g r d", g=2, r=2),
                         in_=zr[i][:, 0:128].rearrange("p (g d) -> p g d", g=2).unsqueeze(2).broadcast_to([128, 2, 2, 64]))
                    defer(lambda dup=dup, dk_=dk_, blk=blk: transp_to(dup, dk_, 2, sF[:, 0:2, blk * 128:(blk + 1) * 128], (sFk, blk)))
                    run_deferred(3)
                run_deferred(0)
                rE = [(sEk, b) for b in range(16)]
                S.dma("pool", rE, [("KCRT", hp)], out=KCRT[:, t0:t0 + 2048], in_=sE[:, 0, :])
                S.dma("pool", rE, [("VCRT", hp)], out=VCRT[:, t0:t0 + 2048], in_=sE[:, 1, :])
                S.dma("pool", rE, [("KST", hp)], out=KST[:, :, t0:t0 + 2048].rearrange("g p t -> p g t"), in_=sE[:, 2:4, :])
                S.dma("pool", [(sFk, b) for b in range(16)], [("KWT", hp)],
                      out=KWT[:, :, t0:t0 + 2048].rearrange("g p t -> p g t"), in_=sF[:, 0:2, :])
            S.barrier()

        if stop_after == "PA":
            _finish(nc, S, out_d, es)
            return nc

        build_rest(nc, S, es, locals(), dbg=dbg, stop_after=stop_after)
    return nc


def _finish(nc, S, out_d, es):
    z = es.enter_context(nc.sbuf_tensor("zfin", [128, D], F32))
    S.op("dve", [], ["zfin"], "memset", z[:], 0.0)
    for i in range(16):
        S.dma("sp", ["zfin"], [("out", i)], out=out_d[i * 128:(i + 1) * 128, :], in_=z[:])
    S.barrier()


def build_rest(nc, S, es, L, dbg=False, stop_after=None):
    g_ = lambda n: L[n]
    sbt, bank, bank_bf, ident, gates, rstd_from_ss = (g_("sbt"), g_("bank"), g_("bank_bf"), g_("ident"), g_("gates"),
                                                      g_("rstd_from_ss"))
    out_d = g_("out_d")

    def finish():
        _finish(nc, S, out_d, es)

    KCT = sbt(es, "KCT", [128, 2, 2, 256], BF16)
    VCO = sbt(es, "VCO", [128, 2, 2, 129], BF16)
    S.op("dve", [], ["KCT"], "memset", KCT[:], 0.0)
    S.op("dve", [], ["VCO"], "memset", VCO[:], 0.0)

    CW1, CW2, KCRT, VCRT = g_("CW1"), g_("CW2"), g_("KCRT"), g_("VCRT")
    with contextlib.ExitStack() as pb_:
        kvT = [sbt(pb_, f"kvT{i}", [128, SEQ], BF16) for i in range(2)]
        S.dma("sp", [("KCRT", 0), ("KCRT", 1)], ["kvT0"], out=kvT[0][:], in_=KCRT[:, :])
        S.dma("sp", [("VCRT", 0), ("VCRT", 1)], ["kvT1"], out=kvT[1][:], in_=VCRT[:, :])
        W1 = [[sbt(pb_, f"W1_{kv}{g}", [128, 32, 256], BF16) for g in range(2)] for kv in range(2)]
        W2 = [sbt(pb_, f"W2_{kv}", [128, 2, 64], BF16) for kv in range(2)]
        for kv in range(2):
            for half in range(2):
                S.op("dve" if half == 0 else "pool", [], [(f"W1_{kv}", half, "z")], "memset", W1[kv][half][(1 - half) * 64:(2 - half) * 64, :, :], 0.0)
                for l4 in range(4):
                    S.dma("sp", [("CW1", r) for r in range(0, 4096, 1024)], [(f"W1_{kv}", half)], out=W1[kv][half][half * 64:(half + 1) * 64, l4 * 8:(l4 + 1) * 8, :],
                          in_=CW1[kv * 2048 + l4 * 512:kv * 2048 + (l4 + 1) * 512, :].rearrange("(l d) h -> d l h", d=64))
            S.dma("sp", [("CW2", 0)], [f"W2_{kv}"], out=W2[kv][:], in_=CW2[kv * 256:(kv + 1) * 256, :].rearrange("(c p) d -> p c d", p=128))
        posf = sbt(pb_, "posf", [128, 2, 32], F32)
        posb = sbt(pb_, "posb", [128, 2, 32], BF16)
        for half in range(2):
            S.dma("sp", [], ["posf"], out=posf[half * 64:(half + 1) * 64], in_=g_("cmp_posT")[:, :, :])
        S.op("dve", ["posf"], ["posb"], "tensor_copy", out=posb[:], in_=posf[:])
        gk = sbt(pb_, "gk", [128, 64], F32)
        S.dma("sp", [], ["gk"], out=gk[:], in_=g_("nsa_k_norm").broadcast_to([128, 64]))
        biasT = sbt(pb_, "biasT", [128, 4], F32)
        GH = sbt(pb_, "GH", [128, 8, 256], BF16)
        S.op("dve", [], [("GH", a, b, c) for a in range(2) for b in range(2) for c in range(2)], "memset", GH[:], 0.0)
        u = [sbt(pb_, f"u{i}", [128, 256], F32) for i in range(2)]
        u2 = [sbt(pb_, f"u2{i}", [128, 256], F32) for i in range(2)]
        sg = [sbt(pb_, f"sg{i}", [128, 256], F32) for i in range(2)]
        ssk = sbt(pb_, "ssk", [128, 4], F32)
        rsk = sbt(pb_, "rsk", [128, 4], F32)
        kcn2 = [sbt(pb_, f"kcn2{i}", [128, 256], BF16) for i in range(2)]
        for i in range(2):
            S.op("dve", [], [f"kcn2{i}"], "memset", kcn2[i][:], 0.0)
        junkb = sbt(pb_, "junkb", [128, 64], F32)
        S.op("dve", ["VCO"], ["VCO"], "memset", VCO[:, :, :, 64:65], 1.0)
        for g in range(2):
            S.dma("sp", ["VCO"], ["VCO"], out=VCO[:, g, :, 65:129], in_=g_("c_ovl")[:, :, :])
        pbb, pbk = bank(4)
        for kv in range(2):
            for hc in range(2):
                col = kv * 2 + hc
                for l in range(32):
                    S.op("pe", [(f"W1_{kv}", 0), (f"W1_{kv}", 0, "z"), "posb"], [pbk], "matmul", pbb[:, col:col + 1],
                         lhsT=W1[kv][0][:, l, hc * 128:(hc + 1) * 128], rhs=posb[:, kv, l:l + 1],
                         start=(l == 0), stop=(l == 31), skip_group_check=True)
        S.op("act", [pbk], ["biasT"], "activation", out=biasT[:], in_=pbb[:, 0:4], func=AF.Copy)
        n = 0
        for kv in range(2):
            for g in range(2):
                for hc in range(2):
                    pb, pk = bank(n % 4)
                    for l in range(32):
                        S.op("pe", [(f"W1_{kv}", g), (f"W1_{kv}", g, "z"), f"kvT{kv}"], [pk], "matmul", pb[:, 0:255],
                             lhsT=W1[kv][g][:, l, hc * 128:(hc + 1) * 128],
                             rhs=kvT[kv][:, l:l + 4065:16], start=(l == 0), stop=(l == 31))
                    i = n % 2
                    col = kv * 2 + hc
                    S.op("act", [pk, "biasT"], [f"u{i}"], "activation", out=u[i][:, 0:255], in_=pb[:, 0:255],
                         func=AF.Identity, bias=biasT[:, col:col + 1], scale=1.0)
                    S.op("pool", [f"u{i}"], [f"u2{i}"], "tensor_tensor", out=u2[i][:, 0:255], in0=u[i][:, 0:255],
                         in1=u[i][:, 0:255], op=ALU.mult)
                    S.op("pool", [f"u2{i}"], [f"u2{i}"], "tensor_scalar", out=u2[i][:, 0:255], in0=u2[i][:, 0:255],
                         scalar1=0.044715, scalar2=1.0, op0=ALU.mult, op1=ALU.add)
                    S.op("pool", [f"u2{i}", f"u{i}"], [f"u2{i}"], "tensor_tensor", out=u2[i][:, 0:255], in0=u2[i][:, 0:255],
                         in1=u[i][:, 0:255], op=ALU.mult)
                    S.op("act", [f"u2{i}"], [f"sg{i}"], "activation", out=sg[i][:, 0:255], in_=u2[i][:, 0:255],
                         func=AF.Sigmoid, scale=1.5957691216057308)
                    S.op("dve", [f"u{i}", f"sg{i}"], [("GH", kv, g, hc)], "tensor_tensor", out=GH[:, kv * 4 + g * 2 + hc, 0:255],
                         in0=u[i][:, 0:255], in1=sg[i][:, 0:255], op=ALU.mult)
                    n += 1
        n = 0
        for kv in range(2):
            for g in range(2):
                for c in range(2):
                    nn = 128 if c == 0 else 127
                    pb, pk = bank(n % 4)
                    for hc in range(2):
                        S.op("pe", [("GH", kv, g, hc), f"W2_{kv}"], [pk], "matmul", pb[0:nn, 0:64],
                             lhsT=GH[:, kv * 4 + g * 2 + hc, c * 128:c * 128 + nn], rhs=W2[kv][:, hc, :],
                             start=(hc == 0), stop=(hc == 1))
                    if kv == 0:
                        i = n % 2
                        col = g * 2 + c
                        S.op("act", [pk], ["junkb", ("ssk", col)], "activation", out=junkb[0:nn, :], in_=pb[0:nn, 0:64],
                             func=AF.Square, accum_out=ssk[0:nn, col:col + 1])
                        rstd_from_ss(ssk[0:nn, col:col + 1], rsk[0:nn, col:col + 1], 1, [("ssk", col)], [("rsk", col)], 1.0 / 64)
                        S.op("dve", [pk, ("rsk", col), "gk"], [f"kcn2{i}"], "scalar_tensor_tensor", out=kcn2[i][0:nn, 0:64],
                             in0=pb[0:nn, 0:64], scalar=rsk[0:nn, col:col + 1], in1=gk[0:nn, :], op0=ALU.mult, op1=ALU.mult)
                        S.op("dve", [f"kcn2{i}"], [f"kcn2{i}"], "tensor_copy", out=kcn2[i][0:nn, 192:256], in_=kcn2[i][0:nn, 0:64])
                        tb, tk = bank_bf(6 + i)
                        for par in range(2):
                            S.op("pe", [f"kcn2{i}", "ident"], [tk], "transpose", tb[:, par * 128:par * 128 + nn], kcn2[i][0:nn, par * 128:(par + 1) * 128],
                                 ident[0:nn, 0:nn])
                        S.op("act", [tk, "KCT"], ["KCT"], "activation", out=KCT[:, :, g, c * 128:c * 128 + nn],
                             in_=tb[:, 0:256].rearrange("p (a n) -> p a n", a=2)[:, :, 0:nn], func=AF.Copy)
                    else:
                        S.op("act", [pk, "VCO"], ["VCO"], "activation", out=VCO[0:nn, g, c, 0:64], in_=pb[0:nn, 0:64], func=AF.Copy)
                    n += 1
        if dbg:
            DKC = nc.dram_tensor("DKC", [128, 2, 2, 256], BF16, kind="ExternalOutput").ap()
            DVC = nc.dram_tensor("DVC", [128, 2, 2, 129], BF16, kind="ExternalOutput").ap()
            S.dma("sp", ["KCT"], ["DKC"], out=DKC[:, :, :, :], in_=KCT[:])
            S.dma("sp", ["VCO"], ["DVC"], out=DVC[:, :, :, :], in_=VCO[:])
        S.barrier()
    if stop_after == "PB":
        return finish()

    QDT, KDT, VD, YAT = g_("QDT"), g_("KDT"), g_("VD"), g_("YAT")
    with contextlib.ExitStack() as pc:
      if not PD_DEBUG["skip_pc"]:
        dm = sbt(pc, "dm", [128, 8, 512], BF16)
        S.dma("sp", [], ["dm"], out=dm[:], in_=g_("c_dmask")[:, :, :])
        lt = sbt(pc, "lt", [128, 256], F32)
        S.dma("sp", [], ["lt"], out=lt[:], in_=g_("diff_lambda").broadcast_to([128, 256]))
        prod = sbt(pc, "prod", [128, 128], F32)
        sums = sbt(pc, "sums", [128, 2], F32)
        ex = sbt(pc, "ex", [128, 2], F32)
        nlam = sbt(pc, "nlam", [128, 1], F32)
        S.op("dve", ["lt"], ["prod"], "tensor_tensor", out=prod[:].rearrange("p (a d) -> p a d", a=2),
             in0=lt[:].rearrange("p (a b d) -> p a b d", a=2, b=2)[:, :, 0, :],
             in1=lt[:].rearrange("p (a b d) -> p a b d", a=2, b=2)[:, :, 1, :], op=ALU.mult)
        S.op("dve", ["prod"], ["sums"], "tensor_reduce", out=sums[:], in_=prod[:].rearrange("p (a d) -> p a d", a=2),
             axis=AX.X, op=ALU.add)
        S.op("act", ["sums"], ["ex"], "activation", out=ex[:], in_=sums[:], func=AF.Exp)
        S.op("dve", ["ex"], ["nlam"], "tensor_tensor", out=nlam[:], in0=ex[:, 1:2], in1=ex[:, 0:1], op=ALU.subtract)
        S.op("dve", ["nlam"], ["nlam"], "tensor_scalar", out=nlam[:], in0=nlam[:], scalar1=-0.2, scalar2=None, op0=ALU.add)
        gsub = sbt(pc, "gsub", [128, 128], F32)
        S.dma("sp", [], ["gsub"], out=gsub[:], in_=g_("diff_subln").broadcast_to([128, 128]))
        S.op("dve", ["gsub"], ["gsub"], "tensor_scalar", out=gsub[:], in0=gsub[:], scalar1=0.8, scalar2=None, op0=ALU.mult)
        KT = [sbt(pc, f"KT{i}", [128, SEQ], BF16) for i in range(2)]
        VT = [sbt(pc, f"VT{i}", [128, 32, 129], BF16) for i in range(2)]
        QT = [[sbt(pc, f"QT{i}{c}", [128, NOWN], BF16) for c in range(2)] for i in range(2)]
        for i in range(2):
            for c in range(2):
                S.op("dve", [], [("QTz", i, c)], "memset", QT[i][c][(1 - c) * 64:(2 - c) * 64, :], 0.0)
        YAs = [sbt(pc, f"YAs{i}", [128, NOWN], BF16) for i in range(2)]
        E = [sbt(pc, f"E{i}", [128, 512], BF16) for i in range(4)]
        rd = [sbt(pc, f"rd{i}", [128, 4], F32) for i in range(2)]
        t0_ = [sbt(pc, f"t0_{i}", [128, 128], F32) for i in range(2)]
        o_ = [sbt(pc, f"o_{i}", [128, 128], F32) for i in range(2)]
        yb_ = [sbt(pc, f"yb_{i}", [128, 128], BF16) for i in range(2)]
        junkc = sbt(pc, "junkc", [128, 128], BF16)
        for i in range(2):
            S.op("dve", [], [("VT1", i)], "memset", VT[i][:, :, 128:129], 1.0)
        nf = 0
        LA = 3

        def pc_loads(h):
            i = h % 2
            S.dma("sp", [("KDT", (h // 4) * 4, 0), ("KDT", (h // 4) * 4, 1)], [("KT", i)], out=KT[i][:], in_=KDT[h, :, :])
            for q4 in range(8):
                S.dma("sp", [("VD", c0, hp, b) for c0 in (2048, 2560) for hp in range(2) for b in range(16)], [("VT", i, q4)],
                      out=VT[i][:, q4 * 4:(q4 + 1) * 4, 0:128],
                      in_=VD[q4 * 512:(q4 + 1) * 512, h * 128:(h + 1) * 128].rearrange("(kb p) d -> p kb d", p=128))
            for c in range(2):
                S.dma("sp", [("QN", 0, 0), ("QN", 512, 0)], [("QT", i, c)], out=QT[i][c][c * 64:(c + 1) * 64, :], in_=QDT[h, c * 64:(c + 1) * 64, :])

        def pc_front(u, n):
            h, G, kb, c = u
            i = h % 2
            r = kb - 8 * G
            pb, pk = bank(n % 4)
            e = E[n % 4]; ek = f"E{n % 4}"
            q_lo = max(r, 0) // 2 * 128
            S.op("pe", [("KT", i), ("QT", i, c), ("QTz", i, c)], [pk], "matmul", pb[:, q_lo:512], lhsT=KT[i][:, kb * 128:(kb + 1) * 128],
                 rhs=QT[i][c][:, G * 512 + q_lo:(G + 1) * 512], start=True, stop=True)
            S.op("act", [pk], [ek], "activation", out=e[:, q_lo:512], in_=pb[:, q_lo:512], func=AF.Exp, scale=0.125)
            if r >= 0:
                S.op("dve", [ek, "dm"], [ek], "tensor_tensor", out=e[:, q_lo:512], in0=e[:, q_lo:512], in1=dm[:, r, q_lo:512], op=ALU.mult)

        def pc_back(u, n):
            h, G, kb, c = u
            i = h % 2
            r = kb - 8 * G
            e = E[n % 4]; ek = f"E{n % 4}"
            for t in range(4):
                if r > 2 * t + 1:
                    continue
                a = c * 4 + t
                ab, ak = bank(4 + a // 3)
                off = (a % 3) * 129
                S.op("pe", [ek, ("VT", i, kb // 4), ("VT1", i)], [ak], "matmul", ab[:, off:off + 129],
                     lhsT=e[:, t * 128:(t + 1) * 128], rhs=VT[i][:, kb, :], start=(kb == 0 and a % 3 == 0),
                     stop=(kb == 8 * G + 2 * t + 1), skip_group_check=True)

        def pc_final(h, G):
            nonlocal nf
            i = h % 2
            for t in range(4):
                f = nf % 2; nf += 1
                a0, a1 = t, 4 + t
                b0, k0 = bank(4 + a0 // 3); b1, k1 = bank(4 + a1 // 3)
                O0 = b0[:, (a0 % 3) * 129:(a0 % 3) * 129 + 129]
                O1 = b1[:, (a1 % 3) * 129:(a1 % 3) * 129 + 129]
                rk = f"rd{f}"
                S.op("dve", [k0], [(rk, 0)], "reciprocal", out=rd[f][:, 0:1], in_=O0[:, 128:129])
                S.op("dve", [k1], [(rk, 1)], "reciprocal", out=rd[f][:, 1:2], in_=O1[:, 128:129])
                S.op("dve", [(rk, 1), "nlam"], [(rk, 2)], "tensor_tensor", out=rd[f][:, 2:3], in0=rd[f][:, 1:2], in1=nlam[:], op=ALU.mult)
                S.op("dve", [k0, (rk, 0)], [f"t0_{f}"], "tensor_scalar", out=t0_[f][:], in0=O0[:, 0:128], scalar1=rd[f][:, 0:1],
                     scalar2=None, op0=ALU.mult)
                S.op("dve", [k1, (rk, 2), f"t0_{f}"], [f"o_{f}"], "scalar_tensor_tensor", out=o_[f][:], in0=O1[:, 0:128],
                     scalar=rd[f][:, 2:3], in1=t0_[f][:], op0=ALU.mult, op1=ALU.add)
                S.op("act", [f"o_{f}"], ["junkc", (rk, 3)], "activation", out=junkc[:], in_=o_[f][:], func=AF.Square,
                     accum_out=rd[f][:, 3:4])
                rstd_from_ss(rd[f][:, 3:4], rd[f][:, 3:4], 1, [(rk, 3)], [(rk, 3)], 1.0 / 128)
                S.op("dve", [f"o_{f}", (rk, 3), "gsub"], [f"yb_{f}"], "scalar_tensor_tensor", out=yb_[f][:], in0=o_[f][:],
                     scalar=rd[f][:, 3:4], in1=gsub[:], op0=ALU.mult, op1=ALU.mult)
                tb, tk = bank_bf(7)
                S.op("pe", [f"yb_{f}", "ident"], [tk], "transpose", tb[:, 0:128], yb_[f][:], ident[:])
                q0 = (G * 4 + t) * 128
                S.op("act", [tk], [("YAs", i, G * 4 + t)], "activation", out=YAs[i][:, q0:q0 + 128], in_=tb[:, 0:128], func=AF.Copy)
            if G == 3:
                S.dma("pool", [("YAs", i, b) for b in range(16)], [("YAT", h)], out=YAT[h, :, :], in_=YAs[i][:])

        units = [(h, G, kb, c) for h in range(8) for G in range(4) for kb in range(8 * G + 8) for c in range(2)]
        pc_loads(0)
        for n in range(len(units) + LA):
            if n < len(units):
                u = units[n]
                if u[1] == 0 and u[2] == 2 and u[3] == 0 and u[0] + 1 < 8:
                    pc_loads(u[0] + 1)
                pc_front(u, n)
                if n % 8 == 0:
                    L["conv_pop"](1)
            if n >= LA:
                ub = units[n - LA]
                pc_back(ub, n - LA)
                if ub[2] == 8 * ub[1] + 7 and ub[3] == 1:
                    pc_final(ub[0], ub[1])
        L["conv_pop"](len(L["pending_conv"]))
        S.barrier()
    if stop_after == "PC":
        return finish()

    NQRT, NQPT, KST, KWT, VS, VW, YBT = g_("NQRT"), g_("NQPT"), g_("KST"), g_("KWT"), g_("VS"), g_("VW"), g_("YBT")
    with contextlib.ExitStack() as pd:
        m128 = sbt(pd, "m128", [128, 2, 8, 128], BF16)
        mc = sbt(pd, "mc", [128, 16, 2, 128], BF16)
        valid = sbt(pd, "valid", [128, 16, 64], F32)
        addc = sbt(pd, "addc", [128, 16, 64], F32)
        eexp = sbt(pd, "eexp", [128, 32, 128], BF16)
        S.op("dve", [], ["eexpz"], "memset", eexp[64:128, :, :], 0.0)
        S.dma("sp", [], ["m128"], out=m128[:], in_=g_("c_m128")[:, :, :, :])
        S.dma("sp", [], ["mc"], out=mc[:], in_=g_("c_mc")[:, :, :, :])
        S.dma("sp", [], ["valid"], out=valid[:], in_=g_("c_valid")[:, :, :])
        S.dma("sp", [], ["addc"], out=addc[:], in_=g_("c_addc")[:, :, :])
        S.dma("sp", [], ["eexp"], out=eexp[0:64, :, :], in_=g_("c_eexp")[:, :, :])
        KS = [sbt(pd, f"KS{par}", [128, SEQ], BF16) for par in range(2)]
        KW = [sbt(pd, f"KW{par}", [128, SEQ], BF16) for par in range(2)]
        for par in range(2):
            S.op("dve", [], [("KSz", par)], "memset", KS[par][(1 - par) * 64:(2 - par) * 64, :], 0.0)
            S.op("pool", [], [("KWz", par)], "memset", KW[par][(1 - par) * 64:(2 - par) * 64, :], 0.0)
        VS1 = sbt(pd, "VS1", [128, 32, 65], BF16)
        VW1 = sbt(pd, "VW1", [128, 32, 65], BF16)
        QR = sbt(pd, "QR", [128, 4, NOWN], BF16)
        QP = sbt(pd, "QP", [128, 4, NOWN], BF16)
        ecmp = sbt(pd, "ecmp", [128, 2, 2, 512], BF16)
        es_ = [sbt(pd, f"es{i}", [128, 512], BF16) for i in range(4)]
        M4 = [sbt(pd, f"M4{i}", [128, 512], BF16) for i in range(2)]
        Y = [sbt(pd, f"Y{i}", [128, 8, 64], F32) for i in range(2)]
        Yb = sbt(pd, "Yb", [128, 512], BF16)
        IMP = sbt(pd, "IMP", [128, 64], F32)
        sc = sbt(pd, "sc", [128, 64], F32)
        sc2 = sbt(pd, "sc2", [128, 64], F32)
        m8a = sbt(pd, "m8a", [128, 8], F32)
        m8b = sbt(pd, "m8b", [128, 8], F32)
        selb = sbt(pd, "selb", [128, 128], BF16)
        selT = [sbt(pd, f"selT{i}", [128, 128], BF16) for i in range(2)]
        S.op("dve", [], ["selbz"], "memset", selb[:, 64:128], 0.0)
        den8 = sbt(pd, "den8", [128, 8], F32)
        rd8 = sbt(pd, "rd8", [128, 8], F32)
        coef = sbt(pd, "coef", [128, 8], F32)
        ystg = sbt(pd, "ystg", [128, 4, NOWN], BF16)
        S.op("dve", [], ["VS1o"], "memset", VS1[:, :, 64:65], 1.0)
        S.op("dve", [], ["VW1o"], "memset", VW1[:, :, 64:65], 1.0)
        nsc = 0
        for g in range(2):
            for par in range(2):
                S.dma("sp", [("KST", 0), ("KST", 1)], [("KS", par)], out=KS[par][par * 64:(par + 1) * 64, :], in_=KST[g, par * 64:(par + 1) * 64, :])
                S.dma("sp", [("KWT", 0), ("KWT", 1)], [("KW", par)], out=KW[par][par * 64:(par + 1) * 64, :], in_=KWT[g, par * 64:(par + 1) * 64, :])
            for q4 in range(4):
                S.dma("sp", [("VS", b) for b in range(32)], [("VS1", q4)], out=VS1[:, q4 * 8:(q4 + 1) * 8, 0:64],
                      in_=VS[q4 * 1024:(q4 + 1) * 1024, g * 64:(g + 1) * 64].rearrange("(kb p) d -> p kb d", p=128))
                S.dma("sp", [("VW", b) for b in range(32)], [("VW1", q4)], out=VW1[:, q4 * 8:(q4 + 1) * 8, 0:64],
                      in_=VW[q4 * 1024:(q4 + 1) * 1024, g * 64:(g + 1) * 64].rearrange("(kb p) d -> p kb d", p=128))
            S.dma("sp", [("QN", 3072, 0), ("QN", 3584, 0)], ["QR"], out=QR[:], in_=NQRT[g * 4:(g + 1) * 4].rearrange("h p t -> p h t"))
            S.dma("sp", [("QN", 3072, 1), ("QN", 3584, 1)], ["QP"], out=QP[:], in_=NQPT[g * 4:(g + 1) * 4].rearrange("h p t -> p h t"))
            def gv_of(j):
                return gates[:, j, g * 24:(g + 1) * 24].rearrange("p (r b) -> p r b", b=3)

            def fin_branch(j, nper, width, branch, first):
                Yj = Y[j % 2]; yk = f"Y{j % 2}"
                nb_ = (8 + nper - 1) // nper
                for b in range(nb_):
                    ab, ak = bank(4 + b)
                    nh = min(nper, 8 - b * nper)
                    S.op("dve", [ak], ["den8"], "tensor_scalar", out=den8[:, b * nper:b * nper + nh].unsqueeze(2),
                         in0=ab[:, 0:nh * width].rearrange("p (r c) -> p r c", c=width)[:, :, 64:65], scalar1=1e-30,
                         scalar2=None, op0=ALU.max)
                S.op("dve", ["den8"], ["rd8"], "reciprocal", out=rd8[:], in_=den8[:])
                S.op("dve", ["rd8", ("gates", j)], ["coef"], "tensor_tensor", out=coef[:], in0=rd8[:], in1=gv_of(j)[:, :, branch], op=ALU.mult)
                for r in range(8):
                    ab, ak = bank(4 + r // nper)
                    off = (r % nper) * width
                    if first:
                        S.op("dve", [ak, "coef"], [(yk, r)], "tensor_scalar", out=Yj[:, r, :], in0=ab[:, off:off + 64],
                             scalar1=coef[:, r:r + 1], scalar2=None, op0=ALU.mult)
                    else:
                        S.op("dve", [ak, "coef", (yk, r)], [(yk, r)], "scalar_tensor_tensor", out=Yj[:, r, :], in0=ab[:, off:off + 64],
                             scalar=coef[:, r:r + 1], in1=Yj[:, r, :], op0=ALU.mult, op1=ALU.add)

            def chain(j):
                nonlocal nsc
                q0 = j * 128
                for c in range(2):
                    nn = 128 if c == 0 else 127
                    for par in range(2):
                        pb, pk = bank(nsc % 4); nsc += 1
                        S.op("pe", ["KCT", "QP"], [pk], "matmul", pb[0:nn, :].rearrange("p (a q) -> p a q", a=4),
                             lhsT=KCT[:, par, g, c * 128:c * 128 + nn], rhs=QP[:, :, q0:q0 + 128],
                             start=True, stop=True)
                        S.op("act", [pk], [("ecmp", c, par)], "activation", out=ecmp[0:nn, c, par, :], in_=pb[0:nn, :], func=AF.Exp, scale=0.125)
                        ev = ecmp[0:nn, c, par, :].rearrange("p (a q) -> p a q", a=4)
                        S.op("dve", [("ecmp", c, par), "mc"], [("ecmp", c, par)], "tensor_tensor", out=ev, in0=ev,
                             in1=mc[0:nn, j, c, :].unsqueeze(1).broadcast_to([nn, 4, 128]), op=ALU.mult)
                for r in range(8):
                    ii, par = r // 2, r % 2
                    ab, ak = bank(4 + r // 3)
                    off = (r % 3) * 129
                    for c in range(2):
                        nn = 128 if c == 0 else 127
                        S.op("pe", [("ecmp", c, par), "VCO"], [ak], "matmul", ab[:, off:off + 129],
                             lhsT=ecmp[0:nn, c, par, ii * 128:(ii + 1) * 128], rhs=VCO[0:nn, g, c, :],
                             start=(c == 0 and r % 3 == 0), stop=(c == 1), skip_group_check=True)
                fin_branch(j, 3, 129, 0, True)
                for r in range(8):
                    ab, ak = bank(4 + r // 3)
                    off = (r % 3) * 129
                    if r == 0:
                        S.op("dve", [ak, "rd8"], ["IMP"], "tensor_scalar", out=IMP[:], in0=ab[:, off + 65:off + 129], scalar1=rd8[:, 0:1],
                             scalar2=None, op0=ALU.mult)
                    else:
                        S.op("dve", [ak, "rd8", "IMP"], ["IMP"], "scalar_tensor_tensor", out=IMP[:], in0=ab[:, off + 65:off + 129],
                             scalar=rd8[:, r:r + 1], in1=IMP[:], op0=ALU.mult, op1=ALU.add)
                S.op("dve", ["IMP", "valid"], ["sc"], "tensor_tensor", out=sc[:], in0=IMP[:], in1=valid[:, j, :], op=ALU.mult)
                S.op("dve", ["sc", "addc"], ["sc"], "tensor_tensor", out=sc[:], in0=sc[:], in1=addc[:, j, :], op=ALU.add)
                S.op("dve", ["sc"], ["m8a"], "max", out=m8a[:], in_=sc[:])
                S.op("dve", ["sc", "m8a"], ["sc2"], "match_replace", out=sc2[:], in_to_replace=m8a[:], in_values=sc[:], imm_value=-3.0)
                S.op("dve", ["sc2"], ["m8b"], "max", out=m8b[:], in_=sc2[:])
                S.op("dve", ["sc", "m8b"], ["selb"], "tensor_scalar", out=selb[:, 0:64], in0=sc[:], scalar1=m8b[:, 7:8], scalar2=None, op0=ALU.is_ge)
                tb, tk = bank_bf(7)
                S.op("pe", ["selb", "selbz", "ident"], [tk], "transpose", tb[:, 0:128], selb[:], ident[:])
                S.op("act", [tk], [f"selT{j % 2}"], "activation", out=selT[j % 2][:], in_=tb[:, 0:128], func=AF.Copy)

            def maskgen(j, kb4):
                jp = j & 1
                nkb = 2 * j + 2
                nb = min(4, nkb - kb4)
                mi = (kb4 // 4) % 2
                pm, pmk = bank(6)
                for q_ in range(nb):
                    S.op("pe", ["eexp", "eexpz", f"selT{j % 2}"], [pmk], "matmul", pm[:, q_ * 128:(q_ + 1) * 128], lhsT=eexp[:, kb4 + q_, :],
                         rhs=selT[j % 2][:, :], start=True, stop=True, skip_group_check=True)
                ncaus = sum(1 for q_ in range(nb) if kb4 + q_ >= 2 * j)
                nplain = nb - ncaus
                if nplain > 0:
                    S.op("act", [pmk], [(f"M4{mi}", q2) for q2 in range(nplain)], "activation", out=M4[mi][:, 0:nplain * 128],
                         in_=pm[:, 0:nplain * 128], func=AF.Copy)
                for q_ in range(nplain, nb):
                    kb = kb4 + q_
                    S.op("dve", [pmk, "m128"], [(f"M4{mi}", q_)], "tensor_tensor", out=M4[mi][:, q_ * 128:(q_ + 1) * 128],
                         in0=pm[:, q_ * 128:(q_ + 1) * 128], in1=m128[:, jp, 6 + (kb - 2 * j), :], op=ALU.mult)

            def attn_units(j, kbs, Kt, Kk, V1, Vk, maskfn, pre=None):
                nonlocal nsc
                q0 = j * 128
                units = [(kb, par) for kb in kbs for par in range(2)]
                base = nsc
                nsc += len(units)
                LA_ = 3
                for n in range(len(units) + LA_):
                    if n < len(units):
                        kb, par = units[n]
                        if pre is not None and par == 0:
                            pre(kb)
                        mk_ = maskfn(kb)
                        pb, pk = bank((base + n) % 4)
                        e = es_[(base + n) % 4]; ek = f"es{(base + n) % 4}"
                        S.op("pe", [(Kk, par), (Kk + "z", par), "QR"], [pk], "matmul", pb[:, :].rearrange("p (a q) -> p a q", a=4),
                             lhsT=Kt[par][:, kb * 128:(kb + 1) * 128], rhs=QR[:, :, q0:q0 + 128], start=True, stop=True)
                        S.op("act", [pk], [ek], "activation", out=e[:], in_=pb[:, :], func=AF.Exp, scale=0.125)
                        if mk_ is not None:
                            map_, mkey = mk_
                            ev = e[:].rearrange("p (a q) -> p a q", a=4)
                            S.op("dve", [ek, mkey], [ek], "tensor_tensor", out=ev, in0=ev,
                                 in1=map_.unsqueeze(1).broadcast_to([128, 4, 128]), op=ALU.mult)
                    if n >= LA_:
                        kb, par = units[n - LA_]
                        e = es_[(base + n - LA_) % 4]; ek = f"es{(base + n - LA_) % 4}"
                        for ii in range(4):
                            r = 2 * ii + par
                            ab, ak = bank(4 + r // 4)
                            off = (r % 4) * 65
                            S.op("pe", [ek, (Vk, kb // 8), Vk + "o"], [ak], "matmul", ab[:, off:off + 65], lhsT=e[:, ii * 128:(ii + 1) * 128],
                                 rhs=V1[:, kb, :], start=(kb == kbs[0] and r % 4 == 0), stop=(kb == kbs[-1]), skip_group_check=True)

            chain(0)
            for j in range(16):
                jp = j & 1
                q0 = j * 128
                if j + 1 < 16:
                    chain(j + 1)
                nkb = 2 * j + 2
                if PD_DEBUG["sel"]:
                    maskgen(j, 0)

                    def pre(kb, j=j, nkb=nkb):
                        if kb % 4 == 0 and kb + 4 < nkb:
                            maskgen(j, kb + 4)

                    def smask(kb):
                        return (M4[(kb // 4) % 2][:, (kb % 4) * 128:(kb % 4 + 1) * 128], (f"M4{(kb // 4) % 2}", kb % 4))

                    attn_units(j, list(range(nkb)), KS, "KS", VS1, "VS1", smask, pre)
                    fin_branch(j, 4, 65, 1, False)
                if PD_DEBUG["win"]:
                    kbs = [kb for kb in range(2 * j - 4, 2 * j + 2) if kb >= 0]

                    def wmask(kb, j=j, jp=jp):
                        idx = kb - (2 * j - 4)
                        if idx in (2, 3):
                            return None
                        return (m128[:, jp, idx, :], "m128")

                    attn_units(j, kbs, KW, "KW", VW1, "VW1", wmask)
                    fin_branch(j, 4, 65, 2, False)
                yk = f"Y{j % 2}"
                S.op("act", [(yk, r) for r in range(8)], ["Yb"], "activation", out=Yb[:], in_=Y[j % 2][:].rearrange("p r d -> p (r d)"), func=AF.Copy)
                tb, tk = bank_bf(7)
                for ii in range(4):
                    S.op("pe", ["Yb", "ident"], [tk], "transpose", tb[:, ii * 128:(ii + 1) * 128], Yb[:, ii * 128:(ii + 1) * 128], ident[:])
                S.op("act", [tk], [("ystg", j)], "activation", out=ystg[:, :, q0:q0 + 128],
                     in_=tb[:, 0:512].rearrange("p (c t) -> p c t", c=4), func=AF.Copy)
            S.dma("pool", [("ystg", j) for j in range(16)], [("YBT", g)], out=YBT[g * 4:(g + 1) * 4].rearrange("h p t -> p h t"), in_=ystg[:])
        S.barrier()
    if stop_after == "PD":
        return finish()

    build_rowlocal(nc, S, es, L)


def build_rowlocal(nc, S, es, L):
    g_ = lambda n: L[n]
    sbt, bank, bank_bf, ident, rstd_from_ss = g_("sbt"), g_("bank"), g_("bank_bf"), g_("ident"), g_("rstd_from_ss")
    x_own, p_own, out_d = g_("x_own"), g_("p_own"), g_("out_d")
    YAT, YBT, SGT = g_("YAT"), g_("YBT"), g_("SGT")
    WPD, WPN, WOUT, WUP, WDN, WPP, WPG = g_("WPD"), g_("WPN"), g_("WOUT"), g_("WUP"), g_("WDN"), g_("WPP"), g_("WPG")
    with contextlib.ExitStack() as pe_:
        gvec = sbt(pe_, "gvec", [128, D], F32)
        aT = sbt(pe_, "aT", [128, 64, 512], BF16)
        wbig = [sbt(pe_, f"wbig{i}", [128, 16, 512], BF16) for i in range(2)]
        x1 = sbt(pe_, "x1", [128, 4, D], F32)
        hT2 = sbt(pe_, "hT2", [128, 16, 512], BF16)
        wpp = sbt(pe_, "wpp", [128, 2, D], BF16)
        hb2 = sbt(pe_, "hb2", [128, D], BF16)
        sgt = [sbt(pe_, f"sgt{i}", [128, 2, 512], BF16) for i in range(2)]
        t1 = sbt(pe_, "t1", [128, 512], F32)
        t2 = sbt(pe_, "t2", [128, 512], F32)
        xin = [sbt(pe_, f"xin{i}", [128, 512], F32) for i in range(2)]
        rl = [sbt(pe_, f"rl{i}", [128, 512], F32) for i in range(2)]
        gt = [sbt(pe_, f"gt{i}", [128, 512], F32) for i in range(2)]
        ev = [sbt(pe_, f"ev{i}", [128, 512], F32) for i in range(2)]
        ss1 = sbt(pe_, "ss1", [128, 8], F32)
        rs1 = sbt(pe_, "rs1", [128, 8], F32)
        ssE = sbt(pe_, "ssE", [128, 16], F32)
        ssE4 = sbt(pe_, "ssE4", [128, 4], F32)
        rsE = sbt(pe_, "rsE", [128, 4], F32)
        pt = sbt(pe_, "pt", [128, 256], F32)
        ptb = sbt(pe_, "ptb", [128, 256], BF16)
        pT = sbt(pe_, "pT", [128, 2, 512], BF16)
        S.dma("sp", [("WPP", 0)], ["wpp"], out=wpp[:], in_=WPP[:, :].rearrange("(k p) c -> p k c", p=128))
        cn = {"w": 0, "ps": 0, "sg": 0, "x": 0, "r": 0, "g": 0, "e": 0}

        def nextw():
            i = cn["w"] % 2; cn["w"] += 1
            return wbig[i], f"wbig{i}"

        def nextbank():
            b = cn["ps"] % 4; cn["ps"] += 1
            return bank(b)

        def to_hT2(blk):
            for half in range(2):
                pb, pk = bank_bf(6 + half)
                for k in range(8):
                    kk = half * 8 + k
                    S.op("pe", ["hb2", "ident"], [pk], "transpose", pb[:, k * 128:(k + 1) * 128], hb2[:, kk * 128:(kk + 1) * 128], ident[:])
                dst = hT2[:, half * 8:half * 8 + 8, blk * 128:(blk + 1) * 128]
                srcv = pb.rearrange("p (k t) -> p k t", k=8)
                if half == 0:
                    S.op("act", [pk], [("hT2", blk, 0)], "activation", out=dst, in_=srcv, func=AF.Copy)
                else:
                    S.op("dve", [pk], [("hT2", blk, 1)], "tensor_copy", out=dst, in_=srcv)

        hT2keys = [("hT2", b, h) for b in range(4) for h in range(2)]
        for tt in range(4):
            tok0 = tt * 512
            S.dma("sp", [("YAT", h) for h in range(8)], [("aT", k) for k in range(8)], out=aT[:, 0:8, :],
                  in_=YAT[:, :, tok0:tok0 + 512].rearrange("h p t -> p h t"))
            S.dma("sp", [("YBT", 0), ("YBT", 1)], [("aT", k) for k in range(8, 16)], out=aT[:, 8:16, :],
                  in_=YBT[:, :, tok0:tok0 + 512].rearrange("h p t -> p h t"))
            for cc in range(4):
                w, wk = nextw()
                S.dma("sp", [("WPD", r) for r in range(0, 1024, 256)], [(wk, 0)], out=w[:, 0:8, :],
                      in_=WPD[:, cc * 512:(cc + 1) * 512].rearrange("(k p) c -> p k c", p=128))
                S.dma("sp", [("WPN", r) for r in range(0, 1024, 256)], [(wk, 1)], out=w[:, 8:16, :],
                      in_=WPN[:, cc * 512:(cc + 1) * 512].rearrange("(k p) c -> p k c", p=128))
                for f in range(4):
                    fidx = cc * 4 + f
                    si = cn["sg"] % 2; cn["sg"] += 1
                    for gi in range(2):
                        S.dma("sp", [("SGT", gi, fidx * 128, tt)], [(f"sgt{si}", gi)], out=sgt[si][:, gi, :],
                              in_=SGT[gi, fidx * 128:(fidx + 1) * 128, tok0:tok0 + 512])
                    pA, pAk = nextbank()
                    pB, pBk = nextbank()
                    for k in range(8):
                        S.op("pe", [(wk, 0), ("aT", k)], [pAk], "matmul", pA[:, :], lhsT=w[:, k, f * 128:(f + 1) * 128], rhs=aT[:, k, :],
                             start=(k == 0), stop=(k == 7))
                    for k in range(8):
                        S.op("pe", [(wk, 1), ("aT", 8 + k)], [pBk], "matmul", pB[:, :], lhsT=w[:, 8 + k, f * 128:(f + 1) * 128],
                             rhs=aT[:, 8 + k, :], start=(k == 0), stop=(k == 7))
                    S.op("dve", [pAk, (f"sgt{si}", 0)], ["t1"], "tensor_tensor", out=t1[:], in0=pA[:, :], in1=sgt[si][:, 0, :], op=ALU.mult)
                    S.op("dve", [pBk, (f"sgt{si}", 1)], ["t2"], "tensor_tensor", out=t2[:], in0=pB[:, :], in1=sgt[si][:, 1, :], op=ALU.mult)
                    S.op("pool", ["t1", "t2"], [("aT", 16 + fidx)], "tensor_tensor", out=aT[:, 16 + fidx, :], in0=t1[:], in1=t2[:], op=ALU.add)
            for cc in range(4):
                w, wk = nextw()
                S.dma("sp", [("WOUT", r) for r in range(0, D, 256)], [(wk, 0), (wk, 1)], out=w[:],
                      in_=WOUT[:, cc * 512:(cc + 1) * 512].rearrange("(k p) c -> p k c", p=128))
                for blk in range(4):
                    pb, pk = nextbank()
                    for k in range(16):
                        S.op("pe", [(wk, 0), (wk, 1), ("aT", 16 + k)], [pk], "matmul", pb[:, :], lhsT=aT[:, 16 + k, blk * 128:(blk + 1) * 128],
                             rhs=w[:, k, :], start=(k == 0), stop=(k == 15))
                    xi = cn["x"] % 2; cn["x"] += 1
                    S.dma("sp", [], [f"xin{xi}"], out=xin[xi][:], in_=x_own[tok0 + blk * 128:tok0 + (blk + 1) * 128, cc * 512:(cc + 1) * 512])
                    S.op("dve", [pk, f"xin{xi}"], [("x1", blk, cc)], "tensor_tensor", out=x1[:, blk, cc * 512:(cc + 1) * 512], in0=pb[:, :],
                         in1=xin[xi][:], op=ALU.add)
            S.dma("sp", [], ["gvec"], out=gvec[:], in_=g_("norm_mlp").broadcast_to([128, D]))
            for blk in range(4):
                xk = [("x1", blk, c) for c in range(4)]
                S.op("act", xk, ["hb2", ("ss1", blk)], "activation", out=hb2[:], in_=x1[:, blk, :], func=AF.Square, accum_out=ss1[:, blk:blk + 1])
                rstd_from_ss(ss1[:, blk:blk + 1], rs1[:, blk:blk + 1], 1, [("ss1", blk)], [("rs1", blk)], 1.0 / D)
                S.op("dve", xk + [("rs1", blk), "gvec"], ["hb2"], "scalar_tensor_tensor", out=hb2[:], in0=x1[:, blk, :],
                     scalar=rs1[:, blk:blk + 1], in1=gvec[:], op0=ALU.mult, op1=ALU.mult)
                to_hT2(blk)
            for uc in range(16):
                w, wk = nextw()
                S.dma("sp", [("WUP", r) for r in range(0, D, 256)], [(wk, 0), (wk, 1)], out=w[:],
                      in_=WUP[:, uc * 512:(uc + 1) * 512].rearrange("(k p) c -> p k c", p=128))
                for f in range(4):
                    pb, pk = nextbank()
                    for k in range(16):
                        S.op("pe", [(wk, 0), (wk, 1)] + hT2keys, [pk], "matmul", pb[:, :], lhsT=w[:, k, f * 128:(f + 1) * 128], rhs=hT2[:, k, :],
                             start=(k == 0), stop=(k == 15))
                    ri = cn["r"] % 2; cn["r"] += 1
                    S.op("act", [pk], [f"rl{ri}"], "activation", out=rl[ri][:], in_=pb[:, :], func=AF.Relu)
                    S.op("pool", [f"rl{ri}"], [("aT", uc * 4 + f)], "tensor_tensor", out=aT[:, uc * 4 + f, :], in0=rl[ri][:], in1=rl[ri][:], op=ALU.mult)
            for fc in range(4):
                base = 0 if fc % 2 == 0 else 4
                for kg in range(4):
                    w, wk = nextw()
                    S.dma("sp", [("WDN", r) for r in range(0, DFF, 1024)], [(wk, 0), (wk, 1)], out=w[:],
                          in_=WDN[kg * 2048:(kg + 1) * 2048, fc * 512:(fc + 1) * 512].rearrange("(k p) c -> p k c", p=128))
                    for k in range(16):
                        ffc = kg * 16 + k
                        for blk in range(4):
                            pb, pk = bank(base + blk)
                            S.op("pe", [(wk, 0), (wk, 1), ("aT", ffc)], [pk], "matmul", pb[:, :], lhsT=aT[:, ffc, blk * 128:(blk + 1) * 128],
                                 rhs=w[:, k, :], start=(ffc == 0), stop=(ffc == 63))
                for blk in range(4):
                    pb, pk = bank(base + blk)
                    S.op("dve", [pk, ("x1", blk, fc)], [("x1", blk, fc)], "tensor_tensor", out=x1[:, blk, fc * 512:(fc + 1) * 512], in0=pb[:, :],
                         in1=x1[:, blk, fc * 512:(fc + 1) * 512], op=ALU.add)
            S.dma("sp", [], ["gvec"], out=gvec[:], in_=g_("norm_ple").broadcast_to([128, D]))
            for blk in range(4):
                xk = [("x1", blk, c) for c in range(4)]
                S.op("act", xk, ["hb2", ("ss1", 4 + blk)], "activation", out=hb2[:], in_=x1[:, blk, :], func=AF.Square,
                     accum_out=ss1[:, 4 + blk:5 + blk])
                rstd_from_ss(ss1[:, 4 + blk:5 + blk], rs1[:, 4 + blk:5 + blk], 1, [("ss1", 4 + blk)], [("rs1", 4 + blk)], 1.0 / D)
                S.op("dve", xk + [("rs1", 4 + blk)], ["hb2"], "tensor_scalar", out=hb2[:], in0=x1[:, blk, :], scalar1=rs1[:, 4 + blk:5 + blk],
                     scalar2=None, op0=ALU.mult)
                to_hT2(blk)
                S.dma("sp", [], ["pt"], out=pt[:], in_=p_own[tok0 + blk * 128:tok0 + (blk + 1) * 128, :])
                S.op("dve", ["pt"], ["ptb"], "tensor_copy", out=ptb[:], in_=pt[:])
                pb, pk = bank_bf(6)
                for k in range(2):
                    S.op("pe", ["ptb", "ident"], [pk], "transpose", pb[:, k * 128:(k + 1) * 128], ptb[:, k * 128:(k + 1) * 128], ident[:])
                S.op("act", [pk], [("pT", blk)], "activation", out=pT[:, :, blk * 128:(blk + 1) * 128],
                     in_=pb[:, 0:256].rearrange("p (k t) -> p k t", k=2), func=AF.Copy)
            for blk in range(4):
                for cc in range(4):
                    pb, pk = bank(4 + cn["e"] % 2); cn["e"] += 1
                    for k in range(2):
                        S.op("pe", [("pT", blk), "wpp"], [pk], "matmul", pb[:, :], lhsT=pT[:, k, blk * 128:(blk + 1) * 128],
                             rhs=wpp[:, k, cc * 512:(cc + 1) * 512], start=(k == 0), stop=(k == 1))
                    S.op("act", [pk], ["hb2", ("ssE", blk * 4 + cc)], "activation", out=hb2[:, 0:512], in_=pb[:, :], func=AF.Square,
                         accum_out=ssE[:, blk * 4 + cc:blk * 4 + cc + 1])
            S.op("dve", [("ssE", i) for i in range(16)], ["ssE4"], "tensor_reduce", out=ssE4[:], in_=ssE[:].rearrange("p (b c) -> p b c", c=4),
                 axis=AX.X, op=ALU.add)
            rstd_from_ss(ssE4[:], rsE[:], 4, ["ssE4"], ["rsE"], 1.0 / D)
            for cc in range(4):
                w, wk = nextw()
                S.dma("sp", [("WPG", r) for r in range(0, D, 256)], [(wk, 0), (wk, 1)], out=w[:],
                      in_=WPG[:, cc * 512:(cc + 1) * 512].rearrange("(k p) c -> p k c", p=128))
                for blk in range(4):
                    pg, pgk = nextbank()
                    for k in range(16):
                        S.op("pe", [(wk, 0), (wk, 1), ("hT2", blk, 0), ("hT2", blk, 1)], [pgk], "matmul", pg[:, :],
                             lhsT=hT2[:, k, blk * 128:(blk + 1) * 128], rhs=w[:, k, :], start=(k == 0), stop=(k == 15))
                    pe2, pe2k = bank(4 + cn["e"] % 2); cn["e"] += 1
                    for k in range(2):
                        S.op("pe", [("pT", blk), "wpp"], [pe2k], "matmul", pe2[:, :], lhsT=pT[:, k, blk * 128:(blk + 1) * 128],
                             rhs=wpp[:, k, cc * 512:(cc + 1) * 512], start=(k == 0), stop=(k == 1))
                    gi = cn["g"] % 2; cn["g"] += 1
                    S.op("act", [pgk], [f"gt{gi}"], "activation", out=gt[gi][:], in_=pg[:, :], func=AF.Sigmoid)
                    S.op("dve", [pe2k, "rsE", "gvec"], [f"ev{gi}"], "scalar_tensor_tensor", out=ev[gi][:], in0=pe2[:, :], scalar=rsE[:, blk:blk + 1],
                         in1=gvec[:, cc * 512:(cc + 1) * 512], op0=ALU.mult, op1=ALU.mult)
                    S.op("pool", [f"ev{gi}", f"gt{gi}"], [f"ev{gi}"], "tensor_tensor", out=ev[gi][:], in0=ev[gi][:], in1=gt[gi][:], op=ALU.mult)
                    S.op("pool", [f"ev{gi}", ("x1", blk, cc)], [("x1", blk, cc)], "tensor_tensor", out=x1[:, blk, cc * 512:(cc + 1) * 512],
                         in0=x1[:, blk, cc * 512:(cc + 1) * 512], in1=ev[gi][:], op=ALU.add)
            for blk in range(4):
                S.dma("sp", [("x1", blk, c) for c in range(4)], [("out", tt, blk)], out=out_d[tok0 + blk * 128:tok0 + (blk + 1) * 128, :],
                      in_=x1[:, blk, :])
        S.barrier()


def _consts(hf):
    bf = ml_dtypes.bfloat16
    c = {}
    invf = np.power(np.float32(500000.0), -np.arange(0, 16, 2, dtype=np.float32) / np.float32(16)).astype(np.float32)
    c["c_invf"] = np.ascontiguousarray(np.broadcast_to(invf[None, :], (128, 8))).astype(np.float32)
    k = np.arange(128)[:, None, None]
    r = np.arange(8)[None, :, None]
    qq = np.arange(512)[None, None, :]
    t = qq // 128
    qpos = (2 * t + ((t & 1) ^ hf)) * 128 + (qq % 128)
    c["c_dmask"] = ((r * 128 + k) <= qpos).astype(bf)
    m128 = np.zeros((128, 2, 8, 128), np.float32)
    kk = np.arange(128)[:, None]
    mq = np.arange(128)[None, :]
    for jp in range(2):
        p = jp ^ hf
        q = p * 128 + mq
        for idx in range(6):
            key = (idx - 4) * 128 + kk
            dist = q - key
            m128[:, jp, idx, :] = ((dist >= 0) & (dist < 512))
        for idx in range(6, 8):
            key = (idx - 6) * 128 + kk
            m128[:, jp, idx, :] = (key <= q)
    c["c_m128"] = m128.astype(bf)
    mc = np.zeros((128, 16, 2, 128), np.float32)
    valid = np.zeros((128, 16, 64), np.float32)
    addc = np.zeros((128, 16, 64), np.float32)
    sel = np.arange(64)[None, :]
    for j in range(16):
        qp = own_block(j, hf) * 128 + np.arange(128)
        for ch in range(2):
            ng = ch * 128 + np.arange(128)
            mc[:, j, ch, :] = ((ng[:, None] <= 254) & (16 * ng[:, None] + 31 <= qp[None, :]))
        qb = (qp // 64)[:, None]
        v = sel <= qb
        f = (sel == 0) | (sel == qb) | (sel == qb - 1)
        valid[:, j, :] = v
        addc[:, j, :] = np.where(v, 1e4 * f, -1.0)
    c["c_mc"] = mc.astype(bf)
    c["c_valid"] = valid
    c["c_addc"] = addc.astype(np.float32)
    ovl = np.zeros((128, 2, 64), np.float32)
    for ch in range(2):
        ng = ch * 128 + np.arange(128)
        cs = 16 * ng[:, None]
        ssb = 64 * np.arange(64)[None, :]
        ovl[:, ch, :] = ((cs < ssb + 64) & (cs + 32 > ssb) & (ng[:, None] <= 254))
    c["c_ovl"] = ovl.astype(bf)
    jj = np.arange(64)[:, None, None]
    kb = np.arange(32)[None, :, None]
    k2 = np.arange(128)[None, None, :]
    c["c_eexp"] = (jj == 2 * kb + k2 // 64).astype(bf)
    return c


def make_in_maps(inputs):
    f = lambda a: np.ascontiguousarray(np.asarray(a))
    x = f(inputs["x"]); p = f(inputs["p"])[0]; pos = f(inputs["positions"]).astype(np.int32)
    shared = {
        "norm_mix": f(inputs["norm_mix"]).reshape(1, D),
        "w_in": f(inputs["w_in"])[0],
        "diff_q_norm": f(inputs["diff_q_norm"]).reshape(1, 64),
        "diff_k_norm": f(inputs["diff_k_norm"]).reshape(1, 64),
        "diff_lambda": f(inputs["diff_lambda"]).reshape(1, 256),
        "diff_subln": f(inputs["diff_subln"]).reshape(1, 128),
        "nsa_q_norm": f(inputs["nsa_q_norm"]).reshape(1, 64),
        "nsa_k_norm": f(inputs["nsa_k_norm"]).reshape(1, 64),
        "cmp_posT": f(np.transpose(f(inputs["cmp_pos"])[0], (2, 0, 1))),
        "cmp_w1": f(inputs["cmp_w1"])[0].reshape(4096, 256),
        "cmp_w2": f(inputs["cmp_w2"])[0].reshape(512, 64),
        "w_proj_diff": f(inputs["w_proj_diff"])[0],
        "w_proj_nsa": f(inputs["w_proj_nsa"])[0],
        "w_out": f(inputs["w_out"])[0],
        "norm_mlp": f(inputs["norm_mlp"]).reshape(1, D),
        "w_mlp_up": f(inputs["w_mlp_up"])[0],
        "w_mlp_down": f(inputs["w_mlp_down"])[0],
        "w_ple_proj": f(inputs["w_ple_proj"])[0],
        "norm_ple": f(inputs["norm_ple"]).reshape(1, D),
        "w_ple_gate": f(inputs["w_ple_gate"])[0],
    }
    cst = [_consts(0), _consts(1)]
    maps = []
    for c in range(8):
        b, hf = c // 2, c % 2
        blks = [own_block(j, hf) for j in range(16)]
        rows = np.concatenate([np.arange(bk * 128, (bk + 1) * 128) for bk in blks])
        m = dict(shared)
        m.update(cst[hf])
        m["x_all"] = x[b]
        m["x_own"] = f(x[b][rows])
        m["p_own"] = f(p[b][rows])
        m["posT_all"] = f(pos[b].reshape(32, 128).T)
        m["posT_own"] = f(pos[b][rows].reshape(16, 128).T)
        maps.append(m)
    return maps


def assemble(outs):
    res = np.zeros((4, SEQ, D), np.float32)
    for c in range(8):
        b, hf = c // 2, c % 2
        o = np.asarray(outs[c])
        for j in range(16):
            bk = own_block(j, hf)
            res[b, bk * 128:(bk + 1) * 128] = o[j * 128:(j + 1) * 128]
    return res


def kernel(**inputs):
    nc = build_nc()
    maps = make_in_maps(inputs)
    r = run_bass_kernel_spmd(nc, maps, core_ids=list(range(8)))
    return assemble([r.results[c]["out"] for c in range(8)])
```

```python
import contextlib
import math
import numpy as np
import ml_dtypes
import concourse.bass as bass
import concourse.mybir as mybir
from concourse.bass_utils import run_bass_kernel_spmd

F32, BF16, I32 = mybir.dt.float32, mybir.dt.bfloat16, mybir.dt.int32
AF = mybir.ActivationFunctionType
ALU = mybir.AluOpType
AX = mybir.AxisListType

D = 2048
SEQ = 4096
NOWN = 2048
IN_W = 9008
DFF = 8192
EPS = 1e-6
TWO_PI = 2.0 * math.pi


class Sched:
    def __init__(self, nc, es):
        self.nc = nc
        self.eng = {"pe": nc.tensor, "act": nc.scalar, "dve": nc.vector, "pool": nc.gpsimd, "sp": nc.sync}
        self.sem = {e: es.enter_context(nc.semaphore("s_" + e)) for e in ("pe", "act", "dve", "pool")}
        self.cnt = {e: 0 for e in self.sem}
        self.NDS = 12
        self.dsem = {q: [es.enter_context(nc.semaphore(f"d_{q}{i}")) for i in range(self.NDS)] for q in ("sp", "pool", "act")}
        self.dcnt = {q: [0] * self.NDS for q in self.dsem}
        self.dnext = {q: 0 for q in self.dsem}
        self.waited = {e: {} for e in self.eng}
        self.res = {}
        self.semname = {}
        self.n_wait = 0
        self.n_inst = 0

    def _wait(self, e, tok):
        if tok is None:
            return
        sem, val, owner = tok
        if e == "pe" and owner == "pe":
            return
        w = self.waited[e]
        if w.get(id(sem), 0) >= val:
            return
        self.eng[e].wait_ge(sem, val)
        self.n_wait += 1
        w[id(sem)] = val

    def _deps(self, e, reads, writes):
        for k in reads:
            r = self.res.get(k)
            if r:
                self._wait(e, r[0])
        for k in writes:
            r = self.res.get(k)
            if r:
                self._wait(e, r[0])
                for t in r[1].values():
                    self._wait(e, t)

    def _commit(self, tok, reads, writes):
        for k in reads:
            r = self.res.setdefault(k, [None, {}])
            r[1][id(tok[0])] = tok
        for k in writes:
            self.res[k] = [tok, {}]

    def op(self, e, reads, writes, meth, *args, **kw):
        ps_r = [k for k in reads if isinstance(k, tuple) and k[0] == "ps"]
        if ps_r:
            reads = [k for k in reads if k not in ps_r]
            writes = list(writes) + ps_r
        self._deps(e, reads, writes)
        self.cnt[e] += 1
        tok = (self.sem[e], self.cnt[e], e)
        getattr(self.eng[e], meth)(*args, **kw).then_inc(self.sem[e], 1)
        self.n_inst += 1
        self._commit(tok, reads, writes)
        return tok

    def dma(self, q, reads, writes, out, in_, **kw):
        i = self.dnext[q]
        self.dnext[q] = (i + 1) % self.NDS
        sem = self.dsem[q][i]
        if self.dcnt[q][i] > 0:
            self._wait(q, (sem, 16 * self.dcnt[q][i], "dma"))
        self._deps(q, reads, writes)
        self.dcnt[q][i] += 1
        tok = (sem, 16 * self.dcnt[q][i], "dma")
        self.eng[q].dma_start(out=out, in_=in_, **kw).then_inc(sem, 16)
        self.n_inst += 1
        self._commit(tok, reads, writes)
        return tok

    def barrier(self):
        toks = [(self.sem[e], self.cnt[e], "bar") for e in self.sem if self.cnt[e] > 0]
        for q in self.dsem:
            for i in range(self.NDS):
                if self.dcnt[q][i] > 0:
                    toks.append((self.dsem[q][i], 16 * self.dcnt[q][i], "dma"))
        for e in self.eng:
            for t in toks:
                self._wait(e, t)
        self.res = {}


PD_DEBUG = {"topk": True, "sel": True, "win": True, "skip_pc": False}


def own_block(j, hf):
    return 2 * j + ((j & 1) ^ hf)


def build_nc(dbg=False, stop_after=None):
    nc = bass.Bass("TRN2", target_bir_lowering=False)

    def din(name, shape, dt=F32):
        return nc.dram_tensor(name, list(shape), dt, kind="ExternalInput").ap()

    def dscr(name, shape, dt=BF16):
        if dbg:
            return nc.dram_tensor(name, list(shape), dt, kind="ExternalOutput").ap()
        return nc.dram_tensor(name, list(shape), dt).ap()

    x_all = din("x_all", [SEQ, D])
    x_own = din("x_own", [NOWN, D])
    p_own = din("p_own", [NOWN, 256])
    posT_all = din("posT_all", [128, 32], I32)
    posT_own = din("posT_own", [128, 16], I32)
    norm_mix = din("norm_mix", [1, D])
    w_in = din("w_in", [D, IN_W])
    diff_q_norm = din("diff_q_norm", [1, 64])
    diff_k_norm = din("diff_k_norm", [1, 64])
    diff_lambda = din("diff_lambda", [1, 256])
    diff_subln = din("diff_subln", [1, 128])
    nsa_q_norm = din("nsa_q_norm", [1, 64])
    nsa_k_norm = din("nsa_k_norm", [1, 64])
    cmp_posT = din("cmp_posT", [64, 2, 32])
    cmp_w1 = din("cmp_w1", [2 * 2048, 256])
    cmp_w2 = din("cmp_w2", [2 * 256, 64])
    w_proj_diff = din("w_proj_diff", [1024, D])
    w_proj_nsa = din("w_proj_nsa", [1024, D])
    w_out = din("w_out", [D, D])
    norm_mlp = din("norm_mlp", [1, D])
    w_mlp_up = din("w_mlp_up", [D, DFF])
    w_mlp_down = din("w_mlp_down", [DFF, D])
    w_ple_proj = din("w_ple_proj", [256, D])
    norm_ple = din("norm_ple", [1, D])
    w_ple_gate = din("w_ple_gate", [D, D])
    c_invf = din("c_invf", [128, 8])
    c_dmask = din("c_dmask", [128, 8, 512], BF16)
    c_m128 = din("c_m128", [128, 2, 8, 128], BF16)
    c_mc = din("c_mc", [128, 16, 2, 128], BF16)
    c_valid = din("c_valid", [128, 16, 64])
    c_addc = din("c_addc", [128, 16, 64])
    c_ovl = din("c_ovl", [128, 2, 64], BF16)
    c_eexp = din("c_eexp", [64, 32, 128], BF16)

    out_d = nc.dram_tensor("out", [NOWN, D], F32, kind="ExternalOutput").ap()

    W_IN = dscr("W_IN", [D, IN_W]) if not dbg else nc.dram_tensor("W_IN", [D, IN_W], BF16).ap()
    mk = lambda n, s: nc.dram_tensor(n, list(s), BF16).ap()
    CW1 = mk("CW1", [2 * 2048, 256]); CW2 = mk("CW2", [2 * 256, 64])
    WPD = mk("WPD", [1024, D]); WPN = mk("WPN", [1024, D]); WOUT = mk("WOUT", [D, D])
    WUP = mk("WUP", [D, DFF]); WDN = mk("WDN", [DFF, D]); WPP = mk("WPP", [256, D]); WPG = mk("WPG", [D, D])
    QDT = dscr("QDT", [8, 128, NOWN])
    KDT = dscr("KDT", [8, 128, SEQ])
    VD = dscr("VD", [SEQ, 1024])
    NQRT = dscr("NQRT", [8, 128, NOWN])
    NQPT = dscr("NQPT", [8, 128, NOWN])
    KCRT = dscr("KCRT", [128, SEQ]); VCRT = dscr("VCRT", [128, SEQ])
    KST = dscr("KST", [2, 128, SEQ]); KWT = dscr("KWT", [2, 128, SEQ])
    VS = dscr("VS", [SEQ, 128]); VW = dscr("VW", [SEQ, 128])
    SGT = dscr("SGT", [2, D, NOWN])
    YAT = dscr("YAT", [8, 128, NOWN])
    YBT = dscr("YBT", [8, 128, NOWN])

    es = contextlib.ExitStack()
    with es:
        S = Sched(nc, es)

        def sbt(stack, name, shape, dt):
            return stack.enter_context(nc.sbuf_tensor(name, list(shape), dt))

        PS = [es.enter_context(nc.psum_tensor(f"ps{i}", [128, 1024], F32)) for i in range(4)]

        def bank(i):
            return PS[i // 2][:, (i % 2) * 512:(i % 2) * 512 + 512], ("ps", i)

        def bank_bf(i):
            return PS[i // 2][:, (i % 2) * 512:(i % 2) * 512 + 512].bitcast(BF16), ("ps", i)

        ident = sbt(es, "ident", [128, 128], BF16)
        idf = sbt(es, "idf", [128, 128], F32)
        mhalf = sbt(es, "mhalf", [128, 64], F32)
        gates = sbt(es, "gates", [128, 16, 48], F32)
        CSA = sbt(es, "CSA", [128, 32, 32], F32)
        CSO = sbt(es, "CSO", [128, 16, 32], F32)
        S.op("pool", [], ["idf"], "iota", idf[:], pattern=[[1, 128]], base=0, channel_multiplier=-1,
             allow_small_or_imprecise_dtypes=True)
        S.op("dve", ["idf"], ["ident"], "tensor_single_scalar", out=ident[:], in_=idf[:], scalar=0.0, op=ALU.is_equal)
        S.op("pool", [], ["mhalf"], "memset", mhalf[:], -0.5)

        def rstd_from_ss(ss_ap, out_ap, n, keys_r, keys_w, inv_n):
            S.op("pool", keys_r, keys_w, "tensor_scalar", out=out_ap, in0=ss_ap, scalar1=inv_n, scalar2=EPS,
                 op0=ALU.mult, op1=ALU.add)
            P = out_ap.shape[0]
            S.op("pool", keys_w + ["mhalf"], keys_w, "tensor_tensor", out=out_ap, in0=out_ap, in1=mhalf[0:P, 0:n], op=ALU.pow)

        def conv(dst, src, R, key, step=256):
            for r0 in range(0, R, step):
                r1 = min(R, r0 + step)
                S.dma("pool", [], [(key, r0)], out=dst[r0:r1, :], in_=src[r0:r1, :], max_dma_last_dim=8192)

        win_chunks = list(((0, 512), (512, 512), (3072, 512), (3584, 512), (4864, 48)) + tuple((4912 + i * 512, 512) for i in range(8)) +
                          ((1024, 512), (1536, 512), (2048, 512), (2560, 512), (4096, 512), (4608, 256)))

        def win_pop(k=1):
            for _ in range(k):
                if win_chunks:
                    c0, n = win_chunks.pop(0)
                    S.dma("pool", [], [("W_IN", c0)], out=W_IN[:, c0:c0 + n], in_=w_in[:, c0:c0 + n], max_dma_last_dim=8192)

        win_pop(3)
        pending_conv = []

        def conv_later(dst, src, R, key, step=256):
            for r0 in range(0, R, step):
                r1 = min(R, r0 + step)
                pending_conv.append((dst, src, r0, r1, key))

        def conv_pop(n=1):
            for _ in range(n):
                if pending_conv:
                    dst, src, r0, r1, key = pending_conv.pop(0)
                    S.dma("pool", [], [(key, r0)], out=dst[r0:r1, :], in_=src[r0:r1, :], max_dma_last_dim=8192)

        conv_later(CW1, cmp_w1, 4096, "CW1", 1024); conv_later(CW2, cmp_w2, 512, "CW2", 512)
        conv_later(WPD, w_proj_diff, 1024, "WPD"); conv_later(WPN, w_proj_nsa, 1024, "WPN"); conv_later(WOUT, w_out, D, "WOUT")
        conv_later(WUP, w_mlp_up, D, "WUP"); conv_later(WDN, w_mlp_down, DFF, "WDN", 1024)
        conv_later(WPP, w_ple_proj, 256, "WPP"); conv_later(WPG, w_ple_gate, D, "WPG")

        with contextlib.ExitStack() as ps_:
            invf = sbt(ps_, "invf", [128, 8], F32)
            S.dma("sp", [], ["invf"], out=invf[:], in_=c_invf[:, :])
            for nm, src, nb, CS in (("a", posT_all, 32, CSA), ("o", posT_own, 16, CSO)):
                pi = sbt(ps_, "pi" + nm, [128, nb], I32)
                pf = sbt(ps_, "pf" + nm, [128, nb], F32)
                ang = sbt(ps_, "ang" + nm, [128, nb, 8], F32)
                kf = sbt(ps_, "kf" + nm, [128, nb, 8], F32)
                ki = sbt(ps_, "ki" + nm, [128, nb, 8], I32)
                r0 = sbt(ps_, "r0" + nm, [128, nb, 8], F32)
                r1 = sbt(ps_, "r1" + nm, [128, nb, 8], F32)
                mm = sbt(ps_, "mm" + nm, [128, nb, 8], F32)
                S.dma("sp", [], ["pi" + nm], out=pi[:], in_=src[:, :])
                S.op("dve", ["pi" + nm], ["pf" + nm], "tensor_copy", out=pf[:], in_=pi[:])
                S.op("dve", ["pf" + nm, "invf"], ["ang" + nm], "tensor_tensor", out=ang[:],
                     in0=pf[:].unsqueeze(2).broadcast_to([128, nb, 8]),
                     in1=invf[:].unsqueeze(1).broadcast_to([128, nb, 8]), op=ALU.mult)
                S.op("dve", ["ang" + nm], ["kf" + nm], "tensor_scalar", out=kf[:], in0=ang[:], scalar1=1.0 / TWO_PI,
                     scalar2=None, op0=ALU.mult)
                S.op("dve", ["kf" + nm], ["ki" + nm], "tensor_copy", out=ki[:], in_=kf[:])
                S.op("dve", ["ki" + nm], ["kf" + nm], "tensor_copy", out=kf[:], in_=ki[:])
                S.op("dve", ["kf" + nm, "ang" + nm], ["r0" + nm], "scalar_tensor_tensor", out=r0[:], in0=kf[:],
                     scalar=-TWO_PI, in1=ang[:], op0=ALU.mult, op1=ALU.add)
                for which, shift in ((1, 0.0), (0, math.pi / 2)):
                    S.op("dve", ["r0" + nm], ["r1" + nm], "tensor_scalar", out=r1[:], in0=r0[:], scalar1=shift,
                         scalar2=None, op0=ALU.add)
                    for thr, op_, add in ((math.pi, ALU.is_gt, -TWO_PI), (-math.pi, ALU.is_lt, TWO_PI)):
                        S.op("dve", ["r1" + nm], ["mm" + nm], "tensor_scalar", out=mm[:], in0=r1[:], scalar1=thr,
                             scalar2=add, op0=op_, op1=ALU.mult)
                        S.op("dve", ["r1" + nm, "mm" + nm], ["r1" + nm], "tensor_tensor", out=r1[:], in0=r1[:],
                             in1=mm[:], op=ALU.add)
                    S.op("dve", ["r1" + nm], ["r1" + nm], "tensor_scalar", out=r1[:], in0=r1[:], scalar1=math.pi,
                         scalar2=-math.pi, op0=ALU.min, op1=ALU.max)
                    S.op("act", ["r1" + nm], ["CS" + nm], "activation", out=CS[:, :, which * 8:which * 8 + 8],
                         in_=r1[:], func=AF.Sin)
                S.op("dve", ["CS" + nm], ["CS" + nm], "tensor_copy", out=CS[:, :, 16:24], in_=CS[:, :, 8:16])
                S.op("dve", ["CS" + nm], ["CS" + nm], "tensor_copy", out=CS[:, :, 24:32], in_=CS[:, :, 0:8])
            S.barrier()

        with contextlib.ExitStack() as pa:
            gmix = sbt(pa, "gmix", [128, D], F32)
            S.dma("sp", [], ["gmix"], out=gmix[:], in_=norm_mix.broadcast_to([128, D]))
            gq = {}
            for nm, src in (("dq", diff_q_norm), ("dk", diff_k_norm), ("nq", nsa_q_norm), ("nk", nsa_k_norm)):
                gq[nm] = sbt(pa, "g_" + nm, [128, 64], F32)
                S.dma("sp", [], ["g_" + nm], out=gq[nm][:], in_=src.broadcast_to([128, 64]))
            xt = [sbt(pa, f"xt{i}", [128, D], F32) for i in range(2)]
            hb = [sbt(pa, f"hb{i}", [128, D], BF16) for i in range(2)]
            hT = sbt(pa, "hT", [128, 16, 2048], BF16)
            wc = [sbt(pa, f"wc{i}", [128, 16, 512], BF16) for i in range(2)]
            ss = sbt(pa, "ss", [128, 16], F32)
            rs = sbt(pa, "rs", [128, 16], F32)
            NZ = 3
            sqt = [sbt(pa, f"sqt{i}", [128, 512], BF16) for i in range(NZ)]
            ssh = [sbt(pa, f"ssh{i}", [128, 8], F32) for i in range(NZ)]
            rsh = [sbt(pa, f"rsh{i}", [128, 8], F32) for i in range(NZ)]
            zn = [sbt(pa, f"zn{i}", [128, 512], F32) for i in range(NZ)]
            zr = [sbt(pa, f"zr{i}", [128, 512], BF16) for i in range(NZ)]
            zp = [sbt(pa, f"zp{i}", [128, 512], BF16) for i in range(NZ)]
            rt = [sbt(pa, f"rt{i}", [128, 2, 8, 16], F32) for i in range(NZ)]
            rot = [sbt(pa, f"rot{i}", [128, 8, 16], F32) for i in range(NZ)]
            stg = [sbt(pa, f"stg{i}", [128, 4, 2048], BF16) for i in range(2)]
            stgH = sbt(pa, "stgH", [128, 2, 2048], BF16)
            NV = 6
            vst = [sbt(pa, f"vst{i}", [128, 512], BF16) for i in range(NV)]
            dupb = [sbt(pa, f"dupb{i}", [128, 256], BF16) for i in range(NV)]
            dfr = []
            LAB = 2

            def defer(fn):
                dfr.append(fn)

            def run_deferred(keep):
                while len(dfr) > keep:
                    dfr.pop(0)()
            cnt = {"w": 0, "z": 0, "v": 0, "ps": 0, "tp": 0, "d": 0, "x": 0}

            hq = []
            cur = {"pass": 0}

            def build_hT_blk(xsrc, blk0, i):
                j = cnt["x"] % 2; cnt["x"] += 1
                xb = xt[j]; xk = f"xt{j}"
                S.dma("sp", [], [xk], out=xb[:], in_=xsrc[(blk0 + i) * 128:(blk0 + i + 1) * 128, :])
                hk = f"hb{j}"
                S.op("act", [xk], [hk, ("ss", i)], "activation", out=hb[j][:], in_=xb[:], func=AF.Square,
                     accum_out=ss[:, i:i + 1])
                rstd_from_ss(ss[:, i:i + 1], rs[:, i:i + 1], 1, [("ss", i)], [("rs", i)], 1.0 / D)
                S.op("dve", [xk, ("rs", i), "gmix"], [hk], "scalar_tensor_tensor", out=hb[j][:], in0=xb[:],
                     scalar=rs[:, i:i + 1], in1=gmix[:], op0=ALU.mult, op1=ALU.mult)
                for half in range(2):
                    bi = 6 + half
                    pb, pk = bank_bf(bi)
                    for k in range(8):
                        kk = half * 8 + k
                        S.op("pe", [hk, "ident"], [pk], "transpose", pb[:, k * 128:(k + 1) * 128],
                             hb[j][:, kk * 128:(kk + 1) * 128], ident[:])
                    dst = hT[:, half * 8:half * 8 + 8, i * 128:(i + 1) * 128]
                    srcv = pb.rearrange("p (k t) -> p k t", k=8)
                    if half == 0:
                        S.op("act", [pk], [("hT", i, 0)], "activation", out=dst, in_=srcv, func=AF.Copy)
                    else:
                        S.op("dve", [pk], [("hT", i, 1)], "tensor_copy", out=dst, in_=srcv)

            def queue_hT(xsrc, blk0, pass_id):
                hq.extend([(pass_id, xsrc, blk0, i) for i in range(16)])

            def pop_hT(n=1, force=False):
                for _ in range(n):
                    if hq and (force or hq[0][0] == cur["pass"]):
                        _, xsrc, blk0, i = hq.pop(0)
                        build_hT_blk(xsrc, blk0, i)

            def load_w(c0, n):
                win_pop(1)
                i = cnt["w"] % 2; cnt["w"] += 1
                S.dma("sp", [("W_IN", c0)], [f"wc{i}"], out=wc[i][:, :, 0:n],
                      in_=W_IN[:, c0:c0 + n].rearrange("(k p) c -> p k c", p=128))
                return wc[i], f"wc{i}"

            def mm_tm(w, wk, blk, n, coff=0):
                pop_hT(1)
                bi = cnt["ps"] % 4; cnt["ps"] += 1
                pb, pk = bank(bi)
                if cnt.get("conv") and cnt["conv"] <= 5:
                    conv_pop(1); cnt["conv"] += 1
                for k in range(16):
                    S.op("pe", [wk, ("hT", blk, 0), ("hT", blk, 1)], [pk], "matmul", pb[:, 0:n], lhsT=hT[:, k, blk * 128:(blk + 1) * 128],
                         rhs=w[:, k, coff:coff + n], start=(k == 0), stop=(k == 15))
                return pb, pk

            def normrope(pb, pk, c0, nh, gname, cs, csk, blk, want_plain=False):
                i = cnt["z"] % NZ; cnt["z"] += 1
                n = nh * 64
                pv = pb[:, c0:c0 + n]
                pv3 = pv.rearrange("p (h d) -> p h d", d=64)
                S.op("act", [pk], [f"sqt{i}"], "activation", out=sqt[i][:, 0:n], in_=pv, func=AF.Square)
                S.op("dve", [f"sqt{i}"], [f"ssh{i}"], "tensor_reduce", out=ssh[i][:, 0:nh],
                     in_=sqt[i][:, 0:n].rearrange("p (h d) -> p h d", d=64), axis=AX.X, op=ALU.add)
                rstd_from_ss(ssh[i][:, 0:nh], rsh[i][:, 0:nh], nh, [f"ssh{i}"], [f"rsh{i}"], 1.0 / 64)
                z3 = zn[i][:, 0:n].rearrange("p (h d) -> p h d", d=64)
                S.op("dve", [pk, "g_" + gname], [f"zn{i}"], "tensor_tensor", out=z3, in0=pv3,
                     in1=gq[gname][:].unsqueeze(1).broadcast_to([128, nh, 64]), op=ALU.mult)
                x12 = z3[:, :, 0:16]
                r = rt[i]; rk = f"rt{i}"
                ro = rot[i][:, 0:nh, :]
                S.op("pool", [f"zn{i}", csk], [(rk, 0)], "tensor_tensor", out=r[:, 0, 0:nh, :], in0=x12,
                     in1=cs[:, blk, 0:16].unsqueeze(1).broadcast_to([128, nh, 16]), op=ALU.mult)
                S.op("pool", [f"zn{i}", csk], [(rk, 1)], "tensor_tensor", out=r[:, 1, 0:nh, :], in0=x12,
                     in1=cs[:, blk, 16:32].unsqueeze(1).broadcast_to([128, nh, 16]), op=ALU.mult)
                S.op("dve", [(rk, 0)], [f"rot{i}"], "tensor_tensor", out=ro[:, :, 0:8],
                     in0=r[:, 0, 0:nh, 0:8], in1=r[:, 0, 0:nh, 8:16], op=ALU.subtract)
                S.op("dve", [(rk, 1), f"rot{i}"], [f"rot{i}"], "tensor_tensor", out=ro[:, :, 8:16],
                     in0=r[:, 1, 0:nh, 8:16], in1=r[:, 1, 0:nh, 0:8], op=ALU.add)
                rb = rsh[i][:, 0:nh].unsqueeze(2)
                zr3 = zr[i][:, 0:n].rearrange("p (h d) -> p h d", d=64)
                S.op("dve", [f"zn{i}", f"rsh{i}"], [f"zr{i}"], "tensor_tensor", out=zr3[:, :, 16:64], in0=z3[:, :, 16:64],
                     in1=rb.broadcast_to([128, nh, 48]), op=ALU.mult)
                S.op("dve", [f"rot{i}", f"rsh{i}", f"zr{i}"], [f"zr{i}"], "tensor_tensor", out=zr3[:, :, 0:16], in0=ro,
                     in1=rb.broadcast_to([128, nh, 16]), op=ALU.mult)
                if want_plain:
                    zp3 = zp[i][:, 0:n].rearrange("p (h d) -> p h d", d=64)
                    S.op("dve", [f"zn{i}", f"rsh{i}"], [f"zp{i}"], "tensor_tensor", out=zp3, in0=z3,
                         in1=rb.broadcast_to([128, nh, 64]), op=ALU.mult)
                return i

            def transp_to(src_ap, src_key, ncol128, dst_ap, dst_key):
                bi = 4 + cnt["tp"] % 2; cnt["tp"] += 1
                pb, pk = bank_bf(bi)
                for c in range(ncol128):
                    S.op("pe", [src_key, "ident"], [pk], "transpose", pb[:, c * 128:(c + 1) * 128],
                         src_ap[:, c * 128:(c + 1) * 128], ident[:])
                S.op("act", [pk], [dst_key], "activation", out=dst_ap,
                     in_=pb[:, 0:ncol128 * 128].rearrange("p (c t) -> p c t", c=ncol128), func=AF.Copy)

            sidx = {"i": 0}

            def new_stage():
                i = sidx["i"] % 2; sidx["i"] += 1
                return stg[i], f"stg{i}"

            queue_hT(x_own, 0, 0)
            pop_hT(3)
            for (c0, gname, dstR, dstP) in ((0, "dq", QDT, None), (512, "dq", QDT, None),
                                            (3072, "nq", NQRT, NQPT), (3584, "nq", NQRT, NQPT)):
                w, wk = load_w(c0, 512)
                sR, sRk = new_stage()
                if dstP is not None:
                    sP, sPk = new_stage()
                for blk in range(16):
                    pb, pk = mm_tm(w, wk, blk, 512)
                    i = normrope(pb, pk, 0, 8, gname, CSO, "CSo", blk, want_plain=dstP is not None)
                    defer(lambda i=i, blk=blk, sR=sR, sRk=sRk: transp_to(zr[i], f"zr{i}", 4, sR[:, :, blk * 128:(blk + 1) * 128], (sRk, blk)))
                    if dstP is not None:
                        defer(lambda i=i, blk=blk, sP=sP, sPk=sPk: transp_to(zp[i], f"zp{i}", 4, sP[:, :, blk * 128:(blk + 1) * 128], (sPk, blk)))
                    run_deferred(LAB * (2 if dstP is not None else 1))
                run_deferred(0)
                h0 = (c0 % 1024) // 128
                S.dma("pool", [(sRk, b) for b in range(16)], [("QN", c0, 0)], out=dstR[h0:h0 + 4].rearrange("h p t -> p h t"),
                      in_=sR[:])
                if dstP is not None:
                    S.dma("pool", [(sPk, b) for b in range(16)], [("QN", c0, 1)],
                          out=dstP[h0:h0 + 4].rearrange("h p t -> p h t"), in_=sP[:])
            w, wk = load_w(4864, 48)
            for blk in range(16):
                pb, pk = mm_tm(w, wk, blk, 48)
                S.op("act", [pk], [("gates", blk)], "activation", out=gates[:, blk, :], in_=pb[:, 0:48], func=AF.Sigmoid)
            for gi in range(2):
                for cc_ in range(4):
                    c0 = 4912 + gi * 2048 + cc_ * 512
                    w, wk = load_w(c0, 512)
                    last_fm = (gi == 1 and cc_ == 3)
                    if last_fm:
                        queue_hT(x_all, 0, 1)
                    for f in range(4):
                        for tg in range(4):
                            bi = cnt["ps"] % 4; cnt["ps"] += 1
                            pb, pk = bank(bi)
                            for k in range(16):
                                S.op("pe", [wk] + [("hT", tg * 4 + b, hh) for b in range(4) for hh in range(2)], [pk], "matmul", pb[:, :],
                                     lhsT=w[:, k, f * 128:(f + 1) * 128], rhs=hT[:, k, tg * 512:(tg + 1) * 512],
                                     start=(k == 0), stop=(k == 15))
                            vi = cnt["v"] % NV; cnt["v"] += 1
                            S.op("act", [pk], [f"vst{vi}"], "activation", out=vst[vi][:], in_=pb[:, :], func=AF.Sigmoid)
                            fr = cc_ * 512 + f * 128
                            S.dma("pool", [f"vst{vi}"], [("SGT", gi, fr, tg)], out=SGT[gi, fr:fr + 128, tg * 512:(tg + 1) * 512],
                                  in_=vst[vi][:])
                            if last_fm and f == 3:
                                pop_hT(4, force=True)
            for hp in range(2):
                t0 = hp * 2048
                cnt["conv"] = 1
                cur["pass"] = 1 + hp
                pop_hT(16, force=True)
                for c0 in (1024, 1536):
                    w, wk = load_w(c0, 512)
                    sR, sRk = new_stage()
                    for blk in range(16):
                        pb, pk = mm_tm(w, wk, blk, 512)
                        i = normrope(pb, pk, 0, 8, "dk", CSA, "CSa", hp * 16 + blk)
                        defer(lambda i=i, blk=blk, sR=sR, sRk=sRk: transp_to(zr[i], f"zr{i}", 4, sR[:, :, blk * 128:(blk + 1) * 128], (sRk, blk)))
                        run_deferred(LAB)
                    run_deferred(0)
                    h0 = (c0 - 1024) // 128
                    S.dma("pool", [(sRk, b) for b in range(16)], [("KDT", h0, hp)],
                          out=KDT[h0:h0 + 4, :, t0:t0 + 2048].rearrange("h p t -> p h t"), in_=sR[:])
                for c0 in (2048, 2560):
                    w, wk = load_w(c0, 512)
                    for blk in range(16):
                        pb, pk = mm_tm(w, wk, blk, 512)
                        vi = cnt["v"] % NV; cnt["v"] += 1
                        S.op("act", [pk], [f"vst{vi}"], "activation", out=vst[vi][:], in_=pb[:, :], func=AF.Copy)
                        S.dma("pool", [f"vst{vi}"], [("VD", c0, hp, blk)],
                              out=VD[t0 + blk * 128:t0 + (blk + 1) * 128, c0 - 2048:c0 - 2048 + 512], in_=vst[vi][:])
                w, wk = load_w(4096, 512)
                w2_, wk2 = load_w(4608, 256)
                if hp == 0:
                    queue_hT(x_all, 16, 2)
                sE, sEk = new_stage()
                sF, sFk = stgH, "stgH"
                for blk in range(16):
                    gb = hp * 16 + blk
                    pb, pk = mm_tm(w, wk, blk, 512)
                    vi = cnt["v"] % NV; cnt["v"] += 1
                    S.op("act", [pk], [f"vst{vi}"], "activation", out=vst[vi][:, 0:256], in_=pb[:, 0:256], func=AF.Copy)
                    S.op("act", [pk], [f"vst{vi}"], "activation", out=vst[vi][:, 256:384], in_=pb[:, 384:512], func=AF.Copy)
                    S.dma("pool", [f"vst{vi}"], [("VS", gb)], out=VS[gb * 128:(gb + 1) * 128, :], in_=vst[vi][:, 256:384])
                    defer(lambda vi=vi, blk=blk: transp_to(vst[vi], f"vst{vi}", 2, sE[:, 0:2, blk * 128:(blk + 1) * 128], (sEk, blk)))
                    i = normrope(pb, pk, 256, 2, "nk", CSA, "CSa", gb)
                    zi = cnt["d"] % NV; cnt["d"] += 1
                    dup = dupb[zi]; dk_ = f"dupb{zi}"
                    S.op("dve", [f"zr{i}"], [dk_], "tensor_copy", out=dup[:, 0:256].rearrange("p (g r d) -> p g r d", g=2, r=2),
                         in_=zr[i][:, 0:128].rearrange("p (g d) -> p g d", g=2).unsqueeze(2).broadcast_to([128, 2, 2, 64]))
                    defer(lambda dup=dup, dk_=dk_, blk=blk: transp_to(dup, dk_, 2, sE[:, 2:4, blk * 128:(blk + 1) * 128], (sEk, blk)))
                    pb2, pk2 = mm_tm(w2_, wk2, blk, 256)
                    vi = cnt["v"] % NV; cnt["v"] += 1
                    S.op("act", [pk2], [f"vst{vi}"], "activation", out=vst[vi][:, 0:128], in_=pb2[:, 128:256], func=AF.Copy)
                    S.dma("pool", [f"vst{vi}"], [("VW", gb)], out=VW[gb * 128:(gb + 1) * 128, :], in_=vst[vi][:, 0:128])
                    i = normrope(pb2, pk2, 0, 2, "nk", CSA, "CSa", gb)
                    zi = cnt["d"] % NV; cnt["d"] += 1
                    dup = dupb[zi]; dk_ = f"dupb{zi}"
                    S.op("dve", [f"zr{i}"], [dk_], "tensor_copy", out=dup[:, 0:256].rearrange("p (g r d) -> p g r d", g=2, r=2),
                         in_=zr[i][:, 0:128].rearrange("p (g d) -> p g d", g=2).unsqueeze(2).broadcast_to([128, 2, 2, 64]))
                    defer(lambda dup=dup, dk_=dk_, blk=blk: transp_to(dup, dk_, 2, sF[:, 0:2, blk * 128:(blk + 1) * 128], (sFk, blk)))
                    run_deferred(3)
                    pop_hT(1, force=True)
                run_deferred(0)
                rE = [(sEk, b) for b in range(16)]
                S.dma("pool", rE, [("KCRT", hp)], out=KCRT[:, t0:t0 + 2048], in_=sE[:, 0, :])
                S.dma("pool", rE, [("VCRT", hp)], out=VCRT[:, t0:t0 + 2048], in_=sE[:, 1, :])
                S.dma("pool", rE, [("KST", hp)], out=KST[:, :, t0:t0 + 2048].rearrange("g p t -> p g t"), in_=sE[:, 2:4, :])
                S.dma("pool", [(sFk, b) for b in range(16)], [("KWT", hp)],
                      out=KWT[:, :, t0:t0 + 2048].rearrange("g p t -> p g t"), in_=sF[:, 0:2, :])
            S.barrier()

        if stop_after == "PA":
            _finish(nc, S, out_d, es)
            return nc

        build_rest(nc, S, es, locals(), dbg=dbg, stop_after=stop_after)
    return nc


def _finish(nc, S, out_d, es):
    z = es.enter_context(nc.sbuf_tensor("zfin", [128, D], F32))
    S.op("dve", [], ["zfin"], "memset", z[:], 0.0)
    for i in range(16):
        S.dma("sp", ["zfin"], [("out", i)], out=out_d[i * 128:(i + 1) * 128, :], in_=z[:])
    S.barrier()


def build_rest(nc, S, es, L, dbg=False, stop_after=None):
    g_ = lambda n: L[n]
    sbt, bank, bank_bf, ident, gates, rstd_from_ss = (g_("sbt"), g_("bank"), g_("bank_bf"), g_("ident"), g_("gates"),
                                                      g_("rstd_from_ss"))
    out_d = g_("out_d")

    def finish():
        _finish(nc, S, out_d, es)

    KCT = sbt(es, "KCT", [128, 2, 2, 256], BF16)
    VCO = sbt(es, "VCO", [128, 2, 2, 129], BF16)
    S.op("dve", [], ["KCT"], "memset", KCT[:], 0.0)
    S.op("dve", [], ["VCO"], "memset", VCO[:], 0.0)

    CW1, CW2, KCRT, VCRT = g_("CW1"), g_("CW2"), g_("KCRT"), g_("VCRT")
    with contextlib.ExitStack() as pb_:
        kvT = [sbt(pb_, f"kvT{i}", [128, SEQ], BF16) for i in range(2)]
        S.dma("sp", [("KCRT", 0), ("KCRT", 1)], ["kvT0"], out=kvT[0][:], in_=KCRT[:, :])
        S.dma("sp", [("VCRT", 0), ("VCRT", 1)], ["kvT1"], out=kvT[1][:], in_=VCRT[:, :])
        W1 = [[sbt(pb_, f"W1_{kv}{g}", [128, 32, 256], BF16) for g in range(2)] for kv in range(2)]
        W2 = [sbt(pb_, f"W2_{kv}", [128, 2, 64], BF16) for kv in range(2)]
        for kv in range(2):
            for half in range(2):
                S.op("dve" if half == 0 else "pool", [], [(f"W1_{kv}", half, "z")], "memset", W1[kv][half][(1 - half) * 64:(2 - half) * 64, :, :], 0.0)
                for l4 in range(4):
                    S.dma("sp", [("CW1", r) for r in range(0, 4096, 1024)], [(f"W1_{kv}", half)], out=W1[kv][half][half * 64:(half + 1) * 64, l4 * 8:(l4 + 1) * 8, :],
                          in_=CW1[kv * 2048 + l4 * 512:kv * 2048 + (l4 + 1) * 512, :].rearrange("(l d) h -> d l h", d=64))
            S.dma("sp", [("CW2", 0)], [f"W2_{kv}"], out=W2[kv][:], in_=CW2[kv * 256:(kv + 1) * 256, :].rearrange("(c p) d -> p c d", p=128))
        posf = sbt(pb_, "posf", [128, 2, 32], F32)
        posb = sbt(pb_, "posb", [128, 2, 32], BF16)
        for half in range(2):
            S.dma("sp", [], ["posf"], out=posf[half * 64:(half + 1) * 64], in_=g_("cmp_posT")[:, :, :])
        S.op("dve", ["posf"], ["posb"], "tensor_copy", out=posb[:], in_=posf[:])
        gk = sbt(pb_, "gk", [128, 64], F32)
        S.dma("sp", [], ["gk"], out=gk[:], in_=g_("nsa_k_norm").broadcast_to([128, 64]))
        biasT = sbt(pb_, "biasT", [128, 4], F32)
        GH = sbt(pb_, "GH", [128, 8, 256], BF16)
        S.op("dve", [], [("GH", a, b, c) for a in range(2) for b in range(2) for c in range(2)], "memset", GH[:], 0.0)
        u = [sbt(pb_, f"u{i}", [128, 256], F32) for i in range(2)]
        u2 = [sbt(pb_, f"u2{i}", [128, 256], F32) for i in range(2)]
        sg = [sbt(pb_, f"sg{i}", [128, 256], F32) for i in range(2)]
        ssk = sbt(pb_, "ssk", [128, 4], F32)
        rsk = sbt(pb_, "rsk", [128, 4], F32)
        kcn2 = [sbt(pb_, f"kcn2{i}", [128, 256], BF16) for i in range(2)]
        for i in range(2):
            S.op("dve", [], [f"kcn2{i}"], "memset", kcn2[i][:], 0.0)
        junkb = sbt(pb_, "junkb", [128, 64], F32)
        S.op("dve", ["VCO"], ["VCO"], "memset", VCO[:, :, :, 64:65], 1.0)
        for g in range(2):
            S.dma("sp", ["VCO"], ["VCO"], out=VCO[:, g, :, 65:129], in_=g_("c_ovl")[:, :, :])
        pbb, pbk = bank(4)
        for kv in range(2):
            for hc in range(2):
                col = kv * 2 + hc
                for l in range(32):
                    S.op("pe", [(f"W1_{kv}", 0), (f"W1_{kv}", 0, "z"), "posb"], [pbk], "matmul", pbb[:, col:col + 1],
                         lhsT=W1[kv][0][:, l, hc * 128:(hc + 1) * 128], rhs=posb[:, kv, l:l + 1],
                         start=(l == 0), stop=(l == 31), skip_group_check=True)
        S.op("act", [pbk], ["biasT"], "activation", out=biasT[:], in_=pbb[:, 0:4], func=AF.Copy)
        n = 0
        for kv in range(2):
            for g in range(2):
                for hc in range(2):
                    pb, pk = bank(n % 4)
                    for l in range(32):
                        S.op("pe", [(f"W1_{kv}", g), (f"W1_{kv}", g, "z"), f"kvT{kv}"], [pk], "matmul", pb[:, 0:255],
                             lhsT=W1[kv][g][:, l, hc * 128:(hc + 1) * 128],
                             rhs=kvT[kv][:, l:l + 4065:16], start=(l == 0), stop=(l == 31))
                    i = n % 2
                    col = kv * 2 + hc
                    S.op("act", [pk, "biasT"], [f"u{i}"], "activation", out=u[i][:, 0:255], in_=pb[:, 0:255],
                         func=AF.Identity, bias=biasT[:, col:col + 1], scale=1.0)
                    S.op("pool", [f"u{i}"], [f"u2{i}"], "tensor_tensor", out=u2[i][:, 0:255], in0=u[i][:, 0:255],
                         in1=u[i][:, 0:255], op=ALU.mult)
                    S.op("pool", [f"u2{i}"], [f"u2{i}"], "tensor_scalar", out=u2[i][:, 0:255], in0=u2[i][:, 0:255],
                         scalar1=0.044715, scalar2=1.0, op0=ALU.mult, op1=ALU.add)
                    S.op("pool", [f"u2{i}", f"u{i}"], [f"u2{i}"], "tensor_tensor", out=u2[i][:, 0:255], in0=u2[i][:, 0:255],
                         in1=u[i][:, 0:255], op=ALU.mult)
                    S.op("act", [f"u2{i}"], [f"sg{i}"], "activation", out=sg[i][:, 0:255], in_=u2[i][:, 0:255],
                         func=AF.Sigmoid, scale=1.5957691216057308)
                    S.op("dve", [f"u{i}", f"sg{i}"], [("GH", kv, g, hc)], "tensor_tensor", out=GH[:, kv * 4 + g * 2 + hc, 0:255],
                         in0=u[i][:, 0:255], in1=sg[i][:, 0:255], op=ALU.mult)
                    n += 1
        n = 0
        for kv in range(2):
            for g in range(2):
                for c in range(2):
                    nn = 128 if c == 0 else 127
                    pb, pk = bank(n % 4)
                    for hc in range(2):
                        S.op("pe", [("GH", kv, g, hc), f"W2_{kv}"], [pk], "matmul", pb[0:nn, 0:64],
                             lhsT=GH[:, kv * 4 + g * 2 + hc, c * 128:c * 128 + nn], rhs=W2[kv][:, hc, :],
                             start=(hc == 0), stop=(hc == 1))
                    if kv == 0:
                        i = n % 2
                        col = g * 2 + c
                        S.op("act", [pk], ["junkb", ("ssk", col)], "activation", out=junkb[0:nn, :], in_=pb[0:nn, 0:64],
                             func=AF.Square, accum_out=ssk[0:nn, col:col + 1])
                        rstd_from_ss(ssk[0:nn, col:col + 1], rsk[0:nn, col:col + 1], 1, [("ssk", col)], [("rsk", col)], 1.0 / 64)
                        S.op("dve", [pk, ("rsk", col), "gk"], [f"kcn2{i}"], "scalar_tensor_tensor", out=kcn2[i][0:nn, 0:64],
                             in0=pb[0:nn, 0:64], scalar=rsk[0:nn, col:col + 1], in1=gk[0:nn, :], op0=ALU.mult, op1=ALU.mult)
                        S.op("dve", [f"kcn2{i}"], [f"kcn2{i}"], "tensor_copy", out=kcn2[i][0:nn, 192:256], in_=kcn2[i][0:nn, 0:64])
                        tb, tk = bank_bf(6 + i)
                        for par in range(2):
                            S.op("pe", [f"kcn2{i}", "ident"], [tk], "transpose", tb[:, par * 128:par * 128 + nn], kcn2[i][0:nn, par * 128:(par + 1) * 128],
                                 ident[0:nn, 0:nn])
                        S.op("act", [tk, "KCT"], ["KCT"], "activation", out=KCT[:, :, g, c * 128:c * 128 + nn],
                             in_=tb[:, 0:256].rearrange("p (a n) -> p a n", a=2)[:, :, 0:nn], func=AF.Copy)
                    else:
                        S.op("act", [pk, "VCO"], ["VCO"], "activation", out=VCO[0:nn, g, c, 0:64], in_=pb[0:nn, 0:64], func=AF.Copy)
                    n += 1
        if dbg:
            DKC = nc.dram_tensor("DKC", [128, 2, 2, 256], BF16, kind="ExternalOutput").ap()
            DVC = nc.dram_tensor("DVC", [128, 2, 2, 129], BF16, kind="ExternalOutput").ap()
            S.dma("sp", ["KCT"], ["DKC"], out=DKC[:, :, :, :], in_=KCT[:])
            S.dma("sp", ["VCO"], ["DVC"], out=DVC[:, :, :, :], in_=VCO[:])
        S.barrier()
    if stop_after == "PB":
        return finish()

    QDT, KDT, VD, YAT = g_("QDT"), g_("KDT"), g_("VD"), g_("YAT")
    with contextlib.ExitStack() as pc:
      if not PD_DEBUG["skip_pc"]:
        dm = sbt(pc, "dm", [128, 8, 512], BF16)
        S.dma("sp", [], ["dm"], out=dm[:], in_=g_("c_dmask")[:, :, :])
        lt = sbt(pc, "lt", [128, 256], F32)
        S.dma("sp", [], ["lt"], out=lt[:], in_=g_("diff_lambda").broadcast_to([128, 256]))
        prod = sbt(pc, "prod", [128, 128], F32)
        sums = sbt(pc, "sums", [128, 2], F32)
        ex = sbt(pc, "ex", [128, 2], F32)
        nlam = sbt(pc, "nlam", [128, 1], F32)
        S.op("dve", ["lt"], ["prod"], "tensor_tensor", out=prod[:].rearrange("p (a d) -> p a d", a=2),
             in0=lt[:].rearrange("p (a b d) -> p a b d", a=2, b=2)[:, :, 0, :],
             in1=lt[:].rearrange("p (a b d) -> p a b d", a=2, b=2)[:, :, 1, :], op=ALU.mult)
        S.op("dve", ["prod"], ["sums"], "tensor_reduce", out=sums[:], in_=prod[:].rearrange("p (a d) -> p a d", a=2),
             axis=AX.X, op=ALU.add)
        S.op("act", ["sums"], ["ex"], "activation", out=ex[:], in_=sums[:], func=AF.Exp)
        S.op("dve", ["ex"], ["nlam"], "tensor_tensor", out=nlam[:], in0=ex[:, 1:2], in1=ex[:, 0:1], op=ALU.subtract)
        S.op("dve", ["nlam"], ["nlam"], "tensor_scalar", out=nlam[:], in0=nlam[:], scalar1=-0.2, scalar2=None, op0=ALU.add)
        gsub = sbt(pc, "gsub", [128, 128], F32)
        S.dma("sp", [], ["gsub"], out=gsub[:], in_=g_("diff_subln").broadcast_to([128, 128]))
        S.op("dve", ["gsub"], ["gsub"], "tensor_scalar", out=gsub[:], in0=gsub[:], scalar1=0.8, scalar2=None, op0=ALU.mult)
        KT = [sbt(pc, f"KT{i}", [128, SEQ], BF16) for i in range(2)]
        VT = [sbt(pc, f"VT{i}", [128, 32, 129], BF16) for i in range(2)]
        QT = [[sbt(pc, f"QT{i}{c}", [128, NOWN], BF16) for c in range(2)] for i in range(2)]
        for i in range(2):
            for c in range(2):
                S.op("dve", [], [("QTz", i, c)], "memset", QT[i][c][(1 - c) * 64:(2 - c) * 64, :], 0.0)
        YAs = [sbt(pc, f"YAs{i}", [128, NOWN], BF16) for i in range(2)]
        E = [sbt(pc, f"E{i}", [128, 512], BF16) for i in range(4)]
        rd = [sbt(pc, f"rd{i}", [128, 4], F32) for i in range(2)]
        t0_ = [sbt(pc, f"t0_{i}", [128, 128], F32) for i in range(2)]
        o_ = [sbt(pc, f"o_{i}", [128, 128], F32) for i in range(2)]
        yb_ = [sbt(pc, f"yb_{i}", [128, 128], BF16) for i in range(2)]
        junkc = sbt(pc, "junkc", [128, 128], BF16)
        for i in range(2):
            S.op("dve", [], [("VT1", i)], "memset", VT[i][:, :, 128:129], 1.0)
        nf = 0
        LA = 3

        def pc_loads(h):
            i = h % 2
            S.dma("sp", [("KDT", (h // 4) * 4, 0), ("KDT", (h // 4) * 4, 1)], [("KT", i)], out=KT[i][:], in_=KDT[h, :, :])
            for q4 in range(8):
                S.dma("sp", [("VD", c0, hp, b) for c0 in (2048, 2560) for hp in range(2) for b in range(16)], [("VT", i, q4)],
                      out=VT[i][:, q4 * 4:(q4 + 1) * 4, 0:128],
                      in_=VD[q4 * 512:(q4 + 1) * 512, h * 128:(h + 1) * 128].rearrange("(kb p) d -> p kb d", p=128))
            for c in range(2):
                S.dma("sp", [("QN", 0, 0), ("QN", 512, 0)], [("QT", i, c)], out=QT[i][c][c * 64:(c + 1) * 64, :], in_=QDT[h, c * 64:(c + 1) * 64, :])

        def pc_front(u, n):
            h, G, kb, c = u
            i = h % 2
            r = kb - 8 * G
            pb, pk = bank(n % 4)
            e = E[n % 4]; ek = f"E{n % 4}"
            q_lo = max(r, 0) // 2 * 128
            S.op("pe", [("KT", i), ("QT", i, c), ("QTz", i, c)], [pk], "matmul", pb[:, q_lo:512], lhsT=KT[i][:, kb * 128:(kb + 1) * 128],
                 rhs=QT[i][c][:, G * 512 + q_lo:(G + 1) * 512], start=True, stop=True)
            S.op("act", [pk], [ek], "activation", out=e[:, q_lo:512], in_=pb[:, q_lo:512], func=AF.Exp, scale=0.125)
            if r >= 0:
                S.op("dve", [ek, "dm"], [ek], "tensor_tensor", out=e[:, q_lo:512], in0=e[:, q_lo:512], in1=dm[:, r, q_lo:512], op=ALU.mult)

        def pc_back(u, n):
            h, G, kb, c = u
            i = h % 2
            r = kb - 8 * G
            e = E[n % 4]; ek = f"E{n % 4}"
            for t in range(4):
                if r > 2 * t + 1:
                    continue
                a = c * 4 + t
                ab, ak = bank(4 + a // 3)
                off = (a % 3) * 129
                S.op("pe", [ek, ("VT", i, kb // 4), ("VT1", i)], [ak], "matmul", ab[:, off:off + 129],
                     lhsT=e[:, t * 128:(t + 1) * 128], rhs=VT[i][:, kb, :], start=(kb == 0 and a % 3 == 0),
                     stop=(kb == 8 * G + 2 * t + 1), skip_group_check=True)

        def pc_final(h, G):
            nonlocal nf
            i = h % 2
            for t in range(4):
                f = nf % 2; nf += 1
                a0, a1 = t, 4 + t
                b0, k0 = bank(4 + a0 // 3); b1, k1 = bank(4 + a1 // 3)
                O0 = b0[:, (a0 % 3) * 129:(a0 % 3) * 129 + 129]
                O1 = b1[:, (a1 % 3) * 129:(a1 % 3) * 129 + 129]
                rk = f"rd{f}"
                S.op("dve", [k0], [(rk, 0)], "reciprocal", out=rd[f][:, 0:1], in_=O0[:, 128:129])
                S.op("dve", [k1], [(rk, 1)], "reciprocal", out=rd[f][:, 1:2], in_=O1[:, 128:129])
                S.op("dve", [(rk, 1), "nlam"], [(rk, 2)], "tensor_tensor", out=rd[f][:, 2:3], in0=rd[f][:, 1:2], in1=nlam[:], op=ALU.mult)
                S.op("dve", [k0, (rk, 0)], [f"t0_{f}"], "tensor_scalar", out=t0_[f][:], in0=O0[:, 0:128], scalar1=rd[f][:, 0:1],
                     scalar2=None, op0=ALU.mult)
                S.op("dve", [k1, (rk, 2), f"t0_{f}"], [f"o_{f}"], "scalar_tensor_tensor", out=o_[f][:], in0=O1[:, 0:128],
                     scalar=rd[f][:, 2:3], in1=t0_[f][:], op0=ALU.mult, op1=ALU.add)
                S.op("act", [f"o_{f}"], ["junkc", (rk, 3)], "activation", out=junkc[:], in_=o_[f][:], func=AF.Square,
                     accum_out=rd[f][:, 3:4])
                rstd_from_ss(rd[f][:, 3:4], rd[f][:, 3:4], 1, [(rk, 3)], [(rk, 3)], 1.0 / 128)
                S.op("dve", [f"o_{f}", (rk, 3), "gsub"], [f"yb_{f}"], "scalar_tensor_tensor", out=yb_[f][:], in0=o_[f][:],
                     scalar=rd[f][:, 3:4], in1=gsub[:], op0=ALU.mult, op1=ALU.mult)
                tb, tk = bank_bf(7)
                S.op("pe", [f"yb_{f}", "ident"], [tk], "transpose", tb[:, 0:128], yb_[f][:], ident[:])
                q0 = (G * 4 + t) * 128
                S.op("act", [tk], [("YAs", i, G * 4 + t)], "activation", out=YAs[i][:, q0:q0 + 128], in_=tb[:, 0:128], func=AF.Copy)
            if G == 3:
                S.dma("pool", [("YAs", i, b) for b in range(16)], [("YAT", h)], out=YAT[h, :, :], in_=YAs[i][:])

        units = [(h, G, kb, c) for h in range(8) for G in range(4) for kb in range(8 * G + 8) for c in range(2)]
        pc_loads(0)
        for n in range(len(units) + LA):
            if n < len(units):
                u = units[n]
                if u[1] == 0 and u[2] == 2 and u[3] == 0 and u[0] + 1 < 8:
                    pc_loads(u[0] + 1)
                pc_front(u, n)
            if n >= LA:
                ub = units[n - LA]
                pc_back(ub, n - LA)
                if ub[2] == 8 * ub[1] + 7 and ub[3] == 1:
                    pc_final(ub[0], ub[1])
        pass
        S.barrier()
    if stop_after == "PC":
        return finish()

    NQRT, NQPT, KST, KWT, VS, VW, YBT = g_("NQRT"), g_("NQPT"), g_("KST"), g_("KWT"), g_("VS"), g_("VW"), g_("YBT")
    with contextlib.ExitStack() as pd:
        m128 = sbt(pd, "m128", [128, 2, 8, 128], BF16)
        mc = sbt(pd, "mc", [128, 16, 2, 128], BF16)
        valid = sbt(pd, "valid", [128, 16, 64], F32)
        addc = sbt(pd, "addc", [128, 16, 64], F32)
        eexp = sbt(pd, "eexp", [128, 32, 128], BF16)
        S.op("dve", [], ["eexpz"], "memset", eexp[64:128, :, :], 0.0)
        S.dma("sp", [], ["m128"], out=m128[:], in_=g_("c_m128")[:, :, :, :])
        S.dma("sp", [], ["mc"], out=mc[:], in_=g_("c_mc")[:, :, :, :])
        S.dma("sp", [], ["valid"], out=valid[:], in_=g_("c_valid")[:, :, :])
        S.dma("sp", [], ["addc"], out=addc[:], in_=g_("c_addc")[:, :, :])
        S.dma("sp", [], ["eexp"], out=eexp[0:64, :, :], in_=g_("c_eexp")[:, :, :])
        KS = [sbt(pd, f"KS{par}", [128, SEQ], BF16) for par in range(2)]
        KW = [sbt(pd, f"KW{par}", [128, SEQ], BF16) for par in range(2)]
        for par in range(2):
            S.op("dve", [], [("KSz", par)], "memset", KS[par][(1 - par) * 64:(2 - par) * 64, :], 0.0)
            S.op("pool", [], [("KWz", par)], "memset", KW[par][(1 - par) * 64:(2 - par) * 64, :], 0.0)
        VS1 = sbt(pd, "VS1", [128, 32, 65], BF16)
        VW1 = sbt(pd, "VW1", [128, 32, 65], BF16)
        QR = sbt(pd, "QR", [128, 4, NOWN], BF16)
        QP = sbt(pd, "QP", [128, 4, NOWN], BF16)
        ecmp = sbt(pd, "ecmp", [128, 2, 2, 512], BF16)
        es_ = [sbt(pd, f"es{i}", [128, 512], BF16) for i in range(4)]
        M4 = [sbt(pd, f"M4{i}", [128, 512], BF16) for i in range(2)]
        Y = [sbt(pd, f"Y{i}", [128, 8, 64], F32) for i in range(2)]
        Yb = sbt(pd, "Yb", [128, 512], BF16)
        IMP = sbt(pd, "IMP", [128, 64], F32)
        tmpI = sbt(pd, "tmpI", [128, 8, 64], F32)
        tmpY = sbt(pd, "tmpY", [128, 8, 64], F32)
        sc = sbt(pd, "sc", [128, 64], F32)
        sc2 = sbt(pd, "sc2", [128, 64], F32)
        m8a = sbt(pd, "m8a", [128, 8], F32)
        m8b = sbt(pd, "m8b", [128, 8], F32)
        selb = sbt(pd, "selb", [128, 128], BF16)
        selT = [sbt(pd, f"selT{i}", [128, 128], BF16) for i in range(2)]
        S.op("dve", [], ["selbz"], "memset", selb[:, 64:128], 0.0)
        den8 = sbt(pd, "den8", [128, 8], F32)
        rd8 = sbt(pd, "rd8", [128, 8], F32)
        coef = sbt(pd, "coef", [128, 8], F32)
        ystg = sbt(pd, "ystg", [128, 4, NOWN], BF16)
        S.op("dve", [], ["VS1o"], "memset", VS1[:, :, 64:65], 1.0)
        S.op("dve", [], ["VW1o"], "memset", VW1[:, :, 64:65], 1.0)
        nsc = 0
        for g in range(2):
            for par in range(2):
                S.dma("sp", [("KST", 0), ("KST", 1)], [("KS", par)], out=KS[par][par * 64:(par + 1) * 64, :], in_=KST[g, par * 64:(par + 1) * 64, :])
                S.dma("sp", [("KWT", 0), ("KWT", 1)], [("KW", par)], out=KW[par][par * 64:(par + 1) * 64, :], in_=KWT[g, par * 64:(par + 1) * 64, :])
            for q4 in range(4):
                S.dma("sp", [("VS", b) for b in range(32)], [("VS1", q4)], out=VS1[:, q4 * 8:(q4 + 1) * 8, 0:64],
                      in_=VS[q4 * 1024:(q4 + 1) * 1024, g * 64:(g + 1) * 64].rearrange("(kb p) d -> p kb d", p=128))
                S.dma("sp", [("VW", b) for b in range(32)], [("VW1", q4)], out=VW1[:, q4 * 8:(q4 + 1) * 8, 0:64],
                      in_=VW[q4 * 1024:(q4 + 1) * 1024, g * 64:(g + 1) * 64].rearrange("(kb p) d -> p kb d", p=128))
            S.dma("sp", [("QN", 3072, 0), ("QN", 3584, 0)], ["QR"], out=QR[:], in_=NQRT[g * 4:(g + 1) * 4].rearrange("h p t -> p h t"))
            S.dma("sp", [("QN", 3072, 1), ("QN", 3584, 1)], ["QP"], out=QP[:], in_=NQPT[g * 4:(g + 1) * 4].rearrange("h p t -> p h t"))
            def gv_of(j):
                return gates[:, j, g * 24:(g + 1) * 24].rearrange("p (r b) -> p r b", b=3)

            def fin_branch(j, nper, width, branch, first):
                Yj = Y[j % 2]; yk = f"Y{j % 2}"
                nb_ = (8 + nper - 1) // nper
                for b in range(nb_):
                    ab, ak = bank(4 + b)
                    nh = min(nper, 8 - b * nper)
                    S.op("dve", [ak], ["den8"], "tensor_scalar", out=den8[:, b * nper:b * nper + nh].unsqueeze(2),
                         in0=ab[:, 0:nh * width].rearrange("p (r c) -> p r c", c=width)[:, :, 64:65], scalar1=1e-30,
                         scalar2=None, op0=ALU.max)
                S.op("dve", ["den8"], ["rd8"], "reciprocal", out=rd8[:], in_=den8[:])
                S.op("dve", ["rd8", ("gates", j)], ["coef"], "tensor_tensor", out=coef[:], in0=rd8[:], in1=gv_of(j)[:, :, branch], op=ALU.mult)
                for b in range(nb_):
                    ab, ak = bank(4 + b)
                    r0 = b * nper
                    nh = min(nper, 8 - r0)
                    accv = ab[:, 0:nh * width].rearrange("p (r c) -> p r c", c=width)[:, :, 0:64]
                    cb = coef[:, r0:r0 + nh].unsqueeze(2).broadcast_to([128, nh, 64])
                    ykeys = [(yk, r) for r in range(r0, r0 + nh)]
                    if first:
                        S.op("dve", [ak, "coef"], ykeys, "tensor_tensor", out=Yj[:, r0:r0 + nh, :], in0=accv, in1=cb, op=ALU.mult)
                    else:
                        tkeys = [("tmpY", r) for r in range(r0, r0 + nh)]
                        S.op("dve", [ak, "coef"], tkeys, "tensor_tensor", out=tmpY[:, r0:r0 + nh, :], in0=accv, in1=cb, op=ALU.mult)
                        S.op("pool", tkeys + ykeys, ykeys, "tensor_tensor", out=Yj[:, r0:r0 + nh, :], in0=Yj[:, r0:r0 + nh, :],
                             in1=tmpY[:, r0:r0 + nh, :], op=ALU.add)

            def chain(j):
                nonlocal nsc
                q0 = j * 128
                for c in range(2):
                    nn = 128 if c == 0 else 127
                    for par in range(2):
                        pb, pk = bank(nsc % 4); nsc += 1
                        S.op("pe", ["KCT", "QP"], [pk], "matmul", pb[0:nn, :].rearrange("p (a q) -> p a q", a=4),
                             lhsT=KCT[:, par, g, c * 128:c * 128 + nn], rhs=QP[:, :, q0:q0 + 128],
                             start=True, stop=True)
                        S.op("act", [pk], [("ecmp", c, par)], "activation", out=ecmp[0:nn, c, par, :], in_=pb[0:nn, :], func=AF.Exp, scale=0.125)
                        ev = ecmp[0:nn, c, par, :].rearrange("p (a q) -> p a q", a=4)
                        S.op("dve", [("ecmp", c, par), "mc"], [("ecmp", c, par)], "tensor_tensor", out=ev, in0=ev,
                             in1=mc[0:nn, j, c, :].unsqueeze(1).broadcast_to([nn, 4, 128]), op=ALU.mult)
                for r in range(8):
                    ii, par = r // 2, r % 2
                    ab, ak = bank(4 + r // 3)
                    off = (r % 3) * 129
                    for c in range(2):
                        nn = 128 if c == 0 else 127
                        S.op("pe", [("ecmp", c, par), "VCO"], [ak], "matmul", ab[:, off:off + 129],
                             lhsT=ecmp[0:nn, c, par, ii * 128:(ii + 1) * 128], rhs=VCO[0:nn, g, c, :],
                             start=(c == 0 and r % 3 == 0), stop=(c == 1), skip_group_check=True)
                fin_branch(j, 3, 129, 0, True)
                for b in range(3):
                    ab, ak = bank(4 + b)
                    r0 = b * 3
                    nh = min(3, 8 - r0)
                    S.op("dve", [ak, "rd8"], [("tmpI", b)], "tensor_tensor", out=tmpI[:, r0:r0 + nh, :],
                         in0=ab[:, 0:nh * 129].rearrange("p (r c) -> p r c", c=129)[:, :, 65:129],
                         in1=rd8[:, r0:r0 + nh].unsqueeze(2).broadcast_to([128, nh, 64]), op=ALU.mult)
                S.op("dve", [("tmpI", b) for b in range(3)], ["IMP"], "tensor_reduce", out=IMP[:], in_=tmpI[:].rearrange("p r j -> p j r"),
                     axis=AX.X, op=ALU.add)
                S.op("dve", ["IMP", "valid"], ["sc"], "tensor_tensor", out=sc[:], in0=IMP[:], in1=valid[:, j, :], op=ALU.mult)
                S.op("dve", ["sc", "addc"], ["sc"], "tensor_tensor", out=sc[:], in0=sc[:], in1=addc[:, j, :], op=ALU.add)
                S.op("dve", ["sc"], ["m8a"], "max", out=m8a[:], in_=sc[:])
                S.op("dve", ["sc", "m8a"], ["sc2"], "match_replace", out=sc2[:], in_to_replace=m8a[:], in_values=sc[:], imm_value=-3.0)
                S.op("dve", ["sc2"], ["m8b"], "max", out=m8b[:], in_=sc2[:])
                S.op("dve", ["sc", "m8b"], ["selb"], "tensor_scalar", out=selb[:, 0:64], in0=sc[:], scalar1=m8b[:, 7:8], scalar2=None, op0=ALU.is_ge)
                tb, tk = bank_bf(7)
                S.op("pe", ["selb", "selbz", "ident"], [tk], "transpose", tb[:, 0:128], selb[:], ident[:])
                S.op("act", [tk], [f"selT{j % 2}"], "activation", out=selT[j % 2][:], in_=tb[:, 0:128], func=AF.Copy)

            def maskgen(j, kb4):
                jp = j & 1
                nkb = 2 * j + 2
                nb = min(4, nkb - kb4)
                mi = (kb4 // 4) % 2
                pm, pmk = bank(6)
                for q_ in range(nb):
                    S.op("pe", ["eexp", "eexpz", f"selT{j % 2}"], [pmk], "matmul", pm[:, q_ * 128:(q_ + 1) * 128], lhsT=eexp[:, kb4 + q_, :],
                         rhs=selT[j % 2][:, :], start=True, stop=True, skip_group_check=True)
                ncaus = sum(1 for q_ in range(nb) if kb4 + q_ >= 2 * j)
                nplain = nb - ncaus
                if nplain > 0:
                    S.op("act", [pmk], [(f"M4{mi}", q2) for q2 in range(nplain)], "activation", out=M4[mi][:, 0:nplain * 128],
                         in_=pm[:, 0:nplain * 128], func=AF.Copy)
                for q_ in range(nplain, nb):
                    kb = kb4 + q_
                    S.op("dve", [pmk, "m128"], [(f"M4{mi}", q_)], "tensor_tensor", out=M4[mi][:, q_ * 128:(q_ + 1) * 128],
                         in0=pm[:, q_ * 128:(q_ + 1) * 128], in1=m128[:, jp, 6 + (kb - 2 * j), :], op=ALU.mult)

            def attn_units(j, kbs, Kt, Kk, V1, Vk, maskfn, pre=None):
                nonlocal nsc
                q0 = j * 128
                units = [(kb, par) for kb in kbs for par in range(2)]
                base = nsc
                nsc += len(units)
                LA_ = 3
                for n in range(len(units) + LA_):
                    if n < len(units):
                        kb, par = units[n]
                        if pre is not None and par == 0:
                            pre(kb)
                        mk_ = maskfn(kb)
                        if (base + n) % 12 == 0:
                            L["conv_pop"](1)
                        pb, pk = bank((base + n) % 4)
                        e = es_[(base + n) % 4]; ek = f"es{(base + n) % 4}"
                        S.op("pe", [(Kk, par), (Kk + "z", par), "QR"], [pk], "matmul", pb[:, :].rearrange("p (a q) -> p a q", a=4),
                             lhsT=Kt[par][:, kb * 128:(kb + 1) * 128], rhs=QR[:, :, q0:q0 + 128], start=True, stop=True)
                        S.op("act", [pk], [ek], "activation", out=e[:], in_=pb[:, :], func=AF.Exp, scale=0.125)
                        if mk_ is not None:
                            map_, mkey = mk_
                            ev = e[:].rearrange("p (a q) -> p a q", a=4)
                            S.op("dve", [ek, mkey], [ek], "tensor_tensor", out=ev, in0=ev,
                                 in1=map_.unsqueeze(1).broadcast_to([128, 4, 128]), op=ALU.mult)
                    if n >= LA_:
                        kb, par = units[n - LA_]
                        e = es_[(base + n - LA_) % 4]; ek = f"es{(base + n - LA_) % 4}"
                        for ii in range(4):
                            r = 2 * ii + par
                            ab, ak = bank(4 + r // 4)
                            off = (r % 4) * 65
                            S.op("pe", [ek, (Vk, kb // 8), Vk + "o"], [ak], "matmul", ab[:, off:off + 65], lhsT=e[:, ii * 128:(ii + 1) * 128],
                                 rhs=V1[:, kb, :], start=(kb == kbs[0] and r % 4 == 0), stop=(kb == kbs[-1]), skip_group_check=True)

            chain(0)
            for j in range(16):
                jp = j & 1
                q0 = j * 128
                if j + 1 < 16:
                    chain(j + 1)
                nkb = 2 * j + 2
                if PD_DEBUG["sel"]:
                    maskgen(j, 0)

                    def pre(kb, j=j, nkb=nkb):
                        if kb % 4 == 0 and kb + 4 < nkb:
                            maskgen(j, kb + 4)

                    def smask(kb):
                        return (M4[(kb // 4) % 2][:, (kb % 4) * 128:(kb % 4 + 1) * 128], (f"M4{(kb // 4) % 2}", kb % 4))

                    attn_units(j, list(range(nkb)), KS, "KS", VS1, "VS1", smask, pre)
                    fin_branch(j, 4, 65, 1, False)
                if PD_DEBUG["win"]:
                    kbs = [kb for kb in range(2 * j - 4, 2 * j + 2) if kb >= 0]

                    def wmask(kb, j=j, jp=jp):
                        idx = kb - (2 * j - 4)
                        if idx in (2, 3):
                            return None
                        return (m128[:, jp, idx, :], "m128")

                    attn_units(j, kbs, KW, "KW", VW1, "VW1", wmask)
                    fin_branch(j, 4, 65, 2, False)
                yk = f"Y{j % 2}"
                S.op("act", [(yk, r) for r in range(8)], ["Yb"], "activation", out=Yb[:], in_=Y[j % 2][:].rearrange("p r d -> p (r d)"), func=AF.Copy)
                tb, tk = bank_bf(7)
                for ii in range(4):
                    S.op("pe", ["Yb", "ident"], [tk], "transpose", tb[:, ii * 128:(ii + 1) * 128], Yb[:, ii * 128:(ii + 1) * 128], ident[:])
                S.op("act", [tk], [("ystg", j)], "activation", out=ystg[:, :, q0:q0 + 128],
                     in_=tb[:, 0:512].rearrange("p (c t) -> p c t", c=4), func=AF.Copy)
            if g == 1:
                L["conv_pop"](len(L["pending_conv"]))
            S.dma("pool", [("ystg", j) for j in range(16)], [("YBT", g)], out=YBT[g * 4:(g + 1) * 4].rearrange("h p t -> p h t"), in_=ystg[:])
        S.barrier()
    if stop_after == "PD":
        return finish()

    build_rowlocal(nc, S, es, L)


def build_rowlocal(nc, S, es, L):
    g_ = lambda n: L[n]
    sbt, bank, bank_bf, ident, rstd_from_ss = g_("sbt"), g_("bank"), g_("bank_bf"), g_("ident"), g_("rstd_from_ss")
    x_own, p_own, out_d = g_("x_own"), g_("p_own"), g_("out_d")
    YAT, YBT, SGT = g_("YAT"), g_("YBT"), g_("SGT")
    WPD, WPN, WOUT, WUP, WDN, WPP, WPG = g_("WPD"), g_("WPN"), g_("WOUT"), g_("WUP"), g_("WDN"), g_("WPP"), g_("WPG")
    with contextlib.ExitStack() as pe_:
        gvec = sbt(pe_, "gvec", [128, D], F32)
        aT = sbt(pe_, "aT", [128, 64, 512], BF16)
        wbig = [sbt(pe_, f"wbig{i}", [128, 16, 512], BF16) for i in range(2)]
        x1 = sbt(pe_, "x1", [128, 4, D], F32)
        hT2 = sbt(pe_, "hT2", [128, 16, 512], BF16)
        wpp = sbt(pe_, "wpp", [128, 2, D], BF16)
        hb2 = sbt(pe_, "hb2", [128, D], BF16)
        sgt = [sbt(pe_, f"sgt{i}", [128, 2, 512], BF16) for i in range(2)]
        t1 = sbt(pe_, "t1", [128, 512], F32)
        t2 = sbt(pe_, "t2", [128, 512], F32)
        xin = [sbt(pe_, f"xin{i}", [128, 512], F32) for i in range(2)]
        rl = [sbt(pe_, f"rl{i}", [128, 512], F32) for i in range(2)]
        gt = [sbt(pe_, f"gt{i}", [128, 512], F32) for i in range(2)]
        ev = [sbt(pe_, f"ev{i}", [128, 512], F32) for i in range(2)]
        ss1 = sbt(pe_, "ss1", [128, 8], F32)
        rs1 = sbt(pe_, "rs1", [128, 8], F32)
        ssE = sbt(pe_, "ssE", [128, 16], F32)
        ssE4 = sbt(pe_, "ssE4", [128, 4], F32)
        rsE = sbt(pe_, "rsE", [128, 4], F32)
        pt = sbt(pe_, "pt", [128, 256], F32)
        ptb = sbt(pe_, "ptb", [128, 256], BF16)
        pT = sbt(pe_, "pT", [128, 2, 512], BF16)
        S.dma("sp", [("WPP", 0)], ["wpp"], out=wpp[:], in_=WPP[:, :].rearrange("(k p) c -> p k c", p=128))
        cn = {"w": 0, "ps": 0, "sg": 0, "x": 0, "r": 0, "g": 0, "e": 0}

        def nextw():
            i = cn["w"] % 2; cn["w"] += 1
            return wbig[i], f"wbig{i}"

        def nextbank():
            b = cn["ps"] % 4; cn["ps"] += 1
            return bank(b)

        def to_hT2(blk):
            for half in range(2):
                pb, pk = bank_bf(6 + half)
                for k in range(8):
                    kk = half * 8 + k
                    S.op("pe", ["hb2", "ident"], [pk], "transpose", pb[:, k * 128:(k + 1) * 128], hb2[:, kk * 128:(kk + 1) * 128], ident[:])
                dst = hT2[:, half * 8:half * 8 + 8, blk * 128:(blk + 1) * 128]
                srcv = pb.rearrange("p (k t) -> p k t", k=8)
                if half == 0:
                    S.op("act", [pk], [("hT2", blk, 0)], "activation", out=dst, in_=srcv, func=AF.Copy)
                else:
                    S.op("dve", [pk], [("hT2", blk, 1)], "tensor_copy", out=dst, in_=srcv)

        hT2keys = [("hT2", b, h) for b in range(4) for h in range(2)]
        for tt in range(4):
            tok0 = tt * 512
            S.dma("sp", [("YAT", h) for h in range(8)], [("aT", k) for k in range(8)], out=aT[:, 0:8, :],
                  in_=YAT[:, :, tok0:tok0 + 512].rearrange("h p t -> p h t"))
            S.dma("sp", [("YBT", 0), ("YBT", 1)], [("aT", k) for k in range(8, 16)], out=aT[:, 8:16, :],
                  in_=YBT[:, :, tok0:tok0 + 512].rearrange("h p t -> p h t"))
            for cc in range(4):
                w, wk = nextw()
                S.dma("sp", [("WPD", r) for r in range(0, 1024, 256)], [(wk, 0)], out=w[:, 0:8, :],
                      in_=WPD[:, cc * 512:(cc + 1) * 512].rearrange("(k p) c -> p k c", p=128))
                S.dma("sp", [("WPN", r) for r in range(0, 1024, 256)], [(wk, 1)], out=w[:, 8:16, :],
                      in_=WPN[:, cc * 512:(cc + 1) * 512].rearrange("(k p) c -> p k c", p=128))
                for f in range(4):
                    fidx = cc * 4 + f
                    si = cn["sg"] % 2; cn["sg"] += 1
                    for gi in range(2):
                        S.dma("sp", [("SGT", gi, fidx * 128, tt)], [(f"sgt{si}", gi)], out=sgt[si][:, gi, :],
                              in_=SGT[gi, fidx * 128:(fidx + 1) * 128, tok0:tok0 + 512])
                    pA, pAk = nextbank()
                    pB, pBk = nextbank()
                    for k in range(8):
                        S.op("pe", [(wk, 0), ("aT", k)], [pAk], "matmul", pA[:, :], lhsT=w[:, k, f * 128:(f + 1) * 128], rhs=aT[:, k, :],
                             start=(k == 0), stop=(k == 7))
                    for k in range(8):
                        S.op("pe", [(wk, 1), ("aT", 8 + k)], [pBk], "matmul", pB[:, :], lhsT=w[:, 8 + k, f * 128:(f + 1) * 128],
                             rhs=aT[:, 8 + k, :], start=(k == 0), stop=(k == 7))
                    S.op("dve", [pAk, (f"sgt{si}", 0)], ["t1"], "tensor_tensor", out=t1[:], in0=pA[:, :], in1=sgt[si][:, 0, :], op=ALU.mult)
                    S.op("dve", [pBk, (f"sgt{si}", 1)], ["t2"], "tensor_tensor", out=t2[:], in0=pB[:, :], in1=sgt[si][:, 1, :], op=ALU.mult)
                    S.op("pool", ["t1", "t2"], [("aT", 16 + fidx)], "tensor_tensor", out=aT[:, 16 + fidx, :], in0=t1[:], in1=t2[:], op=ALU.add)
            for cc in range(4):
                w, wk = nextw()
                S.dma("sp", [("WOUT", r) for r in range(0, D, 256)], [(wk, 0), (wk, 1)], out=w[:],
                      in_=WOUT[:, cc * 512:(cc + 1) * 512].rearrange("(k p) c -> p k c", p=128))
                for blk in range(4):
                    pb, pk = nextbank()
                    for k in range(16):
                        S.op("pe", [(wk, 0), (wk, 1), ("aT", 16 + k)], [pk], "matmul", pb[:, :], lhsT=aT[:, 16 + k, blk * 128:(blk + 1) * 128],
                             rhs=w[:, k, :], start=(k == 0), stop=(k == 15))
                    xi = cn["x"] % 2; cn["x"] += 1
                    S.dma("sp", [], [f"xin{xi}"], out=xin[xi][:], in_=x_own[tok0 + blk * 128:tok0 + (blk + 1) * 128, cc * 512:(cc + 1) * 512])
                    S.op("dve", [pk, f"xin{xi}"], [("x1", blk, cc)], "tensor_tensor", out=x1[:, blk, cc * 512:(cc + 1) * 512], in0=pb[:, :],
                         in1=xin[xi][:], op=ALU.add)
            S.dma("sp", [], ["gvec"], out=gvec[:], in_=g_("norm_mlp").broadcast_to([128, D]))
            for blk in range(4):
                xk = [("x1", blk, c) for c in range(4)]
                S.op("act", xk, ["hb2", ("ss1", blk)], "activation", out=hb2[:], in_=x1[:, blk, :], func=AF.Square, accum_out=ss1[:, blk:blk + 1])
                rstd_from_ss(ss1[:, blk:blk + 1], rs1[:, blk:blk + 1], 1, [("ss1", blk)], [("rs1", blk)], 1.0 / D)
                S.op("dve", xk + [("rs1", blk), "gvec"], ["hb2"], "scalar_tensor_tensor", out=hb2[:], in0=x1[:, blk, :],
                     scalar=rs1[:, blk:blk + 1], in1=gvec[:], op0=ALU.mult, op1=ALU.mult)
                to_hT2(blk)
            for uc in range(16):
                w, wk = nextw()
                S.dma("sp", [("WUP", r) for r in range(0, D, 256)], [(wk, 0), (wk, 1)], out=w[:],
                      in_=WUP[:, uc * 512:(uc + 1) * 512].rearrange("(k p) c -> p k c", p=128))
                for f in range(4):
                    pb, pk = nextbank()
                    for k in range(16):
                        S.op("pe", [(wk, 0), (wk, 1)] + hT2keys, [pk], "matmul", pb[:, :], lhsT=w[:, k, f * 128:(f + 1) * 128], rhs=hT2[:, k, :],
                             start=(k == 0), stop=(k == 15))
                    ri = cn["r"] % 2; cn["r"] += 1
                    S.op("act", [pk], [f"rl{ri}"], "activation", out=rl[ri][:], in_=pb[:, :], func=AF.Relu)
                    S.op("pool", [f"rl{ri}"], [("aT", uc * 4 + f)], "tensor_tensor", out=aT[:, uc * 4 + f, :], in0=rl[ri][:], in1=rl[ri][:], op=ALU.mult)
            for fc in range(4):
                base = 0 if fc % 2 == 0 else 4
                for kg in range(4):
                    w, wk = nextw()
                    S.dma("sp", [("WDN", r) for r in range(0, DFF, 1024)], [(wk, 0), (wk, 1)], out=w[:],
                          in_=WDN[kg * 2048:(kg + 1) * 2048, fc * 512:(fc + 1) * 512].rearrange("(k p) c -> p k c", p=128))
                    for k in range(16):
                        ffc = kg * 16 + k
                        for blk in range(4):
                            pb, pk = bank(base + blk)
                            S.op("pe", [(wk, 0), (wk, 1), ("aT", ffc)], [pk], "matmul", pb[:, :], lhsT=aT[:, ffc, blk * 128:(blk + 1) * 128],
                                 rhs=w[:, k, :], start=(ffc == 0), stop=(ffc == 63))
                for blk in range(4):
                    pb, pk = bank(base + blk)
                    S.op("dve", [pk, ("x1", blk, fc)], [("x1", blk, fc)], "tensor_tensor", out=x1[:, blk, fc * 512:(fc + 1) * 512], in0=pb[:, :],
                         in1=x1[:, blk, fc * 512:(fc + 1) * 512], op=ALU.add)
            S.dma("sp", [], ["gvec"], out=gvec[:], in_=g_("norm_ple").broadcast_to([128, D]))
            for blk in range(4):
                xk = [("x1", blk, c) for c in range(4)]
                S.op("act", xk, ["hb2", ("ss1", 4 + blk)], "activation", out=hb2[:], in_=x1[:, blk, :], func=AF.Square,
                     accum_out=ss1[:, 4 + blk:5 + blk])
                rstd_from_ss(ss1[:, 4 + blk:5 + blk], rs1[:, 4 + blk:5 + blk], 1, [("ss1", 4 + blk)], [("rs1", 4 + blk)], 1.0 / D)
                S.op("dve", xk + [("rs1", 4 + blk)], ["hb2"], "tensor_scalar", out=hb2[:], in0=x1[:, blk, :], scalar1=rs1[:, 4 + blk:5 + blk],
                     scalar2=None, op0=ALU.mult)
                to_hT2(blk)
                S.dma("sp", [], ["pt"], out=pt[:], in_=p_own[tok0 + blk * 128:tok0 + (blk + 1) * 128, :])
                S.op("dve", ["pt"], ["ptb"], "tensor_copy", out=ptb[:], in_=pt[:])
                pb, pk = bank_bf(6)
                for k in range(2):
                    S.op("pe", ["ptb", "ident"], [pk], "transpose", pb[:, k * 128:(k + 1) * 128], ptb[:, k * 128:(k + 1) * 128], ident[:])
                S.op("act", [pk], [("pT", blk)], "activation", out=pT[:, :, blk * 128:(blk + 1) * 128],
                     in_=pb[:, 0:256].rearrange("p (k t) -> p k t", k=2), func=AF.Copy)
            for blk in range(4):
                for cc in range(4):
                    pb, pk = bank(4 + cn["e"] % 2); cn["e"] += 1
                    for k in range(2):
                        S.op("pe", [("pT", blk), "wpp"], [pk], "matmul", pb[:, :], lhsT=pT[:, k, blk * 128:(blk + 1) * 128],
                             rhs=wpp[:, k, cc * 512:(cc + 1) * 512], start=(k == 0), stop=(k == 1))
                    S.op("act", [pk], ["hb2", ("ssE", blk * 4 + cc)], "activation", out=hb2[:, 0:512], in_=pb[:, :], func=AF.Square,
                         accum_out=ssE[:, blk * 4 + cc:blk * 4 + cc + 1])
            S.op("dve", [("ssE", i) for i in range(16)], ["ssE4"], "tensor_reduce", out=ssE4[:], in_=ssE[:].rearrange("p (b c) -> p b c", c=4),
                 axis=AX.X, op=ALU.add)
            rstd_from_ss(ssE4[:], rsE[:], 4, ["ssE4"], ["rsE"], 1.0 / D)
            for cc in range(4):
                w, wk = nextw()
                S.dma("sp", [("WPG", r) for r in range(0, D, 256)], [(wk, 0), (wk, 1)], out=w[:],
                      in_=WPG[:, cc * 512:(cc + 1) * 512].rearrange("(k p) c -> p k c", p=128))
                for blk in range(4):
                    pg, pgk = nextbank()
                    for k in range(16):
                        S.op("pe", [(wk, 0), (wk, 1), ("hT2", blk, 0), ("hT2", blk, 1)], [pgk], "matmul", pg[:, :],
                             lhsT=hT2[:, k, blk * 128:(blk + 1) * 128], rhs=w[:, k, :], start=(k == 0), stop=(k == 15))
                    pe2, pe2k = bank(4 + cn["e"] % 2); cn["e"] += 1
                    for k in range(2):
                        S.op("pe", [("pT", blk), "wpp"], [pe2k], "matmul", pe2[:, :], lhsT=pT[:, k, blk * 128:(blk + 1) * 128],
                             rhs=wpp[:, k, cc * 512:(cc + 1) * 512], start=(k == 0), stop=(k == 1))
                    gi = cn["g"] % 2; cn["g"] += 1
                    S.op("act", [pgk], [f"gt{gi}"], "activation", out=gt[gi][:], in_=pg[:, :], func=AF.Sigmoid)
                    S.op("dve", [pe2k, "rsE", "gvec"], [f"ev{gi}"], "scalar_tensor_tensor", out=ev[gi][:], in0=pe2[:, :], scalar=rsE[:, blk:blk + 1],
                         in1=gvec[:, cc * 512:(cc + 1) * 512], op0=ALU.mult, op1=ALU.mult)
                    S.op("pool", [f"ev{gi}", f"gt{gi}"], [f"ev{gi}"], "tensor_tensor", out=ev[gi][:], in0=ev[gi][:], in1=gt[gi][:], op=ALU.mult)
                    S.op("pool", [f"ev{gi}", ("x1", blk, cc)], [("x1", blk, cc)], "tensor_tensor", out=x1[:, blk, cc * 512:(cc + 1) * 512],
                         in0=x1[:, blk, cc * 512:(cc + 1) * 512], in1=ev[gi][:], op=ALU.add)
            for blk in range(4):
                S.dma("sp", [("x1", blk, c) for c in range(4)], [("out", tt, blk)], out=out_d[tok0 + blk * 128:tok0 + (blk + 1) * 128, :],
                      in_=x1[:, blk, :])
        S.barrier()


def _consts(hf):
    bf = ml_dtypes.bfloat16
    c = {}
    invf = np.power(np.float32(500000.0), -np.arange(0, 16, 2, dtype=np.float32) / np.float32(16)).astype(np.float32)
    c["c_invf"] = np.ascontiguousarray(np.broadcast_to(invf[None, :], (128, 8))).astype(np.float32)
    k = np.arange(128)[:, None, None]
    r = np.arange(8)[None, :, None]
    qq = np.arange(512)[None, None, :]
    t = qq // 128
    qpos = (2 * t + ((t & 1) ^ hf)) * 128 + (qq % 128)
    c["c_dmask"] = ((r * 128 + k) <= qpos).astype(bf)
    m128 = np.zeros((128, 2, 8, 128), np.float32)
    kk = np.arange(128)[:, None]
    mq = np.arange(128)[None, :]
    for jp in range(2):
        p = jp ^ hf
        q = p * 128 + mq
        for idx in range(6):
            key = (idx - 4) * 128 + kk
            dist = q - key
            m128[:, jp, idx, :] = ((dist >= 0) & (dist < 512))
        for idx in range(6, 8):
            key = (idx - 6) * 128 + kk
            m128[:, jp, idx, :] = (key <= q)
    c["c_m128"] = m128.astype(bf)
    mc = np.zeros((128, 16, 2, 128), np.float32)
    valid = np.zeros((128, 16, 64), np.float32)
    addc = np.zeros((128, 16, 64), np.float32)
    sel = np.arange(64)[None, :]
    for j in range(16):
        qp = own_block(j, hf) * 128 + np.arange(128)
        for ch in range(2):
            ng = ch * 128 + np.arange(128)
            mc[:, j, ch, :] = ((ng[:, None] <= 254) & (16 * ng[:, None] + 31 <= qp[None, :]))
        qb = (qp // 64)[:, None]
        v = sel <= qb
        f = (sel == 0) | (sel == qb) | (sel == qb - 1)
        valid[:, j, :] = v
        addc[:, j, :] = np.where(v, 1e4 * f, -1.0)
    c["c_mc"] = mc.astype(bf)
    c["c_valid"] = valid
    c["c_addc"] = addc.astype(np.float32)
    ovl = np.zeros((128, 2, 64), np.float32)
    for ch in range(2):
        ng = ch * 128 + np.arange(128)
        cs = 16 * ng[:, None]
        ssb = 64 * np.arange(64)[None, :]
        ovl[:, ch, :] = ((cs < ssb + 64) & (cs + 32 > ssb) & (ng[:, None] <= 254))
    c["c_ovl"] = ovl.astype(bf)
    jj = np.arange(64)[:, None, None]
    kb = np.arange(32)[None, :, None]
    k2 = np.arange(128)[None, None, :]
    c["c_eexp"] = (jj == 2 * kb + k2 // 64).astype(bf)
    return c


def make_in_maps(inputs):
    f = lambda a: np.ascontiguousarray(np.asarray(a))
    x = f(inputs["x"]); p = f(inputs["p"])[0]; pos = f(inputs["positions"]).astype(np.int32)
    shared = {
        "norm_mix": f(inputs["norm_mix"]).reshape(1, D),
        "w_in": f(inputs["w_in"])[0],
        "diff_q_norm": f(inputs["diff_q_norm"]).reshape(1, 64),
        "diff_k_norm": f(inputs["diff_k_norm"]).reshape(1, 64),
        "diff_lambda": f(inputs["diff_lambda"]).reshape(1, 256),
        "diff_subln": f(inputs["diff_subln"]).reshape(1, 128),
        "nsa_q_norm": f(inputs["nsa_q_norm"]).reshape(1, 64),
        "nsa_k_norm": f(inputs["nsa_k_norm"]).reshape(1, 64),
        "cmp_posT": f(np.transpose(f(inputs["cmp_pos"])[0], (2, 0, 1))),
        "cmp_w1": f(inputs["cmp_w1"])[0].reshape(4096, 256),
        "cmp_w2": f(inputs["cmp_w2"])[0].reshape(512, 64),
        "w_proj_diff": f(inputs["w_proj_diff"])[0],
        "w_proj_nsa": f(inputs["w_proj_nsa"])[0],
        "w_out": f(inputs["w_out"])[0],
        "norm_mlp": f(inputs["norm_mlp"]).reshape(1, D),
        "w_mlp_up": f(inputs["w_mlp_up"])[0],
        "w_mlp_down": f(inputs["w_mlp_down"])[0],
        "w_ple_proj": f(inputs["w_ple_proj"])[0],
        "norm_ple": f(inputs["norm_ple"]).reshape(1, D),
        "w_ple_gate": f(inputs["w_ple_gate"])[0],
    }
    cst = [_consts(0), _consts(1)]
    maps = []
    for c in range(8):
        b, hf = c // 2, c % 2
        blks = [own_block(j, hf) for j in range(16)]
        rows = np.concatenate([np.arange(bk * 128, (bk + 1) * 128) for bk in blks])
        m = dict(shared)
        m.update(cst[hf])
        m["x_all"] = x[b]
        m["x_own"] = f(x[b][rows])
        m["p_own"] = f(p[b][rows])
        m["posT_all"] = f(pos[b].reshape(32, 128).T)
        m["posT_own"] = f(pos[b][rows].reshape(16, 128).T)
        maps.append(m)
    return maps


def assemble(outs):
    res = np.zeros((4, SEQ, D), np.float32)
    for c in range(8):
        b, hf = c // 2, c % 2
        o = np.asarray(outs[c])
        for j in range(16):
            bk = own_block(j, hf)
            res[b, bk * 128:(bk + 1) * 128] = o[j * 128:(j + 1) * 128]
    return res


def kernel(**inputs):
    nc = build_nc()
    maps = make_in_maps(inputs)
    r = run_bass_kernel_spmd(nc, maps, core_ids=list(range(8)))
    return assemble([r.results[c]["out"] for c in range(8)])
```

```python
import contextlib
import math
import numpy as np
import ml_dtypes
import concourse.bass as bass
import concourse.mybir as mybir
from concourse.bass_utils import run_bass_kernel_spmd

F32, BF16, I32 = mybir.dt.float32, mybir.dt.bfloat16, mybir.dt.int32
AF = mybir.ActivationFunctionType
ALU = mybir.AluOpType
AX = mybir.AxisListType

D = 2048
SEQ = 4096
NOWN = 2048
IN_W = 9008
DFF = 8192
EPS = 1e-6
TWO_PI = 2.0 * math.pi


class Sched:
    def __init__(self, nc, es):
        self.nc = nc
        self.eng = {"pe": nc.tensor, "act": nc.scalar, "dve": nc.vector, "pool": nc.gpsimd, "sp": nc.sync}
        self.sem = {e: es.enter_context(nc.semaphore("s_" + e)) for e in ("pe", "act", "dve", "pool")}
        self.cnt = {e: 0 for e in self.sem}
        self.NDS = 12
        self.dsem = {q: [es.enter_context(nc.semaphore(f"d_{q}{i}")) for i in range(self.NDS)] for q in ("sp", "pool", "act")}
        self.dcnt = {q: [0] * self.NDS for q in self.dsem}
        self.dnext = {q: 0 for q in self.dsem}
        self.waited = {e: {} for e in self.eng}
        self.res = {}
        self.semname = {}
        self.n_wait = 0
        self.n_inst = 0

    def _wait(self, e, tok):
        if tok is None:
            return
        sem, val, owner = tok
        if e == "pe" and owner == "pe":
            return
        w = self.waited[e]
        if w.get(id(sem), 0) >= val:
            return
        self.eng[e].wait_ge(sem, val)
        self.n_wait += 1
        w[id(sem)] = val

    def _deps(self, e, reads, writes):
        for k in reads:
            r = self.res.get(k)
            if r:
                self._wait(e, r[0])
        for k in writes:
            r = self.res.get(k)
            if r:
                self._wait(e, r[0])
                for t in r[1].values():
                    self._wait(e, t)

    def _commit(self, tok, reads, writes):
        for k in reads:
            r = self.res.setdefault(k, [None, {}])
            r[1][id(tok[0])] = tok
        for k in writes:
            self.res[k] = [tok, {}]

    def op(self, e, reads, writes, meth, *args, **kw):
        ps_r = [k for k in reads if isinstance(k, tuple) and k[0] == "ps"]
        if ps_r:
            reads = [k for k in reads if k not in ps_r]
            writes = list(writes) + ps_r
        self._deps(e, reads, writes)
        self.cnt[e] += 1
        tok = (self.sem[e], self.cnt[e], e)
        getattr(self.eng[e], meth)(*args, **kw).then_inc(self.sem[e], 1)
        self.n_inst += 1
        self._commit(tok, reads, writes)
        return tok

    def dma(self, q, reads, writes, out, in_, **kw):
        i = self.dnext[q]
        self.dnext[q] = (i + 1) % self.NDS
        sem = self.dsem[q][i]
        if self.dcnt[q][i] > 0:
            self._wait(q, (sem, 16 * self.dcnt[q][i], "dma"))
        self._deps(q, reads, writes)
        self.dcnt[q][i] += 1
        tok = (sem, 16 * self.dcnt[q][i], "dma")
        self.eng[q].dma_start(out=out, in_=in_, **kw).then_inc(sem, 16)
        self.n_inst += 1
        self._commit(tok, reads, writes)
        return tok

    def barrier(self):
        toks = [(self.sem[e], self.cnt[e], "bar") for e in self.sem if self.cnt[e] > 0]
        for q in self.dsem:
            for i in range(self.NDS):
                if self.dcnt[q][i] > 0:
                    toks.append((self.dsem[q][i], 16 * self.dcnt[q][i], "dma"))
        for e in self.eng:
            for t in toks:
                self._wait(e, t)
        self.res = {}


PD_DEBUG = {"topk": True, "sel": True, "win": True, "skip_pc": False}


def own_block(j, hf):
    return 2 * j + ((j & 1) ^ hf)


def build_nc(dbg=False, stop_after=None):
    nc = bass.Bass("TRN2", target_bir_lowering=False)

    def din(name, shape, dt=F32):
        return nc.dram_tensor(name, list(shape), dt, kind="ExternalInput").ap()

    def dscr(name, shape, dt=BF16):
        if dbg:
            return nc.dram_tensor(name, list(shape), dt, kind="ExternalOutput").ap()
        return nc.dram_tensor(name, list(shape), dt).ap()

    x_all = din("x_all", [SEQ, D])
    x_own = din("x_own", [NOWN, D])
    p_own = din("p_own", [NOWN, 256])
    posT_all = din("posT_all", [128, 32], I32)
    posT_own = din("posT_own", [128, 16], I32)
    norm_mix = din("norm_mix", [1, D])
    w_in = din("w_in", [D, IN_W])
    diff_q_norm = din("diff_q_norm", [1, 64])
    diff_k_norm = din("diff_k_norm", [1, 64])
    diff_lambda = din("diff_lambda", [1, 256])
    diff_subln = din("diff_subln", [1, 128])
    nsa_q_norm = din("nsa_q_norm", [1, 64])
    nsa_k_norm = din("nsa_k_norm", [1, 64])
    cmp_posT = din("cmp_posT", [64, 2, 32])
    cmp_w1 = din("cmp_w1", [2 * 2048, 256])
    cmp_w2 = din("cmp_w2", [2 * 256, 64])
    w_proj_diff = din("w_proj_diff", [1024, D])
    w_proj_nsa = din("w_proj_nsa", [1024, D])
    w_out = din("w_out", [D, D])
    norm_mlp = din("norm_mlp", [1, D])
    w_mlp_up = din("w_mlp_up", [D, DFF])
    w_mlp_down = din("w_mlp_down", [DFF, D])
    w_ple_proj = din("w_ple_proj", [256, D])
    norm_ple = din("norm_ple", [1, D])
    w_ple_gate = din("w_ple_gate", [D, D])
    c_invf = din("c_invf", [128, 8])
    c_dmask = din("c_dmask", [128, 8, 512], BF16)
    c_m128 = din("c_m128", [128, 2, 8, 128], BF16)
    c_mc = din("c_mc", [128, 16, 2, 128], BF16)
    c_valid = din("c_valid", [128, 16, 64])
    c_addc = din("c_addc", [128, 16, 64])
    c_ovl = din("c_ovl", [128, 2, 64], BF16)
    c_eexp = din("c_eexp", [64, 32, 128], BF16)

    out_d = nc.dram_tensor("out", [NOWN, D], F32, kind="ExternalOutput").ap()

    W_IN = dscr("W_IN", [D, IN_W]) if not dbg else nc.dram_tensor("W_IN", [D, IN_W], BF16).ap()
    mk = lambda n, s: nc.dram_tensor(n, list(s), BF16).ap()
    CW1 = mk("CW1", [2 * 2048, 256]); CW2 = mk("CW2", [2 * 256, 64])
    WPD = mk("WPD", [1024, D]); WPN = mk("WPN", [1024, D]); WOUT = mk("WOUT", [D, D])
    WUP = mk("WUP", [D, DFF]); WDN = mk("WDN", [DFF, D]); WPP = mk("WPP", [256, D]); WPG = mk("WPG", [D, D])
    QDT = dscr("QDT", [8, 128, NOWN])
    KDT = dscr("KDT", [8, 128, SEQ])
    VD = dscr("VD", [SEQ, 1024])
    NQRT = dscr("NQRT", [8, 128, NOWN])
    NQPT = dscr("NQPT", [8, 128, NOWN])
    KCRT = dscr("KCRT", [128, SEQ]); VCRT = dscr("VCRT", [128, SEQ])
    KST = dscr("KST", [2, 128, SEQ]); KWT = dscr("KWT", [2, 128, SEQ])
    VS = dscr("VS", [SEQ, 128]); VW = dscr("VW", [SEQ, 128])
    SGT = dscr("SGT", [2, D, NOWN])
    YAT = dscr("YAT", [8, 128, NOWN])
    YBT = dscr("YBT", [8, 128, NOWN])

    es = contextlib.ExitStack()
    with es:
        S = Sched(nc, es)

        def sbt(stack, name, shape, dt):
            return stack.enter_context(nc.sbuf_tensor(name, list(shape), dt))

        PS = [es.enter_context(nc.psum_tensor(f"ps{i}", [128, 1024], F32)) for i in range(4)]

        def bank(i):
            return PS[i // 2][:, (i % 2) * 512:(i % 2) * 512 + 512], ("ps", i)

        def bank_bf(i):
            return PS[i // 2][:, (i % 2) * 512:(i % 2) * 512 + 512].bitcast(BF16), ("ps", i)

        ident = sbt(es, "ident", [128, 128], BF16)
        idf = sbt(es, "idf", [128, 128], F32)
        mhalf = sbt(es, "mhalf", [128, 64], F32)
        gates = sbt(es, "gates", [128, 16, 48], F32)
        CSA = sbt(es, "CSA", [128, 32, 32], F32)
        CSO = sbt(es, "CSO", [128, 16, 32], F32)
        S.op("pool", [], ["idf"], "iota", idf[:], pattern=[[1, 128]], base=0, channel_multiplier=-1,
             allow_small_or_imprecise_dtypes=True)
        S.op("dve", ["idf"], ["ident"], "tensor_single_scalar", out=ident[:], in_=idf[:], scalar=0.0, op=ALU.is_equal)
        S.op("pool", [], ["mhalf"], "memset", mhalf[:], -0.5)

        def rstd_from_ss(ss_ap, out_ap, n, keys_r, keys_w, inv_n):
            S.op("pool", keys_r, keys_w, "tensor_scalar", out=out_ap, in0=ss_ap, scalar1=inv_n, scalar2=EPS,
                 op0=ALU.mult, op1=ALU.add)
            P = out_ap.shape[0]
            S.op("pool", keys_w + ["mhalf"], keys_w, "tensor_tensor", out=out_ap, in0=out_ap, in1=mhalf[0:P, 0:n], op=ALU.pow)

        def conv(dst, src, R, key, step=256):
            for r0 in range(0, R, step):
                r1 = min(R, r0 + step)
                S.dma("pool", [], [(key, r0)], out=dst[r0:r1, :], in_=src[r0:r1, :], max_dma_last_dim=8192)

        win_chunks = list(((0, 512), (512, 512), (3072, 512), (3584, 512), (4864, 48)) + tuple((4912 + i * 512, 512) for i in range(8)) +
                          ((1024, 512), (1536, 512), (2048, 512), (2560, 512), (4096, 512), (4608, 256)))

        def win_pop(k=1):
            for _ in range(k):
                if win_chunks:
                    c0, n = win_chunks.pop(0)
                    S.dma("pool", [], [("W_IN", c0)], out=W_IN[:, c0:c0 + n], in_=w_in[:, c0:c0 + n], max_dma_last_dim=8192)

        win_pop(3)
        pending_conv = []

        def conv_later(dst, src, R, key, step=256):
            for r0 in range(0, R, step):
                r1 = min(R, r0 + step)
                pending_conv.append((dst, src, r0, r1, key))

        def conv_pop(n=1):
            for _ in range(n):
                if pending_conv:
                    dst, src, r0, r1, key = pending_conv.pop(0)
                    S.dma("pool", [], [(key, r0)], out=dst[r0:r1, :], in_=src[r0:r1, :], max_dma_last_dim=8192)

        conv_later(CW1, cmp_w1, 4096, "CW1", 1024); conv_later(CW2, cmp_w2, 512, "CW2", 512)
        conv_later(WPD, w_proj_diff, 1024, "WPD"); conv_later(WPN, w_proj_nsa, 1024, "WPN"); conv_later(WOUT, w_out, D, "WOUT")
        conv_later(WUP, w_mlp_up, D, "WUP"); conv_later(WDN, w_mlp_down, DFF, "WDN", 1024)
        conv_later(WPP, w_ple_proj, 256, "WPP"); conv_later(WPG, w_ple_gate, D, "WPG")

        with contextlib.ExitStack() as ps_:
            invf = sbt(ps_, "invf", [128, 8], F32)
            S.dma("sp", [], ["invf"], out=invf[:], in_=c_invf[:, :])
            for nm, src, nb, CS in (("a", posT_all, 32, CSA), ("o", posT_own, 16, CSO)):
                pi = sbt(ps_, "pi" + nm, [128, nb], I32)
                pf = sbt(ps_, "pf" + nm, [128, nb], F32)
                ang = sbt(ps_, "ang" + nm, [128, nb, 8], F32)
                kf = sbt(ps_, "kf" + nm, [128, nb, 8], F32)
                ki = sbt(ps_, "ki" + nm, [128, nb, 8], I32)
                r0 = sbt(ps_, "r0" + nm, [128, nb, 8], F32)
                r1 = sbt(ps_, "r1" + nm, [128, nb, 8], F32)
                mm = sbt(ps_, "mm" + nm, [128, nb, 8], F32)
                S.dma("sp", [], ["pi" + nm], out=pi[:], in_=src[:, :])
                S.op("dve", ["pi" + nm], ["pf" + nm], "tensor_copy", out=pf[:], in_=pi[:])
                S.op("dve", ["pf" + nm, "invf"], ["ang" + nm], "tensor_tensor", out=ang[:],
                     in0=pf[:].unsqueeze(2).broadcast_to([128, nb, 8]),
                     in1=invf[:].unsqueeze(1).broadcast_to([128, nb, 8]), op=ALU.mult)
                S.op("dve", ["ang" + nm], ["kf" + nm], "tensor_scalar", out=kf[:], in0=ang[:], scalar1=1.0 / TWO_PI,
                     scalar2=None, op0=ALU.mult)
                S.op("dve", ["kf" + nm], ["ki" + nm], "tensor_copy", out=ki[:], in_=kf[:])
                S.op("dve", ["ki" + nm], ["kf" + nm], "tensor_copy", out=kf[:], in_=ki[:])
                S.op("dve", ["kf" + nm, "ang" + nm], ["r0" + nm], "scalar_tensor_tensor", out=r0[:], in0=kf[:],
                     scalar=-TWO_PI, in1=ang[:], op0=ALU.mult, op1=ALU.add)
                for which, shift in ((1, 0.0), (0, math.pi / 2)):
                    S.op("dve", ["r0" + nm], ["r1" + nm], "tensor_scalar", out=r1[:], in0=r0[:], scalar1=shift,
                         scalar2=None, op0=ALU.add)
                    for thr, op_, add in ((math.pi, ALU.is_gt, -TWO_PI), (-math.pi, ALU.is_lt, TWO_PI)):
                        S.op("dve", ["r1" + nm], ["mm" + nm], "tensor_scalar", out=mm[:], in0=r1[:], scalar1=thr,
                             scalar2=add, op0=op_, op1=ALU.mult)
                        S.op("dve", ["r1" + nm, "mm" + nm], ["r1" + nm], "tensor_tensor", out=r1[:], in0=r1[:],
                             in1=mm[:], op=ALU.add)
                    S.op("dve", ["r1" + nm], ["r1" + nm], "tensor_scalar", out=r1[:], in0=r1[:], scalar1=math.pi,
                         scalar2=-math.pi, op0=ALU.min, op1=ALU.max)
                    S.op("act", ["r1" + nm], ["CS" + nm], "activation", out=CS[:, :, which * 8:which * 8 + 8],
                         in_=r1[:], func=AF.Sin)
                S.op("dve", ["CS" + nm], ["CS" + nm], "tensor_copy", out=CS[:, :, 16:24], in_=CS[:, :, 8:16])
                S.op("dve", ["CS" + nm], ["CS" + nm], "tensor_copy", out=CS[:, :, 24:32], in_=CS[:, :, 0:8])
            S.barrier()

        with contextlib.ExitStack() as pa:
            gmix = sbt(pa, "gmix", [128, D], F32)
            S.dma("sp", [], ["gmix"], out=gmix[:], in_=norm_mix.broadcast_to([128, D]))
            gq = {}
            for nm, src in (("dq", diff_q_norm), ("dk", diff_k_norm), ("nq", nsa_q_norm), ("nk", nsa_k_norm)):
                gq[nm] = sbt(pa, "g_" + nm, [128, 64], F32)
                S.dma("sp", [], ["g_" + nm], out=gq[nm][:], in_=src.broadcast_to([128, 64]))
            xt = [sbt(pa, f"xt{i}", [128, D], F32) for i in range(2)]
            hb = [sbt(pa, f"hb{i}", [128, D], BF16) for i in range(2)]
            hT = sbt(pa, "hT", [128, 16, 2048], BF16)
            wc = [sbt(pa, f"wc{i}", [128, 16, 512], BF16) for i in range(2)]
            ss = sbt(pa, "ss", [128, 16], F32)
            rs = sbt(pa, "rs", [128, 16], F32)
            NZ = 3
            sqt = [sbt(pa, f"sqt{i}", [128, 512], BF16) for i in range(NZ)]
            ssh = [sbt(pa, f"ssh{i}", [128, 8], F32) for i in range(NZ)]
            rsh = [sbt(pa, f"rsh{i}", [128, 8], F32) for i in range(NZ)]
            zn = [sbt(pa, f"zn{i}", [128, 512], F32) for i in range(NZ)]
            zr = [sbt(pa, f"zr{i}", [128, 512], BF16) for i in range(NZ)]
            zp = [sbt(pa, f"zp{i}", [128, 512], BF16) for i in range(NZ)]
            rt = [sbt(pa, f"rt{i}", [128, 2, 8, 16], F32) for i in range(NZ)]
            rot = [sbt(pa, f"rot{i}", [128, 8, 16], F32) for i in range(NZ)]
            stg = [sbt(pa, f"stg{i}", [128, 4, 2048], BF16) for i in range(2)]
            stgH = sbt(pa, "stgH", [128, 2, 2048], BF16)
            NV = 6
            vst = [sbt(pa, f"vst{i}", [128, 512], BF16) for i in range(NV)]
            dupb = [sbt(pa, f"dupb{i}", [128, 256], BF16) for i in range(NV)]
            dfr = []
            LAB = 2

            def defer(fn):
                dfr.append(fn)

            def run_deferred(keep):
                while len(dfr) > keep:
                    dfr.pop(0)()
            cnt = {"w": 0, "z": 0, "v": 0, "ps": 0, "tp": 0, "d": 0, "x": 0}

            hq = []
            cur = {"pass": 0}

            def build_hT_blk(xsrc, blk0, i):
                j = cnt["x"] % 2; cnt["x"] += 1
                xb = xt[j]; xk = f"xt{j}"
                S.dma("sp", [], [xk], out=xb[:], in_=xsrc[(blk0 + i) * 128:(blk0 + i + 1) * 128, :])
                hk = f"hb{j}"
                S.op("act", [xk], [hk, ("ss", i)], "activation", out=hb[j][:], in_=xb[:], func=AF.Square,
                     accum_out=ss[:, i:i + 1])
                rstd_from_ss(ss[:, i:i + 1], rs[:, i:i + 1], 1, [("ss", i)], [("rs", i)], 1.0 / D)
                S.op("dve", [xk, ("rs", i), "gmix"], [hk], "scalar_tensor_tensor", out=hb[j][:], in0=xb[:],
                     scalar=rs[:, i:i + 1], in1=gmix[:], op0=ALU.mult, op1=ALU.mult)
                for half in range(2):
                    bi = 6 + half
                    pb, pk = bank_bf(bi)
                    for k in range(8):
                        kk = half * 8 + k
                        S.op("pe", [hk, "ident"], [pk], "transpose", pb[:, k * 128:(k + 1) * 128],
                             hb[j][:, kk * 128:(kk + 1) * 128], ident[:])
                    dst = hT[:, half * 8:half * 8 + 8, i * 128:(i + 1) * 128]
                    srcv = pb.rearrange("p (k t) -> p k t", k=8)
                    if half == 0:
                        S.op("act", [pk], [("hT", i, 0)], "activation", out=dst, in_=srcv, func=AF.Copy)
                    else:
                        S.op("dve", [pk], [("hT", i, 1)], "tensor_copy", out=dst, in_=srcv)

            def queue_hT(xsrc, blk0, pass_id):
                hq.extend([(pass_id, xsrc, blk0, i) for i in range(16)])

            def pop_hT(n=1, force=False):
                for _ in range(n):
                    if hq and (force or hq[0][0] == cur["pass"]):
                        _, xsrc, blk0, i = hq.pop(0)
                        build_hT_blk(xsrc, blk0, i)

            def load_w(c0, n):
                win_pop(1)
                i = cnt["w"] % 2; cnt["w"] += 1
                S.dma("sp", [("W_IN", c0)], [f"wc{i}"], out=wc[i][:, :, 0:n],
                      in_=W_IN[:, c0:c0 + n].rearrange("(k p) c -> p k c", p=128))
                return wc[i], f"wc{i}"

            def mm_tm(w, wk, blk, n, coff=0):
                pop_hT(1)
                bi = cnt["ps"] % 4; cnt["ps"] += 1
                pb, pk = bank(bi)
                if cnt.get("conv") and cnt["conv"] <= 5:
                    conv_pop(1); cnt["conv"] += 1
                for k in range(16):
                    S.op("pe", [wk, ("hT", blk, 0), ("hT", blk, 1)], [pk], "matmul", pb[:, 0:n], lhsT=hT[:, k, blk * 128:(blk + 1) * 128],
                         rhs=w[:, k, coff:coff + n], start=(k == 0), stop=(k == 15))
                return pb, pk

            def normrope(pb, pk, c0, nh, gname, cs, csk, blk, want_plain=False):
                i = cnt["z"] % NZ; cnt["z"] += 1
                n = nh * 64
                pv = pb[:, c0:c0 + n]
                pv3 = pv.rearrange("p (h d) -> p h d", d=64)
                S.op("act", [pk], [f"sqt{i}"], "activation", out=sqt[i][:, 0:n], in_=pv, func=AF.Square)
                S.op("dve", [f"sqt{i}"], [f"ssh{i}"], "tensor_reduce", out=ssh[i][:, 0:nh],
                     in_=sqt[i][:, 0:n].rearrange("p (h d) -> p h d", d=64), axis=AX.X, op=ALU.add)
                rstd_from_ss(ssh[i][:, 0:nh], rsh[i][:, 0:nh], nh, [f"ssh{i}"], [f"rsh{i}"], 1.0 / 64)
                z3 = zn[i][:, 0:n].rearrange("p (h d) -> p h d", d=64)
                S.op("dve", [pk, "g_" + gname], [f"zn{i}"], "tensor_tensor", out=z3, in0=pv3,
                     in1=gq[gname][:].unsqueeze(1).broadcast_to([128, nh, 64]), op=ALU.mult)
                x12 = z3[:, :, 0:16]
                r = rt[i]; rk = f"rt{i}"
                ro = rot[i][:, 0:nh, :]
                S.op("pool", [f"zn{i}", csk], [(rk, 0)], "tensor_tensor", out=r[:, 0, 0:nh, :], in0=x12,
                     in1=cs[:, blk, 0:16].unsqueeze(1).broadcast_to([128, nh, 16]), op=ALU.mult)
                S.op("pool", [f"zn{i}", csk], [(rk, 1)], "tensor_tensor", out=r[:, 1, 0:nh, :], in0=x12,
                     in1=cs[:, blk, 16:32].unsqueeze(1).broadcast_to([128, nh, 16]), op=ALU.mult)
                S.op("dve", [(rk, 0)], [f"rot{i}"], "tensor_tensor", out=ro[:, :, 0:8],
                     in0=r[:, 0, 0:nh, 0:8], in1=r[:, 0, 0:nh, 8:16], op=ALU.subtract)
                S.op("dve", [(rk, 1), f"rot{i}"], [f"rot{i}"], "tensor_tensor", out=ro[:, :, 8:16],
                     in0=r[:, 1, 0:nh, 8:16], in1=r[:, 1, 0:nh, 0:8], op=ALU.add)
                rb = rsh[i][:, 0:nh].unsqueeze(2)
                zr3 = zr[i][:, 0:n].rearrange("p (h d) -> p h d", d=64)
                S.op("dve", [f"zn{i}", f"rsh{i}"], [f"zr{i}"], "tensor_tensor", out=zr3[:, :, 16:64], in0=z3[:, :, 16:64],
                     in1=rb.broadcast_to([128, nh, 48]), op=ALU.mult)
                S.op("dve", [f"rot{i}", f"rsh{i}", f"zr{i}"], [f"zr{i}"], "tensor_tensor", out=zr3[:, :, 0:16], in0=ro,
                     in1=rb.broadcast_to([128, nh, 16]), op=ALU.mult)
                if want_plain:
                    zp3 = zp[i][:, 0:n].rearrange("p (h d) -> p h d", d=64)
                    S.op("dve", [f"zn{i}", f"rsh{i}"], [f"zp{i}"], "tensor_tensor", out=zp3, in0=z3,
                         in1=rb.broadcast_to([128, nh, 64]), op=ALU.mult)
                return i

            def transp_to(src_ap, src_key, ncol128, dst_ap, dst_key):
                bi = 4 + cnt["tp"] % 2; cnt["tp"] += 1
                pb, pk = bank_bf(bi)
                for c in range(ncol128):
                    S.op("pe", [src_key, "ident"], [pk], "transpose", pb[:, c * 128:(c + 1) * 128],
                         src_ap[:, c * 128:(c + 1) * 128], ident[:])
                S.op("act", [pk], [dst_key], "activation", out=dst_ap,
                     in_=pb[:, 0:ncol128 * 128].rearrange("p (c t) -> p c t", c=ncol128), func=AF.Copy)

            sidx = {"i": 0}

            def new_stage():
                i = sidx["i"] % 2; sidx["i"] += 1
                return stg[i], f"stg{i}"

            queue_hT(x_own, 0, 0)
            pop_hT(3)
            for (c0, gname, dstR, dstP) in ((0, "dq", QDT, None), (512, "dq", QDT, None),
                                            (3072, "nq", NQRT, NQPT), (3584, "nq", NQRT, NQPT)):
                w, wk = load_w(c0, 512)
                sR, sRk = new_stage()
                if dstP is not None:
                    sP, sPk = new_stage()
                for blk in range(16):
                    pb, pk = mm_tm(w, wk, blk, 512)
                    i = normrope(pb, pk, 0, 8, gname, CSO, "CSo", blk, want_plain=dstP is not None)
                    defer(lambda i=i, blk=blk, sR=sR, sRk=sRk: transp_to(zr[i], f"zr{i}", 4, sR[:, :, blk * 128:(blk + 1) * 128], (sRk, blk)))
                    if dstP is not None:
                        defer(lambda i=i, blk=blk, sP=sP, sPk=sPk: transp_to(zp[i], f"zp{i}", 4, sP[:, :, blk * 128:(blk + 1) * 128], (sPk, blk)))
                    run_deferred(LAB * (2 if dstP is not None else 1))
                run_deferred(0)
                h0 = (c0 % 1024) // 128
                S.dma("pool", [(sRk, b) for b in range(16)], [("QN", c0, 0)], out=dstR[h0:h0 + 4].rearrange("h p t -> p h t"),
                      in_=sR[:])
                if dstP is not None:
                    S.dma("pool", [(sPk, b) for b in range(16)], [("QN", c0, 1)],
                          out=dstP[h0:h0 + 4].rearrange("h p t -> p h t"), in_=sP[:])
            w, wk = load_w(4864, 48)
            for blk in range(16):
                pb, pk = mm_tm(w, wk, blk, 48)
                S.op("act", [pk], [("gates", blk)], "activation", out=gates[:, blk, :], in_=pb[:, 0:48], func=AF.Sigmoid)
            for gi in range(2):
                for cc_ in range(4):
                    c0 = 4912 + gi * 2048 + cc_ * 512
                    w, wk = load_w(c0, 512)
                    last_fm = (gi == 1 and cc_ == 3)
                    if last_fm:
                        queue_hT(x_all, 0, 1)
                    for f in range(4):
                        for tg in range(4):
                            bi = cnt["ps"] % 4; cnt["ps"] += 1
                            pb, pk = bank(bi)
                            for k in range(16):
                                S.op("pe", [wk] + [("hT", tg * 4 + b, hh) for b in range(4) for hh in range(2)], [pk], "matmul", pb[:, :],
                                     lhsT=w[:, k, f * 128:(f + 1) * 128], rhs=hT[:, k, tg * 512:(tg + 1) * 512],
                                     start=(k == 0), stop=(k == 15))
                            vi = cnt["v"] % NV; cnt["v"] += 1
                            S.op("act", [pk], [f"vst{vi}"], "activation", out=vst[vi][:], in_=pb[:, :], func=AF.Sigmoid)
                            fr = cc_ * 512 + f * 128
                            S.dma("pool", [f"vst{vi}"], [("SGT", gi, fr, tg)], out=SGT[gi, fr:fr + 128, tg * 512:(tg + 1) * 512],
                                  in_=vst[vi][:])
                            if last_fm and f == 3:
                                pop_hT(4, force=True)
            for hp in range(2):
                t0 = hp * 2048
                cnt["conv"] = 1
                cur["pass"] = 1 + hp
                pop_hT(16, force=True)
                for c0 in (1024, 1536):
                    w, wk = load_w(c0, 512)
                    sR, sRk = new_stage()
                    for blk in range(16):
                        pb, pk = mm_tm(w, wk, blk, 512)
                        i = normrope(pb, pk, 0, 8, "dk", CSA, "CSa", hp * 16 + blk)
                        defer(lambda i=i, blk=blk, sR=sR, sRk=sRk: transp_to(zr[i], f"zr{i}", 4, sR[:, :, blk * 128:(blk + 1) * 128], (sRk, blk)))
                        run_deferred(LAB)
                    run_deferred(0)
                    h0 = (c0 - 1024) // 128
                    S.dma("pool", [(sRk, b) for b in range(16)], [("KDT", h0, hp)],
                          out=KDT[h0:h0 + 4, :, t0:t0 + 2048].rearrange("h p t -> p h t"), in_=sR[:])
                for c0 in (2048, 2560):
                    w, wk = load_w(c0, 512)
                    for blk in range(16):
                        pb, pk = mm_tm(w, wk, blk, 512)
                        vi = cnt["v"] % NV; cnt["v"] += 1
                        S.op("act", [pk], [f"vst{vi}"], "activation", out=vst[vi][:], in_=pb[:, :], func=AF.Copy)
                        S.dma("pool", [f"vst{vi}"], [("VD", c0, hp, blk)],
                              out=VD[t0 + blk * 128:t0 + (blk + 1) * 128, c0 - 2048:c0 - 2048 + 512], in_=vst[vi][:])
                w, wk = load_w(4096, 512)
                w2_, wk2 = load_w(4608, 256)
                if hp == 0:
                    queue_hT(x_all, 16, 2)
                sE, sEk = new_stage()
                sF, sFk = stgH, "stgH"
                for blk in range(16):
                    gb = hp * 16 + blk
                    pb, pk = mm_tm(w, wk, blk, 512)
                    vi = cnt["v"] % NV; cnt["v"] += 1
                    S.op("act", [pk], [f"vst{vi}"], "activation", out=vst[vi][:, 0:256], in_=pb[:, 0:256], func=AF.Copy)
                    S.op("act", [pk], [f"vst{vi}"], "activation", out=vst[vi][:, 256:384], in_=pb[:, 384:512], func=AF.Copy)
                    S.dma("pool", [f"vst{vi}"], [("VS", gb)], out=VS[gb * 128:(gb + 1) * 128, :], in_=vst[vi][:, 256:384])
                    defer(lambda vi=vi, blk=blk: transp_to(vst[vi], f"vst{vi}", 2, sE[:, 0:2, blk * 128:(blk + 1) * 128], (sEk, blk)))
                    i = normrope(pb, pk, 256, 2, "nk", CSA, "CSa", gb)
                    zi = cnt["d"] % NV; cnt["d"] += 1
                    dup = dupb[zi]; dk_ = f"dupb{zi}"
                    S.op("dve", [f"zr{i}"], [dk_], "tensor_copy", out=dup[:, 0:256].rearrange("p (g r d) -> p g r d", g=2, r=2),
                         in_=zr[i][:, 0:128].rearrange("p (g d) -> p g d", g=2).unsqueeze(2).broadcast_to([128, 2, 2, 64]))
                    defer(lambda dup=dup, dk_=dk_, blk=blk: transp_to(dup, dk_, 2, sE[:, 2:4, blk * 128:(blk + 1) * 128], (sEk, blk)))
                    pb2, pk2 = mm_tm(w2_, wk2, blk, 256)
                    vi = cnt["v"] % NV; cnt["v"] += 1
                    S.op("act", [pk2], [f"vst{vi}"], "activation", out=vst[vi][:, 0:128], in_=pb2[:, 128:256], func=AF.Copy)
                    S.dma("pool", [f"vst{vi}"], [("VW", gb)], out=VW[gb * 128:(gb + 1) * 128, :], in_=vst[vi][:, 0:128])
                    i = normrope(pb2, pk2, 0, 2, "nk", CSA, "CSa", gb)
                    zi = cnt["d"] % NV; cnt["d"] += 1
                    dup = dupb[zi]; dk_ = f"dupb{zi}"
                    S.op("dve", [f"zr{i}"], [dk_], "tensor_copy", out=dup[:, 0:256].rearrange("p (g r d) -> p g r d", g=2, r=2),
                         in_=zr[i][:, 0:128].rearrange("p (g d) -> p g d", g=2).unsqueeze(2).broadcast_to([128, 2, 2, 64]))
                    defer(lambda dup=dup, dk_=dk_, blk=blk: transp_to(dup, dk_, 2, sF[:, 0:2, blk * 128:(blk + 1) * 128], (sFk, blk)))
                    run_deferred(3)
                    pop_hT(1, force=True)
                run_deferred(0)
                rE = [(sEk, b) for b in range(16)]
                S.dma("pool", rE, [("KCRT", hp)], out=KCRT[:, t0:t0 + 2048], in_=sE[:, 0, :])
                S.dma("pool", rE, [("VCRT", hp)], out=VCRT[:, t0:t0 + 2048], in_=sE[:, 1, :])
                S.dma("pool", rE, [("KST", hp)], out=KST[:, :, t0:t0 + 2048].rearrange("g p t -> p g t"), in_=sE[:, 2:4, :])
                S.dma("pool", [(sFk, b) for b in range(16)], [("KWT", hp)],
                      out=KWT[:, :, t0:t0 + 2048].rearrange("g p t -> p g t"), in_=sF[:, 0:2, :])
            S.barrier()

        if stop_after == "PA":
            _finish(nc, S, out_d, es)
            return nc

        build_rest(nc, S, es, locals(), dbg=dbg, stop_after=stop_after)
    return nc


def _finish(nc, S, out_d, es):
    z = es.enter_context(nc.sbuf_tensor("zfin", [128, D], F32))
    S.op("dve", [], ["zfin"], "memset", z[:], 0.0)
    for i in range(16):
        S.dma("sp", ["zfin"], [("out", i)], out=out_d[i * 128:(i + 1) * 128, :], in_=z[:])
    S.barrier()


def build_rest(nc, S, es, L, dbg=False, stop_after=None):
    g_ = lambda n: L[n]
    sbt, bank, bank_bf, ident, gates, rstd_from_ss = (g_("sbt"), g_("bank"), g_("bank_bf"), g_("ident"), g_("gates"),
                                                      g_("rstd_from_ss"))
    out_d = g_("out_d")

    def finish():
        _finish(nc, S, out_d, es)

    KCT = sbt(es, "KCT", [128, 2, 2, 256], BF16)
    VCO = sbt(es, "VCO", [128, 2, 2, 129], BF16)
    S.op("dve", [], ["KCT"], "memset", KCT[:], 0.0)
    S.op("dve", [], ["VCO"], "memset", VCO[:], 0.0)

    CW1, CW2, KCRT, VCRT = g_("CW1"), g_("CW2"), g_("KCRT"), g_("VCRT")
    with contextlib.ExitStack() as pb_:
        kvT = [sbt(pb_, f"kvT{i}", [128, SEQ], BF16) for i in range(2)]
        S.dma("sp", [("KCRT", 0), ("KCRT", 1)], ["kvT0"], out=kvT[0][:], in_=KCRT[:, :])
        S.dma("sp", [("VCRT", 0), ("VCRT", 1)], ["kvT1"], out=kvT[1][:], in_=VCRT[:, :])
        W1 = [[sbt(pb_, f"W1_{kv}{g}", [128, 32, 256], BF16) for g in range(2)] for kv in range(2)]
        W2 = [sbt(pb_, f"W2_{kv}", [128, 2, 64], BF16) for kv in range(2)]
        for kv in range(2):
            for half in range(2):
                S.op("dve" if half == 0 else "pool", [], [(f"W1_{kv}", half, "z")], "memset", W1[kv][half][(1 - half) * 64:(2 - half) * 64, :, :], 0.0)
                for l4 in range(4):
                    S.dma("sp", [("CW1", r) for r in range(0, 4096, 1024)], [(f"W1_{kv}", half)], out=W1[kv][half][half * 64:(half + 1) * 64, l4 * 8:(l4 + 1) * 8, :],
                          in_=CW1[kv * 2048 + l4 * 512:kv * 2048 + (l4 + 1) * 512, :].rearrange("(l d) h -> d l h", d=64))
            S.dma("sp", [("CW2", 0)], [f"W2_{kv}"], out=W2[kv][:], in_=CW2[kv * 256:(kv + 1) * 256, :].rearrange("(c p) d -> p c d", p=128))
        posf = sbt(pb_, "posf", [128, 2, 32], F32)
        posb = sbt(pb_, "posb", [128, 2, 32], BF16)
        for half in range(2):
            S.dma("sp", [], ["posf"], out=posf[half * 64:(half + 1) * 64], in_=g_("cmp_posT")[:, :, :])
        S.op("dve", ["posf"], ["posb"], "tensor_copy", out=posb[:], in_=posf[:])
        gk = sbt(pb_, "gk", [128, 64], F32)
        S.dma("sp", [], ["gk"], out=gk[:], in_=g_("nsa_k_norm").broadcast_to([128, 64]))
        biasT = sbt(pb_, "biasT", [128, 4], F32)
        GH = sbt(pb_, "GH", [128, 8, 256], BF16)
        S.op("dve", [], [("GH", a, b, c) for a in range(2) for b in range(2) for c in range(2)], "memset", GH[:], 0.0)
        u = [sbt(pb_, f"u{i}", [128, 256], F32) for i in range(2)]
        u2 = [sbt(pb_, f"u2{i}", [128, 256], F32) for i in range(2)]
        sg = [sbt(pb_, f"sg{i}", [128, 256], F32) for i in range(2)]
        ssk = sbt(pb_, "ssk", [128, 4], F32)
        rsk = sbt(pb_, "rsk", [128, 4], F32)
        kcn2 = [sbt(pb_, f"kcn2{i}", [128, 256], BF16) for i in range(2)]
        for i in range(2):
            S.op("dve", [], [f"kcn2{i}"], "memset", kcn2[i][:], 0.0)
        junkb = sbt(pb_, "junkb", [128, 64], F32)
        S.op("dve", ["VCO"], ["VCO"], "memset", VCO[:, :, :, 64:65], 1.0)
        for g in range(2):
            S.dma("sp", ["VCO"], ["VCO"], out=VCO[:, g, :, 65:129], in_=g_("c_ovl")[:, :, :])
        pbb, pbk = bank(4)
        for kv in range(2):
            for hc in range(2):
                col = kv * 2 + hc
                for l in range(32):
                    S.op("pe", [(f"W1_{kv}", 0), (f"W1_{kv}", 0, "z"), "posb"], [pbk], "matmul", pbb[:, col:col + 1],
                         lhsT=W1[kv][0][:, l, hc * 128:(hc + 1) * 128], rhs=posb[:, kv, l:l + 1],
                         start=(l == 0), stop=(l == 31), skip_group_check=True)
        S.op("act", [pbk], ["biasT"], "activation", out=biasT[:], in_=pbb[:, 0:4], func=AF.Copy)
        n = 0
        for kv in range(2):
            for g in range(2):
                for hc in range(2):
                    pb, pk = bank(n % 4)
                    for l in range(32):
                        S.op("pe", [(f"W1_{kv}", g), (f"W1_{kv}", g, "z"), f"kvT{kv}"], [pk], "matmul", pb[:, 0:255],
                             lhsT=W1[kv][g][:, l, hc * 128:(hc + 1) * 128],
                             rhs=kvT[kv][:, l:l + 4065:16], start=(l == 0), stop=(l == 31))
                    i = n % 2
                    col = kv * 2 + hc
                    S.op("act", [pk, "biasT"], [f"u{i}"], "activation", out=u[i][:, 0:255], in_=pb[:, 0:255],
                         func=AF.Identity, bias=biasT[:, col:col + 1], scale=1.0)
                    S.op("pool", [f"u{i}"], [f"u2{i}"], "tensor_tensor", out=u2[i][:, 0:255], in0=u[i][:, 0:255],
                         in1=u[i][:, 0:255], op=ALU.mult)
                    S.op("pool", [f"u2{i}"], [f"u2{i}"], "tensor_scalar", out=u2[i][:, 0:255], in0=u2[i][:, 0:255],
                         scalar1=0.044715, scalar2=1.0, op0=ALU.mult, op1=ALU.add)
                    S.op("pool", [f"u2{i}", f"u{i}"], [f"u2{i}"], "tensor_tensor", out=u2[i][:, 0:255], in0=u2[i][:, 0:255],
                         in1=u[i][:, 0:255], op=ALU.mult)
                    S.op("act", [f"u2{i}"], [f"sg{i}"], "activation", out=sg[i][:, 0:255], in_=u2[i][:, 0:255],
                         func=AF.Sigmoid, scale=1.5957691216057308)
                    S.op("dve", [f"u{i}", f"sg{i}"], [("GH", kv, g, hc)], "tensor_tensor", out=GH[:, kv * 4 + g * 2 + hc, 0:255],
                         in0=u[i][:, 0:255], in1=sg[i][:, 0:255], op=ALU.mult)
                    n += 1
        n = 0
        for kv in range(2):
            for g in range(2):
                for c in range(2):
                    nn = 128 if c == 0 else 127
                    pb, pk = bank(n % 4)
                    for hc in range(2):
                        S.op("pe", [("GH", kv, g, hc), f"W2_{kv}"], [pk], "matmul", pb[0:nn, 0:64],
                             lhsT=GH[:, kv * 4 + g * 2 + hc, c * 128:c * 128 + nn], rhs=W2[kv][:, hc, :],
                             start=(hc == 0), stop=(hc == 1))
                    if kv == 0:
                        i = n % 2
                        col = g * 2 + c
                        S.op("act", [pk], ["junkb", ("ssk", col)], "activation", out=junkb[0:nn, :], in_=pb[0:nn, 0:64],
                             func=AF.Square, accum_out=ssk[0:nn, col:col + 1])
                        rstd_from_ss(ssk[0:nn, col:col + 1], rsk[0:nn, col:col + 1], 1, [("ssk", col)], [("rsk", col)], 1.0 / 64)
                        S.op("dve", [pk, ("rsk", col), "gk"], [f"kcn2{i}"], "scalar_tensor_tensor", out=kcn2[i][0:nn, 0:64],
                             in0=pb[0:nn, 0:64], scalar=rsk[0:nn, col:col + 1], in1=gk[0:nn, :], op0=ALU.mult, op1=ALU.mult)
                        S.op("dve", [f"kcn2{i}"], [f"kcn2{i}"], "tensor_copy", out=kcn2[i][0:nn, 192:256], in_=kcn2[i][0:nn, 0:64])
                        tb, tk = bank_bf(6 + i)
                        for par in range(2):
                            S.op("pe", [f"kcn2{i}", "ident"], [tk], "transpose", tb[:, par * 128:par * 128 + nn], kcn2[i][0:nn, par * 128:(par + 1) * 128],
                                 ident[0:nn, 0:nn])
                        S.op("act", [tk, "KCT"], ["KCT"], "activation", out=KCT[:, :, g, c * 128:c * 128 + nn],
                             in_=tb[:, 0:256].rearrange("p (a n) -> p a n", a=2)[:, :, 0:nn], func=AF.Copy)
                    else:
                        S.op("act", [pk, "VCO"], ["VCO"], "activation", out=VCO[0:nn, g, c, 0:64], in_=pb[0:nn, 0:64], func=AF.Copy)
                    n += 1
        if dbg:
            DKC = nc.dram_tensor("DKC", [128, 2, 2, 256], BF16, kind="ExternalOutput").ap()
            DVC = nc.dram_tensor("DVC", [128, 2, 2, 129], BF16, kind="ExternalOutput").ap()
            S.dma("sp", ["KCT"], ["DKC"], out=DKC[:, :, :, :], in_=KCT[:])
            S.dma("sp", ["VCO"], ["DVC"], out=DVC[:, :, :, :], in_=VCO[:])
        S.barrier()
    if stop_after == "PB":
        return finish()

    QDT, KDT, VD, YAT = g_("QDT"), g_("KDT"), g_("VD"), g_("YAT")
    with contextlib.ExitStack() as pc:
      if not PD_DEBUG["skip_pc"]:
        dm = sbt(pc, "dm", [128, 8, 512], BF16)
        S.dma("sp", [], ["dm"], out=dm[:], in_=g_("c_dmask")[:, :, :])
        lt = sbt(pc, "lt", [128, 256], F32)
        S.dma("sp", [], ["lt"], out=lt[:], in_=g_("diff_lambda").broadcast_to([128, 256]))
        prod = sbt(pc, "prod", [128, 128], F32)
        sums = sbt(pc, "sums", [128, 2], F32)
        ex = sbt(pc, "ex", [128, 2], F32)
        nlam = sbt(pc, "nlam", [128, 1], F32)
        S.op("dve", ["lt"], ["prod"], "tensor_tensor", out=prod[:].rearrange("p (a d) -> p a d", a=2),
             in0=lt[:].rearrange("p (a b d) -> p a b d", a=2, b=2)[:, :, 0, :],
             in1=lt[:].rearrange("p (a b d) -> p a b d", a=2, b=2)[:, :, 1, :], op=ALU.mult)
        S.op("dve", ["prod"], ["sums"], "tensor_reduce", out=sums[:], in_=prod[:].rearrange("p (a d) -> p a d", a=2),
             axis=AX.X, op=ALU.add)
        S.op("act", ["sums"], ["ex"], "activation", out=ex[:], in_=sums[:], func=AF.Exp)
        S.op("dve", ["ex"], ["nlam"], "tensor_tensor", out=nlam[:], in0=ex[:, 1:2], in1=ex[:, 0:1], op=ALU.subtract)
        S.op("dve", ["nlam"], ["nlam"], "tensor_scalar", out=nlam[:], in0=nlam[:], scalar1=-0.2, scalar2=None, op0=ALU.add)
        gsub = sbt(pc, "gsub", [128, 128], F32)
        S.dma("sp", [], ["gsub"], out=gsub[:], in_=g_("diff_subln").broadcast_to([128, 128]))
        S.op("dve", ["gsub"], ["gsub"], "tensor_scalar", out=gsub[:], in0=gsub[:], scalar1=0.8, scalar2=None, op0=ALU.mult)
        KT = [sbt(pc, f"KT{i}", [128, SEQ], BF16) for i in range(2)]
        VT = [sbt(pc, f"VT{i}", [128, 32, 129], BF16) for i in range(2)]
        QT = [[sbt(pc, f"QT{i}{c}", [128, NOWN], BF16) for c in range(2)] for i in range(2)]
        for i in range(2):
            for c in range(2):
                S.op("dve", [], [("QTz", i, c)], "memset", QT[i][c][(1 - c) * 64:(2 - c) * 64, :], 0.0)
        YAs = [sbt(pc, f"YAs{i}", [128, NOWN], BF16) for i in range(2)]
        E = [sbt(pc, f"E{i}", [128, 512], BF16) for i in range(4)]
        rd = [sbt(pc, f"rd{i}", [128, 4], F32) for i in range(2)]
        t0_ = [sbt(pc, f"t0_{i}", [128, 128], F32) for i in range(2)]
        o_ = [sbt(pc, f"o_{i}", [128, 128], F32) for i in range(2)]
        yb_ = [sbt(pc, f"yb_{i}", [128, 128], BF16) for i in range(2)]
        junkc = sbt(pc, "junkc", [128, 128], BF16)
        for i in range(2):
            S.op("dve", [], [("VT1", i)], "memset", VT[i][:, :, 128:129], 1.0)
        nf = 0
        LA = 3

        def pc_loads(h):
            i = h % 2
            S.dma("sp", [("KDT", (h // 4) * 4, 0), ("KDT", (h // 4) * 4, 1)], [("KT", i)], out=KT[i][:], in_=KDT[h, :, :])
            for q4 in range(8):
                S.dma("sp", [("VD", c0, hp, b) for c0 in (2048, 2560) for hp in range(2) for b in range(16)], [("VT", i, q4)],
                      out=VT[i][:, q4 * 4:(q4 + 1) * 4, 0:128],
                      in_=VD[q4 * 512:(q4 + 1) * 512, h * 128:(h + 1) * 128].rearrange("(kb p) d -> p kb d", p=128))
            for c in range(2):
                S.dma("sp", [("QN", 0, 0), ("QN", 512, 0)], [("QT", i, c)], out=QT[i][c][c * 64:(c + 1) * 64, :], in_=QDT[h, c * 64:(c + 1) * 64, :])

        def pc_front(u, n):
            h, G, kb, c = u
            i = h % 2
            r = kb - 8 * G
            pb, pk = bank(n % 4)
            e = E[n % 4]; ek = f"E{n % 4}"
            q_lo = max(r, 0) // 2 * 128
            S.op("pe", [("KT", i), ("QT", i, c), ("QTz", i, c)], [pk], "matmul", pb[:, q_lo:512], lhsT=KT[i][:, kb * 128:(kb + 1) * 128],
                 rhs=QT[i][c][:, G * 512 + q_lo:(G + 1) * 512], start=True, stop=True)
            S.op("act", [pk], [ek], "activation", out=e[:, q_lo:512], in_=pb[:, q_lo:512], func=AF.Exp, scale=0.125)
            if r >= 0:
                S.op("dve", [ek, "dm"], [ek], "tensor_tensor", out=e[:, q_lo:512], in0=e[:, q_lo:512], in1=dm[:, r, q_lo:512], op=ALU.mult)

        def pc_back(u, n):
            h, G, kb, c = u
            i = h % 2
            r = kb - 8 * G
            e = E[n % 4]; ek = f"E{n % 4}"
            for t in range(4):
                if r > 2 * t + 1:
                    continue
                a = c * 4 + t
                ab, ak = bank(4 + a // 3)
                off = (a % 3) * 129
                S.op("pe", [ek, ("VT", i, kb // 4), ("VT1", i)], [ak], "matmul", ab[:, off:off + 129],
                     lhsT=e[:, t * 128:(t + 1) * 128], rhs=VT[i][:, kb, :], start=(kb == 0 and a % 3 == 0),
                     stop=(kb == 8 * G + 2 * t + 1), skip_group_check=True)

        def pc_final(h, G):
            nonlocal nf
            i = h % 2
            for t in range(4):
                f = nf % 2; nf += 1
                a0, a1 = t, 4 + t
                b0, k0 = bank(4 + a0 // 3); b1, k1 = bank(4 + a1 // 3)
                O0 = b0[:, (a0 % 3) * 129:(a0 % 3) * 129 + 129]
                O1 = b1[:, (a1 % 3) * 129:(a1 % 3) * 129 + 129]
                rk = f"rd{f}"
                S.op("dve", [k0], [(rk, 0)], "reciprocal", out=rd[f][:, 0:1], in_=O0[:, 128:129])
                S.op("dve", [k1], [(rk, 1)], "reciprocal", out=rd[f][:, 1:2], in_=O1[:, 128:129])
                S.op("dve", [(rk, 1), "nlam"], [(rk, 2)], "tensor_tensor", out=rd[f][:, 2:3], in0=rd[f][:, 1:2], in1=nlam[:], op=ALU.mult)
                S.op("dve", [k0, (rk, 0)], [f"t0_{f}"], "tensor_scalar", out=t0_[f][:], in0=O0[:, 0:128], scalar1=rd[f][:, 0:1],
                     scalar2=None, op0=ALU.mult)
                S.op("dve", [k1, (rk, 2), f"t0_{f}"], [f"o_{f}"], "scalar_tensor_tensor", out=o_[f][:], in0=O1[:, 0:128],
                     scalar=rd[f][:, 2:3], in1=t0_[f][:], op0=ALU.mult, op1=ALU.add)
                S.op("act", [f"o_{f}"], ["junkc", (rk, 3)], "activation", out=junkc[:], in_=o_[f][:], func=AF.Square,
                     accum_out=rd[f][:, 3:4])
                rstd_from_ss(rd[f][:, 3:4], rd[f][:, 3:4], 1, [(rk, 3)], [(rk, 3)], 1.0 / 128)
                S.op("dve", [f"o_{f}", (rk, 3), "gsub"], [f"yb_{f}"], "scalar_tensor_tensor", out=yb_[f][:], in0=o_[f][:],
                     scalar=rd[f][:, 3:4], in1=gsub[:], op0=ALU.mult, op1=ALU.mult)
                tb, tk = bank_bf(7)
                S.op("pe", [f"yb_{f}", "ident"], [tk], "transpose", tb[:, 0:128], yb_[f][:], ident[:])
                q0 = (G * 4 + t) * 128
                S.op("act", [tk], [("YAs", i, G * 4 + t)], "activation", out=YAs[i][:, q0:q0 + 128], in_=tb[:, 0:128], func=AF.Copy)
            if G == 3:
                S.dma("pool", [("YAs", i, b) for b in range(16)], [("YAT", h)], out=YAT[h, :, :], in_=YAs[i][:])

        units = [(h, G, kb, c) for h in range(8) for G in range(4) for kb in range(8 * G + 8) for c in range(2)]
        pc_loads(0)
        for n in range(len(units) + LA):
            if n < len(units):
                u = units[n]
                if u[1] == 0 and u[2] == 2 and u[3] == 0 and u[0] + 1 < 8:
                    pc_loads(u[0] + 1)
                pc_front(u, n)
                if n % 26 == 13:
                    L["conv_pop"](1)
            if n >= LA:
                ub = units[n - LA]
                pc_back(ub, n - LA)
                if ub[2] == 8 * ub[1] + 7 and ub[3] == 1:
                    pc_final(ub[0], ub[1])
        pass
        S.barrier()
    if stop_after == "PC":
        return finish()

    NQRT, NQPT, KST, KWT, VS, VW, YBT = g_("NQRT"), g_("NQPT"), g_("KST"), g_("KWT"), g_("VS"), g_("VW"), g_("YBT")
    with contextlib.ExitStack() as pd:
        m128 = sbt(pd, "m128", [128, 2, 8, 128], BF16)
        mc = sbt(pd, "mc", [128, 16, 2, 128], BF16)
        valid = sbt(pd, "valid", [128, 16, 64], F32)
        addc = sbt(pd, "addc", [128, 16, 64], F32)
        eexp = sbt(pd, "eexp", [128, 32, 128], BF16)
        S.op("dve", [], ["eexpz"], "memset", eexp[64:128, :, :], 0.0)
        S.dma("sp", [], ["m128"], out=m128[:], in_=g_("c_m128")[:, :, :, :])
        S.dma("sp", [], ["mc"], out=mc[:], in_=g_("c_mc")[:, :, :, :])
        S.dma("sp", [], ["valid"], out=valid[:], in_=g_("c_valid")[:, :, :])
        S.dma("sp", [], ["addc"], out=addc[:], in_=g_("c_addc")[:, :, :])
        S.dma("sp", [], ["eexp"], out=eexp[0:64, :, :], in_=g_("c_eexp")[:, :, :])
        KS = [sbt(pd, f"KS{par}", [128, SEQ], BF16) for par in range(2)]
        KW = [sbt(pd, f"KW{par}", [128, SEQ], BF16) for par in range(2)]
        for par in range(2):
            S.op("dve", [], [("KSz", par)], "memset", KS[par][(1 - par) * 64:(2 - par) * 64, :], 0.0)
            S.op("pool", [], [("KWz", par)], "memset", KW[par][(1 - par) * 64:(2 - par) * 64, :], 0.0)
        VS1 = sbt(pd, "VS1", [128, 32, 65], BF16)
        VW1 = sbt(pd, "VW1", [128, 32, 65], BF16)
        QR = sbt(pd, "QR", [128, 4, NOWN], BF16)
        QP = sbt(pd, "QP", [128, 4, NOWN], BF16)
        ecmp = sbt(pd, "ecmp", [128, 2, 2, 512], BF16)
        es_ = [sbt(pd, f"es{i}", [128, 512], BF16) for i in range(4)]
        M4 = [sbt(pd, f"M4{i}", [128, 512], BF16) for i in range(2)]
        Y = [sbt(pd, f"Y{i}", [128, 8, 64], F32) for i in range(2)]
        Yb = sbt(pd, "Yb", [128, 512], BF16)
        IMP = sbt(pd, "IMP", [128, 64], F32)
        tmpI = sbt(pd, "tmpI", [128, 8, 64], F32)
        tmpY = sbt(pd, "tmpY", [128, 8, 64], F32)
        sc = sbt(pd, "sc", [128, 64], F32)
        sc2 = sbt(pd, "sc2", [128, 64], F32)
        m8a = sbt(pd, "m8a", [128, 8], F32)
        m8b = sbt(pd, "m8b", [128, 8], F32)
        selb = sbt(pd, "selb", [128, 128], BF16)
        selT = [sbt(pd, f"selT{i}", [128, 128], BF16) for i in range(2)]
        S.op("dve", [], ["selbz"], "memset", selb[:, 64:128], 0.0)
        den8 = sbt(pd, "den8", [128, 8], F32)
        rd8 = sbt(pd, "rd8", [128, 8], F32)
        coef = sbt(pd, "coef", [128, 8], F32)
        ystg = sbt(pd, "ystg", [128, 4, NOWN], BF16)
        S.op("dve", [], ["VS1o"], "memset", VS1[:, :, 64:65], 1.0)
        S.op("dve", [], ["VW1o"], "memset", VW1[:, :, 64:65], 1.0)
        nsc = 0
        for g in range(2):
            for par in range(2):
                S.dma("sp", [("KST", 0), ("KST", 1)], [("KS", par)], out=KS[par][par * 64:(par + 1) * 64, :], in_=KST[g, par * 64:(par + 1) * 64, :])
                S.dma("sp", [("KWT", 0), ("KWT", 1)], [("KW", par)], out=KW[par][par * 64:(par + 1) * 64, :], in_=KWT[g, par * 64:(par + 1) * 64, :])
            for q4 in range(4):
                S.dma("sp", [("VS", b) for b in range(32)], [("VS1", q4)], out=VS1[:, q4 * 8:(q4 + 1) * 8, 0:64],
                      in_=VS[q4 * 1024:(q4 + 1) * 1024, g * 64:(g + 1) * 64].rearrange("(kb p) d -> p kb d", p=128))
                S.dma("sp", [("VW", b) for b in range(32)], [("VW1", q4)], out=VW1[:, q4 * 8:(q4 + 1) * 8, 0:64],
                      in_=VW[q4 * 1024:(q4 + 1) * 1024, g * 64:(g + 1) * 64].rearrange("(kb p) d -> p kb d", p=128))
            S.dma("sp", [("QN", 3072, 0), ("QN", 3584, 0)], ["QR"], out=QR[:], in_=NQRT[g * 4:(g + 1) * 4].rearrange("h p t -> p h t"))
            S.dma("sp", [("QN", 3072, 1), ("QN", 3584, 1)], ["QP"], out=QP[:], in_=NQPT[g * 4:(g + 1) * 4].rearrange("h p t -> p h t"))
            def gv_of(j):
                return gates[:, j, g * 24:(g + 1) * 24].rearrange("p (r b) -> p r b", b=3)

            def fin_branch(j, nper, width, branch, first):
                Yj = Y[j % 2]; yk = f"Y{j % 2}"
                nb_ = (8 + nper - 1) // nper
                for b in range(nb_):
                    ab, ak = bank(4 + b)
                    nh = min(nper, 8 - b * nper)
                    S.op("dve", [ak], ["den8"], "tensor_scalar", out=den8[:, b * nper:b * nper + nh].unsqueeze(2),
                         in0=ab[:, 0:nh * width].rearrange("p (r c) -> p r c", c=width)[:, :, 64:65], scalar1=1e-30,
                         scalar2=None, op0=ALU.max)
                S.op("dve", ["den8"], ["rd8"], "reciprocal", out=rd8[:], in_=den8[:])
                S.op("dve", ["rd8", ("gates", j)], ["coef"], "tensor_tensor", out=coef[:], in0=rd8[:], in1=gv_of(j)[:, :, branch], op=ALU.mult)
                for b in range(nb_):
                    ab, ak = bank(4 + b)
                    r0 = b * nper
                    nh = min(nper, 8 - r0)
                    accv = ab[:, 0:nh * width].rearrange("p (r c) -> p r c", c=width)[:, :, 0:64]
                    cb = coef[:, r0:r0 + nh].unsqueeze(2).broadcast_to([128, nh, 64])
                    ykeys = [(yk, r) for r in range(r0, r0 + nh)]
                    if first:
                        S.op("dve", [ak, "coef"], ykeys, "tensor_tensor", out=Yj[:, r0:r0 + nh, :], in0=accv, in1=cb, op=ALU.mult)
                    else:
                        tkeys = [("tmpY", r) for r in range(r0, r0 + nh)]
                        S.op("dve", [ak, "coef"], tkeys, "tensor_tensor", out=tmpY[:, r0:r0 + nh, :], in0=accv, in1=cb, op=ALU.mult)
                        S.op("pool", tkeys + ykeys, ykeys, "tensor_tensor", out=Yj[:, r0:r0 + nh, :], in0=Yj[:, r0:r0 + nh, :],
                             in1=tmpY[:, r0:r0 + nh, :], op=ALU.add)

            def chain(j):
                nonlocal nsc
                q0 = j * 128
                for c in range(2):
                    nn = 128 if c == 0 else 127
                    for par in range(2):
                        pb, pk = bank(nsc % 4); nsc += 1
                        S.op("pe", ["KCT", "QP"], [pk], "matmul", pb[0:nn, :].rearrange("p (a q) -> p a q", a=4),
                             lhsT=KCT[:, par, g, c * 128:c * 128 + nn], rhs=QP[:, :, q0:q0 + 128],
                             start=True, stop=True)
                        S.op("act", [pk], [("ecmp", c, par)], "activation", out=ecmp[0:nn, c, par, :], in_=pb[0:nn, :], func=AF.Exp, scale=0.125)
                        ev = ecmp[0:nn, c, par, :].rearrange("p (a q) -> p a q", a=4)
                        S.op("dve", [("ecmp", c, par), "mc"], [("ecmp", c, par)], "tensor_tensor", out=ev, in0=ev,
                             in1=mc[0:nn, j, c, :].unsqueeze(1).broadcast_to([nn, 4, 128]), op=ALU.mult)
                for r in range(8):
                    ii, par = r // 2, r % 2
                    ab, ak = bank(4 + r // 3)
                    off = (r % 3) * 129
                    for c in range(2):
                        nn = 128 if c == 0 else 127
                        S.op("pe", [("ecmp", c, par), "VCO"], [ak], "matmul", ab[:, off:off + 129],
                             lhsT=ecmp[0:nn, c, par, ii * 128:(ii + 1) * 128], rhs=VCO[0:nn, g, c, :],
                             start=(c == 0 and r % 3 == 0), stop=(c == 1), skip_group_check=True)
                fin_branch(j, 3, 129, 0, True)
                for b in range(3):
                    ab, ak = bank(4 + b)
                    r0 = b * 3
                    nh = min(3, 8 - r0)
                    S.op("dve", [ak, "rd8"], [("tmpI", b)], "tensor_tensor", out=tmpI[:, r0:r0 + nh, :],
                         in0=ab[:, 0:nh * 129].rearrange("p (r c) -> p r c", c=129)[:, :, 65:129],
                         in1=rd8[:, r0:r0 + nh].unsqueeze(2).broadcast_to([128, nh, 64]), op=ALU.mult)
                S.op("dve", [("tmpI", b) for b in range(3)], ["IMP"], "tensor_reduce", out=IMP[:], in_=tmpI[:].rearrange("p r j -> p j r"),
                     axis=AX.X, op=ALU.add)
                S.op("dve", ["IMP", "valid"], ["sc"], "tensor_tensor", out=sc[:], in0=IMP[:], in1=valid[:, j, :], op=ALU.mult)
                S.op("dve", ["sc", "addc"], ["sc"], "tensor_tensor", out=sc[:], in0=sc[:], in1=addc[:, j, :], op=ALU.add)
                S.op("dve", ["sc"], ["m8a"], "max", out=m8a[:], in_=sc[:])
                S.op("dve", ["sc", "m8a"], ["sc2"], "match_replace", out=sc2[:], in_to_replace=m8a[:], in_values=sc[:], imm_value=-3.0)
                S.op("dve", ["sc2"], ["m8b"], "max", out=m8b[:], in_=sc2[:])
                S.op("dve", ["sc", "m8b"], ["selb"], "tensor_scalar", out=selb[:, 0:64], in0=sc[:], scalar1=m8b[:, 7:8], scalar2=None, op0=ALU.is_ge)
                tb, tk = bank_bf(7)
                S.op("pe", ["selb", "selbz", "ident"], [tk], "transpose", tb[:, 0:128], selb[:], ident[:])
                S.op("act", [tk], [f"selT{j % 2}"], "activation", out=selT[j % 2][:], in_=tb[:, 0:128], func=AF.Copy)

            def maskgen(j, kb4):
                jp = j & 1
                nkb = 2 * j + 2
                nb = min(4, nkb - kb4)
                mi = (kb4 // 4) % 2
                pm, pmk = bank(6)
                for q_ in range(nb):
                    S.op("pe", ["eexp", "eexpz", f"selT{j % 2}"], [pmk], "matmul", pm[:, q_ * 128:(q_ + 1) * 128], lhsT=eexp[:, kb4 + q_, :],
                         rhs=selT[j % 2][:, :], start=True, stop=True, skip_group_check=True)
                ncaus = sum(1 for q_ in range(nb) if kb4 + q_ >= 2 * j)
                nplain = nb - ncaus
                if nplain > 0:
                    S.op("act", [pmk], [(f"M4{mi}", q2) for q2 in range(nplain)], "activation", out=M4[mi][:, 0:nplain * 128],
                         in_=pm[:, 0:nplain * 128], func=AF.Copy)
                for q_ in range(nplain, nb):
                    kb = kb4 + q_
                    S.op("dve", [pmk, "m128"], [(f"M4{mi}", q_)], "tensor_tensor", out=M4[mi][:, q_ * 128:(q_ + 1) * 128],
                         in0=pm[:, q_ * 128:(q_ + 1) * 128], in1=m128[:, jp, 6 + (kb - 2 * j), :], op=ALU.mult)

            def attn_units(j, kbs, Kt, Kk, V1, Vk, maskfn, pre=None):
                nonlocal nsc
                q0 = j * 128
                units = [(kb, par) for kb in kbs for par in range(2)]
                base = nsc
                nsc += len(units)
                LA_ = 3
                for n in range(len(units) + LA_):
                    if n < len(units):
                        kb, par = units[n]
                        if pre is not None and par == 0:
                            pre(kb)
                        mk_ = maskfn(kb)
                        if (base + n) % 26 == 0:
                            L["conv_pop"](1)
                        pb, pk = bank((base + n) % 4)
                        e = es_[(base + n) % 4]; ek = f"es{(base + n) % 4}"
                        S.op("pe", [(Kk, par), (Kk + "z", par), "QR"], [pk], "matmul", pb[:, :].rearrange("p (a q) -> p a q", a=4),
                             lhsT=Kt[par][:, kb * 128:(kb + 1) * 128], rhs=QR[:, :, q0:q0 + 128], start=True, stop=True)
                        S.op("act", [pk], [ek], "activation", out=e[:], in_=pb[:, :], func=AF.Exp, scale=0.125)
                        if mk_ is not None:
                            map_, mkey = mk_
                            ev = e[:].rearrange("p (a q) -> p a q", a=4)
                            S.op("dve", [ek, mkey], [ek], "tensor_tensor", out=ev, in0=ev,
                                 in1=map_.unsqueeze(1).broadcast_to([128, 4, 128]), op=ALU.mult)
                    if n >= LA_:
                        kb, par = units[n - LA_]
                        e = es_[(base + n - LA_) % 4]; ek = f"es{(base + n - LA_) % 4}"
                        for ii in range(4):
                            r = 2 * ii + par
                            ab, ak = bank(4 + r // 4)
                            off = (r % 4) * 65
                            S.op("pe", [ek, (Vk, kb // 8), Vk + "o"], [ak], "matmul", ab[:, off:off + 65], lhsT=e[:, ii * 128:(ii + 1) * 128],
                                 rhs=V1[:, kb, :], start=(kb == kbs[0] and r % 4 == 0), stop=(kb == kbs[-1]), skip_group_check=True)

            chain(0)
            for j in range(16):
                jp = j & 1
                q0 = j * 128
                if j + 1 < 16:
                    chain(j + 1)
                nkb = 2 * j + 2
                if PD_DEBUG["sel"]:
                    maskgen(j, 0)

                    def pre(kb, j=j, nkb=nkb):
                        if kb % 4 == 0 and kb + 4 < nkb:
                            maskgen(j, kb + 4)

                    def smask(kb):
                        return (M4[(kb // 4) % 2][:, (kb % 4) * 128:(kb % 4 + 1) * 128], (f"M4{(kb // 4) % 2}", kb % 4))

                    attn_units(j, list(range(nkb)), KS, "KS", VS1, "VS1", smask, pre)
                    fin_branch(j, 4, 65, 1, False)
                if PD_DEBUG["win"]:
                    kbs = [kb for kb in range(2 * j - 4, 2 * j + 2) if kb >= 0]

                    def wmask(kb, j=j, jp=jp):
                        idx = kb - (2 * j - 4)
                        if idx in (2, 3):
                            return None
                        return (m128[:, jp, idx, :], "m128")

                    attn_units(j, kbs, KW, "KW", VW1, "VW1", wmask)
                    fin_branch(j, 4, 65, 2, False)
                yk = f"Y{j % 2}"
                S.op("act", [(yk, r) for r in range(8)], ["Yb"], "activation", out=Yb[:], in_=Y[j % 2][:].rearrange("p r d -> p (r d)"), func=AF.Copy)
                tb, tk = bank_bf(7)
                for ii in range(4):
                    S.op("pe", ["Yb", "ident"], [tk], "transpose", tb[:, ii * 128:(ii + 1) * 128], Yb[:, ii * 128:(ii + 1) * 128], ident[:])
                S.op("act", [tk], [("ystg", j)], "activation", out=ystg[:, :, q0:q0 + 128],
                     in_=tb[:, 0:512].rearrange("p (c t) -> p c t", c=4), func=AF.Copy)
            if g == 1:
                L["conv_pop"](len(L["pending_conv"]))
            S.dma("pool", [("ystg", j) for j in range(16)], [("YBT", g)], out=YBT[g * 4:(g + 1) * 4].rearrange("h p t -> p h t"), in_=ystg[:])
        S.barrier()
    if stop_after == "PD":
        return finish()

    build_rowlocal(nc, S, es, L)


def build_rowlocal(nc, S, es, L):
    g_ = lambda n: L[n]
    sbt, bank, bank_bf, ident, rstd_from_ss = g_("sbt"), g_("bank"), g_("bank_bf"), g_("ident"), g_("rstd_from_ss")
    x_own, p_own, out_d = g_("x_own"), g_("p_own"), g_("out_d")
    YAT, YBT, SGT = g_("YAT"), g_("YBT"), g_("SGT")
    WPD, WPN, WOUT, WUP, WDN, WPP, WPG = g_("WPD"), g_("WPN"), g_("WOUT"), g_("WUP"), g_("WDN"), g_("WPP"), g_("WPG")
    with contextlib.ExitStack() as pe_:
        gvec = sbt(pe_, "gvec", [128, D], F32)
        aT = sbt(pe_, "aT", [128, 64, 512], BF16)
        wbig = [sbt(pe_, f"wbig{i}", [128, 16, 512], BF16) for i in range(2)]
        x1 = sbt(pe_, "x1", [128, 4, D], F32)
        hT2 = sbt(pe_, "hT2", [128, 16, 512], BF16)
        wpp = sbt(pe_, "wpp", [128, 2, D], BF16)
        hb2s = [sbt(pe_, f"hb2_{i}", [128, D], BF16) for i in range(2)]
        sgt = [sbt(pe_, f"sgt{i}", [128, 2, 512], BF16) for i in range(2)]
        xin = [sbt(pe_, f"xin{i}", [128, 512], F32) for i in range(2)]
        rl = [sbt(pe_, f"rl{i}", [128, 512], F32) for i in range(2)]
        gt = [sbt(pe_, f"gt{i}", [128, 512], F32) for i in range(2)]
        t1, t2 = gt[0], gt[1]
        ev = [sbt(pe_, f"ev{i}", [128, 512], F32) for i in range(2)]
        ss1 = sbt(pe_, "ss1", [128, 8], F32)
        rs1 = sbt(pe_, "rs1", [128, 8], F32)
        ssE = sbt(pe_, "ssE", [128, 16], F32)
        ssE4 = sbt(pe_, "ssE4", [128, 4], F32)
        rsE = sbt(pe_, "rsE", [128, 4], F32)
        pt = sbt(pe_, "pt", [128, 256], F32)
        ptb = sbt(pe_, "ptb", [128, 256], BF16)
        pT = sbt(pe_, "pT", [128, 2, 512], BF16)
        S.dma("sp", [("WPP", 0)], ["wpp"], out=wpp[:], in_=WPP[:, :].rearrange("(k p) c -> p k c", p=128))
        cn = {"w": 0, "ps": 0, "sg": 0, "x": 0, "r": 0, "g": 0, "e": 0, "h": 0}

        def nextw():
            i = cn["w"] % 2; cn["w"] += 1
            return wbig[i], f"wbig{i}"

        def nextbank():
            b = cn["ps"] % 4; cn["ps"] += 1
            return bank(b)

        def to_hT2(blk, hb2, hbk):
            for half in range(2):
                pb, pk = bank_bf(6 + half)
                for k in range(8):
                    kk = half * 8 + k
                    S.op("pe", [hbk, "ident"], [pk], "transpose", pb[:, k * 128:(k + 1) * 128], hb2[:, kk * 128:(kk + 1) * 128], ident[:])
                dst = hT2[:, half * 8:half * 8 + 8, blk * 128:(blk + 1) * 128]
                srcv = pb.rearrange("p (k t) -> p k t", k=8)
                if half == 0:
                    S.op("act", [pk], [("hT2", blk, 0)], "activation", out=dst, in_=srcv, func=AF.Copy)
                else:
                    S.op("dve", [pk], [("hT2", blk, 1)], "tensor_copy", out=dst, in_=srcv)

        hT2keys = [("hT2", b, h) for b in range(4) for h in range(2)]
        for tt in range(4):
            tok0 = tt * 512
            S.dma("sp", [("YAT", h) for h in range(8)], [("aT", k) for k in range(8)], out=aT[:, 0:8, :],
                  in_=YAT[:, :, tok0:tok0 + 512].rearrange("h p t -> p h t"))
            S.dma("sp", [("YBT", 0), ("YBT", 1)], [("aT", k) for k in range(8, 16)], out=aT[:, 8:16, :],
                  in_=YBT[:, :, tok0:tok0 + 512].rearrange("h p t -> p h t"))
            for cc in range(4):
                w, wk = nextw()
                S.dma("sp", [("WPD", r) for r in range(0, 1024, 256)], [(wk, 0)], out=w[:, 0:8, :],
                      in_=WPD[:, cc * 512:(cc + 1) * 512].rearrange("(k p) c -> p k c", p=128))
                S.dma("sp", [("WPN", r) for r in range(0, 1024, 256)], [(wk, 1)], out=w[:, 8:16, :],
                      in_=WPN[:, cc * 512:(cc + 1) * 512].rearrange("(k p) c -> p k c", p=128))
                for f in range(4):
                    fidx = cc * 4 + f
                    si = cn["sg"] % 2; cn["sg"] += 1
                    for gi in range(2):
                        S.dma("sp", [("SGT", gi, fidx * 128, tt)], [(f"sgt{si}", gi)], out=sgt[si][:, gi, :],
                              in_=SGT[gi, fidx * 128:(fidx + 1) * 128, tok0:tok0 + 512])
                    pA, pAk = nextbank()
                    pB, pBk = nextbank()
                    for k in range(8):
                        S.op("pe", [(wk, 0), ("aT", k)], [pAk], "matmul", pA[:, :], lhsT=w[:, k, f * 128:(f + 1) * 128], rhs=aT[:, k, :],
                             start=(k == 0), stop=(k == 7))
                    for k in range(8):
                        S.op("pe", [(wk, 1), ("aT", 8 + k)], [pBk], "matmul", pB[:, :], lhsT=w[:, 8 + k, f * 128:(f + 1) * 128],
                             rhs=aT[:, 8 + k, :], start=(k == 0), stop=(k == 7))
                    S.op("dve", [pAk, (f"sgt{si}", 0)], ["gt0"], "tensor_tensor", out=t1[:], in0=pA[:, :], in1=sgt[si][:, 0, :], op=ALU.mult)
                    S.op("dve", [pBk, (f"sgt{si}", 1)], ["gt1"], "tensor_tensor", out=t2[:], in0=pB[:, :], in1=sgt[si][:, 1, :], op=ALU.mult)
                    S.op("pool", ["gt0", "gt1"], [("aT", 16 + fidx)], "tensor_tensor", out=aT[:, 16 + fidx, :], in0=t1[:], in1=t2[:], op=ALU.add)
            S.dma("sp", [], ["gvec"], out=gvec[:], in_=g_("norm_mlp").broadcast_to([128, D]))

            def norm1(blk):
                j = cn["h"] % 2; cn["h"] += 1
                hb2 = hb2s[j]; hbk = f"hb2_{j}"
                xk = [("x1", blk, c) for c in range(4)]
                S.op("act", xk, [hbk, ("ss1", blk)], "activation", out=hb2[:], in_=x1[:, blk, :], func=AF.Square, accum_out=ss1[:, blk:blk + 1])
                rstd_from_ss(ss1[:, blk:blk + 1], rs1[:, blk:blk + 1], 1, [("ss1", blk)], [("rs1", blk)], 1.0 / D)
                S.op("dve", xk + [("rs1", blk), "gvec"], [hbk], "scalar_tensor_tensor", out=hb2[:], in0=x1[:, blk, :],
                     scalar=rs1[:, blk:blk + 1], in1=gvec[:], op0=ALU.mult, op1=ALU.mult)
                to_hT2(blk, hb2, hbk)

            for cc in range(4):
                w, wk = nextw()
                S.dma("sp", [("WOUT", r) for r in range(0, D, 256)], [(wk, 0), (wk, 1)], out=w[:],
                      in_=WOUT[:, cc * 512:(cc + 1) * 512].rearrange("(k p) c -> p k c", p=128))
                for blk in range(4):
                    pb, pk = nextbank()
                    for k in range(16):
                        S.op("pe", [(wk, 0), (wk, 1), ("aT", 16 + k)], [pk], "matmul", pb[:, :], lhsT=aT[:, 16 + k, blk * 128:(blk + 1) * 128],
                             rhs=w[:, k, :], start=(k == 0), stop=(k == 15))
                    xi = cn["x"] % 2; cn["x"] += 1
                    S.dma("sp", [], [f"xin{xi}"], out=xin[xi][:], in_=x_own[tok0 + blk * 128:tok0 + (blk + 1) * 128, cc * 512:(cc + 1) * 512])
                    S.op("dve", [pk, f"xin{xi}"], [("x1", blk, cc)], "tensor_tensor", out=x1[:, blk, cc * 512:(cc + 1) * 512], in0=pb[:, :],
                         in1=xin[xi][:], op=ALU.add)
                    if cc == 3 and blk >= 1:
                        norm1(blk - 1)
            norm1(3)
            for uc in range(16):
                w, wk = nextw()
                S.dma("sp", [("WUP", r) for r in range(0, D, 256)], [(wk, 0), (wk, 1)], out=w[:],
                      in_=WUP[:, uc * 512:(uc + 1) * 512].rearrange("(k p) c -> p k c", p=128))
                for f in range(4):
                    pb, pk = nextbank()
                    for k in range(16):
                        S.op("pe", [(wk, 0), (wk, 1)] + hT2keys, [pk], "matmul", pb[:, :], lhsT=w[:, k, f * 128:(f + 1) * 128], rhs=hT2[:, k, :],
                             start=(k == 0), stop=(k == 15))
                    ri = cn["r"] % 2; cn["r"] += 1
                    S.op("act", [pk], [f"rl{ri}"], "activation", out=rl[ri][:], in_=pb[:, :], func=AF.Relu)
                    S.op("pool", [f"rl{ri}"], [("aT", uc * 4 + f)], "tensor_tensor", out=aT[:, uc * 4 + f, :], in0=rl[ri][:], in1=rl[ri][:], op=ALU.mult)
            for fc in range(4):
                base = 0 if fc % 2 == 0 else 4
                for kg in range(4):
                    w, wk = nextw()
                    S.dma("sp", [("WDN", r) for r in range(0, DFF, 1024)], [(wk, 0), (wk, 1)], out=w[:],
                          in_=WDN[kg * 2048:(kg + 1) * 2048, fc * 512:(fc + 1) * 512].rearrange("(k p) c -> p k c", p=128))
                    for k in range(16):
                        ffc = kg * 16 + k
                        for blk in range(4):
                            pb, pk = bank(base + blk)
                            S.op("pe", [(wk, 0), (wk, 1), ("aT", ffc)], [pk], "matmul", pb[:, :], lhsT=aT[:, ffc, blk * 128:(blk + 1) * 128],
                                 rhs=w[:, k, :], start=(ffc == 0), stop=(ffc == 63))
                for blk in range(4):
                    pb, pk = bank(base + blk)
                    S.op("dve", [pk, ("x1", blk, fc)], [("x1", blk, fc)], "tensor_tensor", out=x1[:, blk, fc * 512:(fc + 1) * 512], in0=pb[:, :],
                         in1=x1[:, blk, fc * 512:(fc + 1) * 512], op=ALU.add)
            for blk in range(4):
                S.dma("sp", [], ["pt"], out=pt[:], in_=p_own[tok0 + blk * 128:tok0 + (blk + 1) * 128, :])
                S.op("dve", ["pt"], ["ptb"], "tensor_copy", out=ptb[:], in_=pt[:])
                pb, pk = bank_bf(6)
                for k in range(2):
                    S.op("pe", ["ptb", "ident"], [pk], "transpose", pb[:, k * 128:(k + 1) * 128], ptb[:, k * 128:(k + 1) * 128], ident[:])
                S.op("act", [pk], [("pT", blk)], "activation", out=pT[:, :, blk * 128:(blk + 1) * 128],
                     in_=pb[:, 0:256].rearrange("p (k t) -> p k t", k=2), func=AF.Copy)
            for blk in range(4):
                for cc in range(4):
                    pb, pk = bank(4 + cn["e"] % 2); cn["e"] += 1
                    for k in range(2):
                        S.op("pe", [("pT", blk), "wpp"], [pk], "matmul", pb[:, :], lhsT=pT[:, k, blk * 128:(blk + 1) * 128],
                             rhs=wpp[:, k, cc * 512:(cc + 1) * 512], start=(k == 0), stop=(k == 1))
                    gj = cn["g"] % 2; cn["g"] += 1
                    S.op("act", [pk], [f"gt{gj}", ("ssE", blk * 4 + cc)], "activation", out=gt[gj][:], in_=pb[:, :], func=AF.Square,
                         accum_out=ssE[:, blk * 4 + cc:blk * 4 + cc + 1])
            S.op("dve", [("ssE", i) for i in range(16)], ["ssE4"], "tensor_reduce", out=ssE4[:], in_=ssE[:].rearrange("p (b c) -> p b c", c=4),
                 axis=AX.X, op=ALU.add)
            rstd_from_ss(ssE4[:], rsE[:], 4, ["ssE4"], ["rsE"], 1.0 / D)
            for blk in range(4):
                j = cn["h"] % 2; cn["h"] += 1
                hb2 = hb2s[j]; hbk = f"hb2_{j}"
                xk = [("x1", blk, c) for c in range(4)]
                S.op("act", xk, [hbk, ("ss1", 4 + blk)], "activation", out=hb2[:], in_=x1[:, blk, :], func=AF.Square,
                     accum_out=ss1[:, 4 + blk:5 + blk])
                rstd_from_ss(ss1[:, 4 + blk:5 + blk], rs1[:, 4 + blk:5 + blk], 1, [("ss1", 4 + blk)], [("rs1", 4 + blk)], 1.0 / D)
                S.op("dve", xk + [("rs1", 4 + blk)], [hbk], "tensor_scalar", out=hb2[:], in0=x1[:, blk, :], scalar1=rs1[:, 4 + blk:5 + blk],
                     scalar2=None, op0=ALU.mult)
                to_hT2(blk, hb2, hbk)
            S.dma("sp", [], ["gvec"], out=gvec[:], in_=g_("norm_ple").broadcast_to([128, D]))
            for cc in range(4):
                w, wk = nextw()
                S.dma("sp", [("WPG", r) for r in range(0, D, 256)], [(wk, 0), (wk, 1)], out=w[:],
                      in_=WPG[:, cc * 512:(cc + 1) * 512].rearrange("(k p) c -> p k c", p=128))
                for blk in range(4):
                    pg, pgk = nextbank()
                    for k in range(16):
                        S.op("pe", [(wk, 0), (wk, 1), ("hT2", blk, 0), ("hT2", blk, 1)], [pgk], "matmul", pg[:, :],
                             lhsT=hT2[:, k, blk * 128:(blk + 1) * 128], rhs=w[:, k, :], start=(k == 0), stop=(k == 15))
                    pe2, pe2k = bank(4 + cn["e"] % 2); cn["e"] += 1
                    for k in range(2):
                        S.op("pe", [("pT", blk), "wpp"], [pe2k], "matmul", pe2[:, :], lhsT=pT[:, k, blk * 128:(blk + 1) * 128],
                             rhs=wpp[:, k, cc * 512:(cc + 1) * 512], start=(k == 0), stop=(k == 1))
                    gi = cn["g"] % 2; cn["g"] += 1
                    S.op("act", [pgk], [f"gt{gi}"], "activation", out=gt[gi][:], in_=pg[:, :], func=AF.Sigmoid)
                    S.op("dve", [pe2k, "rsE", "gvec"], [f"ev{gi}"], "scalar_tensor_tensor", out=ev[gi][:], in0=pe2[:, :], scalar=rsE[:, blk:blk + 1],
                         in1=gvec[:, cc * 512:(cc + 1) * 512], op0=ALU.mult, op1=ALU.mult)
                    S.op("pool", [f"ev{gi}", f"gt{gi}"], [f"ev{gi}"], "tensor_tensor", out=ev[gi][:], in0=ev[gi][:], in1=gt[gi][:], op=ALU.mult)
                    S.op("pool", [f"ev{gi}", ("x1", blk, cc)], [("x1", blk, cc)], "tensor_tensor", out=x1[:, blk, cc * 512:(cc + 1) * 512],
                         in0=x1[:, blk, cc * 512:(cc + 1) * 512], in1=ev[gi][:], op=ALU.add)
            for blk in range(4):
                S.dma("sp", [("x1", blk, c) for c in range(4)], [("out", tt, blk)], out=out_d[tok0 + blk * 128:tok0 + (blk + 1) * 128, :],
                      in_=x1[:, blk, :])
        S.barrier()


def _consts(hf):
    bf = ml_dtypes.bfloat16
    c = {}
    invf = np.power(np.float32(500000.0), -np.arange(0, 16, 2, dtype=np.float32) / np.float32(16)).astype(np.float32)
    c["c_invf"] = np.ascontiguousarray(np.broadcast_to(invf[None, :], (128, 8))).astype(np.float32)
    k = np.arange(128)[:, None, None]
    r = np.arange(8)[None, :, None]
    qq = np.arange(512)[None, None, :]
    t = qq // 128
    qpos = (2 * t + ((t & 1) ^ hf)) * 128 + (qq % 128)
    c["c_dmask"] = ((r * 128 + k) <= qpos).astype(bf)
    m128 = np.zeros((128, 2, 8, 128), np.float32)
    kk = np.arange(128)[:, None]
    mq = np.arange(128)[None, :]
    for jp in range(2):
        p = jp ^ hf
        q = p * 128 + mq
        for idx in range(6):
            key = (idx - 4) * 128 + kk
            dist = q - key
            m128[:, jp, idx, :] = ((dist >= 0) & (dist < 512))
        for idx in range(6, 8):
            key = (idx - 6) * 128 + kk
            m128[:, jp, idx, :] = (key <= q)
    c["c_m128"] = m128.astype(bf)
    mc = np.zeros((128, 16, 2, 128), np.float32)
    valid = np.zeros((128, 16, 64), np.float32)
    addc = np.zeros((128, 16, 64), np.float32)
    sel = np.arange(64)[None, :]
    for j in range(16):
        qp = own_block(j, hf) * 128 + np.arange(128)
        for ch in range(2):
            ng = ch * 128 + np.arange(128)
            mc[:, j, ch, :] = ((ng[:, None] <= 254) & (16 * ng[:, None] + 31 <= qp[None, :]))
        qb = (qp // 64)[:, None]
        v = sel <= qb
        f = (sel == 0) | (sel == qb) | (sel == qb - 1)
        valid[:, j, :] = v
        addc[:, j, :] = np.where(v, 1e4 * f, -1.0)
    c["c_mc"] = mc.astype(bf)
    c["c_valid"] = valid
    c["c_addc"] = addc.astype(np.float32)
    ovl = np.zeros((128, 2, 64), np.float32)
    for ch in range(2):
        ng = ch * 128 + np.arange(128)
        cs = 16 * ng[:, None]
        ssb = 64 * np.arange(64)[None, :]
        ovl[:, ch, :] = ((cs < ssb + 64) & (cs + 32 > ssb) & (ng[:, None] <= 254))
    c["c_ovl"] = ovl.astype(bf)
    jj = np.arange(64)[:, None, None]
    kb = np.arange(32)[None, :, None]
    k2 = np.arange(128)[None, None, :]
    c["c_eexp"] = (jj == 2 * kb + k2 // 64).astype(bf)
    return c


def make_in_maps(inputs):
    f = lambda a: np.ascontiguousarray(np.asarray(a))
    x = f(inputs["x"]); p = f(inputs["p"])[0]; pos = f(inputs["positions"]).astype(np.int32)
    shared = {
        "norm_mix": f(inputs["norm_mix"]).reshape(1, D),
        "w_in": f(inputs["w_in"])[0],
        "diff_q_norm": f(inputs["diff_q_norm"]).reshape(1, 64),
        "diff_k_norm": f(inputs["diff_k_norm"]).reshape(1, 64),
        "diff_lambda": f(inputs["diff_lambda"]).reshape(1, 256),
        "diff_subln": f(inputs["diff_subln"]).reshape(1, 128),
        "nsa_q_norm": f(inputs["nsa_q_norm"]).reshape(1, 64),
        "nsa_k_norm": f(inputs["nsa_k_norm"]).reshape(1, 64),
        "cmp_posT": f(np.transpose(f(inputs["cmp_pos"])[0], (2, 0, 1))),
        "cmp_w1": f(inputs["cmp_w1"])[0].reshape(4096, 256),
        "cmp_w2": f(inputs["cmp_w2"])[0].reshape(512, 64),
        "w_proj_diff": f(inputs["w_proj_diff"])[0],
        "w_proj_nsa": f(inputs["w_proj_nsa"])[0],
        "w_out": f(inputs["w_out"])[0],
        "norm_mlp": f(inputs["norm_mlp"]).reshape(1, D),
        "w_mlp_up": f(inputs["w_mlp_up"])[0],
        "w_mlp_down": f(inputs["w_mlp_down"])[0],
        "w_ple_proj": f(inputs["w_ple_proj"])[0],
        "norm_ple": f(inputs["norm_ple"]).reshape(1, D),
        "w_ple_gate": f(inputs["w_ple_gate"])[0],
    }
    cst = [_consts(0), _consts(1)]
    maps = []
    for c in range(8):
        b, hf = c // 2, c % 2
        blks = [own_block(j, hf) for j in range(16)]
        rows = np.concatenate([np.arange(bk * 128, (bk + 1) * 128) for bk in blks])
        m = dict(shared)
        m.update(cst[hf])
        m["x_all"] = x[b]
        m["x_own"] = f(x[b][rows])
        m["p_own"] = f(p[b][rows])
        m["posT_all"] = f(pos[b].reshape(32, 128).T)
        m["posT_own"] = f(pos[b][rows].reshape(16, 128).T)
        maps.append(m)
    return maps


def assemble(outs):
    res = np.zeros((4, SEQ, D), np.float32)
    for c in range(8):
        b, hf = c // 2, c % 2
        o = np.asarray(outs[c])
        for j in range(16):
            bk = own_block(j, hf)
            res[b, bk * 128:(bk + 1) * 128] = o[j * 128:(j + 1) * 128]
    return res


def kernel(**inputs):
    nc = build_nc()
    maps = make_in_maps(inputs)
    r = run_bass_kernel_spmd(nc, maps, core_ids=list(range(8)))
    return assemble([r.results[c]["out"] for c in range(8)])
```
